# Optimizing a Trainium2 kernel written in Bass

```python
import math
import jax
import jax.numpy as jnp
from jax import lax
import numpy as np

D_MODEL = 1024
BATCH = 4
SEQ = 8192
DEPTH = 2

SSD_HEADS = 8
SSD_HEAD_DIM = 64
SSD_INNER = SSD_HEADS * SSD_HEAD_DIM
SSD_GROUPS = 2
SSD_STATE = 64
SSD_CONV = 5
SSD_CHUNK = 128
SSD_CONV_CH = SSD_INNER + 2 * SSD_GROUPS * SSD_STATE
S5_GROUP_CH = 16
S5_GROUPS = 24
S5_WIDTH = S5_GROUPS * S5_GROUP_CH
S5_STATE = 64
S5_MAX_RE = -1e-4
RET_HEADS = 8
RET_HEAD_DIM = 64
RET_WIDTH = RET_HEADS * RET_HEAD_DIM
RET_CHUNK = 128
ROPE_BASE = 10000.0
N_BRANCH = 3
D_FF = 2816
EPS = 1e-6
IN_PROJ_SIZES = (SSD_INNER, SSD_CONV_CH, 2 * SSD_HEADS, S5_WIDTH,
                 RET_WIDTH, RET_WIDTH, RET_WIDTH, RET_WIDTH, N_BRANCH * D_MODEL)
D_IN_PROJ = sum(IN_PROJ_SIZES)

kernel_name = 'hybrid_ssd_s5_retention_macaron_encoder'


def rmsnorm(x, g):
    xf = x.astype(jnp.float32)
    y = xf * lax.rsqrt(jnp.mean(xf * xf, axis=-1, keepdims=True) + EPS)
    return (y * g.astype(jnp.float32)).astype(x.dtype)


def swiglu(x, w_gate, w_up, w_down):
    return (jax.nn.silu(x @ w_gate) * (x @ w_up)) @ w_down


def dwconv_centered(x, w, b):
    k, c = w.shape
    pad = k // 2
    y = lax.conv_general_dilated(x, w[:, None, :], window_strides=(1,), padding=[(pad, pad)],
                                 dimension_numbers=('NWC', 'WIO', 'NWC'), feature_group_count=c)
    return y + b


def segsum_exp(a):
    t = a.shape[-1]
    xa = jnp.broadcast_to(a[..., :, None], a.shape + (t,))
    xa = jnp.where(jnp.tril(jnp.ones((t, t), bool), -1), xa, 0.0)
    ss = jnp.cumsum(xa, axis=-2)
    return jnp.exp(jnp.where(jnp.tril(jnp.ones((t, t), bool)), ss, -jnp.inf))


def ssd_causal(x, dt, a, bm, cm):
    b, s, h, p = x.shape
    n = bm.shape[-1]
    L = SSD_CHUNK
    c = s // L
    xd = (x * dt[..., None]).reshape(b, c, L, h, p)
    da = (dt * a).reshape(b, c, L, h).transpose(0, 3, 1, 2)
    bm = bm.reshape(b, c, L, h, n)
    cm = cm.reshape(b, c, L, h, n)
    da_cs = jnp.cumsum(da, axis=-1)
    scores = jnp.einsum('bclhn,bcshn->bhcls', cm, bm) * segsum_exp(da)
    y_diag = jnp.einsum('bhcls,bcshp->bclhp', scores, xd)
    decay_states = jnp.exp(da_cs[..., -1:] - da_cs).transpose(0, 2, 3, 1)
    states = jnp.einsum('bclhn,bclhp->bchpn', bm * decay_states[..., None], xd)
    chunk_decay = segsum_exp(jnp.pad(da_cs[..., -1], ((0, 0), (0, 0), (1, 0))))
    states = jnp.concatenate([jnp.zeros_like(states[:, :1]), states], axis=1)
    states = jnp.einsum('bhzc,bchpn->bzhpn', chunk_decay, states)[:, :-1]
    out_decay = jnp.exp(da_cs).transpose(0, 2, 3, 1)
    y_off = jnp.einsum('bclhn,bchpn->bclhp', cm, states) * out_decay[..., None]
    return (y_diag + y_off).reshape(b, s, h, p)


def ssd_branch(z, xbc, dt_raw, conv_w, conv_b, dt_bias, a_log, d_skip, norm_g):
    b, s, _ = z.shape
    f32 = jnp.float32
    xbc = jax.nn.silu(dwconv_centered(xbc, conv_w, conv_b))
    xs, bm, cm = jnp.split(xbc, [SSD_INNER, SSD_INNER + SSD_GROUPS * SSD_STATE], axis=-1)
    rep = SSD_HEADS // SSD_GROUPS
    xs = xs.astype(f32).reshape(b, s, SSD_HEADS, SSD_HEAD_DIM)
    bm = jnp.repeat(bm.astype(f32).reshape(b, s, SSD_GROUPS, SSD_STATE), rep, axis=2)
    cm = jnp.repeat(cm.astype(f32).reshape(b, s, SSD_GROUPS, SSD_STATE), rep, axis=2)
    dt = jax.nn.softplus(dt_raw.astype(f32).reshape(b, s, 2, SSD_HEADS) + dt_bias.astype(f32))
    a = -jnp.exp(a_log.astype(f32))
    flip = lambda t: jnp.flip(t, axis=1)
    y_fwd = ssd_causal(xs, dt[:, :, 0], a[0], bm, cm)
    y_bwd = flip(ssd_causal(flip(xs), flip(dt[:, :, 1]), a[1], flip(bm), flip(cm)))
    y = y_fwd + y_bwd + d_skip.astype(f32)[:, None] * xs
    y = y.reshape(b, s, SSD_INNER).astype(z.dtype)
    return rmsnorm(y * jax.nn.silu(z), norm_g)


def s5_direction(u, lam_re, lam_im, log_step, b_re, b_im, reverse):
    f32 = jnp.float32
    lam_re = jnp.minimum(lam_re.astype(f32), S5_MAX_RE)
    lam_im = lam_im.astype(f32)
    step = jnp.exp(log_step.astype(f32))[:, None]
    mag = jnp.exp(lam_re * step)
    ang = lam_im * step
    lb_re = mag * jnp.cos(ang)
    lb_im = mag * jnp.sin(ang)
    den = lam_re * lam_re + lam_im * lam_im
    nr = lb_re - 1.0
    coef_re = ((nr * lam_re + lb_im * lam_im) / den)[..., None]
    coef_im = ((lb_im * lam_re - nr * lam_im) / den)[..., None]
    b_re = b_re.astype(f32)
    b_im = b_im.astype(f32)
    bb_re = coef_re * b_re - coef_im * b_im
    bb_im = coef_re * b_im + coef_im * b_re
    bu_re = jnp.einsum('gph,bsgh->bsgp', bb_re, u)
    bu_im = jnp.einsum('gph,bsgh->bsgp', bb_im, u)
    s = u.shape[1]
    a_re = jnp.broadcast_to(lb_re, (1, s) + lb_re.shape)
    a_im = jnp.broadcast_to(lb_im, (1, s) + lb_im.shape)

    def combine(e1, e2):
        a1r, a1i, b1r, b1i = e1
        a2r, a2i, b2r, b2i = e2
        return (a2r * a1r - a2i * a1i,
                a2r * a1i + a2i * a1r,
                a2r * b1r - a2i * b1i + b2r,
                a2r * b1i + a2i * b1r + b2i)

    _, _, h_re, h_im = lax.associative_scan(combine, (a_re, a_im, bu_re, bu_im), reverse=reverse, axis=1)
    return h_re, h_im


def s5_branch(u, lam_re, lam_im, log_step, b_re, b_im, c_re, c_im, d_s5, glu_wv, glu_wg):
    b, s, _ = u.shape
    f32 = jnp.float32
    uf = u.astype(f32).reshape(b, s, S5_GROUPS, S5_GROUP_CH)
    y = d_s5.astype(f32) * uf
    for direction, rev in ((0, False), (1, True)):
        h_re, h_im = s5_direction(uf, lam_re[direction], lam_im[direction], log_step[direction],
                                  b_re, b_im, rev)
        y = y + jnp.einsum('ghp,bsgp->bsgh', c_re[direction].astype(f32), h_re) \
              - jnp.einsum('ghp,bsgp->bsgh', c_im[direction].astype(f32), h_im)
    y = jax.nn.gelu(y.reshape(b, s, S5_WIDTH)).astype(u.dtype)
    return (y @ glu_wv) * jax.nn.sigmoid(y @ glu_wg)


def rotary(t, cos, sin):
    t1, t2 = jnp.split(t, 2, axis=-1)
    return jnp.concatenate([t1 * cos - t2 * sin, t1 * sin + t2 * cos], axis=-1)


def retention_direction(q, k, v, log_gamma, inclusive):
    b, s, h, d = q.shape
    L = RET_CHUNK
    c = s // L
    q = q.reshape(b, c, L, h, d)
    k = k.reshape(b, c, L, h, d)
    v = v.reshape(b, c, L, h, -1)
    idx = jnp.arange(L, dtype=jnp.float32)
    rel = idx[:, None] - idx[None, :]
    mask = jnp.tril(jnp.ones((L, L), bool), 0 if inclusive else -1)
    intra_decay = jnp.where(mask, jnp.exp(log_gamma[:, None, None] * jnp.where(mask, rel, 0.0)), 0.0)
    scores = jnp.einsum('bclhd,bcshd->bhcls', q, k) * intra_decay[:, None]
    y_intra = jnp.einsum('bhcls,bcshe->bclhe', scores, v)
    k_decay = jnp.exp(log_gamma[None, :] * (L - 1.0 - idx)[:, None])
    r = jnp.einsum('bclhd,bclhe->bchde', k * k_decay[:, :, None], v)
    ci = jnp.arange(c, dtype=jnp.float32)
    crel = ci[:, None] - ci[None, :] - 1.0
    cmask = jnp.tril(jnp.ones((c, c), bool), -1)
    chunk_decay = jnp.where(cmask, jnp.exp(log_gamma[:, None, None] * L * jnp.where(cmask, crel, 0.0)), 0.0)
    st = jnp.einsum('hij,bjhde->bihde', chunk_decay, r)
    q_decay = jnp.exp(log_gamma[None, :] * (idx + 1.0)[:, None])
    y_inter = jnp.einsum('bclhd,bchde->bclhe', q * q_decay[:, :, None], st)
    return (y_intra + y_inter).reshape(b, s, h, -1)


def retention_branch(q, k, v, g, norm_g):
    b, s, _ = q.shape
    f32 = jnp.float32
    shape = (b, s, RET_HEADS, RET_HEAD_DIM)
    pos = jnp.arange(s, dtype=f32)
    inv_freq = ROPE_BASE ** (-jnp.arange(0, RET_HEAD_DIM, 2, dtype=f32) / RET_HEAD_DIM)
    ang = pos[:, None] * inv_freq[None, :]
    cos = jnp.cos(ang)[None, :, None, :]
    sin = jnp.sin(ang)[None, :, None, :]
    qf = rotary(q.astype(f32).reshape(shape), cos, sin)
    kf = rotary(k.astype(f32).reshape(shape), cos, sin) * (RET_HEAD_DIM ** -0.5)
    vf = v.astype(f32).reshape(shape)
    log_gamma = jnp.log1p(-jnp.exp2(-5.0 - jnp.arange(RET_HEADS, dtype=f32)))
    flip = lambda t: jnp.flip(t, axis=1)
    y = retention_direction(qf, kf, vf, log_gamma, True) \
        + flip(retention_direction(flip(qf), flip(kf), flip(vf), log_gamma, False))
    y = y * lax.rsqrt(jnp.mean(y * y, axis=-1, keepdims=True) + EPS)
    y = y.reshape(b, s, RET_WIDTH) * norm_g.astype(f32)
    return (jax.nn.silu(g.astype(f32)) * y).astype(g.dtype)


def setup_inputs(seed: int = 0) -> dict:
    key = jax.random.key(seed)
    ks = iter(jax.random.split(key, 48))
    f32 = jnp.float32
    L = DEPTH
    D = D_MODEL

    def nrm(shape, scale):
        return scale * jax.random.normal(next(ks), shape, f32)

    def gain(shape):
        return 1.0 + nrm(shape, 0.02)

    x = jax.random.normal(next(ks), (BATCH, SEQ, D), f32)
    ffn1_norm = gain((L, D))
    ffn1_w_gate = nrm((L, D, D_FF), D ** -0.5)
    ffn1_w_up = nrm((L, D, D_FF), D ** -0.5)
    ffn1_w_down = nrm((L, D_FF, D), D_FF ** -0.5)
    mix_norm = gain((L, D))
    w_in = nrm((L, D, D_IN_PROJ), D ** -0.5)
    b_gate = nrm((L, N_BRANCH * D), 0.02)
    ssd_conv_w = nrm((L, SSD_CONV, SSD_CONV_CH), SSD_CONV ** -0.5)
    ssd_conv_b = nrm((L, SSD_CONV_CH), 0.02)
    dt0 = jnp.exp(jax.random.uniform(next(ks), (L, 2, SSD_HEADS), f32, math.log(1e-3), math.log(1e-1)))
    ssd_dt_bias = dt0 + jnp.log(-jnp.expm1(-dt0))
    ssd_a_log = jnp.log(jax.random.uniform(next(ks), (L, 2, SSD_HEADS), f32, 1.0, 16.0))
    ssd_d = 1.0 + nrm((L, SSD_HEADS), 0.1)
    ssd_norm = gain((L, SSD_INNER))
    w_br_ssd = nrm((L, SSD_INNER, D), SSD_INNER ** -0.5)
    s5_lam_re = -0.5 + nrm((L, 2, S5_GROUPS, S5_STATE), 0.01)
    s5_lam_im = jnp.pi * jnp.arange(S5_STATE, dtype=f32) + nrm((L, 2, S5_GROUPS, S5_STATE), 0.01)
    s5_log_step = jax.random.uniform(next(ks), (L, 2, S5_GROUPS), f32, math.log(1e-3), math.log(1e-1))
    s5_b_re = nrm((L, S5_GROUPS, S5_STATE, S5_GROUP_CH), (2 * S5_GROUP_CH) ** -0.5)
    s5_b_im = nrm((L, S5_GROUPS, S5_STATE, S5_GROUP_CH), (2 * S5_GROUP_CH) ** -0.5)
    s5_c_re = nrm((L, 2, S5_GROUPS, S5_GROUP_CH, S5_STATE), (2 * S5_STATE) ** -0.5)
    s5_c_im = nrm((L, 2, S5_GROUPS, S5_GROUP_CH, S5_STATE), (2 * S5_STATE) ** -0.5)
    s5_d = nrm((L, S5_GROUPS, S5_GROUP_CH), 1.0)
    s5_glu_wv = nrm((L, S5_WIDTH, S5_WIDTH), S5_WIDTH ** -0.5)
    s5_glu_wg = nrm((L, S5_WIDTH, S5_WIDTH), S5_WIDTH ** -0.5)
    w_br_s5 = nrm((L, S5_WIDTH, D), S5_WIDTH ** -0.5)
    ret_norm = gain((L, RET_WIDTH))
    w_br_ret = nrm((L, RET_WIDTH, D), RET_WIDTH ** -0.5)
    w_out = nrm((L, D, D), D ** -0.5)
    ffn2_norm = gain((L, D))
    ffn2_w_gate = nrm((L, D, D_FF), D ** -0.5)
    ffn2_w_up = nrm((L, D, D_FF), D ** -0.5)
    ffn2_w_down = nrm((L, D_FF, D), D_FF ** -0.5)
    final_norm = gain((D,))
    return {'x': x,
            'ffn1_norm': ffn1_norm, 'ffn1_w_gate': ffn1_w_gate, 'ffn1_w_up': ffn1_w_up, 'ffn1_w_down': ffn1_w_down,
            'mix_norm': mix_norm, 'w_in': w_in, 'b_gate': b_gate,
            'ssd_conv_w': ssd_conv_w, 'ssd_conv_b': ssd_conv_b, 'ssd_dt_bias': ssd_dt_bias,
            'ssd_a_log': ssd_a_log, 'ssd_d': ssd_d, 'ssd_norm': ssd_norm, 'w_br_ssd': w_br_ssd,
            's5_lam_re': s5_lam_re, 's5_lam_im': s5_lam_im, 's5_log_step': s5_log_step,
            's5_b_re': s5_b_re, 's5_b_im': s5_b_im, 's5_c_re': s5_c_re, 's5_c_im': s5_c_im,
            's5_d': s5_d, 's5_glu_wv': s5_glu_wv, 's5_glu_wg': s5_glu_wg, 'w_br_s5': w_br_s5,
            'ret_norm': ret_norm, 'w_br_ret': w_br_ret,
            'w_out': w_out,
            'ffn2_norm': ffn2_norm, 'ffn2_w_gate': ffn2_w_gate, 'ffn2_w_up': ffn2_w_up, 'ffn2_w_down': ffn2_w_down,
            'final_norm': final_norm}


def reference(x,
              ffn1_norm, ffn1_w_gate, ffn1_w_up, ffn1_w_down,
              mix_norm, w_in, b_gate,
              ssd_conv_w, ssd_conv_b, ssd_dt_bias, ssd_a_log, ssd_d, ssd_norm, w_br_ssd,
              s5_lam_re, s5_lam_im, s5_log_step, s5_b_re, s5_b_im, s5_c_re, s5_c_im,
              s5_d, s5_glu_wv, s5_glu_wg, w_br_s5,
              ret_norm, w_br_ret,
              w_out,
              ffn2_norm, ffn2_w_gate, ffn2_w_up, ffn2_w_down,
              final_norm):
    b, s, d = x.shape
    split_pts = np.cumsum(IN_PROJ_SIZES)[:-1].tolist()
    for i in range(DEPTH):
        x = x + 0.5 * swiglu(rmsnorm(x, ffn1_norm[i]), ffn1_w_gate[i], ffn1_w_up[i], ffn1_w_down[i])
        h = rmsnorm(x, mix_norm[i])
        proj = h @ w_in[i]
        z, xbc, dt_raw, u, q, k, v, g, gate_logits = jnp.split(proj, split_pts, axis=-1)
        y_ssd = ssd_branch(z, xbc, dt_raw, ssd_conv_w[i], ssd_conv_b[i], ssd_dt_bias[i],
                           ssd_a_log[i], ssd_d[i], ssd_norm[i]) @ w_br_ssd[i]
        y_s5 = s5_branch(u, s5_lam_re[i], s5_lam_im[i], s5_log_step[i], s5_b_re[i], s5_b_im[i],
                         s5_c_re[i], s5_c_im[i], s5_d[i], s5_glu_wv[i], s5_glu_wg[i]) @ w_br_s5[i]
        y_ret = retention_branch(q, k, v, g, ret_norm[i]) @ w_br_ret[i]
        gates = jax.nn.sigmoid(gate_logits + b_gate[i]).reshape(b, s, N_BRANCH, d)
        mixed = gates[:, :, 0] * y_ssd + gates[:, :, 1] * y_s5 + gates[:, :, 2] * y_ret
        x = x + mixed @ w_out[i]
        x = x + 0.5 * swiglu(rmsnorm(x, ffn2_norm[i]), ffn2_w_gate[i], ffn2_w_up[i], ffn2_w_down[i])
    return rmsnorm(x, final_norm)
```

```python
import math
from contextlib import ExitStack
import numpy as np
import concourse.bass as bass
import concourse.mybir as mybir
from concourse.bass_utils import run_bass_kernel_spmd

F32 = mybir.dt.float32
F32R = mybir.dt.float32r
ALU = mybir.AluOpType
AF = mybir.ActivationFunctionType
AX = mybir.AxisListType

ENGS = ("pe", "dve", "act", "pool", "sp")
N_DMA_SEMS = 12
D = 1024
DFF = 2816
NFC = 22
EPS = 1e-6
NTM = 2576
NFM = 33


class Buf:
    __slots__ = ("name", "w", "r")

    def __init__(self, name=""):
        self.name = name
        self.w = None
        self.r = []


class Prog:
    def __init__(self, nc):
        self.nc = nc
        self.ops = []
        self.by_eng = {e: [] for e in ENGS}
        self.fence_deps = set()
        self.fence_pending = {e: False for e in ENGS}
        self.since_fence = []
        self.trace = None

    def op(self, eng, fn, reads=(), writes=(), dma=False):
        oid = len(self.ops)
        deps = set()
        for b in reads:
            if b.w is not None:
                deps.add(b.w)
        for b in writes:
            if b.w is not None:
                deps.add(b.w)
            deps.update(b.r)
        for b in reads:
            if not dma:
                b.r = [r for r in b.r if self.ops[r]["dma"] or self.ops[r]["eng"] != eng]
            b.r.append(oid)
        for b in writes:
            b.w = oid
            b.r = []
        if self.fence_pending[eng]:
            deps.update(self.fence_deps)
            self.fence_pending[eng] = False
        deps.discard(oid)
        import sys as _s
        fr = _s._getframe(1)
        ln = []
        while fr is not None and len(ln) < 4:
            ln.append(fr.f_lineno)
            fr = fr.f_back
        self.ops.append(dict(eng=eng, fn=fn, deps=deps, dma=dma, id=oid, ln=ln))
        self.by_eng[eng].append(oid)
        self.since_fence.append(oid)
        return oid

    def fence(self):
        last = {}
        deps = set()
        for oid in self.since_fence:
            o = self.ops[oid]
            if o["dma"]:
                deps.add(oid)
            else:
                last[o["eng"]] = oid
        deps.update(last.values())
        for e in ENGS:
            if self.fence_pending[e]:
                deps.update(self.fence_deps)
                break
        self.fence_deps = deps
        self.fence_pending = {e: True for e in ENGS}
        self.since_fence = []

    def emit(self):
        nc = self.nc
        ops = self.ops
        signaled = set()
        for o in ops:
            for d in o["deps"]:
                if o["eng"] == "pe" and ops[d]["eng"] == "pe" and not ops[d]["dma"] and not o["dma"]:
                    continue
                signaled.add(d)
        eng_cnt = {e: 0 for e in ENGS}
        dma_rr = {e: 0 for e in ENGS}
        dma_cnt = {e: [0] * N_DMA_SEMS for e in ENGS}
        tokens = {}
        dma_prev = {}
        for o in ops:
            e = o["eng"]
            if o["dma"]:
                i = dma_rr[e] % N_DMA_SEMS
                dma_rr[e] += 1
                prev = dma_cnt[e][i]
                dma_cnt[e][i] += 16
                tokens[o["id"]] = (("dma", e, i), dma_cnt[e][i])
                if prev:
                    dma_prev[o["id"]] = (("dma", e, i), prev)
            elif o["id"] in signaled:
                eng_cnt[e] += 1
                tokens[o["id"]] = (("eng", e), eng_cnt[e])
        final_waits = {e: {} for e in ENGS}
        for o in ops:
            if o["dma"]:
                k, v = tokens[o["id"]]
                final_waits[o["eng"]][k] = max(final_waits[o["eng"]].get(k, 0), v)
        used = sorted(set(k for k, _ in tokens.values()), key=str)
        self.eng_cnt = eng_cnt
        with ExitStack() as st:
            sems = {k: st.enter_context(nc.semaphore("s_" + "_".join(map(str, k)))) for k in used}
            block = st.enter_context(nc.Block())
            handles = {"pe": block.tensor, "dve": block.vector, "act": block.scalar,
                       "pool": block.gpsimd, "sp": block.sync}

            def make(e):
                def body(eng):
                    seen = {}
                    for oid in self.by_eng[e]:
                        o = ops[oid]
                        waits = {}
                        for d in o["deps"]:
                            if ops[d]["eng"] == e and not ops[d]["dma"] and e == "pe" and not o["dma"]:
                                continue
                            k, v = tokens[d]
                            waits[k] = max(waits.get(k, 0), v)
                        if oid in dma_prev:
                            k, v = dma_prev[oid]
                            waits[k] = max(waits.get(k, 0), v)
                        for k, v in waits.items():
                            if seen.get(k, 0) >= v:
                                continue
                            seen[k] = v
                            eng.wait_ge(sems[k], v)
                        try:
                            ins = o["fn"](eng)
                        except BaseException:
                            print("FAILED OP lines", o["ln"], "eng", e)
                            raise
                        if self.trace is not None:
                            try:
                                self.trace[ins.ins.name] = o["ln"]
                            except Exception:
                                pass
                        if oid in tokens:
                            k, v = tokens[oid]
                            ins.then_inc(sems[k], 16 if o["dma"] else 1)
                    for k, v in final_waits[e].items():
                        if seen.get(k, 0) < v:
                            eng.wait_ge(sems[k], v)
                return body

            for e in ENGS:
                if self.by_eng[e]:
                    handles[e](make(e))


class KB:
    def __init__(self, nc, st, nr, nf):
        self.nc = nc
        self.P = Prog(nc)
        self.arr = st.enter_context(nc.sbuf_tensor("arr", [128, nr], F32R))
        self.arf = st.enter_context(nc.sbuf_tensor("arf", [128, nf], F32))
        self.cr = st.enter_context(nc.sbuf_tensor("cr", [128, 2048], F32R))
        self.cf = st.enter_context(nc.sbuf_tensor("cf", [128, 1280], F32))
        self.nr, self.nf = nr, nf
        self.pr = self.pf = 0
        self.ps = [st.enter_context(nc.psum_tensor("ps%d" % i, [128, 512], F32)) for i in range(8)]
        self.psb = [Buf("ps%d" % i) for i in range(8)]
        self.dq = 0

    def new_stage(self):
        self.P.fence()
        self.pr = self.pf = 0

    def R(self, n, shape=None):
        a = self.arr[:, self.pr:self.pr + n]
        self.pr += n
        assert self.pr <= self.nr, ("arr overflow", self.pr)
        return a

    def Fm(self, n):
        a = self.arf[:, self.pf:self.pf + n]
        self.pf += n
        assert self.pf <= self.nf, ("arf overflow", self.pf)
        return a

    def dma(self, out, in_, reads=(), writes=(), q=None):
        if q is None:
            q = "pool"
        return self.P.op(q, lambda e, o=out, i=in_: e.dma_start(out=o, in_=i), reads, writes, dma=True)

    def mm(self, out, lhsT, rhs, start, stop, reads=(), writes=()):
        return self.P.op("pe", lambda e, o=out, l=lhsT, r=rhs, s=start, t=stop: e.matmul(o, l, r, start=s, stop=t),
                         reads, writes)

    def act(self, out, in_, func, reads=(), writes=(), bias=None, scale=None):
        kw = {}
        if bias is not None:
            kw["bias"] = bias
        if scale is not None:
            kw["scale"] = scale
        return self.P.op("act", lambda e, o=out, i=in_, f=func, kw=kw: e.activation(o, i, f, **kw), reads, writes)

    def tt(self, eng, out, in0, in1, op, reads=(), writes=()):
        return self.P.op(eng, lambda e, o=out, a=in0, b=in1, p=op: e.tensor_tensor(o, a, b, p), reads, writes)

    def ts(self, eng, out, in0, s1, s2, op0, op1=None, reads=(), writes=()):
        if op1 is None:
            return self.P.op(eng, lambda e, o=out, a=in0, x=s1, p=op0: e.tensor_scalar(o, a, x, None, p), reads, writes)
        return self.P.op(eng, lambda e, o=out, a=in0, x=s1, y=s2, p=op0, q=op1: e.tensor_scalar(o, a, x, y, p, q),
                         reads, writes)

    def stt(self, eng, out, in0, scalar, in1, op0, op1, reads=(), writes=()):
        return self.P.op(eng, lambda e, o=out, a=in0, s=scalar, b=in1, p=op0, q=op1:
                         e.scalar_tensor_tensor(o, a, s, b, p, q), reads, writes)

    def copy(self, eng, out, in_, reads=(), writes=()):
        if eng == "act":
            return self.act(out, in_, AF.Copy, reads, writes)
        return self.P.op(eng, lambda e, o=out, i=in_: e.tensor_copy(o, i), reads, writes)

    def memset(self, eng, out, val, writes=()):
        return self.P.op(eng, lambda e, o=out, v=val: e.memset(o, v), (), writes)

    def recip(self, out, in_, reads=(), writes=()):
        return self.P.op("dve", lambda e, o=out, i=in_: e.reciprocal(o, i), reads, writes)

    def scan(self, out, d0, d1, init, reads=(), writes=()):
        return self.P.op("dve", lambda e, o=out, a=d0, b=d1, i=init: e.tensor_tensor_scan(o, a, b, i, ALU.mult, ALU.add),
                         reads, writes)


def v3(ap, a):
    return ap.rearrange("p (a b) -> p a b", a=a)


def build_program(S, L, dbg=()):
    nc = bass.Bass("TRN2", target_bir_lowering=False)
    NCH = S // 128
    NT = S // 512
    assert S % 512 == 0

    def din(name, shape):
        return nc.dram_tensor(name, list(shape), F32, kind="ExternalInput").ap()

    def dscr(name, shape):
        kind = "ExternalOutput" if name in dbg else "Internal"
        return nc.dram_tensor(name, list(shape), F32, kind=kind).ap()

    I = {}
    I["xT"] = din("xT", [D, S])
    for nm, shp in [("g_ffn1", [L, 128, 8]), ("g_mix", [L, 128, 8]), ("g_ffn2", [L, 128, 8]), ("g_final", [128, 8]),
                    ("wg1", [L, NFC, 128, 1024]), ("wu1", [L, NFC, 128, 1024]), ("wd1", [L, 8, 128, DFF]),
                    ("wg2", [L, NFC, 128, 1024]), ("wu2", [L, NFC, 128, 1024]), ("wd2", [L, 8, 128, DFF]),
                    ("win_fm", [L, NFM, 128, 1024]), ("win_tm", [L, 128, 8 * NTM]), ("bgate", [L, 128, 24]),
                    ("conv_w", [L, 128, 30]), ("conv_b", [L, 128, 6]),
                    ("dtb_rep", [L, 128, 16]), ("alog_rep", [L, 128, 16]), ("dskip_rep", [L, 128, 8]),
                    ("ssdnorm_rep", [L, 128, 512]), ("retnorm_rep", [L, 128, 512]),
                    ("wbr_ssd", [L, 128, 4096]), ("wbr_ret", [L, 128, 4096]), ("wbr_s5", [L, 128, 3072]),
                    ("glu_wv", [L, 128, 1152]), ("glu_wg", [L, 128, 1152]), ("wout", [L, 128, 8192]),
                    ("s5_lre", [L, 2, 128, 12]), ("s5_lim", [L, 2, 128, 12]), ("s5_lstep", [L, 2, 128, 12]),
                    ("s5_bre", [L, 128, 12 * 128]), ("s5_bim", [L, 128, 12 * 128]),
                    ("s5_cre", [L, 2, 128, 12 * 32]), ("s5_cim", [L, 2, 128, 12 * 32]), ("s5_dsel", [L, 128, 12 * 32]),
                    ("c_ident", [128, 128]), ("c_tri", [128, 4 * 128]), ("c_maskneg", [128, 2 * 512]),
                    ("c_negiexp", [16, 16 * 128]), ("c_iexp", [16, 16 * 128]), ("c_retmask", [128, 16 * 128]),
                    ("c_retdec", [128, 2 * 8 + 2 * 8 + 8]), ("c_rope", [S, 64]), ("c_blk64", [128, 128])]:
        I[nm] = din(nm, shp)
    outT = nc.dram_tensor("outT", [D, S], F32, kind="ExternalOutput").ap()

    X = dscr("X", [D, S])
    PF = dscr("PF", [NFM * 128, S])
    PT = dscr("PT", [S, NTM])
    XS = dscr("XS", [S, 512])
    BTM = dscr("BTM", [S, 128])
    BCT = dscr("BCT", [256, S])
    YF = dscr("YF", [S, 512])
    YTS = dscr("YTS", [512, S])
    QKT = dscr("QKT", [1024, S])
    KTM = dscr("KTM", [S, 512])
    RF = dscr("RF", [S, 512])
    YTR = dscr("YTR", [512, S])
    Y5 = dscr("Y5", [384, S])

    with ExitStack() as st:
        kb = KB(nc, st, 33280, 14336)
        P = kb.P
        ps, psb = kb.ps, kb.psb

        cbuf = Buf("consts")
        ident = kb.cr[:, 0:128]
        tri = kb.cr[:, 128:640]
        ones = kb.cr[:, 640:768]
        blk64 = kb.cr[:, 768:896]
        iexp = kb.cr[0:16, 896:896 + 0]
        maskneg = kb.cf[:, 0:1024]
        kb.dma(ident, I["c_ident"], writes=[cbuf])
        kb.dma(tri, I["c_tri"], writes=[cbuf])
        kb.dma(blk64, I["c_blk64"], writes=[cbuf])
        kb.dma(maskneg, I["c_maskneg"], writes=[cbuf], q="sp")
        identf = kb.cf[:, 1024:1152]
        kb.dma(identf, I["c_ident"], writes=[cbuf], q="sp")
        onesf = kb.cf[:, 1152:1280]
        kb.memset("dve", onesf, 1.0, writes=[cbuf])
        kb.copy("dve", ones, onesf, reads=[cbuf], writes=[cbuf])

        xbufs = {}

        def xb(k, t):
            key = (k, t)
            if key not in xbufs:
                xbufs[key] = Buf("x%d_%d" % key)
            return xbufs[key]

        def load_norm(src, t, gain_ap, pfx):
            xf = v3(kb.Fm(4096), 8)
            xn = v3(kb.R(4096), 8)
            sq = [kb.R(512), kb.R(512)]
            sqb = [Buf(), Buf()]
            rstd = kb.Fm(512)
            g = kb.Fm(8)
            bx, bn, br, bg = Buf(), Buf(), Buf(), Buf()
            kb.dma(g, gain_ap, writes=[bg], q="sp")
            bxk = [Buf() for _ in range(8)]
            for k in range(8):
                kb.dma(xf[:, k, :], src[k * 128:(k + 1) * 128, t * 512:(t + 1) * 512], reads=[xb(k, t)],
                       writes=[bxk[k]], q="sp")
                kb.act(sq[k % 2], xf[:, k, :], AF.Square, reads=[bxk[k]], writes=[sqb[k % 2]])
                kb.mm(ps[7][:], ones, sq[k % 2], k == 0, k == 7, reads=[sqb[k % 2], cbuf], writes=[psb[7]])
            kb.ts("dve", rstd, ps[7][:], 1.0 / D, EPS, ALU.mult, ALU.add, reads=[psb[7]], writes=[br])
            kb.act(rstd, rstd, AF.Sqrt, reads=[br], writes=[br])
            kb.recip(rstd, rstd, reads=[br], writes=[br])
            for k in range(8):
                kb.stt("dve", xn[:, k, :], xf[:, k, :], g[:, k:k + 1], rstd, ALU.mult, ALU.mult,
                       reads=[bxk[k], br, bg], writes=[bn])
            return xn, bn, xf, bxk

        def ffn_stage(l, gname, wg, wu, wd):
            for t in range(NT):
                kb.new_stage()
                xn, bn, xf, bxk = load_norm(X, t, I[gname][l], "f")
                actt = v3(kb.R(NFC * 512), NFC)
                bact = [Buf() for _ in range(NFC)]
                wgb = [kb.R(1024) for _ in range(2)]
                wub = [kb.R(1024) for _ in range(2)]
                bwg = [Buf(), Buf()]
                bwu = [Buf(), Buf()]
                sg = [kb.Fm(512), kb.Fm(512)]
                bsg = [Buf(), Buf()]
                for j in range(NFC):
                    kb.dma(wgb[j % 2], wg[l, j], writes=[bwg[j % 2]])
                    kb.dma(wub[j % 2], wu[l, j], writes=[bwu[j % 2]])
                    pg, pu = ps[j % 2], ps[2 + j % 2]
                    for k in range(8):
                        kb.mm(pg[:], wgb[j % 2][:, k * 128:(k + 1) * 128], xn[:, k, :], k == 0, k == 7,
                              reads=[bwg[j % 2], bn], writes=[psb[j % 2]])
                    for k in range(8):
                        kb.mm(pu[:], wub[j % 2][:, k * 128:(k + 1) * 128], xn[:, k, :], k == 0, k == 7,
                              reads=[bwu[j % 2], bn], writes=[psb[2 + j % 2]])
                    kb.act(sg[j % 2], pg[:], AF.Silu, reads=[psb[j % 2]], writes=[bsg[j % 2]])
                    kb.tt("dve", actt[:, j, :], sg[j % 2], pu[:], ALU.mult, reads=[bsg[j % 2], psb[2 + j % 2]],
                          writes=[bact[j]])
                wdb = [kb.R(DFF) for _ in range(2)]
                bwd = [Buf(), Buf()]
                xo = [kb.Fm(512), kb.Fm(512)]
                bxo = [Buf(), Buf()]
                for i in range(8):
                    kb.dma(wdb[i % 2], wd[l, i], writes=[bwd[i % 2]])
                    po = ps[4 + i % 2]
                    for j in range(NFC):
                        kb.mm(po[:], wdb[i % 2][:, j * 128:(j + 1) * 128], actt[:, j, :], j == 0, j == NFC - 1,
                              reads=[bwd[i % 2], bact[j]], writes=[psb[4 + i % 2]])
                    kb.stt("dve", xo[i % 2], po[:], 0.5, xf[:, i, :], ALU.mult, ALU.add,
                           reads=[psb[4 + i % 2], bxk[i]], writes=[bxo[i % 2]])
                    kb.dma(X[i * 128:(i + 1) * 128, t * 512:(t + 1) * 512], xo[i % 2], reads=[bxo[i % 2]],
                           writes=[xb(i, t)], q="sp")

        bPF = Buf("PF")
        bPT = Buf("PT")

        def inproj_stage(l):
            for t in range(NT):
                kb.new_stage()
                xn, bn, xf, bxk = load_norm(X, t, I["g_mix"][l], "m")
                bgt = kb.Fm(24)
                bbg = Buf()
                kb.dma(bgt, I["bgate"][l], writes=[bbg], q="sp")
                wb = [kb.R(1024) for _ in range(2)]
                bw = [Buf(), Buf()]
                so = [kb.Fm(512), kb.Fm(512)]
                bso = [Buf(), Buf()]
                for j in range(NFM):
                    kb.dma(wb[j % 2], I["win_fm"][l, j], writes=[bw[j % 2]])
                    pp = ps[j % 2]
                    for k in range(8):
                        kb.mm(pp[:], wb[j % 2][:, k * 128:(k + 1) * 128], xn[:, k, :], k == 0, k == 7,
                              reads=[bw[j % 2], bn], writes=[psb[j % 2]])
                    if j >= 9:
                        kb.act(so[j % 2], pp[:], AF.Sigmoid, reads=[psb[j % 2], bbg], writes=[bso[j % 2]],
                               bias=bgt[:, j - 9:j - 8])
                    else:
                        kb.copy("dve", so[j % 2], pp[:], reads=[psb[j % 2]], writes=[bso[j % 2]])
                    kb.dma(PF[j * 128:(j + 1) * 128, t * 512:(t + 1) * 512], so[j % 2], reads=[bso[j % 2]],
                           writes=[bPF], q="sp")
                dtb = kb.Fm(16)
                bdtb = Buf()
                kb.dma(dtb, I["dtb_rep"][l], writes=[bdtb], q="sp")
                wt = [kb.R(8 * 512) for _ in range(2)]
                bwt = [Buf(), Buf()]
                st_ = [kb.Fm(512) for _ in range(2)]
                bst = [Buf(), Buf()]
                cnt = 0
                for cb in range(6):
                    c0 = cb * 512
                    w = 512 if cb < 5 else 16
                    wv = v3(wt[cb % 2], 8)
                    kb.dma(wv[:, :, 0:w], v3(I["win_tm"][l], 8)[:, :, c0:c0 + w], writes=[bwt[cb % 2]])
                    for c in range(4):
                        pp = ps[2 + cnt % 2]
                        bp = psb[2 + cnt % 2]
                        for k in range(8):
                            kb.mm(pp[:, 0:w], xn[:, k, c * 128:(c + 1) * 128], wv[:, k, 0:w], k == 0, k == 7,
                                  reads=[bwt[cb % 2], bn], writes=[bp])
                        o = st_[cnt % 2][:, 0:w]
                        bo = bst[cnt % 2]
                        if cb in (0, 4):
                            kb.act(o, pp[:, 0:w], AF.Silu, reads=[bp], writes=[bo])
                        elif cb == 5:
                            kb.tt("dve", o, pp[:, 0:w], dtb, ALU.add, reads=[bp, bdtb], writes=[bo])
                            kb.act(o, o, AF.Exp, reads=[bo], writes=[bo])
                            kb.act(o, o, AF.Ln, reads=[bo], writes=[bo], bias=1.0)
                        else:
                            kb.copy("dve", o, pp[:, 0:w], reads=[bp], writes=[bo])
                        r0 = t * 512 + c * 128
                        kb.dma(PT[r0:r0 + 128, c0:c0 + w], o, reads=[bo], writes=[bPT], q="sp")
                        cnt += 1

        bXS, bBTM, bBCT = Buf("XS"), Buf("BTM"), Buf("BCT")

        def ssd_prep(l):
            kb.new_stage()
            cw = kb.Fm(30)
            cbias = kb.Fm(6)
            bcw = Buf()
            kb.dma(cw, I["conv_w"][l], writes=[bcw], q="sp")
            kb.dma(cbias, I["conv_b"][l], writes=[bcw], q="sp")
            xin = [kb.Fm(516) for _ in range(2)]
            bxin = [Buf(), Buf()]
            acc = [kb.Fm(512) for _ in range(2)]
            bacc = [Buf(), Buf()]
            cv = [kb.R(512) for _ in range(2)]
            bcv = [Buf(), Buf()]
            tm = [kb.Fm(512) for _ in range(2)]
            btm = [Buf(), Buf()]
            n = 0
            for t in range(NT):
                for f in range(6):
                    xi, bi = xin[n % 2], bxin[n % 2]
                    lo = t * 512 - 2
                    hi = t * 512 + 514
                    a, b = max(lo, 0), min(hi, S)
                    if a > lo:
                        kb.memset("pool", xi[:, 0:2], 0.0, writes=[bi])
                    if b < hi:
                        kb.memset("pool", xi[:, 514:516], 0.0, writes=[bi])
                    kb.dma(xi[:, a - lo:b - lo], PF[f * 128:(f + 1) * 128, a:b], reads=[bPF], writes=[bi], q="sp")
                    ac, ba = acc[n % 2], bacc[n % 2]
                    kb.ts("dve", ac, xi[:, 0:512], cw[:, f * 5:f * 5 + 1], None, ALU.mult, reads=[bi, bcw], writes=[ba])
                    for j in range(1, 5):
                        kb.stt("dve", ac, xi[:, j:j + 512], cw[:, f * 5 + j:f * 5 + j + 1], ac, ALU.mult, ALU.add,
                               reads=[bi, bcw, ba], writes=[ba])
                    c_, bc_ = cv[n % 2], bcv[n % 2]
                    kb.act(c_, ac, AF.Silu, reads=[ba, bcw], writes=[bc_], bias=cbias[:, f:f + 1])
                    if f < 5:
                        for c in range(4):
                            pp, bp = ps[(4 * n + c) % 4], psb[(4 * n + c) % 4]
                            kb.mm(pp[:, 0:128], c_[:, c * 128:(c + 1) * 128], ident, True, True,
                                  reads=[bc_, cbuf], writes=[bp])
                            o, bo = tm[c % 2][:, 0:128], btm[c % 2]
                            kb.copy("act" if c % 2 else "dve", o, pp[:, 0:128], reads=[bp], writes=[bo])
                            r0 = t * 512 + c * 128
                            if f < 4:
                                kb.dma(XS[r0:r0 + 128, f * 128:(f + 1) * 128], o, reads=[bo], writes=[bXS], q="sp")
                            else:
                                kb.dma(BTM[r0:r0 + 128, :], o, reads=[bo], writes=[bBTM], q="sp")
                    if f >= 4:
                        kb.dma(BCT[(f - 4) * 128:(f - 3) * 128, t * 512:(t + 1) * 512], c_, reads=[bc_], writes=[bBCT])
                    n += 1

        bYF, bYTS = Buf("YF"), Buf("YTS")

        def ssd_pass(l, dirn):
            kb.new_stage()
            arep = kb.Fm(16)
            dsk = kb.Fm(8)
            gn = kb.Fm(512)
            bpar = Buf()
            kb.dma(arep, I["alog_rep"][l], writes=[bpar], q="sp")
            kb.dma(dsk, I["dskip_rep"][l], writes=[bpar], q="sp")
            kb.dma(gn, I["ssdnorm_rep"][l], writes=[bpar], q="sp")
            kb.act(arep, arep, AF.Exp, reads=[bpar], writes=[bpar])
            kb.ts("dve", arep, arep, -1.0, None, ALU.mult, reads=[bpar], writes=[bpar])
            niexp = v3(kb.Fm(2048)[0:16, :], 16)
            iex = v3(kb.Fm(2048)[0:16, :], 16)
            kb.dma(niexp, v3(I["c_negiexp"], 16), writes=[bpar], q="sp")
            kb.dma(iex, v3(I["c_iexp"], 16), writes=[bpar], q="sp")
            mk = v3(maskneg[:, dirn * 512:(dirn + 1) * 512], 4)
            Sst = kb.R(512)
            bS = Buf()
            kb.ts("dve", Sst, onesf[:, 0:1].to_broadcast([128, 512]), 0.0, None, ALU.mult, reads=[cbuf], writes=[bS])
            NB = 2
            xs_ = [kb.Fm(512) for _ in range(NB)]
            dt_ = [kb.Fm(16) for _ in range(NB)]
            bt_ = [kb.R(128) for _ in range(NB)]
            bct = [kb.R(512) for _ in range(NB)]
            bin_ = [Buf() for _ in range(NB)]
            for i_ in range(NB):
                kb.ts("dve", bct[i_][:, 256:512], onesf[:, 0:1].to_broadcast([128, 256]), 0.0, None, ALU.mult,
                      reads=[cbuf], writes=[bin_[i_]])
            da = [kb.Fm(8) for _ in range(NB)]
            xd = [kb.R(512) for _ in range(NB)]
            xdw = [kb.R(512) for _ in range(NB)]
            csf = [kb.Fm(128) for _ in range(NB)]
            zt = [kb.Fm(1024) for _ in range(NB)]
            ee = [kb.Fm(1024) for _ in range(NB)]
            gt = [kb.Fm(256) for _ in range(NB)]
            mt = [kb.R(1024) for _ in range(NB)]
            ex = [kb.Fm(16) for _ in range(NB)]
            dec = [kb.Fm(8) for _ in range(NB)]
            yt_ = [kb.Fm(512) for _ in range(NB)]
            t2 = [kb.Fm(512) for _ in range(NB)]
            zz = [kb.Fm(512) for _ in range(NB)]
            yo = [kb.R(512) for _ in range(NB)]
            ss = [kb.Fm(8) for _ in range(NB)]
            bw_ = [[Buf() for _ in range(16)] for _ in range(NB)]
            order = range(NCH) if dirn == 0 else range(NCH - 1, -1, -1)
            for n, c in enumerate(order):
                i = n % NB
                B = bw_[i]
                r0 = c * 128
                kb.dma(xs_[i], XS[r0:r0 + 128, :], reads=[bXS], writes=[bin_[i]], q="sp")
                kb.dma(dt_[i], PT[r0:r0 + 128, 2560:2576], reads=[bPT], writes=[bin_[i]], q="sp")
                kb.dma(bt_[i], BTM[r0:r0 + 128, :], reads=[bBTM], writes=[bin_[i]])
                kb.dma(bct[i][:, 0:128], BCT[0:128, r0:r0 + 128], reads=[bBCT], writes=[bin_[i]])
                kb.dma(bct[i][:, 128:256], BCT[128:256, r0:r0 + 128], reads=[bBCT], writes=[bin_[i]])
                for g in range(2):
                    kb.dma(bct[i][g * 64:(g + 1) * 64, 256 + g * 128:384 + g * 128],
                           BCT[128 + g * 64:192 + g * 64, r0:r0 + 128], reads=[bBCT], writes=[bin_[i]])
                dtd = dt_[i][:, dirn * 8:(dirn + 1) * 8]
                kb.tt("dve", da[i], dtd, arep[:, dirn * 8:(dirn + 1) * 8], ALU.mult, reads=[bin_[i], bpar], writes=[B[0]])
                kb.tt("pool", v3(xd[i], 8), v3(xs_[i], 8), dtd.unsqueeze(2).to_broadcast([128, 8, 64]), ALU.mult,
                      reads=[bin_[i]], writes=[B[1]])
                trisel = kb.cf
                tsl = tri[:, 0:128] if dirn == 0 else tri[:, 128:256]
                kb.mm(ps[0][0:8, 0:128], da[i].bitcast(F32), tsl.bitcast(F32), True, True, reads=[B[0], cbuf],
                      writes=[psb[0]])
                kb.copy("act", csf[i][0:8, :], ps[0][0:8, 0:128], reads=[psb[0]], writes=[B[2]])
                kb.tt("dve", v3(zt[i][0:8, :], 8), csf[i][0:8, :].unsqueeze(1).to_broadcast([8, 8, 128]),
                      iex[0:8, 0:8, :], ALU.mult, reads=[B[2], bpar], writes=[B[3]])
                for hb in range(2):
                    pp, bp = ps[1 + hb], psb[1 + hb]
                    kb.mm(pp[:], onesf[0:8, :], zt[i][0:8, hb * 512:(hb + 1) * 512], True, False,
                          reads=[B[3], cbuf], writes=[bp])
                    kb.mm(pp[:], csf[i][0:8, :], niexp[0:8, hb * 4:(hb + 1) * 4, :], False, False,
                          reads=[B[2], bpar], writes=[bp])
                    kb.mm(pp[:], identf, mk, False, True, reads=[cbuf], writes=[bp])
                    kb.act(ee[i][:, hb * 512:(hb + 1) * 512], pp[:], AF.Exp, reads=[bp], writes=[B[4]])
                kb.mm(ps[3][:, 0:256], bct[i][:, 0:128], bct[i][:, 256:512], True, True, reads=[bin_[i]], writes=[psb[3]])
                kb.copy("act", gt[i], ps[3][:, 0:256], reads=[psb[3]], writes=[B[5]])
                for g in range(2):
                    kb.tt("dve" if g else "pool", v3(mt[i][:, g * 512:(g + 1) * 512], 4),
                          v3(ee[i][:, g * 512:(g + 1) * 512], 4),
                          gt[i][:, g * 128:(g + 1) * 128].unsqueeze(1).to_broadcast([128, 4, 128]), ALU.mult,
                          reads=[B[4], B[5]], writes=[B[6]])
                if dirn == 0:
                    wl, ol = tri[:, 256:384], tri[:, 0:128]
                else:
                    wl, ol = tri[:, 384:512], None
                kb.mm(ps[4][:, 0:8], wl.bitcast(F32), da[i], True, True, reads=[B[0], cbuf], writes=[psb[4]])
                if dirn == 0:
                    kb.mm(ps[4][:, 8:16], ol.bitcast(F32), da[i], True, True, reads=[B[0], cbuf], writes=[psb[4]])
                else:
                    kb.mm(ps[4][:, 8:16], onesf, da[i], True, False, reads=[B[0], cbuf], writes=[psb[4]])
                    kb.mm(ps[4][:, 8:16], tri[:, 128:256].bitcast(F32).rearrange("p x -> p x"), da[i], False, True,
                          reads=[B[0], cbuf], writes=[psb[4]])
                kb.act(ex[i], ps[4][:, 0:16], AF.Exp, reads=[psb[4]], writes=[B[7]])
                kb.mm(ps[4][:, 16:24], onesf, da[i], True, True, reads=[B[0], cbuf], writes=[psb[4]])
                kb.act(dec[i], ps[4][:, 16:24], AF.Exp, reads=[psb[4]], writes=[B[8]])
                for h in range(8):
                    kb.mm(ps[5][:, h * 64:(h + 1) * 64], mt[i][:, h * 128:(h + 1) * 128], xd[i][:, h * 64:(h + 1) * 64],
                          True, True, reads=[B[6], B[1]], writes=[psb[5]])
                kb.mm(ps[6][:], bct[i][:, 128:256], Sst, True, True, reads=[bin_[i], bS], writes=[psb[6]])
                kb.tt("dve", v3(t2[i], 8), v3(ps[6][:], 8), ex[i][:, 8:16].unsqueeze(2).to_broadcast([128, 8, 64]),
                      ALU.mult, reads=[psb[6], B[7]], writes=[B[9]])
                kb.tt("dve", yt_[i], ps[5][:], t2[i], ALU.add, reads=[psb[5], B[9]], writes=[B[10]])
                kb.tt("pool", v3(xdw[i], 8), v3(xd[i], 8), ex[i][:, 0:8].unsqueeze(2).to_broadcast([128, 8, 64]),
                      ALU.mult, reads=[B[1], B[7]], writes=[B[11]])
                kb.mm(ps[7][:], bt_[i], xdw[i], True, True, reads=[bin_[i], B[11]], writes=[psb[7]])
                for g in range(2):
                    sl = slice(g * 64, (g + 1) * 64)
                    kb.tt("dve", v3(Sst[sl, g * 256:(g + 1) * 256], 4), v3(Sst[sl, g * 256:(g + 1) * 256], 4),
                          dec[i][sl, g * 4:(g + 1) * 4].unsqueeze(2).to_broadcast([64, 4, 64]), ALU.mult,
                          reads=[bS, B[8]], writes=[bS])
                for g in range(2):
                    sl = slice(g * 64, (g + 1) * 64)
                    kb.tt("dve", Sst[sl, g * 256:(g + 1) * 256], Sst[sl, g * 256:(g + 1) * 256],
                          ps[7][sl, g * 256:(g + 1) * 256], ALU.add, reads=[bS, psb[7]], writes=[bS])
                if dirn == 0:
                    kb.tt("pool", v3(t2[i], 8), v3(xs_[i], 8), dsk.unsqueeze(2).to_broadcast([128, 8, 64]), ALU.mult,
                          reads=[bin_[i], bpar, B[10]], writes=[B[9]])
                    kb.tt("dve", yt_[i], yt_[i], t2[i], ALU.add, reads=[B[9], B[10]], writes=[B[10]])
                    kb.dma(YF[r0:r0 + 128, :], yt_[i], reads=[B[10]], writes=[bYF], q="sp")
                else:
                    kb.dma(t2[i], YF[r0:r0 + 128, :], reads=[bYF, B[9], B[10]], writes=[B[9]], q="sp")
                    kb.dma(zz[i], PT[r0:r0 + 128, 0:512], reads=[bPT], writes=[B[12]], q="sp")
                    kb.tt("dve", yt_[i], yt_[i], t2[i], ALU.add, reads=[B[9], B[10]], writes=[B[10]])
                    kb.tt("dve", yt_[i], yt_[i], zz[i], ALU.mult, reads=[B[12], B[10]], writes=[B[10]])
                    kb.P.op("act", lambda e, o=t2[i], a=yt_[i], s=ss[i]: e.activation(o, a, AF.Square, accum_out=s[:, 0:1]),
                            reads=[B[10], B[9]], writes=[B[9], B[13]])
                    kb.ts("dve", ss[i][:, 0:1], ss[i][:, 0:1], 1.0 / 512, EPS, ALU.mult, ALU.add, reads=[B[13]], writes=[B[13]])
                    kb.act(ss[i][:, 0:1], ss[i][:, 0:1], AF.Sqrt, reads=[B[13]], writes=[B[13]])
                    kb.recip(ss[i][:, 0:1], ss[i][:, 0:1], reads=[B[13]], writes=[B[13]])
                    kb.stt("dve", yo[i], yt_[i], ss[i][:, 0:1], gn, ALU.mult, ALU.mult, reads=[B[10], B[13], bpar],
                           writes=[B[14]])
                    for f in range(4):
                        kb.mm(ps[0][:, f * 128:(f + 1) * 128], yo[i][:, f * 128:(f + 1) * 128], ident, True, True,
                              reads=[B[14], cbuf], writes=[psb[0]])
                    kb.copy("act", t2[i], ps[0][:], reads=[psb[0], B[9]], writes=[B[9]])
                    for f in range(4):
                        kb.dma(YTS[f * 128:(f + 1) * 128, r0:r0 + 128], t2[i][:, f * 128:(f + 1) * 128], reads=[B[9]],
                               writes=[bYTS], q="sp")

        bQKT, bKTM = Buf("QKT"), Buf("KTM")

        def ret_prep(l):
            kb.new_stage()
            NB = 2
            qk = [kb.Fm(1024) for _ in range(NB)]
            rp = [kb.Fm(64) for _ in range(NB)]
            ro = [kb.R(1024) for _ in range(NB)]
            tmp = [kb.Fm(512) for _ in range(NB)]
            oT = [kb.Fm(512) for _ in range(NB)]
            bb = [[Buf() for _ in range(6)] for _ in range(NB)]
            for c in range(NCH):
                i = c % NB
                B = bb[i]
                r0 = c * 128
                kb.dma(qk[i], PT[r0:r0 + 128, 512:1536], reads=[bPT], writes=[B[0]], q="sp")
                kb.dma(rp[i], I["c_rope"][r0:r0 + 128, :], writes=[B[0]], q="sp")
                cosb = rp[i][:, 0:32].unsqueeze(1).to_broadcast([128, 16, 32])
                sinb = rp[i][:, 32:64].unsqueeze(1).to_broadcast([128, 16, 32])
                x4 = qk[i].rearrange("p (h t d) -> p h t d", h=16, t=2)
                o4 = ro[i].rearrange("p (h t d) -> p h t d", h=16, t=2)
                tm4 = tmp[i].rearrange("p (h d) -> p h d", h=16)
                t1, t2_ = x4[:, :, 0, :], x4[:, :, 1, :]
                kb.tt("dve", tm4, t2_, sinb, ALU.mult, reads=[B[0]], writes=[B[1]])
                kb.tt("pool", o4[:, :, 0, :], t1, cosb, ALU.mult, reads=[B[0]], writes=[B[2]])
                kb.tt("dve", o4[:, :, 0, :], o4[:, :, 0, :], tm4, ALU.subtract, reads=[B[1], B[2]], writes=[B[2]])
                kb.tt("dve", tm4, t1, sinb, ALU.mult, reads=[B[0], B[2]], writes=[B[1]])
                kb.tt("pool", o4[:, :, 1, :], t2_, cosb, ALU.mult, reads=[B[0]], writes=[B[3]])
                kb.tt("dve", o4[:, :, 1, :], o4[:, :, 1, :], tm4, ALU.add, reads=[B[1], B[3]], writes=[B[3]])
                kb.ts("dve", ro[i][:, 512:1024], ro[i][:, 512:1024], 0.125, None, ALU.mult, reads=[B[2], B[3]],
                      writes=[B[2], B[3]])
                kb.dma(KTM[r0:r0 + 128, :], ro[i][:, 512:1024], reads=[B[2], B[3]], writes=[bKTM])
                for f in range(8):
                    pp, bp = ps[f % 2], psb[f % 2]
                    kb.mm(pp[:, 0:128], ro[i][:, f * 128:(f + 1) * 128], ident, True, True, reads=[B[2], B[3], cbuf],
                          writes=[bp])
                    o = oT[i][:, (f % 4) * 128:(f % 4 + 1) * 128]
                    kb.copy("act", o, pp[:, 0:128], reads=[bp], writes=[B[4 + (f % 2)]])
                    kb.dma(QKT[f * 128:(f + 1) * 128, r0:r0 + 128], o, reads=[B[4 + (f % 2)]], writes=[bQKT], q="sp")

        bRF, bYTR = Buf("RF"), Buf("YTR")

        def ret_pass(l, dirn):
            kb.new_stage()
            rmask = v3(kb.Fm(1024), 8)
            rdec = kb.Fm(40)
            gn = kb.Fm(512)
            bpar = Buf()
            kb.dma(rmask, v3(I["c_retmask"][:, dirn * 1024:(dirn + 1) * 1024], 8), writes=[bpar], q="sp")
            kb.dma(rdec, I["c_retdec"], writes=[bpar], q="sp")
            kb.dma(gn, I["retnorm_rep"][l], writes=[bpar], q="sp")
            kdec = rdec[:, dirn * 8:dirn * 8 + 8]
            qdec = rdec[:, 16 + dirn * 8:16 + dirn * 8 + 8]
            cdec = rdec[0:64, 32:40]
            Sst = kb.R(512)[0:64, :]
            bS = Buf()
            kb.ts("dve", Sst, onesf[0:64, 0:1].to_broadcast([64, 512]), 0.0, None, ALU.mult, reads=[cbuf], writes=[bS])
            NB = 2
            qkt = [kb.R(2048) for _ in range(NB)]
            ktm = [kb.R(512) for _ in range(NB)]
            vtm = [kb.R(512) for _ in range(NB)]
            vw = [kb.R(512) for _ in range(NB)]
            mt = [kb.R(1024) for _ in range(NB)]
            yt_ = [kb.Fm(512) for _ in range(NB)]
            t2 = [kb.Fm(512) for _ in range(NB)]
            gg = [kb.Fm(512) for _ in range(NB)]
            yo = [kb.R(512) for _ in range(NB)]
            ss = [kb.Fm(16) for _ in range(NB)]
            bb = [[Buf() for _ in range(12)] for _ in range(NB)]
            order = range(NCH) if dirn == 0 else range(NCH - 1, -1, -1)
            for n, c in enumerate(order):
                i = n % NB
                B = bb[i]
                r0 = c * 128
                q3 = v3(qkt[i][0:64, :], 16)
                for h2 in range(16):
                    kb.dma(q3[:, h2, :], QKT[h2 * 64:(h2 + 1) * 64, r0:r0 + 128], reads=[bQKT], writes=[B[0]])
                kb.dma(ktm[i], KTM[r0:r0 + 128, :], reads=[bKTM], writes=[B[0]])
                kb.dma(vtm[i], PT[r0:r0 + 128, 1536:2048], reads=[bPT], writes=[B[0]])
                for h in range(8):
                    kb.mm(ps[h // 4][:, (h % 4) * 128:(h % 4 + 1) * 128], q3[:, 8 + h, :], q3[:, h, :], True, True,
                          reads=[B[0]], writes=[psb[h // 4]])
                for hb in range(2):
                    kb.tt("dve", v3(mt[i][:, hb * 512:(hb + 1) * 512], 4), v3(ps[hb][:], 4), rmask[:, hb * 4:(hb + 1) * 4, :],
                          ALU.mult, reads=[psb[hb], bpar], writes=[B[1]])
                for h in range(8):
                    kb.mm(ps[5][:, h * 64:(h + 1) * 64], mt[i][:, h * 128:(h + 1) * 128], vtm[i][:, h * 64:(h + 1) * 64],
                          True, True, reads=[B[1], B[0]], writes=[psb[5]])
                for h in range(8):
                    kb.mm(ps[6][:, h * 64:(h + 1) * 64], q3[:, h, :], Sst[:, h * 64:(h + 1) * 64], True, True,
                          reads=[B[0], bS], writes=[psb[6]])
                kb.tt("dve", v3(t2[i], 8), v3(ps[6][:], 8), qdec.unsqueeze(2).to_broadcast([128, 8, 64]), ALU.mult,
                      reads=[psb[6], bpar], writes=[B[2]])
                kb.tt("dve", yt_[i], ps[5][:], t2[i], ALU.add, reads=[psb[5], B[2]], writes=[B[3]])
                kb.tt("pool", v3(vw[i], 8), v3(vtm[i], 8), kdec.unsqueeze(2).to_broadcast([128, 8, 64]), ALU.mult,
                      reads=[B[0], bpar], writes=[B[4]])
                for h in range(8):
                    kb.mm(ps[7][0:64, h * 64:(h + 1) * 64], ktm[i][:, h * 64:(h + 1) * 64], vw[i][:, h * 64:(h + 1) * 64],
                          True, True, reads=[B[0], B[4]], writes=[psb[7]])
                kb.tt("dve", v3(Sst, 8), v3(Sst, 8), cdec.unsqueeze(2).to_broadcast([64, 8, 64]), ALU.mult,
                      reads=[bS, bpar], writes=[bS])
                kb.tt("dve", Sst, Sst, ps[7][0:64, :], ALU.add, reads=[bS, psb[7]], writes=[bS])
                if dirn == 0:
                    kb.dma(RF[r0:r0 + 128, :], yt_[i], reads=[B[3]], writes=[bRF], q="sp")
                else:
                    kb.dma(t2[i], RF[r0:r0 + 128, :], reads=[bRF, B[2], B[3]], writes=[B[2]], q="sp")
                    kb.dma(gg[i], PT[r0:r0 + 128, 2048:2560], reads=[bPT], writes=[B[5]], q="sp")
                    kb.tt("dve", yt_[i], yt_[i], t2[i], ALU.add, reads=[B[2], B[3]], writes=[B[3]])
                    kb.tt("pool", t2[i], yt_[i], yt_[i], ALU.mult, reads=[B[3], B[2]], writes=[B[2]])
                    kb.P.op("dve", lambda e, o=ss[i][:, 0:8], a=v3(t2[i], 8): e.tensor_reduce(o, a, AX.X, ALU.add),
                            reads=[B[2]], writes=[B[6]])
                    kb.ts("dve", ss[i][:, 0:8], ss[i][:, 0:8], 1.0 / 64, EPS, ALU.mult, ALU.add, reads=[B[6]], writes=[B[6]])
                    kb.act(ss[i][:, 0:8], ss[i][:, 0:8], AF.Sqrt, reads=[B[6]], writes=[B[6]])
                    kb.recip(ss[i][:, 0:8], ss[i][:, 0:8], reads=[B[6]], writes=[B[6]])
                    kb.tt("dve", v3(yt_[i], 8), v3(yt_[i], 8), ss[i][:, 0:8].unsqueeze(2).to_broadcast([128, 8, 64]),
                          ALU.mult, reads=[B[3], B[6]], writes=[B[3]])
                    kb.tt("pool", yt_[i], yt_[i], gn, ALU.mult, reads=[B[3], bpar], writes=[B[3]])
                    kb.tt("dve", yo[i], yt_[i], gg[i], ALU.mult, reads=[B[3], B[5]], writes=[B[7]])
                    for f in range(4):
                        kb.mm(ps[2][:, f * 128:(f + 1) * 128], yo[i][:, f * 128:(f + 1) * 128], ident, True, True,
                              reads=[B[7], cbuf], writes=[psb[2]])
                    kb.copy("act", t2[i], ps[2][:], reads=[psb[2], B[2]], writes=[B[2]])
                    for f in range(4):
                        kb.dma(YTR[f * 128:(f + 1) * 128, r0:r0 + 128], t2[i][:, f * 128:(f + 1) * 128], reads=[B[2]],
                               writes=[bYTR], q="sp")

        bY5 = Buf("Y5")

        def s5_stage(l):
            kb.new_stage()
            bp_ = Buf("s5par")
            def F(n):
                return kb.Fm(n)
            rho = [F(12), F(12)]
            c0 = [F(12), F(12)]
            s0 = [F(12), F(12)]
            C9 = [F(12), F(12)]
            S9 = [F(12), F(12)]
            cfr = [F(12), F(12)]
            cfi = [F(12), F(12)]
            tA, tB, tC, tD = F(12), F(12), F(12), F(12)
            halfpi = F(1)
            kb.memset("dve", halfpi, math.pi / 2, writes=[bp_])
            for d in range(2):
                lre, lim, stp = F(12), F(12), F(12)
                kb.dma(lre, I["s5_lre"][l, d], writes=[bp_], q="sp")
                kb.dma(lim, I["s5_lim"][l, d], writes=[bp_], q="sp")
                kb.dma(stp, I["s5_lstep"][l, d], writes=[bp_], q="sp")
                R_, W_ = [bp_], [bp_]
                kb.ts("dve", lre, lre, -1e-4, None, ALU.min, reads=R_, writes=W_)
                kb.act(stp, stp, AF.Exp, reads=R_, writes=W_)
                kb.tt("dve", tA, lre, stp, ALU.mult, reads=R_, writes=W_)
                kb.act(rho[d], tA, AF.Exp, reads=R_, writes=W_)
                kb.tt("dve", tB, lim, stp, ALU.mult, reads=R_, writes=W_)
                kb.act(s0[d], tB, AF.Sin, reads=R_, writes=W_, scale=1.0 / 32)
                kb.act(c0[d], tB, AF.Sin, reads=R_, writes=W_, scale=1.0 / 32, bias=halfpi[:, 0:1])

                def dbl(cc, sn):
                    kb.tt("dve", tC, cc, cc, ALU.mult, reads=R_, writes=W_)
                    kb.tt("dve", tD, sn, sn, ALU.mult, reads=R_, writes=W_)
                    kb.tt("dve", tD, tC, tD, ALU.subtract, reads=R_, writes=W_)
                    kb.tt("dve", tC, cc, sn, ALU.mult, reads=R_, writes=W_)
                    kb.ts("dve", sn, tC, 2.0, None, ALU.mult, reads=R_, writes=W_)
                    kb.copy("dve", cc, tD, reads=R_, writes=W_)
                for _ in range(5):
                    dbl(c0[d], s0[d])
                kb.copy("dve", C9[d], c0[d], reads=R_, writes=W_)
                kb.copy("dve", S9[d], s0[d], reads=R_, writes=W_)
                for _ in range(9):
                    dbl(C9[d], S9[d])
                lbr, lbi, den = F(12), F(12), F(12)
                kb.tt("dve", lbr, rho[d], c0[d], ALU.mult, reads=R_, writes=W_)
                kb.tt("dve", lbi, rho[d], s0[d], ALU.mult, reads=R_, writes=W_)
                kb.ts("dve", lbr, lbr, -1.0, None, ALU.add, reads=R_, writes=W_)
                kb.tt("dve", den, lre, lre, ALU.mult, reads=R_, writes=W_)
                kb.tt("dve", tC, lim, lim, ALU.mult, reads=R_, writes=W_)
                kb.tt("dve", den, den, tC, ALU.add, reads=R_, writes=W_)
                kb.recip(den, den, reads=R_, writes=W_)
                kb.tt("dve", tC, lbr, lre, ALU.mult, reads=R_, writes=W_)
                kb.tt("dve", tD, lbi, lim, ALU.mult, reads=R_, writes=W_)
                kb.tt("dve", tC, tC, tD, ALU.add, reads=R_, writes=W_)
                kb.tt("dve", cfr[d], tC, den, ALU.mult, reads=R_, writes=W_)
                kb.tt("dve", tC, lbi, lre, ALU.mult, reads=R_, writes=W_)
                kb.tt("dve", tD, lbr, lim, ALU.mult, reads=R_, writes=W_)
                kb.tt("dve", tC, tC, tD, ALU.subtract, reads=R_, writes=W_)
                kb.tt("dve", cfi[d], tC, den, ALU.mult, reads=R_, writes=W_)
            bre = v3(kb.Fm(1536), 12)
            bim = v3(kb.Fm(1536), 12)
            kb.dma(bre, v3(I["s5_bre"][l], 12), writes=[bp_], q="sp")
            kb.dma(bim, v3(I["s5_bim"][l], 12), writes=[bp_], q="sp")
            dsel = v3(kb.R(384), 12)
            kb.dma(dsel, v3(I["s5_dsel"][l], 12), writes=[bp_])
            cre = [v3(kb.R(384), 12) for _ in range(2)]
            cimn = [v3(kb.R(384), 12) for _ in range(2)]
            for d in range(2):
                kb.dma(cre[d], v3(I["s5_cre"][l, d], 12), writes=[bp_])
                kb.dma(cimn[d], v3(I["s5_cim"][l, d], 12), writes=[bp_])
                kb.ts("dve", cimn[d], cimn[d], -1.0, None, ALU.mult, reads=[bp_], writes=[bp_])
            bbr = kb.R(128)
            bbi = kb.R(128)
            tq = kb.Fm(128)
            btr = kb.R(128)
            bti = kb.R(128)
            cosT, sinT = kb.Fm(512), kb.Fm(512)
            tc_, ts_ = kb.Fm(256), kb.Fm(256)
            rhoT = kb.Fm(512)
            NB = 2
            ub = [kb.R(512) for _ in range(NB)]
            vre = [kb.Fm(512) for _ in range(NB)]
            vim = [kb.Fm(512) for _ in range(NB)]
            wre = [kb.Fm(512) for _ in range(NB)]
            wim = [kb.Fm(512) for _ in range(NB)]
            hre = [kb.R(512) for _ in range(NB)]
            him = [kb.R(512) for _ in range(NB)]
            t1 = [kb.Fm(512) for _ in range(NB)]
            ini = [kb.Fm(4) for _ in range(3)]
            go = [kb.Fm(512) for _ in range(NB)]
            bb = [[Buf() for _ in range(10)] for _ in range(NB)]
            bt = Buf()
            bini = Buf()
            for it in range(12):
                for d in range(2):
                    R_, W_ = [bp_, bt], [bt]
                    kb.ts("dve", tq, bre[:, it, :], cfr[d][:, it:it + 1], None, ALU.mult, reads=R_, writes=W_)
                    kb.stt("dve", tq, bim[:, it, :], cfi[d][:, it:it + 1], tq, ALU.mult, ALU.subtract, reads=R_, writes=W_)
                    kb.ts("dve", bbr, tq, -1.0, None, ALU.mult, reads=R_, writes=W_)
                    kb.ts("dve", tq, bim[:, it, :], cfr[d][:, it:it + 1], None, ALU.mult, reads=R_, writes=W_)
                    kb.stt("dve", bbi, bre[:, it, :], cfi[d][:, it:it + 1], tq, ALU.mult, ALU.add, reads=R_, writes=W_)
                    kb.mm(ps[0][:, 0:128], bbr, ident, True, True, reads=[bt, cbuf], writes=[psb[0]])
                    kb.mm(ps[0][:, 128:256], bbi, ident, True, True, reads=[bt, cbuf], writes=[psb[0]])
                    kb.copy("dve", btr, ps[0][:, 0:128], reads=[psb[0]], writes=W_)
                    kb.copy("dve", bti, ps[0][:, 128:256], reads=[psb[0]], writes=W_)
                    kb.memset("dve", cosT[:, 0:1], 1.0, writes=W_)
                    kb.memset("dve", sinT[:, 0:1], 0.0, writes=W_)
                    kb.copy("dve", tc_[:, 0:1], c0[d][:, it:it + 1], reads=R_, writes=W_)
                    kb.copy("dve", ts_[:, 0:1], s0[d][:, it:it + 1], reads=R_, writes=W_)
                    m = 1
                    while m < 512:
                        ck, sk = tc_[:, 0:1], ts_[:, 0:1]
                        kb.ts("dve", t1[0][:, 0:m], sinT[:, 0:m], sk, None, ALU.mult, reads=R_, writes=W_)
                        kb.stt("dve", cosT[:, m:2 * m], cosT[:, 0:m], ck, t1[0][:, 0:m], ALU.mult, ALU.subtract,
                               reads=R_, writes=W_)
                        kb.ts("dve", t1[0][:, 0:m], cosT[:, 0:m], sk, None, ALU.mult, reads=R_, writes=W_)
                        kb.stt("dve", sinT[:, m:2 * m], sinT[:, 0:m], ck, t1[0][:, 0:m], ALU.mult, ALU.add,
                               reads=R_, writes=W_)
                        kb.tt("dve", tc_[:, 1:2], ck, ck, ALU.mult, reads=R_, writes=W_)
                        kb.tt("dve", tc_[:, 2:3], sk, sk, ALU.mult, reads=R_, writes=W_)
                        kb.tt("dve", tc_[:, 3:4], ck, sk, ALU.mult, reads=R_, writes=W_)
                        kb.tt("dve", tc_[:, 0:1], tc_[:, 1:2], tc_[:, 2:3], ALU.subtract, reads=R_, writes=W_)
                        kb.ts("dve", ts_[:, 0:1], tc_[:, 3:4], 2.0, None, ALU.mult, reads=R_, writes=W_)
                        m *= 2
                    kb.copy("dve", rhoT, rho[d][:, it:it + 1].to_broadcast([128, 512]), reads=R_, writes=W_)
                    kb.memset("dve", ini[0][:, 0:2], 0.0, writes=[bini])
                    order = range(NT) if d == 0 else range(NT - 1, -1, -1)
                    for n, t in enumerate(order):
                        i = n % NB
                        B = bb[i]
                        kb.dma(ub[i], PF[(6 + it // 4) * 128:(7 + it // 4) * 128, t * 512:(t + 1) * 512], reads=[bPF],
                               writes=[B[0]])
                        kb.mm(ps[1][:], btr, ub[i], True, True, reads=[bt, B[0]], writes=[psb[1]])
                        kb.mm(ps[2][:], bti, ub[i], True, True, reads=[bt, B[0]], writes=[psb[2]])
                        pre = ps[1][:] if d == 0 else ps[1][:, ::-1]
                        pim = ps[2][:] if d == 0 else ps[2][:, ::-1]
                        kb.tt("dve", vre[i], pre, cosT, ALU.mult, reads=[psb[1], bt], writes=[B[1]])
                        kb.tt("dve", t1[i], pim, sinT, ALU.mult, reads=[psb[2], bt], writes=[B[2]])
                        kb.tt("pool", vre[i], vre[i], t1[i], ALU.add, reads=[B[1], B[2]], writes=[B[1]])
                        kb.tt("dve", vim[i], pim, cosT, ALU.mult, reads=[psb[2], bt], writes=[B[3]])
                        kb.tt("dve", t1[i], pre, sinT, ALU.mult, reads=[psb[1], bt, B[1]], writes=[B[2]])
                        kb.tt("pool", vim[i], vim[i], t1[i], ALU.subtract, reads=[B[3], B[2]], writes=[B[3]])
                        kb.scan(wre[i], rhoT, vre[i], ini[0][:, 0:1], reads=[B[1], bt, bini], writes=[B[4]])
                        kb.scan(wim[i], rhoT, vim[i], ini[0][:, 1:2], reads=[B[3], bt, bini], writes=[B[5]])
                        kb.ts("dve", ini[1][:, 0:1], wim[i][:, 511:512], S9[d][:, it:it + 1], None, ALU.mult,
                              reads=[B[5], bp_, bini], writes=[bini])
                        kb.ts("dve", ini[1][:, 1:2], wre[i][:, 511:512], S9[d][:, it:it + 1], None, ALU.mult,
                              reads=[B[4], bp_, bini], writes=[bini])
                        kb.stt("dve", ini[0][:, 0:1], wre[i][:, 511:512], C9[d][:, it:it + 1], ini[1][:, 0:1], ALU.mult,
                               ALU.subtract, reads=[B[4], bini], writes=[bini])
                        kb.stt("dve", ini[0][:, 1:2], wim[i][:, 511:512], C9[d][:, it:it + 1], ini[1][:, 1:2], ALU.mult,
                               ALU.add, reads=[B[5], bini], writes=[bini])
                        kb.tt("pool", hre[i], wre[i], cosT, ALU.mult, reads=[B[4], bt], writes=[B[6]])
                        kb.tt("dve", t1[i], wim[i], sinT, ALU.mult, reads=[B[5], bt, B[2]], writes=[B[2]])
                        kb.tt("pool", hre[i], hre[i], t1[i], ALU.subtract, reads=[B[6], B[2]], writes=[B[6]])
                        kb.tt("pool", him[i], wre[i], sinT, ALU.mult, reads=[B[4], bt], writes=[B[7]])
                        kb.tt("dve", t1[i], wim[i], cosT, ALU.mult, reads=[B[5], bt, B[6]], writes=[B[2]])
                        kb.tt("pool", him[i], him[i], t1[i], ALU.add, reads=[B[7], B[2]], writes=[B[7]])
                        kb.mm(ps[3][0:32, :], cre[d][:, it, :], hre[i], True, False, reads=[bp_, B[6]], writes=[psb[3]])
                        last = d == 1
                        kb.mm(ps[3][0:32, :], cimn[d][:, it, :], him[i], False, last, reads=[bp_, B[7]], writes=[psb[3]])
                        yd = Y5[it * 32:(it + 1) * 32, t * 512:(t + 1) * 512]
                        if d == 0:
                            kb.mm(ps[3][0:32, :], dsel[:, it, :], ub[i], False, True, reads=[bp_, B[0]], writes=[psb[3]])
                            kb.copy("act", go[i][0:32, :], ps[3][0:32, :], reads=[psb[3]], writes=[B[8]])
                            kb.dma(yd, go[i][0:32, :], reads=[B[8]], writes=[bY5], q="sp")
                        else:
                            kb.dma(go[i][0:32, :], yd, reads=[bY5], writes=[B[8]], q="sp")
                            kb.tt("dve", go[i][0:32, :], go[i][0:32, :], ps[3][0:32, ::-1], ALU.add, reads=[psb[3], B[8]],
                                  writes=[B[8]])
                            kb.act(go[i][0:32, :], go[i][0:32, :], AF.Gelu, reads=[B[8]], writes=[B[8]])
                            kb.dma(yd, go[i][0:32, :], reads=[B[8]], writes=[bY5], q="sp")

        def merge_stage(l):
            kb.new_stage()
            wbs = v3(kb.R(4096), 4)
            wbr = v3(kb.R(4096), 4)
            wb5 = v3(kb.R(3072), 3)
            wv = v3(kb.R(1152), 3)
            wg_ = v3(kb.R(1152), 3)
            wo = v3(kb.R(8192), 8)
            bw = Buf()
            kb.dma(wbs, v3(I["wbr_ssd"][l], 4), writes=[bw])
            kb.dma(wbr, v3(I["wbr_ret"][l], 4), writes=[bw])
            kb.dma(wb5, v3(I["wbr_s5"][l], 3), writes=[bw])
            kb.dma(wv, v3(I["glu_wv"][l], 3), writes=[bw])
            kb.dma(wg_, v3(I["glu_wg"][l], 3), writes=[bw])
            kb.dma(wo, v3(I["wout"][l], 8), writes=[bw])
            ys = v3(kb.R(2048), 4)
            yr = v3(kb.R(2048), 4)
            y5 = v3(kb.R(1536), 3)
            y5g = v3(kb.R(1536), 3)
            mixed = v3(kb.R(4096), 8)
            sgt = kb.Fm(512)
            gate = [kb.Fm(512) for _ in range(2)]
            tmp = [kb.Fm(512) for _ in range(2)]
            xr = [kb.Fm(512) for _ in range(2)]
            xo = [kb.Fm(512) for _ in range(2)]
            bi, bg5, bmx = Buf(), Buf(), [Buf() for _ in range(8)]
            bsg = Buf()
            bgate = [Buf(), Buf()]
            btmp = [Buf(), Buf()]
            bxr = [Buf(), Buf()]
            bxo = [Buf(), Buf()]
            for t in range(NT):
                ts0 = slice(t * 512, (t + 1) * 512)
                for f in range(4):
                    kb.dma(ys[:, f, :], YTS[f * 128:(f + 1) * 128, ts0], reads=[bYTS], writes=[bi])
                    kb.dma(yr[:, f, :], YTR[f * 128:(f + 1) * 128, ts0], reads=[bYTR], writes=[bi])
                for f in range(3):
                    kb.dma(y5[:, f, :], Y5[f * 128:(f + 1) * 128, ts0], reads=[bY5], writes=[bi])
                for f in range(3):
                    for k in range(3):
                        kb.mm(ps[0][:], wv[:, k, f * 128:(f + 1) * 128], y5[:, k, :], k == 0, k == 2, reads=[bw, bi],
                              writes=[psb[0]])
                    for k in range(3):
                        kb.mm(ps[1][:], wg_[:, k, f * 128:(f + 1) * 128], y5[:, k, :], k == 0, k == 2, reads=[bw, bi],
                              writes=[psb[1]])
                    kb.act(sgt, ps[1][:], AF.Sigmoid, reads=[psb[1]], writes=[bsg])
                    kb.tt("dve", y5g[:, f, :], ps[0][:], sgt, ALU.mult, reads=[psb[0], bsg], writes=[bg5])
                n = 0
                for i in range(8):
                    for br_, (w_, y_, nk, rb) in enumerate(((wbs, ys, 4, bi), (y5g and wb5, y5g, 3, bg5), (wbr, yr, 4, bi))):
                        pp, bp = ps[2 + n % 2], psb[2 + n % 2]
                        for k in range(nk):
                            kb.mm(pp[:], w_[:, k, i * 128:(i + 1) * 128], y_[:, k, :], k == 0, k == nk - 1, reads=[bw, rb],
                                  writes=[bp])
                        gi = 9 + br_ * 8 + i
                        kb.dma(gate[n % 2], PF[gi * 128:(gi + 1) * 128, ts0], reads=[bPF], writes=[bgate[n % 2]], q="sp")
                        if br_ == 0:
                            kb.tt("dve", tmp[i % 2], pp[:], gate[n % 2], ALU.mult, reads=[bp, bgate[n % 2]],
                                  writes=[btmp[i % 2]])
                        elif br_ == 1:
                            kb.tt("dve", gate[n % 2], pp[:], gate[n % 2], ALU.mult, reads=[bp, bgate[n % 2]],
                                  writes=[bgate[n % 2]])
                            kb.tt("pool", tmp[i % 2], tmp[i % 2], gate[n % 2], ALU.add, reads=[bgate[n % 2], btmp[i % 2]],
                                  writes=[btmp[i % 2]])
                        else:
                            kb.tt("dve", gate[n % 2], pp[:], gate[n % 2], ALU.mult, reads=[bp, bgate[n % 2]],
                                  writes=[bgate[n % 2]])
                            kb.tt("dve", mixed[:, i, :], tmp[i % 2], gate[n % 2], ALU.add,
                                  reads=[bgate[n % 2], btmp[i % 2]], writes=[bmx[i]])
                        n += 1
                for i in range(8):
                    pp, bp = ps[4 + i % 2], psb[4 + i % 2]
                    for k in range(8):
                        kb.mm(pp[:], wo[:, k, i * 128:(i + 1) * 128], mixed[:, k, :], k == 0, k == 7, reads=[bw, bmx[k]],
                              writes=[bp])
                    kb.dma(xr[i % 2], X[i * 128:(i + 1) * 128, ts0], reads=[xb(i, t)], writes=[bxr[i % 2]], q="sp")
                    kb.tt("dve", xo[i % 2], pp[:], xr[i % 2], ALU.add, reads=[bp, bxr[i % 2]], writes=[bxo[i % 2]])
                    kb.dma(X[i * 128:(i + 1) * 128, ts0], xo[i % 2], reads=[bxo[i % 2]], writes=[xb(i, t)], q="sp")

        def final_stage():
            for t in range(NT):
                kb.new_stage()
                xn, bn, xf, bxk = load_norm(X, t, I["g_final"], "z")
                o = v3(kb.Fm(4096), 8)
                bo = Buf()
                for k in range(8):
                    kb.copy("dve" if k % 2 else "act", o[:, k, :], xn[:, k, :].bitcast(F32), reads=[bn], writes=[bo])
                    kb.dma(outT[k * 128:(k + 1) * 128, t * 512:(t + 1) * 512], o[:, k, :], reads=[bo], q="sp")

        kb.new_stage()
        cpb = [kb.Fm(2048), kb.Fm(2048)]
        bcp = [Buf(), Buf()]
        n = 0
        for k in range(8):
            for c0_ in range(0, S, 2048):
                w = min(2048, S - c0_)
                kb.dma(cpb[n % 2][:, 0:w], I["xT"][k * 128:(k + 1) * 128, c0_:c0_ + w], writes=[bcp[n % 2]], q="sp")
                wr = [xb(k, tt) for tt in range(c0_ // 512, (c0_ + w) // 512)]
                kb.dma(X[k * 128:(k + 1) * 128, c0_:c0_ + w], cpb[n % 2][:, 0:w], reads=[bcp[n % 2]], writes=wr, q="sp")
                n += 1
        def on(nm):
            return STAGES is None or nm in STAGES
        for l in range(L):
            if on("ffn1"):
                ffn_stage(l, "g_ffn1", I["wg1"], I["wu1"], I["wd1"])
            if on("inproj"):
                inproj_stage(l)
            if on("ssd") or on("ssdprep"):
                ssd_prep(l)
            if on("ssd") or on("ssd0"):
                ssd_pass(l, 0)
            if on("ssd") or on("ssd1"):
                ssd_pass(l, 1)
            if on("ret"):
                ret_prep(l)
                ret_pass(l, 0)
                ret_pass(l, 1)
            if on("s5"):
                s5_stage(l)
            if on("merge"):
                merge_stage(l)
            if on("ffn2"):
                ffn_stage(l, "g_ffn2", I["wg2"], I["wu2"], I["wd2"])
        final_stage()
        P.emit()
    return nc


def _tile_cols(w, nt):
    K, N = w.shape
    return np.ascontiguousarray(w.reshape(K // 128, 128, nt, 128).transpose(2, 1, 0, 3).reshape(nt, 128, (K // 128) * 128))


def _tile_rows(w):
    K, N = w.shape
    return np.ascontiguousarray(w.reshape(K // 128, 128, N).transpose(1, 0, 2).reshape(128, (K // 128) * N))


def _consts(S):
    c = {}
    idx = np.arange(128)
    k, x = idx[:, None], idx[None, :]
    c["c_ident"] = np.eye(128, dtype=np.float32)
    c["c_tri"] = np.concatenate([(k <= x), -1.0 * (k < x), (k > x), (k < x)], axis=1).astype(np.float32)
    NEG = -30000.0
    mf = np.where(x < k, NEG, 0.0)
    mb = np.where(x > k, NEG, 0.0)
    c["c_maskneg"] = np.concatenate([np.tile(mf, (1, 4)), np.tile(mb, (1, 4))], axis=1).astype(np.float32)
    ie = np.zeros((16, 16, 128), np.float32)
    for j in range(16):
        ie[j, j, :] = 1.0
    c["c_iexp"] = ie.reshape(16, 2048)
    c["c_negiexp"] = (-ie).reshape(16, 2048)
    lg = np.log1p(-np.exp2(-5.0 - np.arange(8, dtype=np.float32))).astype(np.float32)
    s_, l_ = idx[:, None].astype(np.float32), idx[None, :].astype(np.float32)
    rm = np.zeros((128, 2, 8, 128), np.float32)
    for h in range(8):
        rm[:, 0, h, :] = np.where(l_ >= s_, np.exp(lg[h] * np.where(l_ >= s_, l_ - s_, 0.0)), 0.0)
        rm[:, 1, h, :] = np.where(s_ > l_, np.exp(lg[h] * np.where(s_ > l_, s_ - l_, 0.0)), 0.0)
    c["c_retmask"] = rm.reshape(128, 2048)
    rd = np.zeros((128, 40), np.float32)
    t = idx.astype(np.float32)[:, None]
    rd[:, 0:8] = np.exp(lg[None, :] * (127.0 - t))
    rd[:, 8:16] = np.exp(lg[None, :] * t)
    rd[:, 16:24] = np.exp(lg[None, :] * (t + 1.0))
    rd[:, 24:32] = np.exp(lg[None, :] * (128.0 - t))
    rd[:, 32:40] = np.exp(lg[None, :] * 128.0)
    c["c_retdec"] = rd
    pos = np.arange(S, dtype=np.float32)
    inv = (10000.0 ** (-np.arange(0, 64, 2, dtype=np.float32) / 64)).astype(np.float32)
    ang = pos[:, None] * inv[None, :]
    c["c_rope"] = np.concatenate([np.cos(ang), np.sin(ang)], axis=1).astype(np.float32)
    b = np.zeros((128, 128), np.float32)
    b[:64, :64] = 1
    b[64:, 64:] = 1
    c["c_blk64"] = b
    return c


def _prep_weights(inp, L):
    f = lambda a: np.ascontiguousarray(np.asarray(a, dtype=np.float32))
    W = {}
    gt = lambda g: np.ascontiguousarray(f(g).reshape(-1, 8, 128).transpose(0, 2, 1))
    W["g_ffn1"], W["g_mix"], W["g_ffn2"] = gt(inp["ffn1_norm"]), gt(inp["mix_norm"]), gt(inp["ffn2_norm"])
    W["g_final"] = gt(inp["final_norm"])[0]
    for n_, a in (("1", "ffn1"), ("2", "ffn2")):
        W["wg" + n_] = np.stack([_tile_cols(f(inp[a + "_w_gate"][l]), NFC) for l in range(L)])
        W["wu" + n_] = np.stack([_tile_cols(f(inp[a + "_w_up"][l]), NFC) for l in range(L)])
        wd = f(inp[a + "_w_down"])
        W["wd" + n_] = np.stack([np.stack([_tile_rows(wd[l][:, i * 128:(i + 1) * 128]) for i in range(8)]) for l in range(L)])
    win = f(inp["w_in"])
    sz = (512, 768, 16, 384, 512, 512, 512, 512, 3072)
    o = np.cumsum((0,) + sz)
    z, xbc, dt, u, q, k, v, g, gates = [win[:, :, o[i]:o[i + 1]] for i in range(9)]
    fm = np.concatenate([xbc, u, gates], axis=2)
    W["win_fm"] = np.stack([_tile_cols(fm[l], NFM) for l in range(L)])
    tm = np.concatenate([z, q, k, v, g, dt], axis=2)
    W["win_tm"] = np.stack([_tile_rows(tm[l]) for l in range(L)])
    W["bgate"] = np.ascontiguousarray(f(inp["b_gate"]).reshape(L, 24, 128).transpose(0, 2, 1))
    cw = f(inp["ssd_conv_w"])
    W["conv_w"] = np.ascontiguousarray(cw.reshape(L, 5, 6, 128).transpose(0, 3, 2, 1).reshape(L, 128, 30))
    W["conv_b"] = np.ascontiguousarray(f(inp["ssd_conv_b"]).reshape(L, 6, 128).transpose(0, 2, 1))
    rep = lambda a: np.ascontiguousarray(np.broadcast_to(a[:, None, :], (L, 128, a.shape[-1])))
    W["dtb_rep"] = rep(f(inp["ssd_dt_bias"]).reshape(L, 16))
    W["alog_rep"] = rep(f(inp["ssd_a_log"]).reshape(L, 16))
    W["dskip_rep"] = rep(f(inp["ssd_d"]))
    W["ssdnorm_rep"] = rep(f(inp["ssd_norm"]))
    W["retnorm_rep"] = rep(f(inp["ret_norm"]))
    W["wbr_ssd"] = np.stack([_tile_rows(f(inp["w_br_ssd"][l])) for l in range(L)])
    W["wbr_ret"] = np.stack([_tile_rows(f(inp["w_br_ret"][l])) for l in range(L)])
    W["wbr_s5"] = np.stack([_tile_rows(f(inp["w_br_s5"][l])) for l in range(L)])
    W["glu_wv"] = np.stack([_tile_rows(f(inp["s5_glu_wv"][l])) for l in range(L)])
    W["glu_wg"] = np.stack([_tile_rows(f(inp["s5_glu_wg"][l])) for l in range(L)])
    W["wout"] = np.stack([_tile_rows(f(inp["w_out"][l])) for l in range(L)])
    st = lambda a: np.ascontiguousarray(a.reshape(L, 2, 12, 128).transpose(0, 1, 3, 2))
    W["s5_lre"], W["s5_lim"] = st(f(inp["s5_lam_re"])), st(f(inp["s5_lam_im"]))
    ls = np.broadcast_to(f(inp["s5_log_step"])[..., None], (L, 2, 24, 64))
    W["s5_lstep"] = st(np.ascontiguousarray(ls))

    def bpad(b):
        out = np.zeros((L, 128, 12, 128), np.float32)
        for g in range(24):
            it, g2, g8 = g // 2, g % 2, g % 8
            out[:, g2 * 64:(g2 + 1) * 64, it, g8 * 16:(g8 + 1) * 16] = b[:, g]
        return out.reshape(L, 128, 12 * 128)
    W["s5_bre"], W["s5_bim"] = bpad(f(inp["s5_b_re"])), bpad(f(inp["s5_b_im"]))

    def cpad(c):
        out = np.zeros((L, 2, 128, 12, 32), np.float32)
        for g in range(24):
            it, g2 = g // 2, g % 2
            out[:, :, g2 * 64:(g2 + 1) * 64, it, g2 * 16:(g2 + 1) * 16] = c[:, :, g].transpose(0, 1, 3, 2)
        return out.reshape(L, 2, 128, 12 * 32)
    W["s5_cre"], W["s5_cim"] = cpad(f(inp["s5_c_re"])), cpad(f(inp["s5_c_im"]))
    dd = f(inp["s5_d"])
    ds = np.zeros((L, 128, 12, 32), np.float32)
    for g in range(24):
        it, g2, g8 = g // 2, g % 2, g % 8
        for h in range(16):
            ds[:, g8 * 16 + h, it, g2 * 16 + h] = dd[:, g, h]
    W["s5_dsel"] = ds.reshape(L, 128, 12 * 32)
    return W


_CACHE = {}
STAGES = None


def run_model(inp, S, L, nseq, dbg=()):
    key = (S, L, tuple(dbg))
    if key not in _CACHE:
        _CACHE[key] = build_program(S, L, dbg)
    nc = _CACHE[key]
    W = _prep_weights(inp, L)
    W.update(_consts(S))
    x = np.asarray(inp["x"], dtype=np.float32)
    ncore = 8 if nseq == 4 else nseq
    maps = []
    for c in range(ncore):
        m = dict(W)
        m["xT"] = np.ascontiguousarray(x[c % nseq].T)
        maps.append(m)
    res = run_bass_kernel_spmd(nc, maps, core_ids=list(range(ncore)))
    out = np.stack([np.ascontiguousarray(res.results[c]["outT"].T) for c in range(nseq)])
    return out, res


def kernel(**inputs):
    x = inputs["x"]
    out, _ = run_model(inputs, x.shape[1], 2, x.shape[0])
    return out.astype(np.float32)
```

```python
import math
from contextlib import ExitStack
import numpy as np
import concourse.bass as bass
import concourse.mybir as mybir
from concourse.bass_utils import run_bass_kernel_spmd

F32 = mybir.dt.float32
F32R = mybir.dt.float32r
ALU = mybir.AluOpType
AF = mybir.ActivationFunctionType
AX = mybir.AxisListType

ENGS = ("pe", "dve", "act", "pool", "sp")
N_DMA_SEMS = 12
D = 1024
DFF = 2816
NFC = 22
EPS = 1e-6
NTM = 2576
NFM = 33


class Buf:
    __slots__ = ("name", "w", "r")

    def __init__(self, name=""):
        self.name = name
        self.w = None
        self.r = []


class Prog:
    def __init__(self, nc):
        self.nc = nc
        self.ops = []
        self.by_eng = {e: [] for e in ENGS}
        self.fence_deps = set()
        self.fence_pending = {e: False for e in ENGS}
        self.since_fence = []
        self.trace = None

    def op(self, eng, fn, reads=(), writes=(), dma=False):
        oid = len(self.ops)
        deps = set()
        for b in reads:
            if b.w is not None:
                deps.add(b.w)
        for b in writes:
            if b.w is not None:
                deps.add(b.w)
            deps.update(b.r)
        for b in reads:
            if not dma:
                b.r = [r for r in b.r if self.ops[r]["dma"] or self.ops[r]["eng"] != eng]
            b.r.append(oid)
        for b in writes:
            b.w = oid
            b.r = []
        if self.fence_pending[eng]:
            deps.update(self.fence_deps)
            self.fence_pending[eng] = False
        deps.discard(oid)
        import sys as _s
        fr = _s._getframe(1)
        ln = []
        while fr is not None and len(ln) < 4:
            ln.append(fr.f_lineno)
            fr = fr.f_back
        self.ops.append(dict(eng=eng, fn=fn, deps=deps, dma=dma, id=oid, ln=ln))
        self.by_eng[eng].append(oid)
        self.since_fence.append(oid)
        return oid

    def fence(self):
        last = {}
        deps = set()
        for oid in self.since_fence:
            o = self.ops[oid]
            if o["dma"]:
                deps.add(oid)
            else:
                last[o["eng"]] = oid
        deps.update(last.values())
        for e in ENGS:
            if self.fence_pending[e]:
                deps.update(self.fence_deps)
                break
        self.fence_deps = deps
        self.fence_pending = {e: True for e in ENGS}
        self.since_fence = []

    def emit(self):
        nc = self.nc
        ops = self.ops
        signaled = set()
        for o in ops:
            for d in o["deps"]:
                if o["eng"] == "pe" and ops[d]["eng"] == "pe" and not ops[d]["dma"] and not o["dma"]:
                    continue
                signaled.add(d)
        eng_cnt = {e: 0 for e in ENGS}
        dma_rr = {e: 0 for e in ENGS}
        dma_cnt = {e: [0] * N_DMA_SEMS for e in ENGS}
        tokens = {}
        dma_prev = {}
        for o in ops:
            e = o["eng"]
            if o["dma"]:
                i = dma_rr[e] % N_DMA_SEMS
                dma_rr[e] += 1
                prev = dma_cnt[e][i]
                dma_cnt[e][i] += 16
                tokens[o["id"]] = (("dma", e, i), dma_cnt[e][i])
                if prev:
                    dma_prev[o["id"]] = (("dma", e, i), prev)
            elif o["id"] in signaled:
                eng_cnt[e] += 1
                tokens[o["id"]] = (("eng", e), eng_cnt[e])
        final_waits = {e: {} for e in ENGS}
        for o in ops:
            if o["dma"]:
                k, v = tokens[o["id"]]
                final_waits[o["eng"]][k] = max(final_waits[o["eng"]].get(k, 0), v)
        used = sorted(set(k for k, _ in tokens.values()), key=str)
        self.eng_cnt = eng_cnt
        with ExitStack() as st:
            sems = {k: st.enter_context(nc.semaphore("s_" + "_".join(map(str, k)))) for k in used}
            block = st.enter_context(nc.Block())
            handles = {"pe": block.tensor, "dve": block.vector, "act": block.scalar,
                       "pool": block.gpsimd, "sp": block.sync}

            def make(e):
                def body(eng):
                    seen = {}
                    for oid in self.by_eng[e]:
                        o = ops[oid]
                        waits = {}
                        for d in o["deps"]:
                            if ops[d]["eng"] == e and not ops[d]["dma"] and e == "pe" and not o["dma"]:
                                continue
                            k, v = tokens[d]
                            waits[k] = max(waits.get(k, 0), v)
                        if oid in dma_prev:
                            k, v = dma_prev[oid]
                            waits[k] = max(waits.get(k, 0), v)
                        for k, v in waits.items():
                            if seen.get(k, 0) >= v:
                                continue
                            seen[k] = v
                            eng.wait_ge(sems[k], v)
                        try:
                            ins = o["fn"](eng)
                        except BaseException:
                            print("FAILED OP lines", o["ln"], "eng", e)
                            raise
                        if self.trace is not None:
                            try:
                                self.trace[ins.ins.name] = o["ln"]
                            except Exception:
                                pass
                        if oid in tokens:
                            k, v = tokens[oid]
                            ins.then_inc(sems[k], 16 if o["dma"] else 1)
                    for k, v in final_waits[e].items():
                        if seen.get(k, 0) < v:
                            eng.wait_ge(sems[k], v)
                return body

            for e in ENGS:
                if self.by_eng[e]:
                    handles[e](make(e))


class KB:
    def __init__(self, nc, st, nr, nf):
        self.nc = nc
        self.P = Prog(nc)
        self.arr = st.enter_context(nc.sbuf_tensor("arr", [128, nr], F32R))
        self.arf = st.enter_context(nc.sbuf_tensor("arf", [128, nf], F32))
        self.cr = st.enter_context(nc.sbuf_tensor("cr", [128, 2048], F32R))
        self.cf = st.enter_context(nc.sbuf_tensor("cf", [128, 1344], F32))
        self.nr, self.nf = nr, nf
        self.pr = self.pf = 0
        self.ps = [st.enter_context(nc.psum_tensor("ps%d" % i, [128, 512], F32)) for i in range(8)]
        self.psb = [Buf("ps%d" % i) for i in range(8)]
        self.dq = 0

    def new_stage(self):
        self.P.fence()
        self.pr = self.pf = 0

    def R(self, n, shape=None):
        a = self.arr[:, self.pr:self.pr + n]
        self.pr += n
        assert self.pr <= self.nr, ("arr overflow", self.pr)
        return a

    def Fm(self, n):
        a = self.arf[:, self.pf:self.pf + n]
        self.pf += n
        assert self.pf <= self.nf, ("arf overflow", self.pf)
        return a

    def dma(self, out, in_, reads=(), writes=(), q=None):
        if q is None:
            q = "pool"
        return self.P.op(q, lambda e, o=out, i=in_: e.dma_start(out=o, in_=i), reads, writes, dma=True)

    def mm(self, out, lhsT, rhs, start, stop, reads=(), writes=()):
        return self.P.op("pe", lambda e, o=out, l=lhsT, r=rhs, s=start, t=stop: e.matmul(o, l, r, start=s, stop=t),
                         reads, writes)

    def act(self, out, in_, func, reads=(), writes=(), bias=None, scale=None):
        kw = {}
        if bias is not None:
            kw["bias"] = bias
        if scale is not None:
            kw["scale"] = scale
        return self.P.op("act", lambda e, o=out, i=in_, f=func, kw=kw: e.activation(o, i, f, **kw), reads, writes)

    def tt(self, eng, out, in0, in1, op, reads=(), writes=()):
        return self.P.op(eng, lambda e, o=out, a=in0, b=in1, p=op: e.tensor_tensor(o, a, b, p), reads, writes)

    def ts(self, eng, out, in0, s1, s2, op0, op1=None, reads=(), writes=()):
        if op1 is None:
            return self.P.op(eng, lambda e, o=out, a=in0, x=s1, p=op0: e.tensor_scalar(o, a, x, None, p), reads, writes)
        return self.P.op(eng, lambda e, o=out, a=in0, x=s1, y=s2, p=op0, q=op1: e.tensor_scalar(o, a, x, y, p, q),
                         reads, writes)

    def stt(self, eng, out, in0, scalar, in1, op0, op1, reads=(), writes=()):
        return self.P.op(eng, lambda e, o=out, a=in0, s=scalar, b=in1, p=op0, q=op1:
                         e.scalar_tensor_tensor(o, a, s, b, p, q), reads, writes)

    def copy(self, eng, out, in_, reads=(), writes=()):
        if eng == "act":
            return self.act(out, in_, AF.Copy, reads, writes)
        return self.P.op(eng, lambda e, o=out, i=in_: e.tensor_copy(o, i), reads, writes)

    def memset(self, eng, out, val, writes=()):
        return self.P.op(eng, lambda e, o=out, v=val: e.memset(o, v), (), writes)

    def recip(self, out, in_, reads=(), writes=()):
        return self.P.op("dve", lambda e, o=out, i=in_: e.reciprocal(o, i), reads, writes)

    def scan(self, out, d0, d1, init, reads=(), writes=()):
        return self.P.op("dve", lambda e, o=out, a=d0, b=d1, i=init: e.tensor_tensor_scan(o, a, b, i, ALU.mult, ALU.add),
                         reads, writes)


def v3(ap, a):
    return ap.rearrange("p (a b) -> p a b", a=a)


def build_program(S, L, dbg=(), pair=0):
    nc = bass.Bass("TRN2", target_bir_lowering=False)
    NCH = S // 128
    NT = S // 512
    assert S % 512 == 0

    def din(name, shape):
        return nc.dram_tensor(name, list(shape), F32, kind="ExternalInput").ap()

    def dscr(name, shape):
        kind = "ExternalOutput" if name in dbg else "Internal"
        return nc.dram_tensor(name, list(shape), F32, kind=kind).ap()

    I = {}
    I["xT"] = din("xT", [D, S])
    for nm, shp in [("g_ffn1", [L, 128, 8]), ("g_mix", [L, 128, 8]), ("g_ffn2", [L, 128, 8]), ("g_final", [128, 8]),
                    ("wg1", [L, NFC, 128, 1024]), ("wu1", [L, NFC, 128, 1024]), ("wd1", [L, 8, 128, DFF]),
                    ("wg2", [L, NFC, 128, 1024]), ("wu2", [L, NFC, 128, 1024]), ("wd2", [L, 8, 128, DFF]),
                    ("win_fm", [L, NFM, 128, 1024]), ("win_tm", [L, 128, 8 * NTM]), ("bgate", [L, 128, 24]),
                    ("conv_w", [L, 128, 30]), ("conv_b", [L, 128, 6]),
                    ("dtb_rep", [L, 128, 16]), ("alog_rep", [L, 128, 16]), ("dskip_rep", [L, 128, 8]),
                    ("ssdnorm_rep", [L, 128, 512]), ("retnorm_rep", [L, 128, 512]),
                    ("wbr_ssd", [L, 128, 4096]), ("wbr_ret", [L, 128, 4096]), ("wbr_s5", [L, 128, 3072]),
                    ("glu_wv", [L, 128, 1152]), ("glu_wg", [L, 128, 1152]), ("wout", [L, 128, 8192]),
                    ("s5_lre", [L, 2, 128, 12]), ("s5_lim", [L, 2, 128, 12]), ("s5_lstep", [L, 2, 128, 12]),
                    ("s5_bre", [L, 128, 12 * 128]), ("s5_bim", [L, 128, 12 * 128]),
                    ("s5_cre", [L, 2, 128, 12 * 32]), ("s5_cim", [L, 2, 128, 12 * 32]), ("s5_dsel", [L, 128, 12 * 32]),
                    ("c_ident", [128, 128]), ("c_tri", [128, 4 * 128]), ("c_maskneg", [128, 2 * 512]),
                    ("c_negiexp", [16, 16 * 128]), ("c_iexp", [16, 16 * 128]), ("c_retmask", [128, 16 * 128]),
                    ("c_retdec", [128, 2 * 8 + 2 * 8 + 8]), ("c_rope", [S, 64]), ("c_blk64", [128, 128])]:
        I[nm] = din(nm, shp)
    outT = nc.dram_tensor("outT", [D, S], F32, kind="ExternalOutput").ap()
    if pair:
        I["pairsel"] = din("pairsel", [128, 2])
        HS = nc.dram_tensor("HS", [128, 12], F32).ap()
        HR = nc.dram_tensor("HR", [256, 12], F32).ap()
        SS = nc.dram_tensor("SS", [128, 1048], F32).ap()
        SR = nc.dram_tensor("SR", [256, 1048], F32).ap()
        rgroups = [[2 * i, 2 * i + 1] for i in range(pair // 2)]

    X = dscr("X", [D, S])
    PF = dscr("PF", [NFM * 128, S])
    PT = dscr("PT", [S, NTM])
    XS = dscr("XS", [S, 512])
    BTM = dscr("BTM", [S, 128])
    BCT = dscr("BCT", [256, S])
    YF = dscr("YF", [S, 512])
    YTS = dscr("YTS", [512, S])
    QKT = dscr("QKT", [1024, S])
    KTM = dscr("KTM", [S, 512])
    RF = dscr("RF", [S, 512])
    YTR = dscr("YTR", [512, S])
    Y5 = dscr("Y5", [384, S])
    Y5A = dscr("Y5A", [384, S])
    Y5B = dscr("Y5B", [384, S])
    QK2 = dscr("QK2", [S // 128, 64, 2048])

    with ExitStack() as st:
        kb = KB(nc, st, 33280, 14336)
        P = kb.P
        ps, psb = kb.ps, kb.psb

        cbuf = Buf("consts")
        ident = kb.cr[:, 0:128]
        tri = kb.cr[:, 128:640]
        ones = kb.cr[:, 640:768]
        blk64 = kb.cr[:, 768:896]
        iexp = kb.cr[0:16, 896:896 + 0]
        maskneg = kb.cf[:, 0:1024]
        kb.dma(ident, I["c_ident"], writes=[cbuf])
        kb.dma(tri, I["c_tri"], writes=[cbuf])
        kb.dma(blk64, I["c_blk64"], writes=[cbuf])
        kb.dma(maskneg, I["c_maskneg"], writes=[cbuf], q="sp")
        identf = kb.cf[:, 1024:1152]
        kb.dma(identf, I["c_ident"], writes=[cbuf], q="sp")
        onesf = kb.cf[:, 1152:1280]
        kb.memset("dve", onesf, 1.0, writes=[cbuf])
        kb.copy("dve", ones, onesf, reads=[cbuf], writes=[cbuf])

        halo = kb.cf[:, 1280:1292]
        psel = kb.cf[:, 1292:1294]
        bhalo = Buf("halo")
        bSS, bSR = Buf("SS"), Buf("SR")
        if pair:
            kb.dma(psel, I["pairsel"], writes=[cbuf], q="sp")

        def coll(src, dst, reads, writes):
            return P.op("pool", lambda e, a=src, b=dst: e.collective_compute("AllGather", ALU.bypass, rgroups, [a], [b]),
                        reads, writes, dma=True)

        def recv_combine(out, c0, c1, rows, tmp0, tmp1, wbuf):
            kb.dma(tmp0, SR[0:rows, c0:c1], reads=[bSR], writes=[wbuf])
            kb.dma(tmp1, SR[128:128 + rows, c0:c1], reads=[bSR], writes=[wbuf])
            kb.ts("dve", tmp0, tmp0.bitcast(F32), psel[0:rows, 0:1], None, ALU.mult, reads=[wbuf, cbuf], writes=[wbuf])
            kb.stt("dve", out, tmp1.bitcast(F32), psel[0:rows, 1:2], tmp0.bitcast(F32), ALU.mult, ALU.add,
                   reads=[wbuf, cbuf], writes=[wbuf])

        def halo_exchange():
            kb.new_stage()
            hb = kb.Fm(12)
            hr = kb.Fm(24)
            b = Buf()
            bHS, bHR = Buf(), Buf()
            for f in range(6):
                kb.dma(hb[:, 2 * f:2 * f + 2], PF[f * 128:(f + 1) * 128, S - 2:S], reads=[bPF], writes=[b], q="sp")
            kb.dma(HS[:, :], hb, reads=[b], writes=[bHS], q="sp")
            coll(HS[:, :], HR[:, :], [bHS], [bHR])
            kb.dma(hr[:, 0:12], HR[0:128, :], reads=[bHR], writes=[b], q="sp")
            kb.dma(hr[:, 12:24], HR[128:256, :], reads=[bHR], writes=[b], q="sp")
            kb.ts("dve", hr[:, 0:12], hr[:, 0:12], psel[:, 0:1], None, ALU.mult, reads=[b, cbuf], writes=[b])
            kb.stt("dve", halo, hr[:, 12:24], psel[:, 1:2], hr[:, 0:12], ALU.mult, ALU.add, reads=[b, cbuf], writes=[bhalo])

        def state_exchange():
            kb.new_stage()
            coll(SS[:, :], SR[:, :], [bSS], [bSR])

        xbufs = {}

        def xb(k, t):
            key = (k, t)
            if key not in xbufs:
                xbufs[key] = Buf("x%d_%d" % key)
            return xbufs[key]

        def load_norm(src, t, gain_ap, pfx):
            xf = v3(kb.Fm(4096), 8)
            xn = v3(kb.R(4096), 8)
            sq = [kb.R(512), kb.R(512)]
            sqb = [Buf(), Buf()]
            rstd = kb.Fm(512)
            g = kb.Fm(8)
            bx, bn, br, bg = Buf(), Buf(), Buf(), Buf()
            kb.dma(g, gain_ap, writes=[bg], q="sp")
            bxk = [Buf() for _ in range(8)]
            for k in range(8):
                kb.dma(xf[:, k, :], src[k * 128:(k + 1) * 128, t * 512:(t + 1) * 512], reads=[xb(k, t)],
                       writes=[bxk[k]], q="sp")
                kb.act(sq[k % 2], xf[:, k, :], AF.Square, reads=[bxk[k]], writes=[sqb[k % 2]])
                kb.mm(ps[7][:], ones, sq[k % 2], k == 0, k == 7, reads=[sqb[k % 2], cbuf], writes=[psb[7]])
            kb.ts("dve", rstd, ps[7][:], 1.0 / D, EPS, ALU.mult, ALU.add, reads=[psb[7]], writes=[br])
            kb.act(rstd, rstd, AF.Sqrt, reads=[br], writes=[br])
            kb.recip(rstd, rstd, reads=[br], writes=[br])
            for k in range(8):
                kb.stt("dve", xn[:, k, :], xf[:, k, :], g[:, k:k + 1], rstd, ALU.mult, ALU.mult,
                       reads=[bxk[k], br, bg], writes=[bn])
            return xn, bn, xf, bxk

        def ffn_stage(l, gname, wg, wu, wd):
            for t in range(NT):
                kb.new_stage()
                xn, bn, xf, bxk = load_norm(X, t, I[gname][l], "f")
                actt = v3(kb.R(NFC * 512), NFC)
                bact = [Buf() for _ in range(NFC)]
                wgb = [kb.R(1024) for _ in range(2)]
                wub = [kb.R(1024) for _ in range(2)]
                bwg = [Buf(), Buf()]
                bwu = [Buf(), Buf()]
                sg = [kb.Fm(512), kb.Fm(512)]
                bsg = [Buf(), Buf()]
                for j in range(NFC):
                    kb.dma(wgb[j % 2], wg[l, j], writes=[bwg[j % 2]])
                    kb.dma(wub[j % 2], wu[l, j], writes=[bwu[j % 2]])
                    pg, pu = ps[j % 2], ps[2 + j % 2]
                    for k in range(8):
                        kb.mm(pg[:], wgb[j % 2][:, k * 128:(k + 1) * 128], xn[:, k, :], k == 0, k == 7,
                              reads=[bwg[j % 2], bn], writes=[psb[j % 2]])
                    for k in range(8):
                        kb.mm(pu[:], wub[j % 2][:, k * 128:(k + 1) * 128], xn[:, k, :], k == 0, k == 7,
                              reads=[bwu[j % 2], bn], writes=[psb[2 + j % 2]])
                    kb.act(sg[j % 2], pg[:], AF.Silu, reads=[psb[j % 2]], writes=[bsg[j % 2]])
                    kb.tt("dve", actt[:, j, :], sg[j % 2], pu[:], ALU.mult, reads=[bsg[j % 2], psb[2 + j % 2]],
                          writes=[bact[j]])
                wdb = [kb.R(DFF) for _ in range(2)]
                bwd = [Buf(), Buf()]
                xo = [kb.Fm(512), kb.Fm(512)]
                bxo = [Buf(), Buf()]
                for i in range(8):
                    kb.dma(wdb[i % 2], wd[l, i], writes=[bwd[i % 2]])
                    po = ps[4 + i % 2]
                    for j in range(NFC):
                        kb.mm(po[:], wdb[i % 2][:, j * 128:(j + 1) * 128], actt[:, j, :], j == 0, j == NFC - 1,
                              reads=[bwd[i % 2], bact[j]], writes=[psb[4 + i % 2]])
                    kb.stt("dve", xo[i % 2], po[:], 0.5, xf[:, i, :], ALU.mult, ALU.add,
                           reads=[psb[4 + i % 2], bxk[i]], writes=[bxo[i % 2]])
                    kb.dma(X[i * 128:(i + 1) * 128, t * 512:(t + 1) * 512], xo[i % 2], reads=[bxo[i % 2]],
                           writes=[xb(i, t)], q="sp")

        bPF = Buf("PF")
        bPT = Buf("PT")

        def inproj_stage(l):
            for t in range(NT):
                kb.new_stage()
                xn, bn, xf, bxk = load_norm(X, t, I["g_mix"][l], "m")
                bgt = kb.Fm(24)
                bbg = Buf()
                kb.dma(bgt, I["bgate"][l], writes=[bbg], q="sp")
                wb = [kb.R(1024) for _ in range(2)]
                bw = [Buf(), Buf()]
                so = [kb.Fm(512), kb.Fm(512)]
                bso = [Buf(), Buf()]
                for j in range(NFM):
                    kb.dma(wb[j % 2], I["win_fm"][l, j], writes=[bw[j % 2]])
                    pp = ps[j % 2]
                    for k in range(8):
                        kb.mm(pp[:], wb[j % 2][:, k * 128:(k + 1) * 128], xn[:, k, :], k == 0, k == 7,
                              reads=[bw[j % 2], bn], writes=[psb[j % 2]])
                    if j >= 9:
                        kb.act(so[j % 2], pp[:], AF.Sigmoid, reads=[psb[j % 2], bbg], writes=[bso[j % 2]],
                               bias=bgt[:, j - 9:j - 8])
                    else:
                        kb.copy("dve", so[j % 2], pp[:], reads=[psb[j % 2]], writes=[bso[j % 2]])
                    kb.dma(PF[j * 128:(j + 1) * 128, t * 512:(t + 1) * 512], so[j % 2], reads=[bso[j % 2]],
                           writes=[bPF], q="sp")
                dtb = kb.Fm(16)
                bdtb = Buf()
                kb.dma(dtb, I["dtb_rep"][l], writes=[bdtb], q="sp")
                wt = [kb.R(8 * 512) for _ in range(2)]
                bwt = [Buf(), Buf()]
                st_ = [kb.Fm(512) for _ in range(2)]
                bst = [Buf(), Buf()]
                cnt = 0
                for cb in range(6):
                    c0 = cb * 512
                    w = 512 if cb < 5 else 16
                    wv = v3(wt[cb % 2], 8)
                    kb.dma(wv[:, :, 0:w], v3(I["win_tm"][l], 8)[:, :, c0:c0 + w], writes=[bwt[cb % 2]])
                    for c in range(4):
                        pp = ps[2 + cnt % 2]
                        bp = psb[2 + cnt % 2]
                        for k in range(8):
                            kb.mm(pp[:, 0:w], xn[:, k, c * 128:(c + 1) * 128], wv[:, k, 0:w], k == 0, k == 7,
                                  reads=[bwt[cb % 2], bn], writes=[bp])
                        o = st_[cnt % 2][:, 0:w]
                        bo = bst[cnt % 2]
                        if cb in (0, 4):
                            kb.act(o, pp[:, 0:w], AF.Silu, reads=[bp], writes=[bo])
                        elif cb == 5:
                            kb.tt("dve", o, pp[:, 0:w], dtb, ALU.add, reads=[bp, bdtb], writes=[bo])
                            kb.act(o, o, AF.Exp, reads=[bo], writes=[bo])
                            kb.act(o, o, AF.Ln, reads=[bo], writes=[bo], bias=1.0)
                        else:
                            kb.copy("dve", o, pp[:, 0:w], reads=[bp], writes=[bo])
                        r0 = t * 512 + c * 128
                        kb.dma(PT[r0:r0 + 128, c0:c0 + w], o, reads=[bo], writes=[bPT], q="sp")
                        cnt += 1

        bXS, bBTM, bBCT = Buf("XS"), Buf("BTM"), Buf("BCT")

        def ssd_prep(l):
            kb.new_stage()
            cw = kb.Fm(30)
            cbias = kb.Fm(6)
            bcw = Buf()
            kb.dma(cw, I["conv_w"][l], writes=[bcw], q="sp")
            kb.dma(cbias, I["conv_b"][l], writes=[bcw], q="sp")
            xin = [kb.Fm(516) for _ in range(2)]
            bxin = [Buf(), Buf()]
            acc = [kb.Fm(512) for _ in range(2)]
            bacc = [Buf(), Buf()]
            cv = [kb.R(512) for _ in range(2)]
            bcv = [Buf(), Buf()]
            tm = [kb.Fm(512) for _ in range(2)]
            btm = [Buf(), Buf()]
            n = 0
            for t in range(NT):
                for f in range(6):
                    xi, bi = xin[n % 2], bxin[n % 2]
                    lo = t * 512 - 2
                    hi = t * 512 + 514
                    a, b = max(lo, 0), min(hi, S)
                    if a > lo:
                        kb.memset("pool", xi[:, 0:2], 0.0, writes=[bi])
                    if b < hi:
                        if pair:
                            kb.copy("pool", xi[:, 514:515], halo[:, 2 * f + 1:2 * f + 2], reads=[bhalo], writes=[bi])
                            kb.copy("pool", xi[:, 515:516], halo[:, 2 * f:2 * f + 1], reads=[bhalo], writes=[bi])
                        else:
                            kb.memset("pool", xi[:, 514:516], 0.0, writes=[bi])
                    kb.dma(xi[:, a - lo:b - lo], PF[f * 128:(f + 1) * 128, a:b], reads=[bPF], writes=[bi], q="sp")
                    ac, ba = acc[n % 2], bacc[n % 2]
                    kb.ts("dve", ac, xi[:, 0:512], cw[:, f * 5:f * 5 + 1], None, ALU.mult, reads=[bi, bcw], writes=[ba])
                    for j in range(1, 5):
                        kb.stt("dve", ac, xi[:, j:j + 512], cw[:, f * 5 + j:f * 5 + j + 1], ac, ALU.mult, ALU.add,
                               reads=[bi, bcw, ba], writes=[ba])
                    c_, bc_ = cv[n % 2], bcv[n % 2]
                    kb.act(c_, ac, AF.Silu, reads=[ba, bcw], writes=[bc_], bias=cbias[:, f:f + 1])
                    if f < 5:
                        for c in range(4):
                            pp, bp = ps[(4 * n + c) % 4], psb[(4 * n + c) % 4]
                            kb.mm(pp[:, 0:128], c_[:, c * 128:(c + 1) * 128], ident, True, True,
                                  reads=[bc_, cbuf], writes=[bp])
                            o, bo = tm[c % 2][:, 0:128], btm[c % 2]
                            kb.copy("act" if c % 2 else "dve", o, pp[:, 0:128], reads=[bp], writes=[bo])
                            r0 = t * 512 + c * 128
                            if f < 4:
                                kb.dma(XS[r0:r0 + 128, f * 128:(f + 1) * 128], o, reads=[bo], writes=[bXS], q="sp")
                            else:
                                kb.dma(BTM[r0:r0 + 128, :], o, reads=[bo], writes=[bBTM], q="sp")
                    if f >= 4:
                        kb.dma(BCT[(f - 4) * 128:(f - 3) * 128, t * 512:(t + 1) * 512], c_, reads=[bc_], writes=[bBCT])
                    n += 1

        bYF, bYTS = Buf("YF"), Buf("YTS")

        def ssd_pass(l, dirn):
            kb.new_stage()
            arep = kb.Fm(16)
            dsk = kb.Fm(8)
            gn = kb.Fm(512)
            bpar = Buf()
            kb.dma(arep, I["alog_rep"][l], writes=[bpar], q="sp")
            kb.dma(dsk, I["dskip_rep"][l], writes=[bpar], q="sp")
            kb.dma(gn, I["ssdnorm_rep"][l], writes=[bpar], q="sp")
            kb.act(arep, arep, AF.Exp, reads=[bpar], writes=[bpar])
            kb.ts("dve", arep, arep, -1.0, None, ALU.mult, reads=[bpar], writes=[bpar])
            niexp = v3(kb.Fm(2048)[0:16, :], 16)
            iex = v3(kb.Fm(2048)[0:16, :], 16)
            kb.dma(niexp, v3(I["c_negiexp"], 16), writes=[bpar], q="sp")
            kb.dma(iex, v3(I["c_iexp"], 16), writes=[bpar], q="sp")
            mk = v3(maskneg[:, dirn * 512:(dirn + 1) * 512], 4)
            Sst = kb.R(512)
            bS = Buf()
            if pair and dirn == 1:
                recv_combine(Sst, 0, 512, 128, kb.R(512), kb.R(512), bS)
            else:
                kb.ts("dve", Sst, onesf[:, 0:1].to_broadcast([128, 512]), 0.0, None, ALU.mult, reads=[cbuf], writes=[bS])
            NB = 2
            xs_ = [kb.Fm(512) for _ in range(NB)]
            dt_ = [kb.Fm(16) for _ in range(NB)]
            bt_ = [kb.R(128) for _ in range(NB)]
            bct = [kb.R(512) for _ in range(NB)]
            bin_ = [Buf() for _ in range(NB)]
            for i_ in range(NB):
                kb.ts("dve", bct[i_][:, 256:512], onesf[:, 0:1].to_broadcast([128, 256]), 0.0, None, ALU.mult,
                      reads=[cbuf], writes=[bin_[i_]])
            da = [kb.Fm(8) for _ in range(NB)]
            xd = [kb.R(512) for _ in range(NB)]
            xdw = [kb.R(512) for _ in range(NB)]
            csf = [kb.Fm(128) for _ in range(NB)]
            zt = [kb.Fm(1024) for _ in range(NB)]
            ee = [kb.Fm(1024) for _ in range(NB)]
            gt = [kb.Fm(256) for _ in range(NB)]
            mt = [kb.R(1024) for _ in range(NB)]
            ex = [kb.Fm(16) for _ in range(NB)]
            dec = [kb.Fm(8) for _ in range(NB)]
            yt_ = [kb.Fm(512) for _ in range(NB)]
            t2 = [kb.Fm(512) for _ in range(NB)]
            zz = [kb.Fm(512) for _ in range(NB)]
            yo = [kb.R(512) for _ in range(NB)]
            ss = [kb.Fm(8) for _ in range(NB)]
            bw_ = [[Buf() for _ in range(16)] for _ in range(NB)]
            order = range(NCH) if dirn == 0 else range(NCH - 1, -1, -1)
            for n, c in enumerate(order):
                i = n % NB
                B = bw_[i]
                r0 = c * 128
                kb.dma(xs_[i], XS[r0:r0 + 128, :], reads=[bXS], writes=[bin_[i]], q="sp")
                kb.dma(dt_[i], PT[r0:r0 + 128, 2560:2576], reads=[bPT], writes=[bin_[i]], q="sp")
                kb.dma(bt_[i], BTM[r0:r0 + 128, :], reads=[bBTM], writes=[bin_[i]])
                kb.dma(bct[i][:, 0:128], BCT[0:128, r0:r0 + 128], reads=[bBCT], writes=[bin_[i]])
                kb.dma(bct[i][:, 128:256], BCT[128:256, r0:r0 + 128], reads=[bBCT], writes=[bin_[i]])
                for g in range(2):
                    kb.dma(bct[i][g * 64:(g + 1) * 64, 256 + g * 128:384 + g * 128],
                           BCT[128 + g * 64:192 + g * 64, r0:r0 + 128], reads=[bBCT], writes=[bin_[i]])
                dtd = dt_[i][:, dirn * 8:(dirn + 1) * 8]
                kb.tt("dve", da[i], dtd, arep[:, dirn * 8:(dirn + 1) * 8], ALU.mult, reads=[bin_[i], bpar], writes=[B[0]])
                kb.tt("pool", v3(xd[i], 8), v3(xs_[i], 8), dtd.unsqueeze(2).to_broadcast([128, 8, 64]), ALU.mult,
                      reads=[bin_[i]], writes=[B[1]])
                trisel = kb.cf
                tsl = tri[:, 0:128] if dirn == 0 else tri[:, 128:256]
                kb.mm(ps[0][0:8, 0:128], da[i].bitcast(F32), tsl.bitcast(F32), True, True, reads=[B[0], cbuf],
                      writes=[psb[0]])
                kb.copy("act", csf[i][0:8, :], ps[0][0:8, 0:128], reads=[psb[0]], writes=[B[2]])
                kb.tt("dve", v3(zt[i][0:8, :], 8), csf[i][0:8, :].unsqueeze(1).to_broadcast([8, 8, 128]),
                      iex[0:8, 0:8, :], ALU.mult, reads=[B[2], bpar], writes=[B[3]])
                for hb in range(2):
                    pp, bp = ps[1 + hb], psb[1 + hb]
                    kb.mm(pp[:], onesf[0:8, :], zt[i][0:8, hb * 512:(hb + 1) * 512], True, False,
                          reads=[B[3], cbuf], writes=[bp])
                    kb.mm(pp[:], csf[i][0:8, :], niexp[0:8, hb * 4:(hb + 1) * 4, :], False, False,
                          reads=[B[2], bpar], writes=[bp])
                    kb.mm(pp[:], identf, mk, False, True, reads=[cbuf], writes=[bp])
                    kb.act(ee[i][:, hb * 512:(hb + 1) * 512], pp[:], AF.Exp, reads=[bp], writes=[B[4]])
                kb.mm(ps[3][:, 0:256], bct[i][:, 0:128], bct[i][:, 256:512], True, True, reads=[bin_[i]], writes=[psb[3]])
                kb.copy("act", gt[i], ps[3][:, 0:256], reads=[psb[3]], writes=[B[5]])
                for g in range(2):
                    kb.tt("dve" if g else "pool", v3(mt[i][:, g * 512:(g + 1) * 512], 4),
                          v3(ee[i][:, g * 512:(g + 1) * 512], 4),
                          gt[i][:, g * 128:(g + 1) * 128].unsqueeze(1).to_broadcast([128, 4, 128]), ALU.mult,
                          reads=[B[4], B[5]], writes=[B[6]])
                if dirn == 0:
                    wl, ol = tri[:, 256:384], tri[:, 0:128]
                else:
                    wl, ol = tri[:, 384:512], None
                kb.mm(ps[4][:, 0:8], wl.bitcast(F32), da[i], True, True, reads=[B[0], cbuf], writes=[psb[4]])
                if dirn == 0:
                    kb.mm(ps[4][:, 8:16], ol.bitcast(F32), da[i], True, True, reads=[B[0], cbuf], writes=[psb[4]])
                else:
                    kb.mm(ps[4][:, 8:16], onesf, da[i], True, False, reads=[B[0], cbuf], writes=[psb[4]])
                    kb.mm(ps[4][:, 8:16], tri[:, 128:256].bitcast(F32).rearrange("p x -> p x"), da[i], False, True,
                          reads=[B[0], cbuf], writes=[psb[4]])
                kb.act(ex[i], ps[4][:, 0:16], AF.Exp, reads=[psb[4]], writes=[B[7]])
                kb.mm(ps[4][:, 16:24], onesf, da[i], True, True, reads=[B[0], cbuf], writes=[psb[4]])
                kb.act(dec[i], ps[4][:, 16:24], AF.Exp, reads=[psb[4]], writes=[B[8]])
                for h in range(8):
                    kb.mm(ps[5][:, h * 64:(h + 1) * 64], mt[i][:, h * 128:(h + 1) * 128], xd[i][:, h * 64:(h + 1) * 64],
                          True, True, reads=[B[6], B[1]], writes=[psb[5]])
                kb.mm(ps[6][:], bct[i][:, 128:256], Sst, True, True, reads=[bin_[i], bS], writes=[psb[6]])
                kb.tt("dve", v3(t2[i], 8), v3(ps[6][:], 8), ex[i][:, 8:16].unsqueeze(2).to_broadcast([128, 8, 64]),
                      ALU.mult, reads=[psb[6], B[7]], writes=[B[9]])
                kb.tt("dve", yt_[i], ps[5][:], t2[i], ALU.add, reads=[psb[5], B[9]], writes=[B[10]])
                kb.tt("pool", v3(xdw[i], 8), v3(xd[i], 8), ex[i][:, 0:8].unsqueeze(2).to_broadcast([128, 8, 64]),
                      ALU.mult, reads=[B[1], B[7]], writes=[B[11]])
                kb.mm(ps[7][:], bt_[i], xdw[i], True, True, reads=[bin_[i], B[11]], writes=[psb[7]])
                for g in range(2):
                    sl = slice(g * 64, (g + 1) * 64)
                    kb.tt("dve", v3(Sst[sl, g * 256:(g + 1) * 256], 4), v3(Sst[sl, g * 256:(g + 1) * 256], 4),
                          dec[i][sl, g * 4:(g + 1) * 4].unsqueeze(2).to_broadcast([64, 4, 64]), ALU.mult,
                          reads=[bS, B[8]], writes=[bS])
                for g in range(2):
                    sl = slice(g * 64, (g + 1) * 64)
                    kb.tt("dve", Sst[sl, g * 256:(g + 1) * 256], Sst[sl, g * 256:(g + 1) * 256],
                          ps[7][sl, g * 256:(g + 1) * 256], ALU.add, reads=[bS, psb[7]], writes=[bS])
                if dirn == 0:
                    kb.tt("pool", v3(t2[i], 8), v3(xs_[i], 8), dsk.unsqueeze(2).to_broadcast([128, 8, 64]), ALU.mult,
                          reads=[bin_[i], bpar, B[10]], writes=[B[9]])
                    kb.tt("dve", yt_[i], yt_[i], t2[i], ALU.add, reads=[B[9], B[10]], writes=[B[10]])
                    kb.dma(YF[r0:r0 + 128, :], yt_[i], reads=[B[10]], writes=[bYF], q="sp")
                else:
                    kb.dma(t2[i], YF[r0:r0 + 128, :], reads=[bYF, B[9], B[10]], writes=[B[9]], q="sp")
                    kb.dma(zz[i], PT[r0:r0 + 128, 0:512], reads=[bPT], writes=[B[12]], q="sp")
                    kb.tt("dve", yt_[i], yt_[i], t2[i], ALU.add, reads=[B[9], B[10]], writes=[B[10]])
                    kb.tt("dve", yt_[i], yt_[i], zz[i], ALU.mult, reads=[B[12], B[10]], writes=[B[10]])
                    kb.P.op("act", lambda e, o=t2[i], a=yt_[i], s=ss[i]: e.activation(o, a, AF.Square, accum_out=s[:, 0:1]),
                            reads=[B[10], B[9]], writes=[B[9], B[13]])
                    kb.ts("dve", ss[i][:, 0:1], ss[i][:, 0:1], 1.0 / 512, EPS, ALU.mult, ALU.add, reads=[B[13]], writes=[B[13]])
                    kb.act(ss[i][:, 0:1], ss[i][:, 0:1], AF.Sqrt, reads=[B[13]], writes=[B[13]])
                    kb.recip(ss[i][:, 0:1], ss[i][:, 0:1], reads=[B[13]], writes=[B[13]])
                    kb.stt("dve", yo[i], yt_[i], ss[i][:, 0:1], gn, ALU.mult, ALU.mult, reads=[B[10], B[13], bpar],
                           writes=[B[14]])
                    for f in range(4):
                        kb.mm(ps[0][:, f * 128:(f + 1) * 128], yo[i][:, f * 128:(f + 1) * 128], ident, True, True,
                              reads=[B[14], cbuf], writes=[psb[0]])
                    kb.copy("act", t2[i], ps[0][:], reads=[psb[0], B[9]], writes=[B[9]])
                    for f in range(4):
                        kb.dma(YTS[f * 128:(f + 1) * 128, r0:r0 + 128], t2[i][:, f * 128:(f + 1) * 128], reads=[B[9]],
                               writes=[bYTS], q="sp")
            if pair and dirn == 0:
                kb.dma(SS[:, 0:512], Sst, reads=[bS], writes=[bSS])

        bQKT, bKTM = Buf("QKT"), Buf("KTM")

        def ret_prep(l):
            kb.new_stage()
            NB = 2
            qk = [kb.Fm(1024) for _ in range(NB)]
            rp = [kb.Fm(64) for _ in range(NB)]
            ro = [kb.R(1024) for _ in range(NB)]
            tmp = [kb.Fm(512) for _ in range(NB)]
            oT = [kb.Fm(512) for _ in range(NB)]
            bb = [[Buf() for _ in range(6)] for _ in range(NB)]
            for c in range(NCH):
                i = c % NB
                B = bb[i]
                r0 = c * 128
                kb.dma(qk[i], PT[r0:r0 + 128, 512:1536], reads=[bPT], writes=[B[0]], q="sp")
                kb.dma(rp[i], I["c_rope"][r0:r0 + 128, :], writes=[B[0]], q="sp")
                cosb = rp[i][:, 0:32].unsqueeze(1).to_broadcast([128, 16, 32])
                sinb = rp[i][:, 32:64].unsqueeze(1).to_broadcast([128, 16, 32])
                x4 = qk[i].rearrange("p (h t d) -> p h t d", h=16, t=2)
                o4 = ro[i].rearrange("p (h t d) -> p h t d", h=16, t=2)
                tm4 = tmp[i].rearrange("p (h d) -> p h d", h=16)
                t1, t2_ = x4[:, :, 0, :], x4[:, :, 1, :]
                kb.tt("dve", tm4, t2_, sinb, ALU.mult, reads=[B[0]], writes=[B[1]])
                kb.tt("pool", o4[:, :, 0, :], t1, cosb, ALU.mult, reads=[B[0]], writes=[B[2]])
                kb.tt("dve", o4[:, :, 0, :], o4[:, :, 0, :], tm4, ALU.subtract, reads=[B[1], B[2]], writes=[B[2]])
                kb.tt("dve", tm4, t1, sinb, ALU.mult, reads=[B[0], B[2]], writes=[B[1]])
                kb.tt("pool", o4[:, :, 1, :], t2_, cosb, ALU.mult, reads=[B[0]], writes=[B[3]])
                kb.tt("dve", o4[:, :, 1, :], o4[:, :, 1, :], tm4, ALU.add, reads=[B[1], B[3]], writes=[B[3]])
                kb.ts("dve", ro[i][:, 512:1024], ro[i][:, 512:1024], 0.125, None, ALU.mult, reads=[B[2], B[3]],
                      writes=[B[2], B[3]])
                kb.dma(KTM[r0:r0 + 128, :], ro[i][:, 512:1024], reads=[B[2], B[3]], writes=[bKTM])
                for f in range(8):
                    pp, bp = ps[f % 2], psb[f % 2]
                    kb.mm(pp[:, 0:128], ro[i][:, f * 128:(f + 1) * 128], ident, True, True, reads=[B[2], B[3], cbuf],
                          writes=[bp])
                    o = oT[i][:, (f % 4) * 128:(f % 4 + 1) * 128]
                    kb.copy("act", o, pp[:, 0:128], reads=[bp], writes=[B[4 + (f % 2)]])
                    for hh in range(2):
                        kb.dma(QK2[c, :, (2 * f + hh) * 128:(2 * f + hh + 1) * 128], o[hh * 64:(hh + 1) * 64, :],
                               reads=[B[4 + (f % 2)]], writes=[bQKT], q="sp")

        bRF, bYTR = Buf("RF"), Buf("YTR")

        def ret_pass(l, dirn):
            kb.new_stage()
            rmask = v3(kb.Fm(1024), 8)
            rdec = kb.Fm(40)
            gn = kb.Fm(512)
            bpar = Buf()
            kb.dma(rmask, v3(I["c_retmask"][:, dirn * 1024:(dirn + 1) * 1024], 8), writes=[bpar], q="sp")
            kb.dma(rdec, I["c_retdec"], writes=[bpar], q="sp")
            kb.dma(gn, I["retnorm_rep"][l], writes=[bpar], q="sp")
            kdec = rdec[:, dirn * 8:dirn * 8 + 8]
            qdec = rdec[:, 16 + dirn * 8:16 + dirn * 8 + 8]
            cdec = rdec[0:64, 32:40]
            Sst = kb.R(512)[0:64, :]
            bS = Buf()
            if pair and dirn == 1:
                recv_combine(Sst, 512, 1024, 64, kb.R(512)[0:64, :], kb.R(512)[0:64, :], bS)
            else:
                kb.ts("dve", Sst, onesf[0:64, 0:1].to_broadcast([64, 512]), 0.0, None, ALU.mult, reads=[cbuf], writes=[bS])
            NB = 2
            qkt = [kb.R(2048) for _ in range(NB)]
            ktm = [kb.R(512) for _ in range(NB)]
            vtm = [kb.R(512) for _ in range(NB)]
            vw = [kb.R(512) for _ in range(NB)]
            mt = [kb.R(1024) for _ in range(NB)]
            yt_ = [kb.Fm(512) for _ in range(NB)]
            t2 = [kb.Fm(512) for _ in range(NB)]
            gg = [kb.Fm(512) for _ in range(NB)]
            yo = [kb.R(512) for _ in range(NB)]
            ss = [kb.Fm(16) for _ in range(NB)]
            bb = [[Buf() for _ in range(12)] for _ in range(NB)]
            order = range(NCH) if dirn == 0 else range(NCH - 1, -1, -1)
            for n, c in enumerate(order):
                i = n % NB
                B = bb[i]
                r0 = c * 128
                q3 = v3(qkt[i][0:64, :], 16)
                kb.dma(qkt[i][0:64, :], QK2[c], reads=[bQKT], writes=[B[0]])
                kb.dma(ktm[i], KTM[r0:r0 + 128, :], reads=[bKTM], writes=[B[0]])
                kb.dma(vtm[i], PT[r0:r0 + 128, 1536:2048], reads=[bPT], writes=[B[0]])
                for h in range(8):
                    kb.mm(ps[h // 4][:, (h % 4) * 128:(h % 4 + 1) * 128], q3[:, 8 + h, :], q3[:, h, :], True, True,
                          reads=[B[0]], writes=[psb[h // 4]])
                for hb in range(2):
                    kb.tt("dve", v3(mt[i][:, hb * 512:(hb + 1) * 512], 4), v3(ps[hb][:], 4), rmask[:, hb * 4:(hb + 1) * 4, :],
                          ALU.mult, reads=[psb[hb], bpar], writes=[B[1]])
                for h in range(8):
                    kb.mm(ps[5][:, h * 64:(h + 1) * 64], mt[i][:, h * 128:(h + 1) * 128], vtm[i][:, h * 64:(h + 1) * 64],
                          True, True, reads=[B[1], B[0]], writes=[psb[5]])
                for h in range(8):
                    kb.mm(ps[6][:, h * 64:(h + 1) * 64], q3[:, h, :], Sst[:, h * 64:(h + 1) * 64], True, True,
                          reads=[B[0], bS], writes=[psb[6]])
                kb.tt("dve", v3(t2[i], 8), v3(ps[6][:], 8), qdec.unsqueeze(2).to_broadcast([128, 8, 64]), ALU.mult,
                      reads=[psb[6], bpar], writes=[B[2]])
                kb.tt("dve", yt_[i], ps[5][:], t2[i], ALU.add, reads=[psb[5], B[2]], writes=[B[3]])
                kb.tt("pool", v3(vw[i], 8), v3(vtm[i], 8), kdec.unsqueeze(2).to_broadcast([128, 8, 64]), ALU.mult,
                      reads=[B[0], bpar], writes=[B[4]])
                for h in range(8):
                    kb.mm(ps[7][0:64, h * 64:(h + 1) * 64], ktm[i][:, h * 64:(h + 1) * 64], vw[i][:, h * 64:(h + 1) * 64],
                          True, True, reads=[B[0], B[4]], writes=[psb[7]])
                kb.tt("dve", v3(Sst, 8), v3(Sst, 8), cdec.unsqueeze(2).to_broadcast([64, 8, 64]), ALU.mult,
                      reads=[bS, bpar], writes=[bS])
                kb.tt("dve", Sst, Sst, ps[7][0:64, :], ALU.add, reads=[bS, psb[7]], writes=[bS])
                if dirn == 0:
                    kb.dma(RF[r0:r0 + 128, :], yt_[i], reads=[B[3]], writes=[bRF], q="sp")
                else:
                    kb.dma(t2[i], RF[r0:r0 + 128, :], reads=[bRF, B[2], B[3]], writes=[B[2]], q="sp")
                    kb.dma(gg[i], PT[r0:r0 + 128, 2048:2560], reads=[bPT], writes=[B[5]], q="sp")
                    kb.tt("dve", yt_[i], yt_[i], t2[i], ALU.add, reads=[B[2], B[3]], writes=[B[3]])
                    kb.tt("pool", t2[i], yt_[i], yt_[i], ALU.mult, reads=[B[3], B[2]], writes=[B[2]])
                    kb.P.op("dve", lambda e, o=ss[i][:, 0:8], a=v3(t2[i], 8): e.tensor_reduce(o, a, AX.X, ALU.add),
                            reads=[B[2]], writes=[B[6]])
                    kb.ts("dve", ss[i][:, 0:8], ss[i][:, 0:8], 1.0 / 64, EPS, ALU.mult, ALU.add, reads=[B[6]], writes=[B[6]])
                    kb.act(ss[i][:, 0:8], ss[i][:, 0:8], AF.Sqrt, reads=[B[6]], writes=[B[6]])
                    kb.recip(ss[i][:, 0:8], ss[i][:, 0:8], reads=[B[6]], writes=[B[6]])
                    kb.tt("dve", v3(yt_[i], 8), v3(yt_[i], 8), ss[i][:, 0:8].unsqueeze(2).to_broadcast([128, 8, 64]),
                          ALU.mult, reads=[B[3], B[6]], writes=[B[3]])
                    kb.tt("pool", yt_[i], yt_[i], gn, ALU.mult, reads=[B[3], bpar], writes=[B[3]])
                    kb.tt("dve", yo[i], yt_[i], gg[i], ALU.mult, reads=[B[3], B[5]], writes=[B[7]])
                    for f in range(4):
                        kb.mm(ps[2][:, f * 128:(f + 1) * 128], yo[i][:, f * 128:(f + 1) * 128], ident, True, True,
                              reads=[B[7], cbuf], writes=[psb[2]])
                    kb.copy("act", t2[i], ps[2][:], reads=[psb[2], B[2]], writes=[B[2]])
                    for f in range(4):
                        kb.dma(YTR[f * 128:(f + 1) * 128, r0:r0 + 128], t2[i][:, f * 128:(f + 1) * 128], reads=[B[2]],
                               writes=[bYTR], q="sp")
            if pair and dirn == 0:
                kb.dma(SS[0:64, 512:1024], Sst, reads=[bS], writes=[bSS])

        bY5 = Buf("Y5")
        bY5A, bY5B = Buf("Y5A"), Buf("Y5B")

        def s5_stage(l, dirs=(0, 1)):
            kb.new_stage()
            bp_ = Buf("s5par")
            def F(n):
                return kb.Fm(n)
            rho = [F(12), F(12)]
            c0 = [F(12), F(12)]
            s0 = [F(12), F(12)]
            C9 = [F(12), F(12)]
            S9 = [F(12), F(12)]
            cfr = [F(12), F(12)]
            cfi = [F(12), F(12)]
            tA, tB, tC, tD = F(12), F(12), F(12), F(12)
            halfpi = F(1)
            kb.memset("dve", halfpi, math.pi / 2, writes=[bp_])
            for d in range(2):
                lre, lim, stp = F(12), F(12), F(12)
                kb.dma(lre, I["s5_lre"][l, d], writes=[bp_], q="sp")
                kb.dma(lim, I["s5_lim"][l, d], writes=[bp_], q="sp")
                kb.dma(stp, I["s5_lstep"][l, d], writes=[bp_], q="sp")
                R_, W_ = [bp_], [bp_]
                kb.ts("dve", lre, lre, -1e-4, None, ALU.min, reads=R_, writes=W_)
                kb.act(stp, stp, AF.Exp, reads=R_, writes=W_)
                kb.tt("dve", tA, lre, stp, ALU.mult, reads=R_, writes=W_)
                kb.act(rho[d], tA, AF.Exp, reads=R_, writes=W_)
                kb.tt("dve", tB, lim, stp, ALU.mult, reads=R_, writes=W_)
                kb.act(s0[d], tB, AF.Sin, reads=R_, writes=W_, scale=1.0 / 32)
                kb.act(c0[d], tB, AF.Sin, reads=R_, writes=W_, scale=1.0 / 32, bias=halfpi[:, 0:1])

                def dbl(cc, sn):
                    kb.tt("dve", tC, cc, cc, ALU.mult, reads=R_, writes=W_)
                    kb.tt("dve", tD, sn, sn, ALU.mult, reads=R_, writes=W_)
                    kb.tt("dve", tD, tC, tD, ALU.subtract, reads=R_, writes=W_)
                    kb.tt("dve", tC, cc, sn, ALU.mult, reads=R_, writes=W_)
                    kb.ts("dve", sn, tC, 2.0, None, ALU.mult, reads=R_, writes=W_)
                    kb.copy("dve", cc, tD, reads=R_, writes=W_)
                for _ in range(5):
                    dbl(c0[d], s0[d])
                kb.copy("dve", C9[d], c0[d], reads=R_, writes=W_)
                kb.copy("dve", S9[d], s0[d], reads=R_, writes=W_)
                for _ in range(9):
                    dbl(C9[d], S9[d])
                lbr, lbi, den = F(12), F(12), F(12)
                kb.tt("dve", lbr, rho[d], c0[d], ALU.mult, reads=R_, writes=W_)
                kb.tt("dve", lbi, rho[d], s0[d], ALU.mult, reads=R_, writes=W_)
                kb.ts("dve", lbr, lbr, -1.0, None, ALU.add, reads=R_, writes=W_)
                kb.tt("dve", den, lre, lre, ALU.mult, reads=R_, writes=W_)
                kb.tt("dve", tC, lim, lim, ALU.mult, reads=R_, writes=W_)
                kb.tt("dve", den, den, tC, ALU.add, reads=R_, writes=W_)
                kb.recip(den, den, reads=R_, writes=W_)
                kb.tt("dve", tC, lbr, lre, ALU.mult, reads=R_, writes=W_)
                kb.tt("dve", tD, lbi, lim, ALU.mult, reads=R_, writes=W_)
                kb.tt("dve", tC, tC, tD, ALU.add, reads=R_, writes=W_)
                kb.tt("dve", cfr[d], tC, den, ALU.mult, reads=R_, writes=W_)
                kb.tt("dve", tC, lbi, lre, ALU.mult, reads=R_, writes=W_)
                kb.tt("dve", tD, lbr, lim, ALU.mult, reads=R_, writes=W_)
                kb.tt("dve", tC, tC, tD, ALU.subtract, reads=R_, writes=W_)
                kb.tt("dve", cfi[d], tC, den, ALU.mult, reads=R_, writes=W_)
            bre = v3(kb.Fm(1536), 12)
            bim = v3(kb.Fm(1536), 12)
            kb.dma(bre, v3(I["s5_bre"][l], 12), writes=[bp_], q="sp")
            kb.dma(bim, v3(I["s5_bim"][l], 12), writes=[bp_], q="sp")
            dsel = v3(kb.R(384), 12)
            kb.dma(dsel, v3(I["s5_dsel"][l], 12), writes=[bp_])
            cre = [v3(kb.R(384), 12) for _ in range(2)]
            cimn = [v3(kb.R(384), 12) for _ in range(2)]
            for d in range(2):
                kb.dma(cre[d], v3(I["s5_cre"][l, d], 12), writes=[bp_])
                kb.dma(cimn[d], v3(I["s5_cim"][l, d], 12), writes=[bp_])
                kb.ts("dve", cimn[d], cimn[d], -1.0, None, ALU.mult, reads=[bp_], writes=[bp_])
            bbr = kb.R(128)
            bbi = kb.R(128)
            tq = kb.Fm(128)
            btr = [kb.R(128), kb.R(128)]
            bti = [kb.R(128), kb.R(128)]
            cosT = [kb.Fm(512), kb.Fm(512)]
            sinT = [kb.Fm(512), kb.Fm(512)]
            rhoT = [kb.Fm(512), kb.Fm(512)]
            tc_, ts_ = kb.Fm(256), kb.Fm(256)
            ub = [kb.R(512) for _ in range(2)]
            vre = [kb.Fm(512) for _ in range(2)]
            vim = [kb.Fm(512) for _ in range(2)]
            wre = [kb.Fm(512) for _ in range(2)]
            wim = [kb.Fm(512) for _ in range(2)]
            hre = [kb.R(512) for _ in range(2)]
            him = [kb.R(512) for _ in range(2)]
            t1 = [kb.Fm(512) for _ in range(2)]
            ini = [[kb.Fm(4), kb.Fm(4)] for _ in range(2)]
            go = [kb.Fm(512) for _ in range(2)]
            bb = [[Buf() for _ in range(10)] for _ in range(2)]
            bt = [Buf(), Buf()]
            bini = [Buf(), Buf()]
            bsc = Buf()
            psi = [(1, 2, 3), (4, 5, 6)]
            for it in range(12):
                for d in dirs:
                    R_, W_ = [bp_, bsc, bt[d]], [bsc, bt[d]]
                    kb.ts("dve", tq, bre[:, it, :], cfr[d][:, it:it + 1], None, ALU.mult, reads=R_, writes=W_)
                    kb.stt("dve", tq, bim[:, it, :], cfi[d][:, it:it + 1], tq, ALU.mult, ALU.subtract, reads=R_, writes=W_)
                    kb.ts("dve", bbr, tq, -1.0, None, ALU.mult, reads=R_, writes=W_)
                    kb.ts("dve", tq, bim[:, it, :], cfr[d][:, it:it + 1], None, ALU.mult, reads=R_, writes=W_)
                    kb.stt("dve", bbi, bre[:, it, :], cfi[d][:, it:it + 1], tq, ALU.mult, ALU.add, reads=R_, writes=W_)
                    kb.mm(ps[0][:, 0:128], bbr, ident, True, True, reads=[bsc, cbuf], writes=[psb[0]])
                    kb.mm(ps[0][:, 128:256], bbi, ident, True, True, reads=[bsc, cbuf], writes=[psb[0]])
                    kb.copy("dve", btr[d], ps[0][:, 0:128], reads=[psb[0]], writes=W_)
                    kb.copy("dve", bti[d], ps[0][:, 128:256], reads=[psb[0]], writes=W_)
                    cT, sT = cosT[d], sinT[d]
                    kb.memset("dve", cT[:, 0:1], 1.0, writes=W_)
                    kb.memset("dve", sT[:, 0:1], 0.0, writes=W_)
                    kb.copy("dve", tc_[:, 0:1], c0[d][:, it:it + 1], reads=R_, writes=W_)
                    kb.copy("dve", ts_[:, 0:1], s0[d][:, it:it + 1], reads=R_, writes=W_)
                    m = 1
                    while m < 512:
                        ck, sk = tc_[:, 0:1], ts_[:, 0:1]
                        tmpv = t1[d][:, 0:m]
                        RX, WX = R_ + [bb[d][2]], W_ + [bb[d][2]]
                        kb.ts("dve", tmpv, sT[:, 0:m], sk, None, ALU.mult, reads=RX, writes=WX)
                        kb.stt("dve", cT[:, m:2 * m], cT[:, 0:m], ck, tmpv, ALU.mult, ALU.subtract, reads=RX, writes=WX)
                        kb.ts("dve", tmpv, cT[:, 0:m], sk, None, ALU.mult, reads=RX, writes=WX)
                        kb.stt("dve", sT[:, m:2 * m], sT[:, 0:m], ck, tmpv, ALU.mult, ALU.add, reads=RX, writes=WX)
                        kb.tt("dve", tc_[:, 1:2], ck, ck, ALU.mult, reads=R_, writes=W_)
                        kb.tt("dve", tc_[:, 2:3], sk, sk, ALU.mult, reads=R_, writes=W_)
                        kb.tt("dve", tc_[:, 3:4], ck, sk, ALU.mult, reads=R_, writes=W_)
                        kb.tt("dve", tc_[:, 0:1], tc_[:, 1:2], tc_[:, 2:3], ALU.subtract, reads=R_, writes=W_)
                        kb.ts("dve", ts_[:, 0:1], tc_[:, 3:4], 2.0, None, ALU.mult, reads=R_, writes=W_)
                        m *= 2
                    kb.copy("dve", rhoT[d], rho[d][:, it:it + 1].to_broadcast([128, 512]), reads=R_, writes=W_)
                    kb.memset("dve", ini[d][0][:, 0:2], 0.0, writes=[bini[d]])
                for n in range(NT):
                    for d in dirs:
                        t = n if d == 0 else NT - 1 - n
                        i = d
                        B = bb[d]
                        p1, p2, p3 = psi[d]
                        cT, sT = cosT[d], sinT[d]
                        kb.dma(ub[i], PF[(6 + it // 4) * 128:(7 + it // 4) * 128, t * 512:(t + 1) * 512], reads=[bPF],
                               writes=[B[0]])
                        kb.mm(ps[p1][:], btr[d], ub[i], True, True, reads=[bt[d], B[0]], writes=[psb[p1]])
                        kb.mm(ps[p2][:], bti[d], ub[i], True, True, reads=[bt[d], B[0]], writes=[psb[p2]])
                        pre = ps[p1][:] if d == 0 else ps[p1][:, ::-1]
                        pim = ps[p2][:] if d == 0 else ps[p2][:, ::-1]
                        kb.tt("dve", vre[i], pre, cT, ALU.mult, reads=[psb[p1], bt[d]], writes=[B[1]])
                        kb.tt("dve", t1[i], pim, sT, ALU.mult, reads=[psb[p2], bt[d]], writes=[B[2]])
                        kb.tt("pool", vre[i], vre[i], t1[i], ALU.add, reads=[B[1], B[2]], writes=[B[1]])
                        kb.tt("dve", vim[i], pim, cT, ALU.mult, reads=[psb[p2], bt[d]], writes=[B[3]])
                        kb.tt("dve", t1[i], pre, sT, ALU.mult, reads=[psb[p1], bt[d], B[1]], writes=[B[2]])
                        kb.tt("pool", vim[i], vim[i], t1[i], ALU.subtract, reads=[B[3], B[2]], writes=[B[3]])
                        kb.scan(wre[i], rhoT[d], vre[i], ini[d][0][:, 0:1], reads=[B[1], bt[d], bini[d]], writes=[B[4]])
                        kb.scan(wim[i], rhoT[d], vim[i], ini[d][0][:, 1:2], reads=[B[3], bt[d], bini[d]], writes=[B[5]])
                        kb.ts("dve", ini[d][1][:, 0:1], wim[i][:, 511:512], S9[d][:, it:it + 1], None, ALU.mult,
                              reads=[B[5], bp_, bini[d]], writes=[bini[d]])
                        kb.ts("dve", ini[d][1][:, 1:2], wre[i][:, 511:512], S9[d][:, it:it + 1], None, ALU.mult,
                              reads=[B[4], bp_, bini[d]], writes=[bini[d]])
                        kb.stt("dve", ini[d][0][:, 0:1], wre[i][:, 511:512], C9[d][:, it:it + 1], ini[d][1][:, 0:1], ALU.mult,
                               ALU.subtract, reads=[B[4], bini[d]], writes=[bini[d]])
                        kb.stt("dve", ini[d][0][:, 1:2], wim[i][:, 511:512], C9[d][:, it:it + 1], ini[d][1][:, 1:2], ALU.mult,
                               ALU.add, reads=[B[5], bini[d]], writes=[bini[d]])
                        kb.tt("pool", hre[i], wre[i], cT, ALU.mult, reads=[B[4], bt[d]], writes=[B[6]])
                        kb.tt("dve", t1[i], wim[i], sT, ALU.mult, reads=[B[5], bt[d], B[2]], writes=[B[2]])
                        kb.tt("pool", hre[i], hre[i], t1[i], ALU.subtract, reads=[B[6], B[2]], writes=[B[6]])
                        kb.tt("pool", him[i], wre[i], sT, ALU.mult, reads=[B[4], bt[d]], writes=[B[7]])
                        kb.tt("dve", t1[i], wim[i], cT, ALU.mult, reads=[B[5], bt[d], B[6]], writes=[B[2]])
                        kb.tt("pool", him[i], him[i], t1[i], ALU.add, reads=[B[7], B[2]], writes=[B[7]])
                        kb.mm(ps[p3][0:32, :], cre[d][:, it, :], hre[i], True, False, reads=[bp_, B[6]], writes=[psb[p3]])
                        kb.mm(ps[p3][0:32, :], cimn[d][:, it, :], him[i], False, d == 1, reads=[bp_, B[7]], writes=[psb[p3]])
                        if d == 0:
                            kb.mm(ps[p3][0:32, :], dsel[:, it, :], ub[i], False, True, reads=[bp_, B[0]], writes=[psb[p3]])
                            kb.copy("act", go[i][0:32, :], ps[p3][0:32, :], reads=[psb[p3]], writes=[B[8]])
                            kb.dma(Y5A[it * 32:(it + 1) * 32, t * 512:(t + 1) * 512], go[i][0:32, :], reads=[B[8]],
                                   writes=[bY5A], q="sp")
                        else:
                            kb.copy("act", go[i][0:32, :], ps[p3][0:32, ::-1], reads=[psb[p3]], writes=[B[8]])
                            kb.dma(Y5B[it * 32:(it + 1) * 32, t * 512:(t + 1) * 512], go[i][0:32, :], reads=[B[8]],
                                   writes=[bY5B], q="sp")
            kb.new_stage()
            ca = [kb.Fm(512), kb.Fm(512)]
            cb_ = [kb.Fm(512), kb.Fm(512)]
            bc = [Buf(), Buf()]
            n = 0
            for r in range(3):
                for t in range(NT):
                    i = n % 2
                    ts0 = slice(t * 512, (t + 1) * 512)
                    kb.dma(ca[i], Y5A[r * 128:(r + 1) * 128, ts0], reads=[bY5A], writes=[bc[i]], q="sp")
                    kb.dma(cb_[i], Y5B[r * 128:(r + 1) * 128, ts0], reads=[bY5B], writes=[bc[i]], q="sp")
                    kb.tt("dve", ca[i], ca[i], cb_[i], ALU.add, reads=[bc[i]], writes=[bc[i]])
                    kb.act(ca[i], ca[i], AF.Gelu, reads=[bc[i]], writes=[bc[i]])
                    kb.dma(Y5[r * 128:(r + 1) * 128, ts0], ca[i], reads=[bc[i]], writes=[bY5], q="sp")
                    n += 1

        def merge_stage(l):
            kb.new_stage()
            wbs = v3(kb.R(4096), 4)
            wbr = v3(kb.R(4096), 4)
            wb5 = v3(kb.R(3072), 3)
            wv = v3(kb.R(1152), 3)
            wg_ = v3(kb.R(1152), 3)
            wo = v3(kb.R(8192), 8)
            bw = Buf()
            kb.dma(wbs, v3(I["wbr_ssd"][l], 4), writes=[bw])
            kb.dma(wbr, v3(I["wbr_ret"][l], 4), writes=[bw])
            kb.dma(wb5, v3(I["wbr_s5"][l], 3), writes=[bw])
            kb.dma(wv, v3(I["glu_wv"][l], 3), writes=[bw])
            kb.dma(wg_, v3(I["glu_wg"][l], 3), writes=[bw])
            kb.dma(wo, v3(I["wout"][l], 8), writes=[bw])
            ys = v3(kb.R(2048), 4)
            yr = v3(kb.R(2048), 4)
            y5 = v3(kb.R(1536), 3)
            y5g = v3(kb.R(1536), 3)
            mixed = v3(kb.R(4096), 8)
            sgt = kb.Fm(512)
            gate = [kb.Fm(512) for _ in range(2)]
            tmp = [kb.Fm(512) for _ in range(2)]
            xr = [kb.Fm(512) for _ in range(2)]
            xo = [kb.Fm(512) for _ in range(2)]
            bi, bg5, bmx = Buf(), Buf(), [Buf() for _ in range(8)]
            bsg = Buf()
            bgate = [Buf(), Buf()]
            btmp = [Buf(), Buf()]
            bxr = [Buf(), Buf()]
            bxo = [Buf(), Buf()]
            for t in range(NT):
                ts0 = slice(t * 512, (t + 1) * 512)
                for f in range(4):
                    kb.dma(ys[:, f, :], YTS[f * 128:(f + 1) * 128, ts0], reads=[bYTS], writes=[bi])
                    kb.dma(yr[:, f, :], YTR[f * 128:(f + 1) * 128, ts0], reads=[bYTR], writes=[bi])
                for f in range(3):
                    kb.dma(y5[:, f, :], Y5[f * 128:(f + 1) * 128, ts0], reads=[bY5], writes=[bi])
                for f in range(3):
                    for k in range(3):
                        kb.mm(ps[0][:], wv[:, k, f * 128:(f + 1) * 128], y5[:, k, :], k == 0, k == 2, reads=[bw, bi],
                              writes=[psb[0]])
                    for k in range(3):
                        kb.mm(ps[1][:], wg_[:, k, f * 128:(f + 1) * 128], y5[:, k, :], k == 0, k == 2, reads=[bw, bi],
                              writes=[psb[1]])
                    kb.act(sgt, ps[1][:], AF.Sigmoid, reads=[psb[1]], writes=[bsg])
                    kb.tt("dve", y5g[:, f, :], ps[0][:], sgt, ALU.mult, reads=[psb[0], bsg], writes=[bg5])
                n = 0
                for i in range(8):
                    for br_, (w_, y_, nk, rb) in enumerate(((wbs, ys, 4, bi), (y5g and wb5, y5g, 3, bg5), (wbr, yr, 4, bi))):
                        pp, bp = ps[2 + n % 2], psb[2 + n % 2]
                        for k in range(nk):
                            kb.mm(pp[:], w_[:, k, i * 128:(i + 1) * 128], y_[:, k, :], k == 0, k == nk - 1, reads=[bw, rb],
                                  writes=[bp])
                        gi = 9 + br_ * 8 + i
                        kb.dma(gate[n % 2], PF[gi * 128:(gi + 1) * 128, ts0], reads=[bPF], writes=[bgate[n % 2]], q="sp")
                        if br_ == 0:
                            kb.tt("dve", tmp[i % 2], pp[:], gate[n % 2], ALU.mult, reads=[bp, bgate[n % 2]],
                                  writes=[btmp[i % 2]])
                        elif br_ == 1:
                            kb.tt("dve", gate[n % 2], pp[:], gate[n % 2], ALU.mult, reads=[bp, bgate[n % 2]],
                                  writes=[bgate[n % 2]])
                            kb.tt("pool", tmp[i % 2], tmp[i % 2], gate[n % 2], ALU.add, reads=[bgate[n % 2], btmp[i % 2]],
                                  writes=[btmp[i % 2]])
                        else:
                            kb.tt("dve", gate[n % 2], pp[:], gate[n % 2], ALU.mult, reads=[bp, bgate[n % 2]],
                                  writes=[bgate[n % 2]])
                            kb.tt("dve", mixed[:, i, :], tmp[i % 2], gate[n % 2], ALU.add,
                                  reads=[bgate[n % 2], btmp[i % 2]], writes=[bmx[i]])
                        n += 1
                for i in range(8):
                    pp, bp = ps[4 + i % 2], psb[4 + i % 2]
                    for k in range(8):
                        kb.mm(pp[:], wo[:, k, i * 128:(i + 1) * 128], mixed[:, k, :], k == 0, k == 7, reads=[bw, bmx[k]],
                              writes=[bp])
                    kb.dma(xr[i % 2], X[i * 128:(i + 1) * 128, ts0], reads=[xb(i, t)], writes=[bxr[i % 2]], q="sp")
                    kb.tt("dve", xo[i % 2], pp[:], xr[i % 2], ALU.add, reads=[bp, bxr[i % 2]], writes=[bxo[i % 2]])
                    kb.dma(X[i * 128:(i + 1) * 128, ts0], xo[i % 2], reads=[bxo[i % 2]], writes=[xb(i, t)], q="sp")

        def final_stage():
            for t in range(NT):
                kb.new_stage()
                xn, bn, xf, bxk = load_norm(X, t, I["g_final"], "z")
                o = v3(kb.Fm(4096), 8)
                bo = Buf()
                for k in range(8):
                    kb.copy("dve" if k % 2 else "act", o[:, k, :], xn[:, k, :].bitcast(F32), reads=[bn], writes=[bo])
                    kb.dma(outT[k * 128:(k + 1) * 128, t * 512:(t + 1) * 512], o[:, k, :], reads=[bo], q="sp")

        kb.new_stage()
        cpb = [kb.Fm(2048), kb.Fm(2048)]
        bcp = [Buf(), Buf()]
        n = 0
        for k in range(8):
            for c0_ in range(0, S, 2048):
                w = min(2048, S - c0_)
                kb.dma(cpb[n % 2][:, 0:w], I["xT"][k * 128:(k + 1) * 128, c0_:c0_ + w], writes=[bcp[n % 2]], q="sp")
                wr = [xb(k, tt) for tt in range(c0_ // 512, (c0_ + w) // 512)]
                kb.dma(X[k * 128:(k + 1) * 128, c0_:c0_ + w], cpb[n % 2][:, 0:w], reads=[bcp[n % 2]], writes=wr, q="sp")
                n += 1
        def on(nm):
            return STAGES is None or nm in STAGES
        for l in range(L):
            if on("ffn1"):
                ffn_stage(l, "g_ffn1", I["wg1"], I["wu1"], I["wd1"])
            if on("inproj"):
                inproj_stage(l)
            if pair:
                halo_exchange()
                ssd_prep(l)
                ret_prep(l)
                ssd_pass(l, 0)
                ret_pass(l, 0)
                s5_stage(l, (0,))
                state_exchange()
                ssd_pass(l, 1)
                ret_pass(l, 1)
                s5_stage(l, (1,))
            else:
                if on("ssd") or on("ssdprep"):
                    ssd_prep(l)
                if on("ssd") or on("ssd0"):
                    ssd_pass(l, 0)
                if on("ssd") or on("ssd1"):
                    ssd_pass(l, 1)
                if on("ret"):
                    ret_prep(l)
                    ret_pass(l, 0)
                    ret_pass(l, 1)
                if on("s5"):
                    s5_stage(l)
            if on("merge"):
                merge_stage(l)
            if on("ffn2"):
                ffn_stage(l, "g_ffn2", I["wg2"], I["wu2"], I["wd2"])
        final_stage()
        P.emit()
    return nc


def _tile_cols(w, nt):
    K, N = w.shape
    return np.ascontiguousarray(w.reshape(K // 128, 128, nt, 128).transpose(2, 1, 0, 3).reshape(nt, 128, (K // 128) * 128))


def _tile_rows(w):
    K, N = w.shape
    return np.ascontiguousarray(w.reshape(K // 128, 128, N).transpose(1, 0, 2).reshape(128, (K // 128) * N))


def _consts(S):
    c = {}
    idx = np.arange(128)
    k, x = idx[:, None], idx[None, :]
    c["c_ident"] = np.eye(128, dtype=np.float32)
    c["c_tri"] = np.concatenate([(k <= x), -1.0 * (k < x), (k > x), (k < x)], axis=1).astype(np.float32)
    NEG = -30000.0
    mf = np.where(x < k, NEG, 0.0)
    mb = np.where(x > k, NEG, 0.0)
    c["c_maskneg"] = np.concatenate([np.tile(mf, (1, 4)), np.tile(mb, (1, 4))], axis=1).astype(np.float32)
    ie = np.zeros((16, 16, 128), np.float32)
    for j in range(16):
        ie[j, j, :] = 1.0
    c["c_iexp"] = ie.reshape(16, 2048)
    c["c_negiexp"] = (-ie).reshape(16, 2048)
    lg = np.log1p(-np.exp2(-5.0 - np.arange(8, dtype=np.float32))).astype(np.float32)
    s_, l_ = idx[:, None].astype(np.float32), idx[None, :].astype(np.float32)
    rm = np.zeros((128, 2, 8, 128), np.float32)
    for h in range(8):
        rm[:, 0, h, :] = np.where(l_ >= s_, np.exp(lg[h] * np.where(l_ >= s_, l_ - s_, 0.0)), 0.0)
        rm[:, 1, h, :] = np.where(s_ > l_, np.exp(lg[h] * np.where(s_ > l_, s_ - l_, 0.0)), 0.0)
    c["c_retmask"] = rm.reshape(128, 2048)
    rd = np.zeros((128, 40), np.float32)
    t = idx.astype(np.float32)[:, None]
    rd[:, 0:8] = np.exp(lg[None, :] * (127.0 - t))
    rd[:, 8:16] = np.exp(lg[None, :] * t)
    rd[:, 16:24] = np.exp(lg[None, :] * (t + 1.0))
    rd[:, 24:32] = np.exp(lg[None, :] * (128.0 - t))
    rd[:, 32:40] = np.exp(lg[None, :] * 128.0)
    c["c_retdec"] = rd
    pos = np.arange(S, dtype=np.float32)
    inv = (10000.0 ** (-np.arange(0, 64, 2, dtype=np.float32) / 64)).astype(np.float32)
    ang = pos[:, None] * inv[None, :]
    c["c_rope"] = np.concatenate([np.cos(ang), np.sin(ang)], axis=1).astype(np.float32)
    b = np.zeros((128, 128), np.float32)
    b[:64, :64] = 1
    b[64:, 64:] = 1
    c["c_blk64"] = b
    return c


def _prep_weights(inp, L):
    f = lambda a: np.ascontiguousarray(np.asarray(a, dtype=np.float32))
    W = {}
    gt = lambda g: np.ascontiguousarray(f(g).reshape(-1, 8, 128).transpose(0, 2, 1))
    W["g_ffn1"], W["g_mix"], W["g_ffn2"] = gt(inp["ffn1_norm"]), gt(inp["mix_norm"]), gt(inp["ffn2_norm"])
    W["g_final"] = gt(inp["final_norm"])[0]
    for n_, a in (("1", "ffn1"), ("2", "ffn2")):
        W["wg" + n_] = np.stack([_tile_cols(f(inp[a + "_w_gate"][l]), NFC) for l in range(L)])
        W["wu" + n_] = np.stack([_tile_cols(f(inp[a + "_w_up"][l]), NFC) for l in range(L)])
        wd = f(inp[a + "_w_down"])
        W["wd" + n_] = np.stack([np.stack([_tile_rows(wd[l][:, i * 128:(i + 1) * 128]) for i in range(8)]) for l in range(L)])
    win = f(inp["w_in"])
    sz = (512, 768, 16, 384, 512, 512, 512, 512, 3072)
    o = np.cumsum((0,) + sz)
    z, xbc, dt, u, q, k, v, g, gates = [win[:, :, o[i]:o[i + 1]] for i in range(9)]
    fm = np.concatenate([xbc, u, gates], axis=2)
    W["win_fm"] = np.stack([_tile_cols(fm[l], NFM) for l in range(L)])
    tm = np.concatenate([z, q, k, v, g, dt], axis=2)
    W["win_tm"] = np.stack([_tile_rows(tm[l]) for l in range(L)])
    W["bgate"] = np.ascontiguousarray(f(inp["b_gate"]).reshape(L, 24, 128).transpose(0, 2, 1))
    cw = f(inp["ssd_conv_w"])
    W["conv_w"] = np.ascontiguousarray(cw.reshape(L, 5, 6, 128).transpose(0, 3, 2, 1).reshape(L, 128, 30))
    W["conv_b"] = np.ascontiguousarray(f(inp["ssd_conv_b"]).reshape(L, 6, 128).transpose(0, 2, 1))
    rep = lambda a: np.ascontiguousarray(np.broadcast_to(a[:, None, :], (L, 128, a.shape[-1])))
    W["dtb_rep"] = rep(f(inp["ssd_dt_bias"]).reshape(L, 16))
    W["alog_rep"] = rep(f(inp["ssd_a_log"]).reshape(L, 16))
    W["dskip_rep"] = rep(f(inp["ssd_d"]))
    W["ssdnorm_rep"] = rep(f(inp["ssd_norm"]))
    W["retnorm_rep"] = rep(f(inp["ret_norm"]))
    W["wbr_ssd"] = np.stack([_tile_rows(f(inp["w_br_ssd"][l])) for l in range(L)])
    W["wbr_ret"] = np.stack([_tile_rows(f(inp["w_br_ret"][l])) for l in range(L)])
    W["wbr_s5"] = np.stack([_tile_rows(f(inp["w_br_s5"][l])) for l in range(L)])
    W["glu_wv"] = np.stack([_tile_rows(f(inp["s5_glu_wv"][l])) for l in range(L)])
    W["glu_wg"] = np.stack([_tile_rows(f(inp["s5_glu_wg"][l])) for l in range(L)])
    W["wout"] = np.stack([_tile_rows(f(inp["w_out"][l])) for l in range(L)])
    st = lambda a: np.ascontiguousarray(a.reshape(L, 2, 12, 128).transpose(0, 1, 3, 2))
    W["s5_lre"], W["s5_lim"] = st(f(inp["s5_lam_re"])), st(f(inp["s5_lam_im"]))
    ls = np.broadcast_to(f(inp["s5_log_step"])[..., None], (L, 2, 24, 64))
    W["s5_lstep"] = st(np.ascontiguousarray(ls))

    def bpad(b):
        out = np.zeros((L, 128, 12, 128), np.float32)
        for g in range(24):
            it, g2, g8 = g // 2, g % 2, g % 8
            out[:, g2 * 64:(g2 + 1) * 64, it, g8 * 16:(g8 + 1) * 16] = b[:, g]
        return out.reshape(L, 128, 12 * 128)
    W["s5_bre"], W["s5_bim"] = bpad(f(inp["s5_b_re"])), bpad(f(inp["s5_b_im"]))

    def cpad(c):
        out = np.zeros((L, 2, 128, 12, 32), np.float32)
        for g in range(24):
            it, g2 = g // 2, g % 2
            out[:, :, g2 * 64:(g2 + 1) * 64, it, g2 * 16:(g2 + 1) * 16] = c[:, :, g].transpose(0, 1, 3, 2)
        return out.reshape(L, 2, 128, 12 * 32)
    W["s5_cre"], W["s5_cim"] = cpad(f(inp["s5_c_re"])), cpad(f(inp["s5_c_im"]))
    dd = f(inp["s5_d"])
    ds = np.zeros((L, 128, 12, 32), np.float32)
    for g in range(24):
        it, g2, g8 = g // 2, g % 2, g % 8
        for h in range(16):
            ds[:, g8 * 16 + h, it, g2 * 16 + h] = dd[:, g, h]
    W["s5_dsel"] = ds.reshape(L, 128, 12 * 32)
    return W


_CACHE = {}
STAGES = None


def _swap_dirs(W):
    V = dict(W)
    sw16 = lambda a: np.ascontiguousarray(np.concatenate([a[..., 8:16], a[..., 0:8]], axis=-1))
    V["dtb_rep"] = sw16(W["dtb_rep"])
    V["alog_rep"] = sw16(W["alog_rep"])
    wt = W["win_tm"].reshape(W["win_tm"].shape[0], 128, 8, NTM).copy()
    wt[..., 2560:2576] = sw16(wt[..., 2560:2576])
    V["win_tm"] = wt.reshape(W["win_tm"].shape)
    cw = W["conv_w"].reshape(-1, 128, 6, 5)
    V["conv_w"] = np.ascontiguousarray(cw[..., ::-1]).reshape(W["conv_w"].shape)
    for k in ("s5_lre", "s5_lim", "s5_lstep", "s5_cre", "s5_cim"):
        V[k] = np.ascontiguousarray(W[k][:, ::-1])
    return V


def run_model(inp, L, pair=True, dbg=()):
    x = np.asarray(inp["x"], dtype=np.float32)
    nseq, Sfull = x.shape[0], x.shape[1]
    W = _prep_weights(inp, L)
    if not pair:
        S = Sfull
        ncore = 8 if nseq == 4 else nseq
        key = (S, L, tuple(dbg), 0)
        if key not in _CACHE:
            _CACHE[key] = build_program(S, L, dbg)
        W.update(_consts(S))
        maps = []
        for c in range(ncore):
            m = dict(W)
            m["xT"] = np.ascontiguousarray(x[c % nseq].T)
            maps.append(m)
        res = run_bass_kernel_spmd(_CACHE[key], maps, core_ids=list(range(ncore)))
        out = np.stack([np.ascontiguousarray(res.results[c]["outT"].T) for c in range(nseq)])
        return out, res
    S = Sfull // 2
    ncore = 2 * nseq
    key = (S, L, tuple(dbg), ncore)
    if key not in _CACHE:
        _CACHE[key] = build_program(S, L, dbg, pair=ncore)
    C0 = _consts(S)
    Wn = dict(W)
    Wn.update(C0)
    Wr = _swap_dirs(W)
    Wr.update(C0)
    idx = np.arange(128)
    s_, l_ = idx[:, None].astype(np.float32), idx[None, :].astype(np.float32)
    lg = np.log1p(-np.exp2(-5.0 - np.arange(8, dtype=np.float32))).astype(np.float32)
    rm = np.zeros((128, 2, 8, 128), np.float32)
    for h in range(8):
        rm[:, 0, h, :] = np.where(l_ > s_, np.exp(lg[h] * np.where(l_ > s_, l_ - s_, 0.0)), 0.0)
        rm[:, 1, h, :] = np.where(s_ >= l_, np.exp(lg[h] * np.where(s_ >= l_, s_ - l_, 0.0)), 0.0)
    Wr["c_retmask"] = rm.reshape(128, 2048)
    inv = (10000.0 ** (-np.arange(0, 64, 2, dtype=np.float32) / 64)).astype(np.float32)

    def rope(pos):
        ang = pos.astype(np.float32)[:, None] * inv[None, :]
        return np.concatenate([np.cos(ang), np.sin(ang)], axis=1).astype(np.float32)
    Wn["c_rope"] = rope(np.arange(S))
    Wr["c_rope"] = rope(Sfull - 1 - np.arange(S))
    Wn["pairsel"] = np.ascontiguousarray(np.broadcast_to(np.array([0.0, 1.0], np.float32), (128, 2)))
    Wr["pairsel"] = np.ascontiguousarray(np.broadcast_to(np.array([1.0, 0.0], np.float32), (128, 2)))
    maps = []
    for c in range(ncore):
        b, hf = c // 2, c % 2
        m = dict(Wn if hf == 0 else Wr)
        xs = x[b, :S] if hf == 0 else x[b, S:][::-1]
        m["xT"] = np.ascontiguousarray(xs.T)
        maps.append(m)
    res = run_bass_kernel_spmd(_CACHE[key], maps, core_ids=list(range(ncore)))
    out = np.empty((nseq, Sfull, D), np.float32)
    for c in range(ncore):
        b, hf = c // 2, c % 2
        o = res.results[c]["outT"].T
        if hf == 0:
            out[b, :S] = o
        else:
            out[b, S:] = o[::-1]
    return out, res


def kernel(**inputs):
    out, _ = run_model(inputs, 2, pair=False)
    return out.astype(np.float32)
```

```python
import math
from contextlib import ExitStack
import numpy as np
import concourse.bass as bass
import concourse.mybir as mybir
from concourse.bass_utils import run_bass_kernel_spmd

F32 = mybir.dt.float32
F32R = mybir.dt.float32r
ALU = mybir.AluOpType
AF = mybir.ActivationFunctionType
AX = mybir.AxisListType

ENGS = ("pe", "dve", "act", "pool", "sp")
N_DMA_SEMS = 12
D = 1024
DFF = 2816
NFC = 22
EPS = 1e-6
NTM = 2576
NFM = 33


class Buf:
    __slots__ = ("name", "w", "r")

    def __init__(self, name=""):
        self.name = name
        self.w = None
        self.r = []


class Prog:
    def __init__(self, nc):
        self.nc = nc
        self.ops = []
        self.by_eng = {e: [] for e in ENGS}
        self.fence_deps = set()
        self.fence_pending = {e: False for e in ENGS}
        self.since_fence = []
        self.trace = None

    def op(self, eng, fn, reads=(), writes=(), dma=False):
        oid = len(self.ops)
        deps = set()
        for b in reads:
            if b.w is not None:
                deps.add(b.w)
        for b in writes:
            if b.w is not None:
                deps.add(b.w)
            deps.update(b.r)
        for b in reads:
            if not dma:
                b.r = [r for r in b.r if self.ops[r]["dma"] or self.ops[r]["eng"] != eng]
            b.r.append(oid)
        for b in writes:
            b.w = oid
            b.r = []
        if self.fence_pending[eng]:
            deps.update(self.fence_deps)
            self.fence_pending[eng] = False
        deps.discard(oid)
        import sys as _s
        fr = _s._getframe(1)
        ln = []
        while fr is not None and len(ln) < 4:
            ln.append(fr.f_lineno)
            fr = fr.f_back
        self.ops.append(dict(eng=eng, fn=fn, deps=deps, dma=dma, id=oid, ln=ln))
        self.by_eng[eng].append(oid)
        self.since_fence.append(oid)
        return oid

    def fence(self):
        last = {}
        deps = set()
        for oid in self.since_fence:
            o = self.ops[oid]
            if o["dma"]:
                deps.add(oid)
            else:
                last[o["eng"]] = oid
        deps.update(last.values())
        for e in ENGS:
            if self.fence_pending[e]:
                deps.update(self.fence_deps)
                break
        self.fence_deps = deps
        self.fence_pending = {e: True for e in ENGS}
        self.since_fence = []

    def emit(self):
        nc = self.nc
        ops = self.ops
        signaled = set()
        for o in ops:
            for d in o["deps"]:
                if o["eng"] == "pe" and ops[d]["eng"] == "pe" and not ops[d]["dma"] and not o["dma"]:
                    continue
                signaled.add(d)
        eng_cnt = {e: 0 for e in ENGS}
        dma_rr = {e: 0 for e in ENGS}
        dma_cnt = {e: [0] * N_DMA_SEMS for e in ENGS}
        tokens = {}
        dma_prev = {}
        for o in ops:
            e = o["eng"]
            if o["dma"]:
                i = dma_rr[e] % N_DMA_SEMS
                dma_rr[e] += 1
                prev = dma_cnt[e][i]
                dma_cnt[e][i] += 16
                tokens[o["id"]] = (("dma", e, i), dma_cnt[e][i])
                if prev:
                    dma_prev[o["id"]] = (("dma", e, i), prev)
            elif o["id"] in signaled:
                eng_cnt[e] += 1
                tokens[o["id"]] = (("eng", e), eng_cnt[e])
        final_waits = {e: {} for e in ENGS}
        for o in ops:
            if o["dma"]:
                k, v = tokens[o["id"]]
                final_waits[o["eng"]][k] = max(final_waits[o["eng"]].get(k, 0), v)
        used = sorted(set(k for k, _ in tokens.values()), key=str)
        self.eng_cnt = eng_cnt
        with ExitStack() as st:
            sems = {k: st.enter_context(nc.semaphore("s_" + "_".join(map(str, k)))) for k in used}
            block = st.enter_context(nc.Block())
            handles = {"pe": block.tensor, "dve": block.vector, "act": block.scalar,
                       "pool": block.gpsimd, "sp": block.sync}

            def make(e):
                def body(eng):
                    seen = {}
                    for oid in self.by_eng[e]:
                        o = ops[oid]
                        waits = {}
                        for d in o["deps"]:
                            if ops[d]["eng"] == e and not ops[d]["dma"] and e == "pe" and not o["dma"]:
                                continue
                            k, v = tokens[d]
                            waits[k] = max(waits.get(k, 0), v)
                        if oid in dma_prev:
                            k, v = dma_prev[oid]
                            waits[k] = max(waits.get(k, 0), v)
                        for k, v in waits.items():
                            if seen.get(k, 0) >= v:
                                continue
                            seen[k] = v
                            eng.wait_ge(sems[k], v)
                        try:
                            ins = o["fn"](eng)
                        except BaseException:
                            print("FAILED OP lines", o["ln"], "eng", e)
                            raise
                        if self.trace is not None:
                            try:
                                self.trace[ins.ins.name] = o["ln"]
                            except Exception:
                                pass
                        if oid in tokens:
                            k, v = tokens[oid]
                            ins.then_inc(sems[k], 16 if o["dma"] else 1)
                    for k, v in final_waits[e].items():
                        if seen.get(k, 0) < v:
                            eng.wait_ge(sems[k], v)
                return body

            for e in ENGS:
                if self.by_eng[e]:
                    handles[e](make(e))


class KB:
    def __init__(self, nc, st, nr, nf):
        self.nc = nc
        self.P = Prog(nc)
        self.arr = st.enter_context(nc.sbuf_tensor("arr", [128, nr], F32R))
        self.arf = st.enter_context(nc.sbuf_tensor("arf", [128, nf], F32))
        self.cr = st.enter_context(nc.sbuf_tensor("cr", [128, 2048], F32R))
        self.cf = st.enter_context(nc.sbuf_tensor("cf", [128, 1344], F32))
        self.nr, self.nf = nr, nf
        self.pr = self.pf = 0
        self.ps = [st.enter_context(nc.psum_tensor("ps%d" % i, [128, 512], F32)) for i in range(8)]
        self.psb = [Buf("ps%d" % i) for i in range(8)]
        self.dq = 0

    def new_stage(self):
        self.P.fence()
        self.pr = self.pf = 0

    def R(self, n, shape=None):
        a = self.arr[:, self.pr:self.pr + n]
        self.pr += n
        assert self.pr <= self.nr, ("arr overflow", self.pr)
        return a

    def Fm(self, n):
        a = self.arf[:, self.pf:self.pf + n]
        self.pf += n
        assert self.pf <= self.nf, ("arf overflow", self.pf)
        return a

    def dma(self, out, in_, reads=(), writes=(), q=None):
        if q is None:
            q = "pool"
        return self.P.op(q, lambda e, o=out, i=in_: e.dma_start(out=o, in_=i), reads, writes, dma=True)

    def mm(self, out, lhsT, rhs, start, stop, reads=(), writes=()):
        return self.P.op("pe", lambda e, o=out, l=lhsT, r=rhs, s=start, t=stop: e.matmul(o, l, r, start=s, stop=t),
                         reads, writes)

    def act(self, out, in_, func, reads=(), writes=(), bias=None, scale=None):
        kw = {}
        if bias is not None:
            kw["bias"] = bias
        if scale is not None:
            kw["scale"] = scale
        return self.P.op("act", lambda e, o=out, i=in_, f=func, kw=kw: e.activation(o, i, f, **kw), reads, writes)

    def tt(self, eng, out, in0, in1, op, reads=(), writes=()):
        return self.P.op(eng, lambda e, o=out, a=in0, b=in1, p=op: e.tensor_tensor(o, a, b, p), reads, writes)

    def ts(self, eng, out, in0, s1, s2, op0, op1=None, reads=(), writes=()):
        if op1 is None:
            return self.P.op(eng, lambda e, o=out, a=in0, x=s1, p=op0: e.tensor_scalar(o, a, x, None, p), reads, writes)
        return self.P.op(eng, lambda e, o=out, a=in0, x=s1, y=s2, p=op0, q=op1: e.tensor_scalar(o, a, x, y, p, q),
                         reads, writes)

    def stt(self, eng, out, in0, scalar, in1, op0, op1, reads=(), writes=()):
        return self.P.op(eng, lambda e, o=out, a=in0, s=scalar, b=in1, p=op0, q=op1:
                         e.scalar_tensor_tensor(o, a, s, b, p, q), reads, writes)

    def copy(self, eng, out, in_, reads=(), writes=()):
        if eng == "act":
            return self.act(out, in_, AF.Copy, reads, writes)
        return self.P.op(eng, lambda e, o=out, i=in_: e.tensor_copy(o, i), reads, writes)

    def memset(self, eng, out, val, writes=()):
        return self.P.op(eng, lambda e, o=out, v=val: e.memset(o, v), (), writes)

    def recip(self, out, in_, reads=(), writes=()):
        return self.P.op("dve", lambda e, o=out, i=in_: e.reciprocal(o, i), reads, writes)

    def scan(self, out, d0, d1, init, reads=(), writes=()):
        return self.P.op("dve", lambda e, o=out, a=d0, b=d1, i=init: e.tensor_tensor_scan(o, a, b, i, ALU.mult, ALU.add),
                         reads, writes)


def v3(ap, a):
    return ap.rearrange("p (a b) -> p a b", a=a)


def build_program(S, L, dbg=(), pair=0):
    nc = bass.Bass("TRN2", target_bir_lowering=False)
    NCH = S // 128
    NT = S // 512
    assert S % 512 == 0

    def din(name, shape):
        return nc.dram_tensor(name, list(shape), F32, kind="ExternalInput").ap()

    def dscr(name, shape):
        kind = "ExternalOutput" if name in dbg else "Internal"
        return nc.dram_tensor(name, list(shape), F32, kind=kind).ap()

    I = {}
    I["xT"] = din("xT", [D, S])
    for nm, shp in [("g_ffn1", [L, 128, 8]), ("g_mix", [L, 128, 8]), ("g_ffn2", [L, 128, 8]), ("g_final", [128, 8]),
                    ("wg1", [L, NFC, 128, 1024]), ("wu1", [L, NFC, 128, 1024]), ("wd1", [L, 8, 128, DFF]),
                    ("wg2", [L, NFC, 128, 1024]), ("wu2", [L, NFC, 128, 1024]), ("wd2", [L, 8, 128, DFF]),
                    ("win_fm", [L, NFM, 128, 1024]), ("win_tm", [L, 128, 8 * NTM]), ("bgate", [L, 128, 24]),
                    ("conv_w", [L, 128, 30]), ("conv_b", [L, 128, 6]),
                    ("dtb_rep", [L, 128, 16]), ("alog_rep", [L, 128, 16]), ("dskip_rep", [L, 128, 8]),
                    ("ssdnorm_rep", [L, 128, 512]), ("retnorm_rep", [L, 128, 512]),
                    ("wbr_ssd", [L, 128, 4096]), ("wbr_ret", [L, 128, 4096]), ("wbr_s5", [L, 128, 3072]),
                    ("glu_wv", [L, 128, 1152]), ("glu_wg", [L, 128, 1152]), ("wout", [L, 128, 8192]),
                    ("s5_lre", [L, 2, 128, 12]), ("s5_lim", [L, 2, 128, 12]), ("s5_lstep", [L, 2, 128, 12]),
                    ("s5_bre", [L, 128, 12 * 128]), ("s5_bim", [L, 128, 12 * 128]),
                    ("s5_cre", [L, 2, 128, 12 * 32]), ("s5_cim", [L, 2, 128, 12 * 32]), ("s5_dsel", [L, 128, 12 * 32]),
                    ("c_ident", [128, 128]), ("c_tri", [128, 4 * 128]), ("c_maskneg", [128, 2 * 512]),
                    ("c_negiexp", [16, 16 * 128]), ("c_iexp", [16, 16 * 128]), ("c_retmask", [128, 16 * 128]),
                    ("c_retdec", [128, 2 * 8 + 2 * 8 + 8]), ("c_rope", [S, 64]), ("c_blk64", [128, 128])]:
        I[nm] = din(nm, shp)
    outT = nc.dram_tensor("outT", [D, S], F32, kind="ExternalOutput").ap()
    if pair:
        I["pairsel"] = din("pairsel", [128, 2])
        HS = nc.dram_tensor("HS", [128, 12], F32).ap()
        HR = nc.dram_tensor("HR", [256, 12], F32).ap()
        SS = nc.dram_tensor("SS", [128, 1048], F32).ap()
        SR = nc.dram_tensor("SR", [256, 1048], F32).ap()
        rgroups = [[2 * i, 2 * i + 1] for i in range(pair // 2)]

    X = dscr("X", [D, S])
    PF = dscr("PF", [NFM * 128, S])
    PT = dscr("PT", [S, NTM])
    XS = dscr("XS", [S, 512])
    BTM = dscr("BTM", [S, 128])
    BCT = dscr("BCT", [256, S])
    YF = dscr("YF", [S, 512])
    YTS = dscr("YTS", [512, S])
    QKT = dscr("QKT", [1024, S])
    KTM = dscr("KTM", [S, 512])
    RF = dscr("RF", [S, 512])
    YTR = dscr("YTR", [512, S])
    Y5 = dscr("Y5", [384, S])
    Y5A = dscr("Y5A", [384, S])
    Y5B = dscr("Y5B", [384, S])
    QK2 = dscr("QK2", [S // 128, 64, 2048])

    with ExitStack() as st:
        kb = KB(nc, st, 33280, 14336)
        P = kb.P
        ps, psb = kb.ps, kb.psb

        cbuf = Buf("consts")
        ident = kb.cr[:, 0:128]
        tri = kb.cr[:, 128:640]
        ones = kb.cr[:, 640:768]
        blk64 = kb.cr[:, 768:896]
        iexp = kb.cr[0:16, 896:896 + 0]
        maskneg = kb.cf[:, 0:1024]
        kb.dma(ident, I["c_ident"], writes=[cbuf])
        kb.dma(tri, I["c_tri"], writes=[cbuf])
        kb.dma(blk64, I["c_blk64"], writes=[cbuf])
        kb.dma(maskneg, I["c_maskneg"], writes=[cbuf], q="sp")
        identf = kb.cf[:, 1024:1152]
        kb.dma(identf, I["c_ident"], writes=[cbuf], q="sp")
        onesf = kb.cf[:, 1152:1280]
        kb.memset("dve", onesf, 1.0, writes=[cbuf])
        kb.copy("dve", ones, onesf, reads=[cbuf], writes=[cbuf])

        halo = kb.cf[:, 1280:1292]
        psel = kb.cf[:, 1292:1294]
        bhalo = Buf("halo")
        bSS, bSR = Buf("SS"), Buf("SR")
        if pair:
            kb.dma(psel, I["pairsel"], writes=[cbuf], q="sp")

        def coll(src, dst, reads, writes):
            return P.op("pool", lambda e, a=src, b=dst: e.collective_compute("AllGather", ALU.bypass, rgroups, [a], [b]),
                        reads, writes, dma=True)

        def recv_combine(out, c0, c1, rows, tmp0, tmp1, wbuf):
            kb.dma(tmp0, SR[0:rows, c0:c1], reads=[bSR], writes=[wbuf])
            kb.dma(tmp1, SR[128:128 + rows, c0:c1], reads=[bSR], writes=[wbuf])
            kb.ts("dve", tmp0, tmp0.bitcast(F32), psel[0:rows, 0:1], None, ALU.mult, reads=[wbuf, cbuf], writes=[wbuf])
            kb.stt("dve", out, tmp1.bitcast(F32), psel[0:rows, 1:2], tmp0.bitcast(F32), ALU.mult, ALU.add,
                   reads=[wbuf, cbuf], writes=[wbuf])

        def halo_exchange():
            kb.new_stage()
            hb = kb.Fm(12)
            hr = kb.Fm(24)
            b = Buf()
            bHS, bHR = Buf(), Buf()
            for f in range(6):
                kb.dma(hb[:, 2 * f:2 * f + 2], PF[f * 128:(f + 1) * 128, S - 2:S], reads=[bPF], writes=[b], q="sp")
            kb.dma(HS[:, :], hb, reads=[b], writes=[bHS], q="sp")
            coll(HS[:, :], HR[:, :], [bHS], [bHR])
            kb.dma(hr[:, 0:12], HR[0:128, :], reads=[bHR], writes=[b], q="sp")
            kb.dma(hr[:, 12:24], HR[128:256, :], reads=[bHR], writes=[b], q="sp")
            kb.ts("dve", hr[:, 0:12], hr[:, 0:12], psel[:, 0:1], None, ALU.mult, reads=[b, cbuf], writes=[b])
            kb.stt("dve", halo, hr[:, 12:24], psel[:, 1:2], hr[:, 0:12], ALU.mult, ALU.add, reads=[b, cbuf], writes=[bhalo])

        def state_exchange():
            kb.new_stage()
            coll(SS[:, :], SR[:, :], [bSS], [bSR])

        xbufs = {}

        def xb(k, t):
            key = (k, t)
            if key not in xbufs:
                xbufs[key] = Buf("x%d_%d" % key)
            return xbufs[key]

        def load_norm(src, t, gain_ap, pfx):
            xf = v3(kb.Fm(4096), 8)
            xn = v3(kb.R(4096), 8)
            sq = [kb.R(512), kb.R(512)]
            sqb = [Buf(), Buf()]
            rstd = kb.Fm(512)
            g = kb.Fm(8)
            bx, bn, br, bg = Buf(), Buf(), Buf(), Buf()
            kb.dma(g, gain_ap, writes=[bg], q="sp")
            bxk = [Buf() for _ in range(8)]
            for k in range(8):
                kb.dma(xf[:, k, :], src[k * 128:(k + 1) * 128, t * 512:(t + 1) * 512], reads=[xb(k, t)],
                       writes=[bxk[k]], q="sp")
                kb.act(sq[k % 2], xf[:, k, :], AF.Square, reads=[bxk[k]], writes=[sqb[k % 2]])
                kb.mm(ps[7][:], ones, sq[k % 2], k == 0, k == 7, reads=[sqb[k % 2], cbuf], writes=[psb[7]])
            kb.ts("dve", rstd, ps[7][:], 1.0 / D, EPS, ALU.mult, ALU.add, reads=[psb[7]], writes=[br])
            kb.act(rstd, rstd, AF.Sqrt, reads=[br], writes=[br])
            kb.recip(rstd, rstd, reads=[br], writes=[br])
            for k in range(8):
                kb.stt("dve", xn[:, k, :], xf[:, k, :], g[:, k:k + 1], rstd, ALU.mult, ALU.mult,
                       reads=[bxk[k], br, bg], writes=[bn])
            return xn, bn, xf, bxk

        def ffn_stage(l, gname, wg, wu, wd):
            TT = 1024
            assert S % TT == 0
            HF = NFC // 2
            for t in range(S // TT):
                kb.new_stage()
                c0 = t * TT
                xn = v3(kb.R(8192), 8)
                sq = [kb.R(1024), kb.R(1024)]
                sqb = [Buf(), Buf()]
                rstd = kb.Fm(1024)
                g = kb.Fm(8)
                acc = v3(kb.Fm(8192), 8)
                bacc = [Buf() for _ in range(8)]
                bn, br, bg = Buf(), Buf(), Buf()
                kb.dma(g, I[gname][l], writes=[bg], q="sp")
                for k in range(8):
                    kb.dma(acc[:, k, :], X[k * 128:(k + 1) * 128, c0:c0 + TT], reads=[xb(k, 2 * t), xb(k, 2 * t + 1)],
                           writes=[bacc[k]], q="sp")
                    kb.act(sq[k % 2], acc[:, k, :], AF.Square, reads=[bacc[k]], writes=[sqb[k % 2]])
                    for hh in range(2):
                        kb.mm(ps[6 + hh][:], ones, sq[k % 2][:, hh * 512:(hh + 1) * 512], k == 0, k == 7,
                              reads=[sqb[k % 2], cbuf], writes=[psb[6 + hh]])
                for hh in range(2):
                    kb.ts("dve", rstd[:, hh * 512:(hh + 1) * 512], ps[6 + hh][:], 1.0 / D, EPS, ALU.mult, ALU.add,
                          reads=[psb[6 + hh]], writes=[br])
                kb.act(rstd, rstd, AF.Sqrt, reads=[br], writes=[br])
                kb.recip(rstd, rstd, reads=[br], writes=[br])
                for k in range(8):
                    kb.stt("dve", xn[:, k, :], acc[:, k, :], g[:, k:k + 1], rstd, ALU.mult, ALU.mult,
                           reads=[bacc[k], br, bg], writes=[bn])
                actt = v3(kb.R(HF * 1024), HF)
                bact = [Buf() for _ in range(HF)]
                wgb = [kb.R(1024) for _ in range(2)]
                wub = [kb.R(1024) for _ in range(2)]
                bwg = [Buf(), Buf()]
                bwu = [Buf(), Buf()]
                sg = [kb.Fm(512), kb.Fm(512)]
                bsg = [Buf(), Buf()]
                wdb = [kb.R(HF * 128) for _ in range(2)]
                bwd = [Buf(), Buf()]
                xr = [kb.Fm(1024), kb.Fm(1024)]
                bxr = [Buf(), Buf()]
                xo = [kb.Fm(512), kb.Fm(512)]
                bxo = [Buf(), Buf()]
                n = 0
                m = 0
                for half in range(2):
                    f0 = half * HF
                    for jj in range(HF):
                        j = f0 + jj
                        kb.dma(wgb[j % 2], wg[l, j], writes=[bwg[j % 2]])
                        kb.dma(wub[j % 2], wu[l, j], writes=[bwu[j % 2]])
                        for hh in range(2):
                            pg, pu = ps[n % 2], ps[2 + n % 2]
                            bpg, bpu = psb[n % 2], psb[2 + n % 2]
                            for k in range(8):
                                kb.mm(pg[:], wgb[j % 2][:, k * 128:(k + 1) * 128], xn[:, k, hh * 512:(hh + 1) * 512],
                                      k == 0, k == 7, reads=[bwg[j % 2], bn], writes=[bpg])
                            for k in range(8):
                                kb.mm(pu[:], wub[j % 2][:, k * 128:(k + 1) * 128], xn[:, k, hh * 512:(hh + 1) * 512],
                                      k == 0, k == 7, reads=[bwu[j % 2], bn], writes=[bpu])
                            kb.act(sg[n % 2], pg[:], AF.Silu, reads=[bpg], writes=[bsg[n % 2]])
                            kb.tt("dve", actt[:, jj, hh * 512:(hh + 1) * 512], sg[n % 2], pu[:], ALU.mult,
                                  reads=[bsg[n % 2], bpu], writes=[bact[jj]])
                            n += 1
                    for i in range(8):
                        kb.dma(wdb[i % 2], wd[l, i][:, f0 * 128:(f0 + HF) * 128], writes=[bwd[i % 2]])
                        if half == 1:
                            kb.dma(xr[i % 2], X[i * 128:(i + 1) * 128, c0:c0 + TT], reads=[xb(i, 2 * t), xb(i, 2 * t + 1)],
                                   writes=[bxr[i % 2]], q="sp")
                        for hh in range(2):
                            po, bpo = ps[4 + m % 2], psb[4 + m % 2]
                            for jj in range(HF):
                                kb.mm(po[:], wdb[i % 2][:, jj * 128:(jj + 1) * 128], actt[:, jj, hh * 512:(hh + 1) * 512],
                                      jj == 0, jj == HF - 1, reads=[bwd[i % 2], bact[jj]], writes=[bpo])
                            av = acc[:, i, hh * 512:(hh + 1) * 512]
                            if half == 0:
                                kb.copy("act", av, po[:], reads=[bpo], writes=[bacc[i]])
                            else:
                                kb.tt("dve", av, av, po[:], ALU.add, reads=[bpo, bacc[i]], writes=[bacc[i]])
                                kb.stt("dve", xo[m % 2], av, 0.5, xr[i % 2][:, hh * 512:(hh + 1) * 512], ALU.mult, ALU.add,
                                       reads=[bacc[i], bxr[i % 2]], writes=[bxo[m % 2]])
                                kb.dma(X[i * 128:(i + 1) * 128, c0 + hh * 512:c0 + (hh + 1) * 512], xo[m % 2],
                                       reads=[bxo[m % 2]], writes=[xb(i, 2 * t + hh)], q="sp")
                            m += 1

        bPF = Buf("PF")
        bPT = Buf("PT")

        def inproj_stage(l):
            for t in range(NT):
                kb.new_stage()
                xn, bn, xf, bxk = load_norm(X, t, I["g_mix"][l], "m")
                bgt = kb.Fm(24)
                bbg = Buf()
                kb.dma(bgt, I["bgate"][l], writes=[bbg], q="sp")
                wb = [kb.R(1024) for _ in range(2)]
                bw = [Buf(), Buf()]
                so = [kb.Fm(512), kb.Fm(512)]
                bso = [Buf(), Buf()]
                for j in range(NFM):
                    kb.dma(wb[j % 2], I["win_fm"][l, j], writes=[bw[j % 2]])
                    pp = ps[j % 2]
                    for k in range(8):
                        kb.mm(pp[:], wb[j % 2][:, k * 128:(k + 1) * 128], xn[:, k, :], k == 0, k == 7,
                              reads=[bw[j % 2], bn], writes=[psb[j % 2]])
                    if j >= 9:
                        kb.act(so[j % 2], pp[:], AF.Sigmoid, reads=[psb[j % 2], bbg], writes=[bso[j % 2]],
                               bias=bgt[:, j - 9:j - 8])
                    else:
                        kb.copy("dve", so[j % 2], pp[:], reads=[psb[j % 2]], writes=[bso[j % 2]])
                    kb.dma(PF[j * 128:(j + 1) * 128, t * 512:(t + 1) * 512], so[j % 2], reads=[bso[j % 2]],
                           writes=[bPF], q="sp")
                dtb = kb.Fm(16)
                bdtb = Buf()
                kb.dma(dtb, I["dtb_rep"][l], writes=[bdtb], q="sp")
                wt = [kb.R(8 * 512) for _ in range(2)]
                bwt = [Buf(), Buf()]
                st_ = [kb.Fm(512) for _ in range(2)]
                bst = [Buf(), Buf()]
                cnt = 0
                for cb in range(6):
                    c0 = cb * 512
                    w = 512 if cb < 5 else 16
                    wv = v3(wt[cb % 2], 8)
                    kb.dma(wv[:, :, 0:w], v3(I["win_tm"][l], 8)[:, :, c0:c0 + w], writes=[bwt[cb % 2]])
                    for c in range(4):
                        pp = ps[2 + cnt % 2]
                        bp = psb[2 + cnt % 2]
                        for k in range(8):
                            kb.mm(pp[:, 0:w], xn[:, k, c * 128:(c + 1) * 128], wv[:, k, 0:w], k == 0, k == 7,
                                  reads=[bwt[cb % 2], bn], writes=[bp])
                        o = st_[cnt % 2][:, 0:w]
                        bo = bst[cnt % 2]
                        if cb in (0, 4):
                            kb.act(o, pp[:, 0:w], AF.Silu, reads=[bp], writes=[bo])
                        elif cb == 5:
                            kb.tt("dve", o, pp[:, 0:w], dtb, ALU.add, reads=[bp, bdtb], writes=[bo])
                            kb.act(o, o, AF.Exp, reads=[bo], writes=[bo])
                            kb.act(o, o, AF.Ln, reads=[bo], writes=[bo], bias=1.0)
                        else:
                            kb.copy("dve", o, pp[:, 0:w], reads=[bp], writes=[bo])
                        r0 = t * 512 + c * 128
                        kb.dma(PT[r0:r0 + 128, c0:c0 + w], o, reads=[bo], writes=[bPT], q="sp")
                        cnt += 1

        bXS, bBTM, bBCT = Buf("XS"), Buf("BTM"), Buf("BCT")

        def ssd_prep(l):
            kb.new_stage()
            cw = kb.Fm(30)
            cbias = kb.Fm(6)
            bcw = Buf()
            kb.dma(cw, I["conv_w"][l], writes=[bcw], q="sp")
            kb.dma(cbias, I["conv_b"][l], writes=[bcw], q="sp")
            xin = [kb.Fm(516) for _ in range(2)]
            bxin = [Buf(), Buf()]
            acc = [kb.Fm(512) for _ in range(2)]
            bacc = [Buf(), Buf()]
            cv = [kb.R(512) for _ in range(2)]
            bcv = [Buf(), Buf()]
            tm = [kb.Fm(512) for _ in range(2)]
            btm = [Buf(), Buf()]
            n = 0
            for t in range(NT):
                for f in range(6):
                    xi, bi = xin[n % 2], bxin[n % 2]
                    lo = t * 512 - 2
                    hi = t * 512 + 514
                    a, b = max(lo, 0), min(hi, S)
                    if a > lo:
                        kb.memset("pool", xi[:, 0:2], 0.0, writes=[bi])
                    if b < hi:
                        if pair:
                            kb.copy("pool", xi[:, 514:515], halo[:, 2 * f + 1:2 * f + 2], reads=[bhalo], writes=[bi])
                            kb.copy("pool", xi[:, 515:516], halo[:, 2 * f:2 * f + 1], reads=[bhalo], writes=[bi])
                        else:
                            kb.memset("pool", xi[:, 514:516], 0.0, writes=[bi])
                    kb.dma(xi[:, a - lo:b - lo], PF[f * 128:(f + 1) * 128, a:b], reads=[bPF], writes=[bi], q="sp")
                    ac, ba = acc[n % 2], bacc[n % 2]
                    kb.ts("dve", ac, xi[:, 0:512], cw[:, f * 5:f * 5 + 1], None, ALU.mult, reads=[bi, bcw], writes=[ba])
                    for j in range(1, 5):
                        kb.stt("dve", ac, xi[:, j:j + 512], cw[:, f * 5 + j:f * 5 + j + 1], ac, ALU.mult, ALU.add,
                               reads=[bi, bcw, ba], writes=[ba])
                    c_, bc_ = cv[n % 2], bcv[n % 2]
                    kb.act(c_, ac, AF.Silu, reads=[ba, bcw], writes=[bc_], bias=cbias[:, f:f + 1])
                    if f < 5:
                        for c in range(4):
                            pp, bp = ps[(4 * n + c) % 4], psb[(4 * n + c) % 4]
                            kb.mm(pp[:, 0:128], c_[:, c * 128:(c + 1) * 128], ident, True, True,
                                  reads=[bc_, cbuf], writes=[bp])
                            o, bo = tm[c % 2][:, 0:128], btm[c % 2]
                            kb.copy("act" if c % 2 else "dve", o, pp[:, 0:128], reads=[bp], writes=[bo])
                            r0 = t * 512 + c * 128
                            if f < 4:
                                kb.dma(XS[r0:r0 + 128, f * 128:(f + 1) * 128], o, reads=[bo], writes=[bXS], q="sp")
                            else:
                                kb.dma(BTM[r0:r0 + 128, :], o, reads=[bo], writes=[bBTM], q="sp")
                    if f >= 4:
                        kb.dma(BCT[(f - 4) * 128:(f - 3) * 128, t * 512:(t + 1) * 512], c_, reads=[bc_], writes=[bBCT])
                    n += 1

        bYF, bYTS = Buf("YF"), Buf("YTS")

        def ssd_pass(l, dirn):
            kb.new_stage()
            arep = kb.Fm(16)
            dsk = kb.Fm(8)
            gn = kb.Fm(512)
            bpar = Buf()
            kb.dma(arep, I["alog_rep"][l], writes=[bpar], q="sp")
            kb.dma(dsk, I["dskip_rep"][l], writes=[bpar], q="sp")
            kb.dma(gn, I["ssdnorm_rep"][l], writes=[bpar], q="sp")
            kb.act(arep, arep, AF.Exp, reads=[bpar], writes=[bpar])
            kb.ts("dve", arep, arep, -1.0, None, ALU.mult, reads=[bpar], writes=[bpar])
            niexp = v3(kb.Fm(2048)[0:16, :], 16)
            iex = v3(kb.Fm(2048)[0:16, :], 16)
            kb.dma(niexp, v3(I["c_negiexp"], 16), writes=[bpar], q="sp")
            kb.dma(iex, v3(I["c_iexp"], 16), writes=[bpar], q="sp")
            mk = v3(maskneg[:, dirn * 512:(dirn + 1) * 512], 4)
            Sst = kb.R(512)
            bS = Buf()
            if pair and dirn == 1:
                recv_combine(Sst, 0, 512, 128, kb.R(512), kb.R(512), bS)
            else:
                kb.ts("dve", Sst, onesf[:, 0:1].to_broadcast([128, 512]), 0.0, None, ALU.mult, reads=[cbuf], writes=[bS])
            NB = 2
            xs_ = [kb.Fm(512) for _ in range(NB)]
            dt_ = [kb.Fm(16) for _ in range(NB)]
            bt_ = [kb.R(128) for _ in range(NB)]
            bct = [kb.R(512) for _ in range(NB)]
            bin_ = [Buf() for _ in range(NB)]
            for i_ in range(NB):
                kb.ts("dve", bct[i_][:, 256:512], onesf[:, 0:1].to_broadcast([128, 256]), 0.0, None, ALU.mult,
                      reads=[cbuf], writes=[bin_[i_]])
            da = [kb.Fm(8) for _ in range(NB)]
            xd = [kb.R(512) for _ in range(NB)]
            xdw = [kb.R(512) for _ in range(NB)]
            csf = [kb.Fm(128) for _ in range(NB)]
            zt = [kb.Fm(1024) for _ in range(NB)]
            ee = [kb.Fm(1024) for _ in range(NB)]
            gt = [kb.Fm(256) for _ in range(NB)]
            mt = [kb.R(1024) for _ in range(NB)]
            ex = [kb.Fm(16) for _ in range(NB)]
            dec = [kb.Fm(8) for _ in range(NB)]
            yt_ = [kb.Fm(512) for _ in range(NB)]
            t2 = [kb.Fm(512) for _ in range(NB)]
            zz = [kb.Fm(512) for _ in range(NB)]
            yo = [kb.R(512) for _ in range(NB)]
            ss = [kb.Fm(8) for _ in range(NB)]
            bw_ = [[Buf() for _ in range(16)] for _ in range(NB)]
            order = range(NCH) if dirn == 0 else range(NCH - 1, -1, -1)
            for n, c in enumerate(order):
                i = n % NB
                B = bw_[i]
                r0 = c * 128
                kb.dma(xs_[i], XS[r0:r0 + 128, :], reads=[bXS], writes=[bin_[i]], q="sp")
                kb.dma(dt_[i], PT[r0:r0 + 128, 2560:2576], reads=[bPT], writes=[bin_[i]], q="sp")
                kb.dma(bt_[i], BTM[r0:r0 + 128, :], reads=[bBTM], writes=[bin_[i]])
                kb.dma(bct[i][:, 0:128], BCT[0:128, r0:r0 + 128], reads=[bBCT], writes=[bin_[i]])
                kb.dma(bct[i][:, 128:256], BCT[128:256, r0:r0 + 128], reads=[bBCT], writes=[bin_[i]])
                for g in range(2):
                    kb.dma(bct[i][g * 64:(g + 1) * 64, 256 + g * 128:384 + g * 128],
                           BCT[128 + g * 64:192 + g * 64, r0:r0 + 128], reads=[bBCT], writes=[bin_[i]])
                dtd = dt_[i][:, dirn * 8:(dirn + 1) * 8]
                kb.tt("dve", da[i], dtd, arep[:, dirn * 8:(dirn + 1) * 8], ALU.mult, reads=[bin_[i], bpar], writes=[B[0]])
                kb.tt("pool", v3(xd[i], 8), v3(xs_[i], 8), dtd.unsqueeze(2).to_broadcast([128, 8, 64]), ALU.mult,
                      reads=[bin_[i]], writes=[B[1]])
                trisel = kb.cf
                tsl = tri[:, 0:128] if dirn == 0 else tri[:, 128:256]
                kb.mm(ps[0][0:8, 0:128], da[i].bitcast(F32), tsl.bitcast(F32), True, True, reads=[B[0], cbuf],
                      writes=[psb[0]])
                kb.copy("act", csf[i][0:8, :], ps[0][0:8, 0:128], reads=[psb[0]], writes=[B[2]])
                kb.tt("dve", v3(zt[i][0:8, :], 8), csf[i][0:8, :].unsqueeze(1).to_broadcast([8, 8, 128]),
                      iex[0:8, 0:8, :], ALU.mult, reads=[B[2], bpar], writes=[B[3]])
                for hb in range(2):
                    pp, bp = ps[1 + hb], psb[1 + hb]
                    kb.mm(pp[:], onesf[0:8, :], zt[i][0:8, hb * 512:(hb + 1) * 512], True, False,
                          reads=[B[3], cbuf], writes=[bp])
                    kb.mm(pp[:], csf[i][0:8, :], niexp[0:8, hb * 4:(hb + 1) * 4, :], False, False,
                          reads=[B[2], bpar], writes=[bp])
                    kb.mm(pp[:], identf, mk, False, True, reads=[cbuf], writes=[bp])
                    kb.act(ee[i][:, hb * 512:(hb + 1) * 512], pp[:], AF.Exp, reads=[bp], writes=[B[4]])
                kb.mm(ps[3][:, 0:256], bct[i][:, 0:128], bct[i][:, 256:512], True, True, reads=[bin_[i]], writes=[psb[3]])
                kb.copy("act", gt[i], ps[3][:, 0:256], reads=[psb[3]], writes=[B[5]])
                for g in range(2):
                    kb.tt("dve" if g else "pool", v3(mt[i][:, g * 512:(g + 1) * 512], 4),
                          v3(ee[i][:, g * 512:(g + 1) * 512], 4),
                          gt[i][:, g * 128:(g + 1) * 128].unsqueeze(1).to_broadcast([128, 4, 128]), ALU.mult,
                          reads=[B[4], B[5]], writes=[B[6]])
                if dirn == 0:
                    wl, ol = tri[:, 256:384], tri[:, 0:128]
                else:
                    wl, ol = tri[:, 384:512], None
                kb.mm(ps[4][:, 0:8], wl.bitcast(F32), da[i], True, True, reads=[B[0], cbuf], writes=[psb[4]])
                if dirn == 0:
                    kb.mm(ps[4][:, 8:16], ol.bitcast(F32), da[i], True, True, reads=[B[0], cbuf], writes=[psb[4]])
                else:
                    kb.mm(ps[4][:, 8:16], onesf, da[i], True, False, reads=[B[0], cbuf], writes=[psb[4]])
                    kb.mm(ps[4][:, 8:16], tri[:, 128:256].bitcast(F32).rearrange("p x -> p x"), da[i], False, True,
                          reads=[B[0], cbuf], writes=[psb[4]])
                kb.act(ex[i], ps[4][:, 0:16], AF.Exp, reads=[psb[4]], writes=[B[7]])
                kb.mm(ps[4][:, 16:24], onesf, da[i], True, True, reads=[B[0], cbuf], writes=[psb[4]])
                kb.act(dec[i], ps[4][:, 16:24], AF.Exp, reads=[psb[4]], writes=[B[8]])
                for h in range(8):
                    kb.mm(ps[5][:, h * 64:(h + 1) * 64], mt[i][:, h * 128:(h + 1) * 128], xd[i][:, h * 64:(h + 1) * 64],
                          True, True, reads=[B[6], B[1]], writes=[psb[5]])
                kb.mm(ps[6][:], bct[i][:, 128:256], Sst, True, True, reads=[bin_[i], bS], writes=[psb[6]])
                kb.tt("dve", v3(t2[i], 8), v3(ps[6][:], 8), ex[i][:, 8:16].unsqueeze(2).to_broadcast([128, 8, 64]),
                      ALU.mult, reads=[psb[6], B[7]], writes=[B[9]])
                kb.tt("dve", yt_[i], ps[5][:], t2[i], ALU.add, reads=[psb[5], B[9]], writes=[B[10]])
                kb.tt("pool", v3(xdw[i], 8), v3(xd[i], 8), ex[i][:, 0:8].unsqueeze(2).to_broadcast([128, 8, 64]),
                      ALU.mult, reads=[B[1], B[7]], writes=[B[11]])
                kb.mm(ps[7][:], bt_[i], xdw[i], True, True, reads=[bin_[i], B[11]], writes=[psb[7]])
                for g in range(2):
                    sl = slice(g * 64, (g + 1) * 64)
                    kb.tt("dve", v3(Sst[sl, g * 256:(g + 1) * 256], 4), v3(Sst[sl, g * 256:(g + 1) * 256], 4),
                          dec[i][sl, g * 4:(g + 1) * 4].unsqueeze(2).to_broadcast([64, 4, 64]), ALU.mult,
                          reads=[bS, B[8]], writes=[bS])
                for g in range(2):
                    sl = slice(g * 64, (g + 1) * 64)
                    kb.tt("dve", Sst[sl, g * 256:(g + 1) * 256], Sst[sl, g * 256:(g + 1) * 256],
                          ps[7][sl, g * 256:(g + 1) * 256], ALU.add, reads=[bS, psb[7]], writes=[bS])
                if dirn == 0:
                    kb.tt("pool", v3(t2[i], 8), v3(xs_[i], 8), dsk.unsqueeze(2).to_broadcast([128, 8, 64]), ALU.mult,
                          reads=[bin_[i], bpar, B[10]], writes=[B[9]])
                    kb.tt("dve", yt_[i], yt_[i], t2[i], ALU.add, reads=[B[9], B[10]], writes=[B[10]])
                    kb.dma(YF[r0:r0 + 128, :], yt_[i], reads=[B[10]], writes=[bYF], q="sp")
                else:
                    kb.dma(t2[i], YF[r0:r0 + 128, :], reads=[bYF, B[9], B[10]], writes=[B[9]], q="sp")
                    kb.dma(zz[i], PT[r0:r0 + 128, 0:512], reads=[bPT], writes=[B[12]], q="sp")
                    kb.tt("dve", yt_[i], yt_[i], t2[i], ALU.add, reads=[B[9], B[10]], writes=[B[10]])
                    kb.tt("dve", yt_[i], yt_[i], zz[i], ALU.mult, reads=[B[12], B[10]], writes=[B[10]])
                    kb.P.op("act", lambda e, o=t2[i], a=yt_[i], s=ss[i]: e.activation(o, a, AF.Square, accum_out=s[:, 0:1]),
                            reads=[B[10], B[9]], writes=[B[9], B[13]])
                    kb.ts("dve", ss[i][:, 0:1], ss[i][:, 0:1], 1.0 / 512, EPS, ALU.mult, ALU.add, reads=[B[13]], writes=[B[13]])
                    kb.act(ss[i][:, 0:1], ss[i][:, 0:1], AF.Sqrt, reads=[B[13]], writes=[B[13]])
                    kb.recip(ss[i][:, 0:1], ss[i][:, 0:1], reads=[B[13]], writes=[B[13]])
                    kb.stt("dve", yo[i], yt_[i], ss[i][:, 0:1], gn, ALU.mult, ALU.mult, reads=[B[10], B[13], bpar],
                           writes=[B[14]])
                    for f in range(4):
                        kb.mm(ps[0][:, f * 128:(f + 1) * 128], yo[i][:, f * 128:(f + 1) * 128], ident, True, True,
                              reads=[B[14], cbuf], writes=[psb[0]])
                    kb.copy("act", t2[i], ps[0][:], reads=[psb[0], B[9]], writes=[B[9]])
                    for f in range(4):
                        kb.dma(YTS[f * 128:(f + 1) * 128, r0:r0 + 128], t2[i][:, f * 128:(f + 1) * 128], reads=[B[9]],
                               writes=[bYTS], q="sp")
            if pair and dirn == 0:
                kb.dma(SS[:, 0:512], Sst, reads=[bS], writes=[bSS])

        bQKT, bKTM = Buf("QKT"), Buf("KTM")

        def ret_prep(l):
            kb.new_stage()
            NB = 2
            qk = [kb.Fm(1024) for _ in range(NB)]
            rp = [kb.Fm(64) for _ in range(NB)]
            ro = [kb.R(1024) for _ in range(NB)]
            tmp = [kb.Fm(512) for _ in range(NB)]
            oT = [kb.Fm(512) for _ in range(NB)]
            bb = [[Buf() for _ in range(6)] for _ in range(NB)]
            for c in range(NCH):
                i = c % NB
                B = bb[i]
                r0 = c * 128
                kb.dma(qk[i], PT[r0:r0 + 128, 512:1536], reads=[bPT], writes=[B[0]], q="sp")
                kb.dma(rp[i], I["c_rope"][r0:r0 + 128, :], writes=[B[0]], q="sp")
                cosb = rp[i][:, 0:32].unsqueeze(1).to_broadcast([128, 16, 32])
                sinb = rp[i][:, 32:64].unsqueeze(1).to_broadcast([128, 16, 32])
                x4 = qk[i].rearrange("p (h t d) -> p h t d", h=16, t=2)
                o4 = ro[i].rearrange("p (h t d) -> p h t d", h=16, t=2)
                tm4 = tmp[i].rearrange("p (h d) -> p h d", h=16)
                t1, t2_ = x4[:, :, 0, :], x4[:, :, 1, :]
                kb.tt("dve", tm4, t2_, sinb, ALU.mult, reads=[B[0]], writes=[B[1]])
                kb.tt("pool", o4[:, :, 0, :], t1, cosb, ALU.mult, reads=[B[0]], writes=[B[2]])
                kb.tt("dve", o4[:, :, 0, :], o4[:, :, 0, :], tm4, ALU.subtract, reads=[B[1], B[2]], writes=[B[2]])
                kb.tt("dve", tm4, t1, sinb, ALU.mult, reads=[B[0], B[2]], writes=[B[1]])
                kb.tt("pool", o4[:, :, 1, :], t2_, cosb, ALU.mult, reads=[B[0]], writes=[B[3]])
                kb.tt("dve", o4[:, :, 1, :], o4[:, :, 1, :], tm4, ALU.add, reads=[B[1], B[3]], writes=[B[3]])
                kb.ts("dve", ro[i][:, 512:1024], ro[i][:, 512:1024], 0.125, None, ALU.mult, reads=[B[2], B[3]],
                      writes=[B[2], B[3]])
                kb.dma(KTM[r0:r0 + 128, :], ro[i][:, 512:1024], reads=[B[2], B[3]], writes=[bKTM])
                for f in range(8):
                    pp, bp = ps[f % 2], psb[f % 2]
                    kb.mm(pp[:, 0:128], ro[i][:, f * 128:(f + 1) * 128], ident, True, True, reads=[B[2], B[3], cbuf],
                          writes=[bp])
                    o = oT[i][:, (f % 4) * 128:(f % 4 + 1) * 128]
                    kb.copy("act", o, pp[:, 0:128], reads=[bp], writes=[B[4 + (f % 2)]])
                    for hh in range(2):
                        kb.dma(QK2[c, :, (2 * f + hh) * 128:(2 * f + hh + 1) * 128], o[hh * 64:(hh + 1) * 64, :],
                               reads=[B[4 + (f % 2)]], writes=[bQKT], q="sp")

        bRF, bYTR = Buf("RF"), Buf("YTR")

        def ret_pass(l, dirn):
            kb.new_stage()
            rmask = v3(kb.Fm(1024), 8)
            rdec = kb.Fm(40)
            gn = kb.Fm(512)
            bpar = Buf()
            kb.dma(rmask, v3(I["c_retmask"][:, dirn * 1024:(dirn + 1) * 1024], 8), writes=[bpar], q="sp")
            kb.dma(rdec, I["c_retdec"], writes=[bpar], q="sp")
            kb.dma(gn, I["retnorm_rep"][l], writes=[bpar], q="sp")
            kdec = rdec[:, dirn * 8:dirn * 8 + 8]
            qdec = rdec[:, 16 + dirn * 8:16 + dirn * 8 + 8]
            cdec = rdec[0:64, 32:40]
            Sst = kb.R(512)[0:64, :]
            bS = Buf()
            if pair and dirn == 1:
                recv_combine(Sst, 512, 1024, 64, kb.R(512)[0:64, :], kb.R(512)[0:64, :], bS)
            else:
                kb.ts("dve", Sst, onesf[0:64, 0:1].to_broadcast([64, 512]), 0.0, None, ALU.mult, reads=[cbuf], writes=[bS])
            NB = 2
            qkt = [kb.R(2048) for _ in range(NB)]
            ktm = [kb.R(512) for _ in range(NB)]
            vtm = [kb.R(512) for _ in range(NB)]
            vw = [kb.R(512) for _ in range(NB)]
            mt = [kb.R(1024) for _ in range(NB)]
            yt_ = [kb.Fm(512) for _ in range(NB)]
            t2 = [kb.Fm(512) for _ in range(NB)]
            gg = [kb.Fm(512) for _ in range(NB)]
            yo = [kb.R(512) for _ in range(NB)]
            ss = [kb.Fm(16) for _ in range(NB)]
            bb = [[Buf() for _ in range(12)] for _ in range(NB)]
            order = range(NCH) if dirn == 0 else range(NCH - 1, -1, -1)
            for n, c in enumerate(order):
                i = n % NB
                B = bb[i]
                r0 = c * 128
                q3 = v3(qkt[i][0:64, :], 16)
                kb.dma(qkt[i][0:64, :], QK2[c], reads=[bQKT], writes=[B[0]])
                kb.dma(ktm[i], KTM[r0:r0 + 128, :], reads=[bKTM], writes=[B[0]])
                kb.dma(vtm[i], PT[r0:r0 + 128, 1536:2048], reads=[bPT], writes=[B[0]])
                for h in range(8):
                    kb.mm(ps[h // 4][:, (h % 4) * 128:(h % 4 + 1) * 128], q3[:, 8 + h, :], q3[:, h, :], True, True,
                          reads=[B[0]], writes=[psb[h // 4]])
                for hb in range(2):
                    kb.tt("dve", v3(mt[i][:, hb * 512:(hb + 1) * 512], 4), v3(ps[hb][:], 4), rmask[:, hb * 4:(hb + 1) * 4, :],
                          ALU.mult, reads=[psb[hb], bpar], writes=[B[1]])
                for h in range(8):
                    kb.mm(ps[5][:, h * 64:(h + 1) * 64], mt[i][:, h * 128:(h + 1) * 128], vtm[i][:, h * 64:(h + 1) * 64],
                          True, True, reads=[B[1], B[0]], writes=[psb[5]])
                for h in range(8):
                    kb.mm(ps[6][:, h * 64:(h + 1) * 64], q3[:, h, :], Sst[:, h * 64:(h + 1) * 64], True, True,
                          reads=[B[0], bS], writes=[psb[6]])
                kb.tt("dve", v3(t2[i], 8), v3(ps[6][:], 8), qdec.unsqueeze(2).to_broadcast([128, 8, 64]), ALU.mult,
                      reads=[psb[6], bpar], writes=[B[2]])
                kb.tt("dve", yt_[i], ps[5][:], t2[i], ALU.add, reads=[psb[5], B[2]], writes=[B[3]])
                kb.tt("pool", v3(vw[i], 8), v3(vtm[i], 8), kdec.unsqueeze(2).to_broadcast([128, 8, 64]), ALU.mult,
                      reads=[B[0], bpar], writes=[B[4]])
                for h in range(8):
                    kb.mm(ps[7][0:64, h * 64:(h + 1) * 64], ktm[i][:, h * 64:(h + 1) * 64], vw[i][:, h * 64:(h + 1) * 64],
                          True, True, reads=[B[0], B[4]], writes=[psb[7]])
                kb.tt("dve", v3(Sst, 8), v3(Sst, 8), cdec.unsqueeze(2).to_broadcast([64, 8, 64]), ALU.mult,
                      reads=[bS, bpar], writes=[bS])
                kb.tt("dve", Sst, Sst, ps[7][0:64, :], ALU.add, reads=[bS, psb[7]], writes=[bS])
                if dirn == 0:
                    kb.dma(RF[r0:r0 + 128, :], yt_[i], reads=[B[3]], writes=[bRF], q="sp")
                else:
                    kb.dma(t2[i], RF[r0:r0 + 128, :], reads=[bRF, B[2], B[3]], writes=[B[2]], q="sp")
                    kb.dma(gg[i], PT[r0:r0 + 128, 2048:2560], reads=[bPT], writes=[B[5]], q="sp")
                    kb.tt("dve", yt_[i], yt_[i], t2[i], ALU.add, reads=[B[2], B[3]], writes=[B[3]])
                    kb.tt("pool", t2[i], yt_[i], yt_[i], ALU.mult, reads=[B[3], B[2]], writes=[B[2]])
                    kb.P.op("dve", lambda e, o=ss[i][:, 0:8], a=v3(t2[i], 8): e.tensor_reduce(o, a, AX.X, ALU.add),
                            reads=[B[2]], writes=[B[6]])
                    kb.ts("dve", ss[i][:, 0:8], ss[i][:, 0:8], 1.0 / 64, EPS, ALU.mult, ALU.add, reads=[B[6]], writes=[B[6]])
                    kb.act(ss[i][:, 0:8], ss[i][:, 0:8], AF.Sqrt, reads=[B[6]], writes=[B[6]])
                    kb.recip(ss[i][:, 0:8], ss[i][:, 0:8], reads=[B[6]], writes=[B[6]])
                    kb.tt("dve", v3(yt_[i], 8), v3(yt_[i], 8), ss[i][:, 0:8].unsqueeze(2).to_broadcast([128, 8, 64]),
                          ALU.mult, reads=[B[3], B[6]], writes=[B[3]])
                    kb.tt("pool", yt_[i], yt_[i], gn, ALU.mult, reads=[B[3], bpar], writes=[B[3]])
                    kb.tt("dve", yo[i], yt_[i], gg[i], ALU.mult, reads=[B[3], B[5]], writes=[B[7]])
                    for f in range(4):
                        kb.mm(ps[2][:, f * 128:(f + 1) * 128], yo[i][:, f * 128:(f + 1) * 128], ident, True, True,
                              reads=[B[7], cbuf], writes=[psb[2]])
                    kb.copy("act", t2[i], ps[2][:], reads=[psb[2], B[2]], writes=[B[2]])
                    for f in range(4):
                        kb.dma(YTR[f * 128:(f + 1) * 128, r0:r0 + 128], t2[i][:, f * 128:(f + 1) * 128], reads=[B[2]],
                               writes=[bYTR], q="sp")
            if pair and dirn == 0:
                kb.dma(SS[0:64, 512:1024], Sst, reads=[bS], writes=[bSS])

        bY5 = Buf("Y5")
        bY5A, bY5B = Buf("Y5A"), Buf("Y5B")

        def s5_stage(l, dirs=(0, 1)):
            kb.new_stage()
            bp_ = Buf("s5par")
            def F(n):
                return kb.Fm(n)
            rho = [F(12), F(12)]
            c0 = [F(12), F(12)]
            s0 = [F(12), F(12)]
            C9 = [F(12), F(12)]
            S9 = [F(12), F(12)]
            cfr = [F(12), F(12)]
            cfi = [F(12), F(12)]
            tA, tB, tC, tD = F(12), F(12), F(12), F(12)
            halfpi = F(1)
            kb.memset("dve", halfpi, math.pi / 2, writes=[bp_])
            for d in range(2):
                lre, lim, stp = F(12), F(12), F(12)
                kb.dma(lre, I["s5_lre"][l, d], writes=[bp_], q="sp")
                kb.dma(lim, I["s5_lim"][l, d], writes=[bp_], q="sp")
                kb.dma(stp, I["s5_lstep"][l, d], writes=[bp_], q="sp")
                R_, W_ = [bp_], [bp_]
                kb.ts("dve", lre, lre, -1e-4, None, ALU.min, reads=R_, writes=W_)
                kb.act(stp, stp, AF.Exp, reads=R_, writes=W_)
                kb.tt("dve", tA, lre, stp, ALU.mult, reads=R_, writes=W_)
                kb.act(rho[d], tA, AF.Exp, reads=R_, writes=W_)
                kb.tt("dve", tB, lim, stp, ALU.mult, reads=R_, writes=W_)
                kb.act(s0[d], tB, AF.Sin, reads=R_, writes=W_, scale=1.0 / 32)
                kb.act(c0[d], tB, AF.Sin, reads=R_, writes=W_, scale=1.0 / 32, bias=halfpi[:, 0:1])

                def dbl(cc, sn):
                    kb.tt("dve", tC, cc, cc, ALU.mult, reads=R_, writes=W_)
                    kb.tt("dve", tD, sn, sn, ALU.mult, reads=R_, writes=W_)
                    kb.tt("dve", tD, tC, tD, ALU.subtract, reads=R_, writes=W_)
                    kb.tt("dve", tC, cc, sn, ALU.mult, reads=R_, writes=W_)
                    kb.ts("dve", sn, tC, 2.0, None, ALU.mult, reads=R_, writes=W_)
                    kb.copy("dve", cc, tD, reads=R_, writes=W_)
                for _ in range(5):
                    dbl(c0[d], s0[d])
                kb.copy("dve", C9[d], c0[d], reads=R_, writes=W_)
                kb.copy("dve", S9[d], s0[d], reads=R_, writes=W_)
                for _ in range(9):
                    dbl(C9[d], S9[d])
                lbr, lbi, den = F(12), F(12), F(12)
                kb.tt("dve", lbr, rho[d], c0[d], ALU.mult, reads=R_, writes=W_)
                kb.tt("dve", lbi, rho[d], s0[d], ALU.mult, reads=R_, writes=W_)
                kb.ts("dve", lbr, lbr, -1.0, None, ALU.add, reads=R_, writes=W_)
                kb.tt("dve", den, lre, lre, ALU.mult, reads=R_, writes=W_)
                kb.tt("dve", tC, lim, lim, ALU.mult, reads=R_, writes=W_)
                kb.tt("dve", den, den, tC, ALU.add, reads=R_, writes=W_)
                kb.recip(den, den, reads=R_, writes=W_)
                kb.tt("dve", tC, lbr, lre, ALU.mult, reads=R_, writes=W_)
                kb.tt("dve", tD, lbi, lim, ALU.mult, reads=R_, writes=W_)
                kb.tt("dve", tC, tC, tD, ALU.add, reads=R_, writes=W_)
                kb.tt("dve", cfr[d], tC, den, ALU.mult, reads=R_, writes=W_)
                kb.tt("dve", tC, lbi, lre, ALU.mult, reads=R_, writes=W_)
                kb.tt("dve", tD, lbr, lim, ALU.mult, reads=R_, writes=W_)
                kb.tt("dve", tC, tC, tD, ALU.subtract, reads=R_, writes=W_)
                kb.tt("dve", cfi[d], tC, den, ALU.mult, reads=R_, writes=W_)
            bre = v3(kb.Fm(1536), 12)
            bim = v3(kb.Fm(1536), 12)
            kb.dma(bre, v3(I["s5_bre"][l], 12), writes=[bp_], q="sp")
            kb.dma(bim, v3(I["s5_bim"][l], 12), writes=[bp_], q="sp")
            dsel = v3(kb.R(384), 12)
            kb.dma(dsel, v3(I["s5_dsel"][l], 12), writes=[bp_])
            cre = [v3(kb.R(384), 12) for _ in range(2)]
            cimn = [v3(kb.R(384), 12) for _ in range(2)]
            for d in range(2):
                kb.dma(cre[d], v3(I["s5_cre"][l, d], 12), writes=[bp_])
                kb.dma(cimn[d], v3(I["s5_cim"][l, d], 12), writes=[bp_])
                kb.ts("dve", cimn[d], cimn[d], -1.0, None, ALU.mult, reads=[bp_], writes=[bp_])
            bbr = kb.R(128)
            bbi = kb.R(128)
            tq = kb.Fm(128)
            btr = [kb.R(128), kb.R(128)]
            bti = [kb.R(128), kb.R(128)]
            cosT = [kb.Fm(512), kb.Fm(512)]
            sinT = [kb.Fm(512), kb.Fm(512)]
            rhoT = [kb.Fm(512), kb.Fm(512)]
            tc_, ts_ = kb.Fm(256), kb.Fm(256)
            ub = [kb.R(512) for _ in range(2)]
            vre = [kb.Fm(512) for _ in range(2)]
            vim = [kb.Fm(512) for _ in range(2)]
            wre = [kb.Fm(512) for _ in range(2)]
            wim = [kb.Fm(512) for _ in range(2)]
            hre = [kb.R(512) for _ in range(2)]
            him = [kb.R(512) for _ in range(2)]
            t1 = [kb.Fm(512) for _ in range(2)]
            ini = [[kb.Fm(4), kb.Fm(4)] for _ in range(2)]
            go = [kb.Fm(512) for _ in range(2)]
            bb = [[Buf() for _ in range(10)] for _ in range(2)]
            bt = [Buf(), Buf()]
            bini = [Buf(), Buf()]
            bsc = Buf()
            psi = [(1, 2, 3), (4, 5, 6)]
            for it in range(12):
                for d in dirs:
                    R_, W_ = [bp_, bsc, bt[d]], [bsc, bt[d]]
                    kb.ts("dve", tq, bre[:, it, :], cfr[d][:, it:it + 1], None, ALU.mult, reads=R_, writes=W_)
                    kb.stt("dve", tq, bim[:, it, :], cfi[d][:, it:it + 1], tq, ALU.mult, ALU.subtract, reads=R_, writes=W_)
                    kb.ts("dve", bbr, tq, -1.0, None, ALU.mult, reads=R_, writes=W_)
                    kb.ts("dve", tq, bim[:, it, :], cfr[d][:, it:it + 1], None, ALU.mult, reads=R_, writes=W_)
                    kb.stt("dve", bbi, bre[:, it, :], cfi[d][:, it:it + 1], tq, ALU.mult, ALU.add, reads=R_, writes=W_)
                    kb.mm(ps[0][:, 0:128], bbr, ident, True, True, reads=[bsc, cbuf], writes=[psb[0]])
                    kb.mm(ps[0][:, 128:256], bbi, ident, True, True, reads=[bsc, cbuf], writes=[psb[0]])
                    kb.copy("dve", btr[d], ps[0][:, 0:128], reads=[psb[0]], writes=W_)
                    kb.copy("dve", bti[d], ps[0][:, 128:256], reads=[psb[0]], writes=W_)
                    cT, sT = cosT[d], sinT[d]
                    kb.memset("dve", cT[:, 0:1], 1.0, writes=W_)
                    kb.memset("dve", sT[:, 0:1], 0.0, writes=W_)
                    kb.copy("dve", tc_[:, 0:1], c0[d][:, it:it + 1], reads=R_, writes=W_)
                    kb.copy("dve", ts_[:, 0:1], s0[d][:, it:it + 1], reads=R_, writes=W_)
                    m = 1
                    while m < 512:
                        ck, sk = tc_[:, 0:1], ts_[:, 0:1]
                        tmpv = t1[d][:, 0:m]
                        RX, WX = R_ + [bb[d][2]], W_ + [bb[d][2]]
                        kb.ts("dve", tmpv, sT[:, 0:m], sk, None, ALU.mult, reads=RX, writes=WX)
                        kb.stt("dve", cT[:, m:2 * m], cT[:, 0:m], ck, tmpv, ALU.mult, ALU.subtract, reads=RX, writes=WX)
                        kb.ts("dve", tmpv, cT[:, 0:m], sk, None, ALU.mult, reads=RX, writes=WX)
                        kb.stt("dve", sT[:, m:2 * m], sT[:, 0:m], ck, tmpv, ALU.mult, ALU.add, reads=RX, writes=WX)
                        kb.tt("dve", tc_[:, 1:2], ck, ck, ALU.mult, reads=R_, writes=W_)
                        kb.tt("dve", tc_[:, 2:3], sk, sk, ALU.mult, reads=R_, writes=W_)
                        kb.tt("dve", tc_[:, 3:4], ck, sk, ALU.mult, reads=R_, writes=W_)
                        kb.tt("dve", tc_[:, 0:1], tc_[:, 1:2], tc_[:, 2:3], ALU.subtract, reads=R_, writes=W_)
                        kb.ts("dve", ts_[:, 0:1], tc_[:, 3:4], 2.0, None, ALU.mult, reads=R_, writes=W_)
                        m *= 2
                    kb.copy("dve", rhoT[d], rho[d][:, it:it + 1].to_broadcast([128, 512]), reads=R_, writes=W_)
                    kb.memset("dve", ini[d][0][:, 0:2], 0.0, writes=[bini[d]])
                for n in range(NT):
                    for d in dirs:
                        t = n if d == 0 else NT - 1 - n
                        i = d
                        B = bb[d]
                        p1, p2, p3 = psi[d]
                        cT, sT = cosT[d], sinT[d]
                        kb.dma(ub[i], PF[(6 + it // 4) * 128:(7 + it // 4) * 128, t * 512:(t + 1) * 512], reads=[bPF],
                               writes=[B[0]])
                        kb.mm(ps[p1][:], btr[d], ub[i], True, True, reads=[bt[d], B[0]], writes=[psb[p1]])
                        kb.mm(ps[p2][:], bti[d], ub[i], True, True, reads=[bt[d], B[0]], writes=[psb[p2]])
                        pre = ps[p1][:] if d == 0 else ps[p1][:, ::-1]
                        pim = ps[p2][:] if d == 0 else ps[p2][:, ::-1]
                        kb.tt("dve", vre[i], pre, cT, ALU.mult, reads=[psb[p1], bt[d]], writes=[B[1]])
                        kb.tt("dve", t1[i], pim, sT, ALU.mult, reads=[psb[p2], bt[d]], writes=[B[2]])
                        kb.tt("pool", vre[i], vre[i], t1[i], ALU.add, reads=[B[1], B[2]], writes=[B[1]])
                        kb.tt("dve", vim[i], pim, cT, ALU.mult, reads=[psb[p2], bt[d]], writes=[B[3]])
                        kb.tt("dve", t1[i], pre, sT, ALU.mult, reads=[psb[p1], bt[d], B[1]], writes=[B[2]])
                        kb.tt("pool", vim[i], vim[i], t1[i], ALU.subtract, reads=[B[3], B[2]], writes=[B[3]])
                        kb.scan(wre[i], rhoT[d], vre[i], ini[d][0][:, 0:1], reads=[B[1], bt[d], bini[d]], writes=[B[4]])
                        kb.scan(wim[i], rhoT[d], vim[i], ini[d][0][:, 1:2], reads=[B[3], bt[d], bini[d]], writes=[B[5]])
                        kb.ts("dve", ini[d][1][:, 0:1], wim[i][:, 511:512], S9[d][:, it:it + 1], None, ALU.mult,
                              reads=[B[5], bp_, bini[d]], writes=[bini[d]])
                        kb.ts("dve", ini[d][1][:, 1:2], wre[i][:, 511:512], S9[d][:, it:it + 1], None, ALU.mult,
                              reads=[B[4], bp_, bini[d]], writes=[bini[d]])
                        kb.stt("dve", ini[d][0][:, 0:1], wre[i][:, 511:512], C9[d][:, it:it + 1], ini[d][1][:, 0:1], ALU.mult,
                               ALU.subtract, reads=[B[4], bini[d]], writes=[bini[d]])
                        kb.stt("dve", ini[d][0][:, 1:2], wim[i][:, 511:512], C9[d][:, it:it + 1], ini[d][1][:, 1:2], ALU.mult,
                               ALU.add, reads=[B[5], bini[d]], writes=[bini[d]])
                        kb.tt("pool", hre[i], wre[i], cT, ALU.mult, reads=[B[4], bt[d]], writes=[B[6]])
                        kb.tt("dve", t1[i], wim[i], sT, ALU.mult, reads=[B[5], bt[d], B[2]], writes=[B[2]])
                        kb.tt("pool", hre[i], hre[i], t1[i], ALU.subtract, reads=[B[6], B[2]], writes=[B[6]])
                        kb.tt("pool", him[i], wre[i], sT, ALU.mult, reads=[B[4], bt[d]], writes=[B[7]])
                        kb.tt("dve", t1[i], wim[i], cT, ALU.mult, reads=[B[5], bt[d], B[6]], writes=[B[2]])
                        kb.tt("pool", him[i], him[i], t1[i], ALU.add, reads=[B[7], B[2]], writes=[B[7]])
                        kb.mm(ps[p3][0:32, :], cre[d][:, it, :], hre[i], True, False, reads=[bp_, B[6]], writes=[psb[p3]])
                        kb.mm(ps[p3][0:32, :], cimn[d][:, it, :], him[i], False, d == 1, reads=[bp_, B[7]], writes=[psb[p3]])
                        if d == 0:
                            kb.mm(ps[p3][0:32, :], dsel[:, it, :], ub[i], False, True, reads=[bp_, B[0]], writes=[psb[p3]])
                            kb.copy("act", go[i][0:32, :], ps[p3][0:32, :], reads=[psb[p3]], writes=[B[8]])
                            kb.dma(Y5A[it * 32:(it + 1) * 32, t * 512:(t + 1) * 512], go[i][0:32, :], reads=[B[8]],
                                   writes=[bY5A], q="sp")
                        else:
                            kb.copy("act", go[i][0:32, :], ps[p3][0:32, ::-1], reads=[psb[p3]], writes=[B[8]])
                            kb.dma(Y5B[it * 32:(it + 1) * 32, t * 512:(t + 1) * 512], go[i][0:32, :], reads=[B[8]],
                                   writes=[bY5B], q="sp")
            kb.new_stage()
            ca = [kb.Fm(512), kb.Fm(512)]
            cb_ = [kb.Fm(512), kb.Fm(512)]
            bc = [Buf(), Buf()]
            n = 0
            for r in range(3):
                for t in range(NT):
                    i = n % 2
                    ts0 = slice(t * 512, (t + 1) * 512)
                    kb.dma(ca[i], Y5A[r * 128:(r + 1) * 128, ts0], reads=[bY5A], writes=[bc[i]], q="sp")
                    kb.dma(cb_[i], Y5B[r * 128:(r + 1) * 128, ts0], reads=[bY5B], writes=[bc[i]], q="sp")
                    kb.tt("dve", ca[i], ca[i], cb_[i], ALU.add, reads=[bc[i]], writes=[bc[i]])
                    kb.act(ca[i], ca[i], AF.Gelu, reads=[bc[i]], writes=[bc[i]])
                    kb.dma(Y5[r * 128:(r + 1) * 128, ts0], ca[i], reads=[bc[i]], writes=[bY5], q="sp")
                    n += 1

        def merge_stage(l):
            kb.new_stage()
            wbs = v3(kb.R(4096), 4)
            wbr = v3(kb.R(4096), 4)
            wb5 = v3(kb.R(3072), 3)
            wv = v3(kb.R(1152), 3)
            wg_ = v3(kb.R(1152), 3)
            wo = v3(kb.R(8192), 8)
            bw = Buf()
            kb.dma(wbs, v3(I["wbr_ssd"][l], 4), writes=[bw])
            kb.dma(wbr, v3(I["wbr_ret"][l], 4), writes=[bw])
            kb.dma(wb5, v3(I["wbr_s5"][l], 3), writes=[bw])
            kb.dma(wv, v3(I["glu_wv"][l], 3), writes=[bw])
            kb.dma(wg_, v3(I["glu_wg"][l], 3), writes=[bw])
            kb.dma(wo, v3(I["wout"][l], 8), writes=[bw])
            ys = v3(kb.R(2048), 4)
            yr = v3(kb.R(2048), 4)
            y5 = v3(kb.R(1536), 3)
            y5g = v3(kb.R(1536), 3)
            mixed = v3(kb.R(4096), 8)
            sgt = kb.Fm(512)
            gate = [kb.Fm(512) for _ in range(2)]
            tmp = [kb.Fm(512) for _ in range(2)]
            xr = [kb.Fm(512) for _ in range(2)]
            xo = [kb.Fm(512) for _ in range(2)]
            bi, bg5, bmx = Buf(), Buf(), [Buf() for _ in range(8)]
            bsg = Buf()
            bgate = [Buf(), Buf()]
            btmp = [Buf(), Buf()]
            bxr = [Buf(), Buf()]
            bxo = [Buf(), Buf()]
            for t in range(NT):
                ts0 = slice(t * 512, (t + 1) * 512)
                for f in range(4):
                    kb.dma(ys[:, f, :], YTS[f * 128:(f + 1) * 128, ts0], reads=[bYTS], writes=[bi])
                    kb.dma(yr[:, f, :], YTR[f * 128:(f + 1) * 128, ts0], reads=[bYTR], writes=[bi])
                for f in range(3):
                    kb.dma(y5[:, f, :], Y5[f * 128:(f + 1) * 128, ts0], reads=[bY5], writes=[bi])
                for f in range(3):
                    for k in range(3):
                        kb.mm(ps[0][:], wv[:, k, f * 128:(f + 1) * 128], y5[:, k, :], k == 0, k == 2, reads=[bw, bi],
                              writes=[psb[0]])
                    for k in range(3):
                        kb.mm(ps[1][:], wg_[:, k, f * 128:(f + 1) * 128], y5[:, k, :], k == 0, k == 2, reads=[bw, bi],
                              writes=[psb[1]])
                    kb.act(sgt, ps[1][:], AF.Sigmoid, reads=[psb[1]], writes=[bsg])
                    kb.tt("dve", y5g[:, f, :], ps[0][:], sgt, ALU.mult, reads=[psb[0], bsg], writes=[bg5])
                n = 0
                for i in range(8):
                    for br_, (w_, y_, nk, rb) in enumerate(((wbs, ys, 4, bi), (y5g and wb5, y5g, 3, bg5), (wbr, yr, 4, bi))):
                        pp, bp = ps[2 + n % 2], psb[2 + n % 2]
                        for k in range(nk):
                            kb.mm(pp[:], w_[:, k, i * 128:(i + 1) * 128], y_[:, k, :], k == 0, k == nk - 1, reads=[bw, rb],
                                  writes=[bp])
                        gi = 9 + br_ * 8 + i
                        kb.dma(gate[n % 2], PF[gi * 128:(gi + 1) * 128, ts0], reads=[bPF], writes=[bgate[n % 2]], q="sp")
                        if br_ == 0:
                            kb.tt("dve", tmp[i % 2], pp[:], gate[n % 2], ALU.mult, reads=[bp, bgate[n % 2]],
                                  writes=[btmp[i % 2]])
                        elif br_ == 1:
                            kb.tt("dve", gate[n % 2], pp[:], gate[n % 2], ALU.mult, reads=[bp, bgate[n % 2]],
                                  writes=[bgate[n % 2]])
                            kb.tt("pool", tmp[i % 2], tmp[i % 2], gate[n % 2], ALU.add, reads=[bgate[n % 2], btmp[i % 2]],
                                  writes=[btmp[i % 2]])
                        else:
                            kb.tt("dve", gate[n % 2], pp[:], gate[n % 2], ALU.mult, reads=[bp, bgate[n % 2]],
                                  writes=[bgate[n % 2]])
                            kb.tt("dve", mixed[:, i, :], tmp[i % 2], gate[n % 2], ALU.add,
                                  reads=[bgate[n % 2], btmp[i % 2]], writes=[bmx[i]])
                        n += 1
                for i in range(8):
                    pp, bp = ps[4 + i % 2], psb[4 + i % 2]
                    for k in range(8):
                        kb.mm(pp[:], wo[:, k, i * 128:(i + 1) * 128], mixed[:, k, :], k == 0, k == 7, reads=[bw, bmx[k]],
                              writes=[bp])
                    kb.dma(xr[i % 2], X[i * 128:(i + 1) * 128, ts0], reads=[xb(i, t)], writes=[bxr[i % 2]], q="sp")
                    kb.tt("dve", xo[i % 2], pp[:], xr[i % 2], ALU.add, reads=[bp, bxr[i % 2]], writes=[bxo[i % 2]])
                    kb.dma(X[i * 128:(i + 1) * 128, ts0], xo[i % 2], reads=[bxo[i % 2]], writes=[xb(i, t)], q="sp")

        def final_stage():
            for t in range(NT):
                kb.new_stage()
                xn, bn, xf, bxk = load_norm(X, t, I["g_final"], "z")
                o = v3(kb.Fm(4096), 8)
                bo = Buf()
                for k in range(8):
                    kb.copy("dve" if k % 2 else "act", o[:, k, :], xn[:, k, :].bitcast(F32), reads=[bn], writes=[bo])
                    kb.dma(outT[k * 128:(k + 1) * 128, t * 512:(t + 1) * 512], o[:, k, :], reads=[bo], q="sp")

        kb.new_stage()
        cpb = [kb.Fm(2048), kb.Fm(2048)]
        bcp = [Buf(), Buf()]
        n = 0
        for k in range(8):
            for c0_ in range(0, S, 2048):
                w = min(2048, S - c0_)
                kb.dma(cpb[n % 2][:, 0:w], I["xT"][k * 128:(k + 1) * 128, c0_:c0_ + w], writes=[bcp[n % 2]], q="sp")
                wr = [xb(k, tt) for tt in range(c0_ // 512, (c0_ + w) // 512)]
                kb.dma(X[k * 128:(k + 1) * 128, c0_:c0_ + w], cpb[n % 2][:, 0:w], reads=[bcp[n % 2]], writes=wr, q="sp")
                n += 1
        def on(nm):
            return STAGES is None or nm in STAGES
        for l in range(L):
            if on("ffn1"):
                ffn_stage(l, "g_ffn1", I["wg1"], I["wu1"], I["wd1"])
            if on("inproj"):
                inproj_stage(l)
            if pair:
                halo_exchange()
                ssd_prep(l)
                ret_prep(l)
                ssd_pass(l, 0)
                ret_pass(l, 0)
                s5_stage(l, (0,))
                state_exchange()
                ssd_pass(l, 1)
                ret_pass(l, 1)
                s5_stage(l, (1,))
            else:
                if on("ssd") or on("ssdprep"):
                    ssd_prep(l)
                if on("ssd") or on("ssd0"):
                    ssd_pass(l, 0)
                if on("ssd") or on("ssd1"):
                    ssd_pass(l, 1)
                if on("ret"):
                    ret_prep(l)
                    ret_pass(l, 0)
                    ret_pass(l, 1)
                if on("s5"):
                    s5_stage(l)
            if on("merge"):
                merge_stage(l)
            if on("ffn2"):
                ffn_stage(l, "g_ffn2", I["wg2"], I["wu2"], I["wd2"])
        final_stage()
        P.emit()
    return nc


def _tile_cols(w, nt):
    K, N = w.shape
    return np.ascontiguousarray(w.reshape(K // 128, 128, nt, 128).transpose(2, 1, 0, 3).reshape(nt, 128, (K // 128) * 128))


def _tile_rows(w):
    K, N = w.shape
    return np.ascontiguousarray(w.reshape(K // 128, 128, N).transpose(1, 0, 2).reshape(128, (K // 128) * N))


def _consts(S):
    c = {}
    idx = np.arange(128)
    k, x = idx[:, None], idx[None, :]
    c["c_ident"] = np.eye(128, dtype=np.float32)
    c["c_tri"] = np.concatenate([(k <= x), -1.0 * (k < x), (k > x), (k < x)], axis=1).astype(np.float32)
    NEG = -30000.0
    mf = np.where(x < k, NEG, 0.0)
    mb = np.where(x > k, NEG, 0.0)
    c["c_maskneg"] = np.concatenate([np.tile(mf, (1, 4)), np.tile(mb, (1, 4))], axis=1).astype(np.float32)
    ie = np.zeros((16, 16, 128), np.float32)
    for j in range(16):
        ie[j, j, :] = 1.0
    c["c_iexp"] = ie.reshape(16, 2048)
    c["c_negiexp"] = (-ie).reshape(16, 2048)
    lg = np.log1p(-np.exp2(-5.0 - np.arange(8, dtype=np.float32))).astype(np.float32)
    s_, l_ = idx[:, None].astype(np.float32), idx[None, :].astype(np.float32)
    rm = np.zeros((128, 2, 8, 128), np.float32)
    for h in range(8):
        rm[:, 0, h, :] = np.where(l_ >= s_, np.exp(lg[h] * np.where(l_ >= s_, l_ - s_, 0.0)), 0.0)
        rm[:, 1, h, :] = np.where(s_ > l_, np.exp(lg[h] * np.where(s_ > l_, s_ - l_, 0.0)), 0.0)
    c["c_retmask"] = rm.reshape(128, 2048)
    rd = np.zeros((128, 40), np.float32)
    t = idx.astype(np.float32)[:, None]
    rd[:, 0:8] = np.exp(lg[None, :] * (127.0 - t))
    rd[:, 8:16] = np.exp(lg[None, :] * t)
    rd[:, 16:24] = np.exp(lg[None, :] * (t + 1.0))
    rd[:, 24:32] = np.exp(lg[None, :] * (128.0 - t))
    rd[:, 32:40] = np.exp(lg[None, :] * 128.0)
    c["c_retdec"] = rd
    pos = np.arange(S, dtype=np.float32)
    inv = (10000.0 ** (-np.arange(0, 64, 2, dtype=np.float32) / 64)).astype(np.float32)
    ang = pos[:, None] * inv[None, :]
    c["c_rope"] = np.concatenate([np.cos(ang), np.sin(ang)], axis=1).astype(np.float32)
    b = np.zeros((128, 128), np.float32)
    b[:64, :64] = 1
    b[64:, 64:] = 1
    c["c_blk64"] = b
    return c


def _prep_weights(inp, L):
    f = lambda a: np.ascontiguousarray(np.asarray(a, dtype=np.float32))
    W = {}
    gt = lambda g: np.ascontiguousarray(f(g).reshape(-1, 8, 128).transpose(0, 2, 1))
    W["g_ffn1"], W["g_mix"], W["g_ffn2"] = gt(inp["ffn1_norm"]), gt(inp["mix_norm"]), gt(inp["ffn2_norm"])
    W["g_final"] = gt(inp["final_norm"])[0]
    for n_, a in (("1", "ffn1"), ("2", "ffn2")):
        W["wg" + n_] = np.stack([_tile_cols(f(inp[a + "_w_gate"][l]), NFC) for l in range(L)])
        W["wu" + n_] = np.stack([_tile_cols(f(inp[a + "_w_up"][l]), NFC) for l in range(L)])
        wd = f(inp[a + "_w_down"])
        W["wd" + n_] = np.stack([np.stack([_tile_rows(wd[l][:, i * 128:(i + 1) * 128]) for i in range(8)]) for l in range(L)])
    win = f(inp["w_in"])
    sz = (512, 768, 16, 384, 512, 512, 512, 512, 3072)
    o = np.cumsum((0,) + sz)
    z, xbc, dt, u, q, k, v, g, gates = [win[:, :, o[i]:o[i + 1]] for i in range(9)]
    fm = np.concatenate([xbc, u, gates], axis=2)
    W["win_fm"] = np.stack([_tile_cols(fm[l], NFM) for l in range(L)])
    tm = np.concatenate([z, q, k, v, g, dt], axis=2)
    W["win_tm"] = np.stack([_tile_rows(tm[l]) for l in range(L)])
    W["bgate"] = np.ascontiguousarray(f(inp["b_gate"]).reshape(L, 24, 128).transpose(0, 2, 1))
    cw = f(inp["ssd_conv_w"])
    W["conv_w"] = np.ascontiguousarray(cw.reshape(L, 5, 6, 128).transpose(0, 3, 2, 1).reshape(L, 128, 30))
    W["conv_b"] = np.ascontiguousarray(f(inp["ssd_conv_b"]).reshape(L, 6, 128).transpose(0, 2, 1))
    rep = lambda a: np.ascontiguousarray(np.broadcast_to(a[:, None, :], (L, 128, a.shape[-1])))
    W["dtb_rep"] = rep(f(inp["ssd_dt_bias"]).reshape(L, 16))
    W["alog_rep"] = rep(f(inp["ssd_a_log"]).reshape(L, 16))
    W["dskip_rep"] = rep(f(inp["ssd_d"]))
    W["ssdnorm_rep"] = rep(f(inp["ssd_norm"]))
    W["retnorm_rep"] = rep(f(inp["ret_norm"]))
    W["wbr_ssd"] = np.stack([_tile_rows(f(inp["w_br_ssd"][l])) for l in range(L)])
    W["wbr_ret"] = np.stack([_tile_rows(f(inp["w_br_ret"][l])) for l in range(L)])
    W["wbr_s5"] = np.stack([_tile_rows(f(inp["w_br_s5"][l])) for l in range(L)])
    W["glu_wv"] = np.stack([_tile_rows(f(inp["s5_glu_wv"][l])) for l in range(L)])
    W["glu_wg"] = np.stack([_tile_rows(f(inp["s5_glu_wg"][l])) for l in range(L)])
    W["wout"] = np.stack([_tile_rows(f(inp["w_out"][l])) for l in range(L)])
    st = lambda a: np.ascontiguousarray(a.reshape(L, 2, 12, 128).transpose(0, 1, 3, 2))
    W["s5_lre"], W["s5_lim"] = st(f(inp["s5_lam_re"])), st(f(inp["s5_lam_im"]))
    ls = np.broadcast_to(f(inp["s5_log_step"])[..., None], (L, 2, 24, 64))
    W["s5_lstep"] = st(np.ascontiguousarray(ls))

    def bpad(b):
        out = np.zeros((L, 128, 12, 128), np.float32)
        for g in range(24):
            it, g2, g8 = g // 2, g % 2, g % 8
            out[:, g2 * 64:(g2 + 1) * 64, it, g8 * 16:(g8 + 1) * 16] = b[:, g]
        return out.reshape(L, 128, 12 * 128)
    W["s5_bre"], W["s5_bim"] = bpad(f(inp["s5_b_re"])), bpad(f(inp["s5_b_im"]))

    def cpad(c):
        out = np.zeros((L, 2, 128, 12, 32), np.float32)
        for g in range(24):
            it, g2 = g // 2, g % 2
            out[:, :, g2 * 64:(g2 + 1) * 64, it, g2 * 16:(g2 + 1) * 16] = c[:, :, g].transpose(0, 1, 3, 2)
        return out.reshape(L, 2, 128, 12 * 32)
    W["s5_cre"], W["s5_cim"] = cpad(f(inp["s5_c_re"])), cpad(f(inp["s5_c_im"]))
    dd = f(inp["s5_d"])
    ds = np.zeros((L, 128, 12, 32), np.float32)
    for g in range(24):
        it, g2, g8 = g // 2, g % 2, g % 8
        for h in range(16):
            ds[:, g8 * 16 + h, it, g2 * 16 + h] = dd[:, g, h]
    W["s5_dsel"] = ds.reshape(L, 128, 12 * 32)
    return W


_CACHE = {}
STAGES = None


def _swap_dirs(W):
    V = dict(W)
    sw16 = lambda a: np.ascontiguousarray(np.concatenate([a[..., 8:16], a[..., 0:8]], axis=-1))
    V["dtb_rep"] = sw16(W["dtb_rep"])
    V["alog_rep"] = sw16(W["alog_rep"])
    wt = W["win_tm"].reshape(W["win_tm"].shape[0], 128, 8, NTM).copy()
    wt[..., 2560:2576] = sw16(wt[..., 2560:2576])
    V["win_tm"] = wt.reshape(W["win_tm"].shape)
    cw = W["conv_w"].reshape(-1, 128, 6, 5)
    V["conv_w"] = np.ascontiguousarray(cw[..., ::-1]).reshape(W["conv_w"].shape)
    for k in ("s5_lre", "s5_lim", "s5_lstep", "s5_cre", "s5_cim"):
        V[k] = np.ascontiguousarray(W[k][:, ::-1])
    return V


def run_model(inp, L, pair=True, dbg=()):
    x = np.asarray(inp["x"], dtype=np.float32)
    nseq, Sfull = x.shape[0], x.shape[1]
    W = _prep_weights(inp, L)
    if not pair:
        S = Sfull
        ncore = 8 if nseq == 4 else nseq
        key = (S, L, tuple(dbg), 0)
        if key not in _CACHE:
            _CACHE[key] = build_program(S, L, dbg)
        W.update(_consts(S))
        maps = []
        for c in range(ncore):
            m = dict(W)
            m["xT"] = np.ascontiguousarray(x[c % nseq].T)
            maps.append(m)
        res = run_bass_kernel_spmd(_CACHE[key], maps, core_ids=list(range(ncore)))
        out = np.stack([np.ascontiguousarray(res.results[c]["outT"].T) for c in range(nseq)])
        return out, res
    S = Sfull // 2
    ncore = 2 * nseq
    key = (S, L, tuple(dbg), ncore)
    if key not in _CACHE:
        _CACHE[key] = build_program(S, L, dbg, pair=ncore)
    C0 = _consts(S)
    Wn = dict(W)
    Wn.update(C0)
    Wr = _swap_dirs(W)
    Wr.update(C0)
    idx = np.arange(128)
    s_, l_ = idx[:, None].astype(np.float32), idx[None, :].astype(np.float32)
    lg = np.log1p(-np.exp2(-5.0 - np.arange(8, dtype=np.float32))).astype(np.float32)
    rm = np.zeros((128, 2, 8, 128), np.float32)
    for h in range(8):
        rm[:, 0, h, :] = np.where(l_ > s_, np.exp(lg[h] * np.where(l_ > s_, l_ - s_, 0.0)), 0.0)
        rm[:, 1, h, :] = np.where(s_ >= l_, np.exp(lg[h] * np.where(s_ >= l_, s_ - l_, 0.0)), 0.0)
    Wr["c_retmask"] = rm.reshape(128, 2048)
    inv = (10000.0 ** (-np.arange(0, 64, 2, dtype=np.float32) / 64)).astype(np.float32)

    def rope(pos):
        ang = pos.astype(np.float32)[:, None] * inv[None, :]
        return np.concatenate([np.cos(ang), np.sin(ang)], axis=1).astype(np.float32)
    Wn["c_rope"] = rope(np.arange(S))
    Wr["c_rope"] = rope(Sfull - 1 - np.arange(S))
    Wn["pairsel"] = np.ascontiguousarray(np.broadcast_to(np.array([0.0, 1.0], np.float32), (128, 2)))
    Wr["pairsel"] = np.ascontiguousarray(np.broadcast_to(np.array([1.0, 0.0], np.float32), (128, 2)))
    maps = []
    for c in range(ncore):
        b, hf = c // 2, c % 2
        m = dict(Wn if hf == 0 else Wr)
        xs = x[b, :S] if hf == 0 else x[b, S:][::-1]
        m["xT"] = np.ascontiguousarray(xs.T)
        maps.append(m)
    res = run_bass_kernel_spmd(_CACHE[key], maps, core_ids=list(range(ncore)))
    out = np.empty((nseq, Sfull, D), np.float32)
    for c in range(ncore):
        b, hf = c // 2, c % 2
        o = res.results[c]["outT"].T
        if hf == 0:
            out[b, :S] = o
        else:
            out[b, S:] = o[::-1]
    return out, res


def kernel(**inputs):
    out, _ = run_model(inputs, 2, pair=False)
    return out.astype(np.float32)
```

```python
import math
from contextlib import ExitStack
import numpy as np
import concourse.bass as bass
import concourse.mybir as mybir
from concourse.bass_utils import run_bass_kernel_spmd

F32 = mybir.dt.float32
F32R = mybir.dt.float32r
ALU = mybir.AluOpType
AF = mybir.ActivationFunctionType
AX = mybir.AxisListType

ENGS = ("pe", "dve", "act", "pool", "sp")
N_DMA_SEMS = 12
D = 1024
DFF = 2816
NFC = 22
EPS = 1e-6
NTM = 2576
NFM = 33


class Buf:
    __slots__ = ("name", "w", "r")

    def __init__(self, name=""):
        self.name = name
        self.w = None
        self.r = []


class Prog:
    def __init__(self, nc):
        self.nc = nc
        self.ops = []
        self.by_eng = {e: [] for e in ENGS}
        self.fence_deps = set()
        self.fence_pending = {e: False for e in ENGS}
        self.since_fence = []
        self.trace = None

    def op(self, eng, fn, reads=(), writes=(), dma=False):
        oid = len(self.ops)
        deps = set()
        for b in reads:
            if b.w is not None:
                deps.add(b.w)
        for b in writes:
            if b.w is not None:
                deps.add(b.w)
            deps.update(b.r)
        for b in reads:
            if not dma:
                b.r = [r for r in b.r if self.ops[r]["dma"] or self.ops[r]["eng"] != eng]
            b.r.append(oid)
        for b in writes:
            b.w = oid
            b.r = []
        if self.fence_pending[eng]:
            deps.update(self.fence_deps)
            self.fence_pending[eng] = False
        deps.discard(oid)
        import sys as _s
        fr = _s._getframe(1)
        ln = []
        while fr is not None and len(ln) < 4:
            ln.append(fr.f_lineno)
            fr = fr.f_back
        self.ops.append(dict(eng=eng, fn=fn, deps=deps, dma=dma, id=oid, ln=ln))
        self.by_eng[eng].append(oid)
        self.since_fence.append(oid)
        return oid

    def fence(self):
        last = {}
        deps = set()
        for oid in self.since_fence:
            o = self.ops[oid]
            if o["dma"]:
                deps.add(oid)
            else:
                last[o["eng"]] = oid
        deps.update(last.values())
        for e in ENGS:
            if self.fence_pending[e]:
                deps.update(self.fence_deps)
                break
        self.fence_deps = deps
        self.fence_pending = {e: True for e in ENGS}
        self.since_fence = []

    def emit(self):
        nc = self.nc
        ops = self.ops
        signaled = set()
        for o in ops:
            for d in o["deps"]:
                if o["eng"] == "pe" and ops[d]["eng"] == "pe" and not ops[d]["dma"] and not o["dma"]:
                    continue
                signaled.add(d)
        eng_cnt = {e: 0 for e in ENGS}
        dma_rr = {e: 0 for e in ENGS}
        dma_cnt = {e: [0] * N_DMA_SEMS for e in ENGS}
        tokens = {}
        dma_prev = {}
        for o in ops:
            e = o["eng"]
            if o["dma"]:
                i = dma_rr[e] % N_DMA_SEMS
                dma_rr[e] += 1
                prev = dma_cnt[e][i]
                dma_cnt[e][i] += 16
                tokens[o["id"]] = (("dma", e, i), dma_cnt[e][i])
                if prev:
                    dma_prev[o["id"]] = (("dma", e, i), prev)
            elif o["id"] in signaled:
                eng_cnt[e] += 1
                tokens[o["id"]] = (("eng", e), eng_cnt[e])
        final_waits = {e: {} for e in ENGS}
        for o in ops:
            if o["dma"]:
                k, v = tokens[o["id"]]
                final_waits[o["eng"]][k] = max(final_waits[o["eng"]].get(k, 0), v)
        used = sorted(set(k for k, _ in tokens.values()), key=str)
        self.eng_cnt = eng_cnt
        with ExitStack() as st:
            sems = {k: st.enter_context(nc.semaphore("s_" + "_".join(map(str, k)))) for k in used}
            block = st.enter_context(nc.Block())
            handles = {"pe": block.tensor, "dve": block.vector, "act": block.scalar,
                       "pool": block.gpsimd, "sp": block.sync}

            def make(e):
                def body(eng):
                    seen = {}
                    for oid in self.by_eng[e]:
                        o = ops[oid]
                        waits = {}
                        for d in o["deps"]:
                            if ops[d]["eng"] == e and not ops[d]["dma"] and e == "pe" and not o["dma"]:
                                continue
                            k, v = tokens[d]
                            waits[k] = max(waits.get(k, 0), v)
                        if oid in dma_prev:
                            k, v = dma_prev[oid]
                            waits[k] = max(waits.get(k, 0), v)
                        for k, v in waits.items():
                            if seen.get(k, 0) >= v:
                                continue
                            seen[k] = v
                            eng.wait_ge(sems[k], v)
                        try:
                            ins = o["fn"](eng)
                        except BaseException:
                            print("FAILED OP lines", o["ln"], "eng", e)
                            raise
                        if self.trace is not None:
                            try:
                                self.trace[ins.ins.name] = o["ln"]
                            except Exception:
                                pass
                        if oid in tokens:
                            k, v = tokens[oid]
                            ins.then_inc(sems[k], 16 if o["dma"] else 1)
                    for k, v in final_waits[e].items():
                        if seen.get(k, 0) < v:
                            eng.wait_ge(sems[k], v)
                return body

            for e in ENGS:
                if self.by_eng[e]:
                    handles[e](make(e))


class KB:
    def __init__(self, nc, st, nr, nf):
        self.nc = nc
        self.P = Prog(nc)
        self.arr = st.enter_context(nc.sbuf_tensor("arr", [128, nr], F32R))
        self.arf = st.enter_context(nc.sbuf_tensor("arf", [128, nf], F32))
        self.cr = st.enter_context(nc.sbuf_tensor("cr", [128, 2048], F32R))
        self.cf = st.enter_context(nc.sbuf_tensor("cf", [128, 1344], F32))
        self.nr, self.nf = nr, nf
        self.pr = self.pf = 0
        self.ps = [st.enter_context(nc.psum_tensor("ps%d" % i, [128, 512], F32)) for i in range(8)]
        self.psb = [Buf("ps%d" % i) for i in range(8)]
        self.dq = 0

    def new_stage(self):
        self.P.fence()
        self.pr = self.pf = 0

    def R(self, n, shape=None):
        a = self.arr[:, self.pr:self.pr + n]
        self.pr += n
        assert self.pr <= self.nr, ("arr overflow", self.pr)
        return a

    def Fm(self, n):
        a = self.arf[:, self.pf:self.pf + n]
        self.pf += n
        assert self.pf <= self.nf, ("arf overflow", self.pf)
        return a

    def dma(self, out, in_, reads=(), writes=(), q=None):
        if q is None:
            q = "pool"
        return self.P.op(q, lambda e, o=out, i=in_: e.dma_start(out=o, in_=i), reads, writes, dma=True)

    def mm(self, out, lhsT, rhs, start, stop, reads=(), writes=()):
        return self.P.op("pe", lambda e, o=out, l=lhsT, r=rhs, s=start, t=stop: e.matmul(o, l, r, start=s, stop=t),
                         reads, writes)

    def act(self, out, in_, func, reads=(), writes=(), bias=None, scale=None):
        kw = {}
        if bias is not None:
            kw["bias"] = bias
        if scale is not None:
            kw["scale"] = scale
        return self.P.op("act", lambda e, o=out, i=in_, f=func, kw=kw: e.activation(o, i, f, **kw), reads, writes)

    def tt(self, eng, out, in0, in1, op, reads=(), writes=()):
        return self.P.op(eng, lambda e, o=out, a=in0, b=in1, p=op: e.tensor_tensor(o, a, b, p), reads, writes)

    def ts(self, eng, out, in0, s1, s2, op0, op1=None, reads=(), writes=()):
        if op1 is None:
            return self.P.op(eng, lambda e, o=out, a=in0, x=s1, p=op0: e.tensor_scalar(o, a, x, None, p), reads, writes)
        return self.P.op(eng, lambda e, o=out, a=in0, x=s1, y=s2, p=op0, q=op1: e.tensor_scalar(o, a, x, y, p, q),
                         reads, writes)

    def stt(self, eng, out, in0, scalar, in1, op0, op1, reads=(), writes=()):
        return self.P.op(eng, lambda e, o=out, a=in0, s=scalar, b=in1, p=op0, q=op1:
                         e.scalar_tensor_tensor(o, a, s, b, p, q), reads, writes)

    def copy(self, eng, out, in_, reads=(), writes=()):
        if eng == "act":
            return self.act(out, in_, AF.Copy, reads, writes)
        return self.P.op(eng, lambda e, o=out, i=in_: e.tensor_copy(o, i), reads, writes)

    def memset(self, eng, out, val, writes=()):
        return self.P.op(eng, lambda e, o=out, v=val: e.memset(o, v), (), writes)

    def recip(self, out, in_, reads=(), writes=()):
        return self.P.op("dve", lambda e, o=out, i=in_: e.reciprocal(o, i), reads, writes)

    def scan(self, out, d0, d1, init, reads=(), writes=()):
        return self.P.op("dve", lambda e, o=out, a=d0, b=d1, i=init: e.tensor_tensor_scan(o, a, b, i, ALU.mult, ALU.add),
                         reads, writes)


def v3(ap, a):
    return ap.rearrange("p (a b) -> p a b", a=a)


def build_program(S, L, dbg=(), pair=0):
    nc = bass.Bass("TRN2", target_bir_lowering=False)
    NCH = S // 128
    NT = S // 512
    assert S % 512 == 0

    def din(name, shape):
        return nc.dram_tensor(name, list(shape), F32, kind="ExternalInput").ap()

    def dscr(name, shape):
        kind = "ExternalOutput" if name in dbg else "Internal"
        return nc.dram_tensor(name, list(shape), F32, kind=kind).ap()

    I = {}
    I["xT"] = din("xT", [D, S])
    for nm, shp in [("g_ffn1", [L, 128, 8]), ("g_mix", [L, 128, 8]), ("g_ffn2", [L, 128, 8]), ("g_final", [128, 8]),
                    ("wg1", [L, NFC, 128, 1024]), ("wu1", [L, NFC, 128, 1024]), ("wd1", [L, 8, 128, DFF]),
                    ("wg2", [L, NFC, 128, 1024]), ("wu2", [L, NFC, 128, 1024]), ("wd2", [L, 8, 128, DFF]),
                    ("win_fm", [L, NFM, 128, 1024]), ("win_tm", [L, 128, 8 * NTM]), ("bgate", [L, 128, 24]),
                    ("conv_w", [L, 128, 30]), ("conv_b", [L, 128, 6]),
                    ("dtb_rep", [L, 128, 16]), ("alog_rep", [L, 128, 16]), ("dskip_rep", [L, 128, 8]),
                    ("ssdnorm_rep", [L, 128, 512]), ("retnorm_rep", [L, 128, 512]),
                    ("wbr_ssd", [L, 128, 4096]), ("wbr_ret", [L, 128, 4096]), ("wbr_s5", [L, 128, 3072]),
                    ("glu_wv", [L, 128, 1152]), ("glu_wg", [L, 128, 1152]), ("wout", [L, 128, 8192]),
                    ("s5_lre", [L, 2, 128, 12]), ("s5_lim", [L, 2, 128, 12]), ("s5_lstep", [L, 2, 128, 12]),
                    ("s5_bre", [L, 128, 12 * 128]), ("s5_bim", [L, 128, 12 * 128]),
                    ("s5_cre", [L, 2, 128, 12 * 32]), ("s5_cim", [L, 2, 128, 12 * 32]), ("s5_dsel", [L, 128, 12 * 32]),
                    ("c_ident", [128, 128]), ("c_tri", [128, 4 * 128]), ("c_maskneg", [128, 2 * 512]),
                    ("c_negiexp", [16, 16 * 128]), ("c_iexp", [16, 16 * 128]), ("c_retmask", [128, 16 * 128]),
                    ("c_retdec", [128, 2 * 8 + 2 * 8 + 8]), ("c_rope", [S, 64]), ("c_blk64", [128, 128])]:
        I[nm] = din(nm, shp)
    outT = nc.dram_tensor("outT", [D, S], F32, kind="ExternalOutput").ap()
    if pair:
        I["pairsel"] = din("pairsel", [128, 2])
        HS = nc.dram_tensor("HS", [128, 12], F32).ap()
        HR = nc.dram_tensor("HR", [256, 12], F32).ap()
        SS = nc.dram_tensor("SS", [128, 1048], F32).ap()
        SR = nc.dram_tensor("SR", [256, 1048], F32).ap()
        rgroups = [[2 * i, 2 * i + 1] for i in range(pair // 2)]

    X = dscr("X", [D, S])
    PF = dscr("PF", [NFM * 128, S])
    PT = dscr("PT", [S, NTM])
    XS = dscr("XS", [S, 512])
    BTM = dscr("BTM", [S, 128])
    BCT = dscr("BCT", [256, S])
    YF = dscr("YF", [S, 512])
    YTS = dscr("YTS", [512, S])
    QKT = dscr("QKT", [1024, S])
    KTM = dscr("KTM", [S, 512])
    RF = dscr("RF", [S, 512])
    YTR = dscr("YTR", [512, S])
    Y5 = dscr("Y5", [384, S])
    Y5A = dscr("Y5A", [384, S])
    Y5B = dscr("Y5B", [384, S])
    QK2 = dscr("QK2", [S // 128, 64, 2048])

    with ExitStack() as st:
        kb = KB(nc, st, 33280, 14336)
        P = kb.P
        ps, psb = kb.ps, kb.psb

        cbuf = Buf("consts")
        ident = kb.cr[:, 0:128]
        tri = kb.cr[:, 128:640]
        ones = kb.cr[:, 640:768]
        blk64 = kb.cr[:, 768:896]
        iexp = kb.cr[0:16, 896:896 + 0]
        maskneg = kb.cf[:, 0:1024]
        kb.dma(ident, I["c_ident"], writes=[cbuf])
        kb.dma(tri, I["c_tri"], writes=[cbuf])
        kb.dma(blk64, I["c_blk64"], writes=[cbuf])
        kb.dma(maskneg, I["c_maskneg"], writes=[cbuf], q="sp")
        identf = kb.cf[:, 1024:1152]
        kb.dma(identf, I["c_ident"], writes=[cbuf], q="sp")
        onesf = kb.cf[:, 1152:1280]
        kb.memset("dve", onesf, 1.0, writes=[cbuf])
        kb.copy("dve", ones, onesf, reads=[cbuf], writes=[cbuf])

        halo = kb.cf[:, 1280:1292]
        psel = kb.cf[:, 1292:1294]
        bhalo = Buf("halo")
        bSS, bSR = Buf("SS"), Buf("SR")
        if pair:
            kb.dma(psel, I["pairsel"], writes=[cbuf], q="sp")

        def coll(src, dst, reads, writes):
            return P.op("pool", lambda e, a=src, b=dst: e.collective_compute("AllGather", ALU.bypass, rgroups, [a], [b]),
                        reads, writes, dma=True)

        def recv_combine(out, c0, c1, rows, tmp0, tmp1, wbuf):
            kb.dma(tmp0, SR[0:rows, c0:c1], reads=[bSR], writes=[wbuf])
            kb.dma(tmp1, SR[128:128 + rows, c0:c1], reads=[bSR], writes=[wbuf])
            kb.ts("dve", tmp0, tmp0.bitcast(F32), psel[0:rows, 0:1], None, ALU.mult, reads=[wbuf, cbuf], writes=[wbuf])
            kb.stt("dve", out, tmp1.bitcast(F32), psel[0:rows, 1:2], tmp0.bitcast(F32), ALU.mult, ALU.add,
                   reads=[wbuf, cbuf], writes=[wbuf])

        def halo_exchange():
            kb.new_stage()
            hb = kb.Fm(12)
            hr = kb.Fm(24)
            b = Buf()
            bHS, bHR = Buf(), Buf()
            for f in range(6):
                kb.dma(hb[:, 2 * f:2 * f + 2], PF[f * 128:(f + 1) * 128, S - 2:S], reads=[bPF], writes=[b], q="sp")
            kb.dma(HS[:, :], hb, reads=[b], writes=[bHS], q="sp")
            coll(HS[:, :], HR[:, :], [bHS], [bHR])
            kb.dma(hr[:, 0:12], HR[0:128, :], reads=[bHR], writes=[b], q="sp")
            kb.dma(hr[:, 12:24], HR[128:256, :], reads=[bHR], writes=[b], q="sp")
            kb.ts("dve", hr[:, 0:12], hr[:, 0:12], psel[:, 0:1], None, ALU.mult, reads=[b, cbuf], writes=[b])
            kb.stt("dve", halo, hr[:, 12:24], psel[:, 1:2], hr[:, 0:12], ALU.mult, ALU.add, reads=[b, cbuf], writes=[bhalo])

        def state_exchange():
            kb.new_stage()
            coll(SS[:, :], SR[:, :], [bSS], [bSR])

        xbufs = {}

        def xb(k, t):
            key = (k, t)
            if key not in xbufs:
                xbufs[key] = Buf("x%d_%d" % key)
            return xbufs[key]

        def load_norm(src, t, gain_ap, pfx):
            xf = v3(kb.Fm(4096), 8)
            xn = v3(kb.R(4096), 8)
            sq = [kb.R(512), kb.R(512)]
            sqb = [Buf(), Buf()]
            rstd = kb.Fm(512)
            g = kb.Fm(8)
            bx, bn, br, bg = Buf(), Buf(), Buf(), Buf()
            kb.dma(g, gain_ap, writes=[bg], q="sp")
            bxk = [Buf() for _ in range(8)]
            for k in range(8):
                kb.dma(xf[:, k, :], src[k * 128:(k + 1) * 128, t * 512:(t + 1) * 512], reads=[xb(k, t)],
                       writes=[bxk[k]], q="sp")
                kb.act(sq[k % 2], xf[:, k, :], AF.Square, reads=[bxk[k]], writes=[sqb[k % 2]])
                kb.mm(ps[7][:], ones, sq[k % 2], k == 0, k == 7, reads=[sqb[k % 2], cbuf], writes=[psb[7]])
            kb.ts("dve", rstd, ps[7][:], 1.0 / D, EPS, ALU.mult, ALU.add, reads=[psb[7]], writes=[br])
            kb.act(rstd, rstd, AF.Sqrt, reads=[br], writes=[br])
            kb.recip(rstd, rstd, reads=[br], writes=[br])
            for k in range(8):
                kb.stt("dve", xn[:, k, :], xf[:, k, :], g[:, k:k + 1], rstd, ALU.mult, ALU.mult,
                       reads=[bxk[k], br, bg], writes=[bn])
            return xn, bn, xf, bxk

        def ffn_stage(l, gname, wg, wu, wd):
            TT = 1024
            assert S % TT == 0
            HF = NFC // 2
            for t in range(S // TT):
                kb.new_stage()
                c0 = t * TT
                xn = v3(kb.R(8192), 8)
                sq = [kb.R(1024), kb.R(1024)]
                sqb = [Buf(), Buf()]
                rstd = kb.Fm(1024)
                g = kb.Fm(8)
                acc = v3(kb.Fm(8192), 8)
                bacc = [Buf() for _ in range(8)]
                bn, br, bg = Buf(), Buf(), Buf()
                kb.dma(g, I[gname][l], writes=[bg], q="sp")
                for k in range(8):
                    kb.dma(acc[:, k, :], X[k * 128:(k + 1) * 128, c0:c0 + TT], reads=[xb(k, 2 * t), xb(k, 2 * t + 1)],
                           writes=[bacc[k]], q="sp")
                    kb.act(sq[k % 2], acc[:, k, :], AF.Square, reads=[bacc[k]], writes=[sqb[k % 2]])
                    for hh in range(2):
                        kb.mm(ps[6 + hh][:], ones, sq[k % 2][:, hh * 512:(hh + 1) * 512], k == 0, k == 7,
                              reads=[sqb[k % 2], cbuf], writes=[psb[6 + hh]])
                for hh in range(2):
                    kb.ts("dve", rstd[:, hh * 512:(hh + 1) * 512], ps[6 + hh][:], 1.0 / D, EPS, ALU.mult, ALU.add,
                          reads=[psb[6 + hh]], writes=[br])
                kb.act(rstd, rstd, AF.Sqrt, reads=[br], writes=[br])
                kb.recip(rstd, rstd, reads=[br], writes=[br])
                for k in range(8):
                    kb.stt("dve", xn[:, k, :], acc[:, k, :], g[:, k:k + 1], rstd, ALU.mult, ALU.mult,
                           reads=[bacc[k], br, bg], writes=[bn])
                actt = v3(kb.R(HF * 1024), HF)
                bact = [Buf() for _ in range(HF)]
                wgb = [kb.R(1024) for _ in range(2)]
                wub = [kb.R(1024) for _ in range(2)]
                bwg = [Buf(), Buf()]
                bwu = [Buf(), Buf()]
                sg = [kb.Fm(512), kb.Fm(512)]
                bsg = [Buf(), Buf()]
                wdb = [kb.R(HF * 128) for _ in range(2)]
                bwd = [Buf(), Buf()]
                xr = [kb.Fm(1024), kb.Fm(1024)]
                bxr = [Buf(), Buf()]
                xo = [kb.Fm(512), kb.Fm(512)]
                bxo = [Buf(), Buf()]
                n = 0
                m = 0
                for half in range(2):
                    f0 = half * HF
                    for jj in range(HF):
                        j = f0 + jj
                        kb.dma(wgb[j % 2], wg[l, j], writes=[bwg[j % 2]])
                        kb.dma(wub[j % 2], wu[l, j], writes=[bwu[j % 2]])
                        for hh in range(2):
                            pg, pu = ps[n % 2], ps[2 + n % 2]
                            bpg, bpu = psb[n % 2], psb[2 + n % 2]
                            for k in range(8):
                                kb.mm(pg[:], wgb[j % 2][:, k * 128:(k + 1) * 128], xn[:, k, hh * 512:(hh + 1) * 512],
                                      k == 0, k == 7, reads=[bwg[j % 2], bn], writes=[bpg])
                            for k in range(8):
                                kb.mm(pu[:], wub[j % 2][:, k * 128:(k + 1) * 128], xn[:, k, hh * 512:(hh + 1) * 512],
                                      k == 0, k == 7, reads=[bwu[j % 2], bn], writes=[bpu])
                            kb.act(sg[n % 2], pg[:], AF.Silu, reads=[bpg], writes=[bsg[n % 2]])
                            kb.tt("dve", actt[:, jj, hh * 512:(hh + 1) * 512], sg[n % 2], pu[:], ALU.mult,
                                  reads=[bsg[n % 2], bpu], writes=[bact[jj]])
                            n += 1
                    for i in range(8):
                        kb.dma(wdb[i % 2], wd[l, i][:, f0 * 128:(f0 + HF) * 128], writes=[bwd[i % 2]])
                        if half == 1:
                            kb.dma(xr[i % 2], X[i * 128:(i + 1) * 128, c0:c0 + TT], reads=[xb(i, 2 * t), xb(i, 2 * t + 1)],
                                   writes=[bxr[i % 2]], q="sp")
                        for hh in range(2):
                            po, bpo = ps[4 + m % 2], psb[4 + m % 2]
                            for jj in range(HF):
                                kb.mm(po[:], wdb[i % 2][:, jj * 128:(jj + 1) * 128], actt[:, jj, hh * 512:(hh + 1) * 512],
                                      jj == 0, jj == HF - 1, reads=[bwd[i % 2], bact[jj]], writes=[bpo])
                            av = acc[:, i, hh * 512:(hh + 1) * 512]
                            if half == 0:
                                kb.copy("act", av, po[:], reads=[bpo], writes=[bacc[i]])
                            else:
                                kb.tt("dve", av, av, po[:], ALU.add, reads=[bpo, bacc[i]], writes=[bacc[i]])
                                kb.stt("dve", xo[m % 2], av, 0.5, xr[i % 2][:, hh * 512:(hh + 1) * 512], ALU.mult, ALU.add,
                                       reads=[bacc[i], bxr[i % 2]], writes=[bxo[m % 2]])
                                kb.dma(X[i * 128:(i + 1) * 128, c0 + hh * 512:c0 + (hh + 1) * 512], xo[m % 2],
                                       reads=[bxo[m % 2]], writes=[xb(i, 2 * t + hh)], q="sp")
                            m += 1

        bPF = Buf("PF")
        bPT = Buf("PT")

        def inproj_stage(l):
            TT = 1024
            assert S % TT == 0
            for t in range(S // TT):
                kb.new_stage()
                c0t = t * TT
                xn = v3(kb.R(8192), 8)
                sq = [kb.R(1024), kb.R(1024)]
                sqb = [Buf(), Buf()]
                rstd = kb.Fm(1024)
                g = kb.Fm(8)
                xf = v3(kb.Fm(8192), 8)
                bxk = [Buf() for _ in range(8)]
                bn, br, bg = Buf(), Buf(), Buf()
                kb.dma(g, I["g_mix"][l], writes=[bg], q="sp")
                for k in range(8):
                    kb.dma(xf[:, k, :], X[k * 128:(k + 1) * 128, c0t:c0t + TT], reads=[xb(k, 2 * t), xb(k, 2 * t + 1)],
                           writes=[bxk[k]], q="sp")
                    kb.act(sq[k % 2], xf[:, k, :], AF.Square, reads=[bxk[k]], writes=[sqb[k % 2]])
                    for hh in range(2):
                        kb.mm(ps[6 + hh][:], ones, sq[k % 2][:, hh * 512:(hh + 1) * 512], k == 0, k == 7,
                              reads=[sqb[k % 2], cbuf], writes=[psb[6 + hh]])
                for hh in range(2):
                    kb.ts("dve", rstd[:, hh * 512:(hh + 1) * 512], ps[6 + hh][:], 1.0 / D, EPS, ALU.mult, ALU.add,
                          reads=[psb[6 + hh]], writes=[br])
                kb.act(rstd, rstd, AF.Sqrt, reads=[br], writes=[br])
                kb.recip(rstd, rstd, reads=[br], writes=[br])
                for k in range(8):
                    kb.stt("dve", xn[:, k, :], xf[:, k, :], g[:, k:k + 1], rstd, ALU.mult, ALU.mult,
                           reads=[bxk[k], br, bg], writes=[bn])
                bgt = kb.Fm(24)
                bbg = Buf()
                kb.dma(bgt, I["bgate"][l], writes=[bbg], q="sp")
                wb = [kb.R(1024) for _ in range(2)]
                bw = [Buf(), Buf()]
                so = [kb.Fm(512), kb.Fm(512)]
                bso = [Buf(), Buf()]
                n = 0
                for j in range(NFM):
                    kb.dma(wb[j % 2], I["win_fm"][l, j], writes=[bw[j % 2]])
                    for hh in range(2):
                        pp, bp = ps[n % 2], psb[n % 2]
                        for k in range(8):
                            kb.mm(pp[:], wb[j % 2][:, k * 128:(k + 1) * 128], xn[:, k, hh * 512:(hh + 1) * 512], k == 0, k == 7,
                                  reads=[bw[j % 2], bn], writes=[bp])
                        if j >= 9:
                            kb.act(so[n % 2], pp[:], AF.Sigmoid, reads=[bp, bbg], writes=[bso[n % 2]],
                                   bias=bgt[:, j - 9:j - 8])
                        else:
                            kb.copy("dve", so[n % 2], pp[:], reads=[bp], writes=[bso[n % 2]])
                        kb.dma(PF[j * 128:(j + 1) * 128, c0t + hh * 512:c0t + (hh + 1) * 512], so[n % 2], reads=[bso[n % 2]],
                               writes=[bPF], q="sp")
                        n += 1
                dtb = kb.Fm(16)
                bdtb = Buf()
                kb.dma(dtb, I["dtb_rep"][l], writes=[bdtb], q="sp")
                wt = [kb.R(8 * 512) for _ in range(2)]
                bwt = [Buf(), Buf()]
                st_ = [kb.Fm(512) for _ in range(2)]
                bst = [Buf(), Buf()]
                cnt = 0
                for cb in range(6):
                    c0 = cb * 512
                    w = 512 if cb < 5 else 16
                    wv = v3(wt[cb % 2], 8)
                    kb.dma(wv[:, :, 0:w], v3(I["win_tm"][l], 8)[:, :, c0:c0 + w], writes=[bwt[cb % 2]])
                    for c in range(TT // 128):
                        pp = ps[2 + cnt % 2]
                        bp = psb[2 + cnt % 2]
                        for k in range(8):
                            kb.mm(pp[:, 0:w], xn[:, k, c * 128:(c + 1) * 128], wv[:, k, 0:w], k == 0, k == 7,
                                  reads=[bwt[cb % 2], bn], writes=[bp])
                        o = st_[cnt % 2][:, 0:w]
                        bo = bst[cnt % 2]
                        if cb in (0, 4):
                            kb.act(o, pp[:, 0:w], AF.Silu, reads=[bp], writes=[bo])
                        elif cb == 5:
                            kb.tt("dve", o, pp[:, 0:w], dtb, ALU.add, reads=[bp, bdtb], writes=[bo])
                            kb.act(o, o, AF.Exp, reads=[bo], writes=[bo])
                            kb.act(o, o, AF.Ln, reads=[bo], writes=[bo], bias=1.0)
                        else:
                            kb.copy("dve", o, pp[:, 0:w], reads=[bp], writes=[bo])
                        r0 = c0t + c * 128
                        kb.dma(PT[r0:r0 + 128, c0:c0 + w], o, reads=[bo], writes=[bPT], q="sp")
                        cnt += 1

        bXS, bBTM, bBCT = Buf("XS"), Buf("BTM"), Buf("BCT")

        def ssd_prep(l):
            kb.new_stage()
            cw = kb.Fm(30)
            cbias = kb.Fm(6)
            bcw = Buf()
            kb.dma(cw, I["conv_w"][l], writes=[bcw], q="sp")
            kb.dma(cbias, I["conv_b"][l], writes=[bcw], q="sp")
            xin = [kb.Fm(516) for _ in range(2)]
            bxin = [Buf(), Buf()]
            acc = [kb.Fm(512) for _ in range(2)]
            bacc = [Buf(), Buf()]
            cv = [kb.R(512) for _ in range(2)]
            bcv = [Buf(), Buf()]
            tm = [kb.Fm(512) for _ in range(2)]
            btm = [Buf(), Buf()]
            n = 0
            for t in range(NT):
                for f in range(6):
                    xi, bi = xin[n % 2], bxin[n % 2]
                    lo = t * 512 - 2
                    hi = t * 512 + 514
                    a, b = max(lo, 0), min(hi, S)
                    if a > lo:
                        kb.memset("pool", xi[:, 0:2], 0.0, writes=[bi])
                    if b < hi:
                        if pair:
                            kb.copy("pool", xi[:, 514:515], halo[:, 2 * f + 1:2 * f + 2], reads=[bhalo], writes=[bi])
                            kb.copy("pool", xi[:, 515:516], halo[:, 2 * f:2 * f + 1], reads=[bhalo], writes=[bi])
                        else:
                            kb.memset("pool", xi[:, 514:516], 0.0, writes=[bi])
                    kb.dma(xi[:, a - lo:b - lo], PF[f * 128:(f + 1) * 128, a:b], reads=[bPF], writes=[bi], q="sp")
                    ac, ba = acc[n % 2], bacc[n % 2]
                    kb.ts("dve", ac, xi[:, 0:512], cw[:, f * 5:f * 5 + 1], None, ALU.mult, reads=[bi, bcw], writes=[ba])
                    for j in range(1, 5):
                        kb.stt("dve", ac, xi[:, j:j + 512], cw[:, f * 5 + j:f * 5 + j + 1], ac, ALU.mult, ALU.add,
                               reads=[bi, bcw, ba], writes=[ba])
                    c_, bc_ = cv[n % 2], bcv[n % 2]
                    kb.act(c_, ac, AF.Silu, reads=[ba, bcw], writes=[bc_], bias=cbias[:, f:f + 1])
                    if f < 5:
                        for c in range(4):
                            pp, bp = ps[(4 * n + c) % 4], psb[(4 * n + c) % 4]
                            kb.mm(pp[:, 0:128], c_[:, c * 128:(c + 1) * 128], ident, True, True,
                                  reads=[bc_, cbuf], writes=[bp])
                            o, bo = tm[c % 2][:, 0:128], btm[c % 2]
                            kb.copy("act" if c % 2 else "dve", o, pp[:, 0:128], reads=[bp], writes=[bo])
                            r0 = t * 512 + c * 128
                            if f < 4:
                                kb.dma(XS[r0:r0 + 128, f * 128:(f + 1) * 128], o, reads=[bo], writes=[bXS], q="sp")
                            else:
                                kb.dma(BTM[r0:r0 + 128, :], o, reads=[bo], writes=[bBTM], q="sp")
                    if f >= 4:
                        kb.dma(BCT[(f - 4) * 128:(f - 3) * 128, t * 512:(t + 1) * 512], c_, reads=[bc_], writes=[bBCT])
                    n += 1

        bYF, bYTS = Buf("YF"), Buf("YTS")

        def ssd_pass(l, dirn):
            kb.new_stage()
            arep = kb.Fm(16)
            dsk = kb.Fm(8)
            gn = kb.Fm(512)
            bpar = Buf()
            kb.dma(arep, I["alog_rep"][l], writes=[bpar], q="sp")
            kb.dma(dsk, I["dskip_rep"][l], writes=[bpar], q="sp")
            kb.dma(gn, I["ssdnorm_rep"][l], writes=[bpar], q="sp")
            kb.act(arep, arep, AF.Exp, reads=[bpar], writes=[bpar])
            kb.ts("dve", arep, arep, -1.0, None, ALU.mult, reads=[bpar], writes=[bpar])
            niexp = v3(kb.Fm(2048)[0:16, :], 16)
            iex = v3(kb.Fm(2048)[0:16, :], 16)
            kb.dma(niexp, v3(I["c_negiexp"], 16), writes=[bpar], q="sp")
            kb.dma(iex, v3(I["c_iexp"], 16), writes=[bpar], q="sp")
            mk = v3(maskneg[:, dirn * 512:(dirn + 1) * 512], 4)
            Sst = kb.R(512)
            bS = Buf()
            if pair and dirn == 1:
                recv_combine(Sst, 0, 512, 128, kb.R(512), kb.R(512), bS)
            else:
                kb.ts("dve", Sst, onesf[:, 0:1].to_broadcast([128, 512]), 0.0, None, ALU.mult, reads=[cbuf], writes=[bS])
            NB = 2
            xs_ = [kb.Fm(512) for _ in range(NB)]
            dt_ = [kb.Fm(16) for _ in range(NB)]
            bt_ = [kb.R(128) for _ in range(NB)]
            bct = [kb.R(512) for _ in range(NB)]
            bin_ = [Buf() for _ in range(NB)]
            for i_ in range(NB):
                kb.ts("dve", bct[i_][:, 256:512], onesf[:, 0:1].to_broadcast([128, 256]), 0.0, None, ALU.mult,
                      reads=[cbuf], writes=[bin_[i_]])
            da = [kb.Fm(8) for _ in range(NB)]
            xd = [kb.R(512) for _ in range(NB)]
            xdw = [kb.R(512) for _ in range(NB)]
            csf = [kb.Fm(128) for _ in range(NB)]
            zt = [kb.Fm(1024) for _ in range(NB)]
            ee = [kb.Fm(1024) for _ in range(NB)]
            gt = [kb.Fm(256) for _ in range(NB)]
            mt = [kb.R(1024) for _ in range(NB)]
            ex = [kb.Fm(16) for _ in range(NB)]
            dec = [kb.Fm(8) for _ in range(NB)]
            yt_ = [kb.Fm(512) for _ in range(NB)]
            t2 = [kb.Fm(512) for _ in range(NB)]
            zz = [kb.Fm(512) for _ in range(NB)]
            yo = [kb.R(512) for _ in range(NB)]
            ss = [kb.Fm(8) for _ in range(NB)]
            bw_ = [[Buf() for _ in range(16)] for _ in range(NB)]
            order = range(NCH) if dirn == 0 else range(NCH - 1, -1, -1)
            for n, c in enumerate(order):
                i = n % NB
                B = bw_[i]
                r0 = c * 128
                kb.dma(xs_[i], XS[r0:r0 + 128, :], reads=[bXS], writes=[bin_[i]], q="sp")
                kb.dma(dt_[i], PT[r0:r0 + 128, 2560:2576], reads=[bPT], writes=[bin_[i]], q="sp")
                kb.dma(bt_[i], BTM[r0:r0 + 128, :], reads=[bBTM], writes=[bin_[i]])
                kb.dma(bct[i][:, 0:128], BCT[0:128, r0:r0 + 128], reads=[bBCT], writes=[bin_[i]])
                kb.dma(bct[i][:, 128:256], BCT[128:256, r0:r0 + 128], reads=[bBCT], writes=[bin_[i]])
                for g in range(2):
                    kb.dma(bct[i][g * 64:(g + 1) * 64, 256 + g * 128:384 + g * 128],
                           BCT[128 + g * 64:192 + g * 64, r0:r0 + 128], reads=[bBCT], writes=[bin_[i]])
                dtd = dt_[i][:, dirn * 8:(dirn + 1) * 8]
                kb.tt("dve", da[i], dtd, arep[:, dirn * 8:(dirn + 1) * 8], ALU.mult, reads=[bin_[i], bpar], writes=[B[0]])
                kb.tt("pool", v3(xd[i], 8), v3(xs_[i], 8), dtd.unsqueeze(2).to_broadcast([128, 8, 64]), ALU.mult,
                      reads=[bin_[i]], writes=[B[1]])
                trisel = kb.cf
                tsl = tri[:, 0:128] if dirn == 0 else tri[:, 128:256]
                kb.mm(ps[0][0:8, 0:128], da[i].bitcast(F32), tsl.bitcast(F32), True, True, reads=[B[0], cbuf],
                      writes=[psb[0]])
                kb.copy("act", csf[i][0:8, :], ps[0][0:8, 0:128], reads=[psb[0]], writes=[B[2]])
                kb.tt("dve", v3(zt[i][0:8, :], 8), csf[i][0:8, :].unsqueeze(1).to_broadcast([8, 8, 128]),
                      iex[0:8, 0:8, :], ALU.mult, reads=[B[2], bpar], writes=[B[3]])
                for hb in range(2):
                    pp, bp = ps[1 + hb], psb[1 + hb]
                    kb.mm(pp[:], onesf[0:8, :], zt[i][0:8, hb * 512:(hb + 1) * 512], True, False,
                          reads=[B[3], cbuf], writes=[bp])
                    kb.mm(pp[:], csf[i][0:8, :], niexp[0:8, hb * 4:(hb + 1) * 4, :], False, False,
                          reads=[B[2], bpar], writes=[bp])
                    kb.mm(pp[:], identf, mk, False, True, reads=[cbuf], writes=[bp])
                    kb.act(ee[i][:, hb * 512:(hb + 1) * 512], pp[:], AF.Exp, reads=[bp], writes=[B[4]])
                kb.mm(ps[3][:, 0:256], bct[i][:, 0:128], bct[i][:, 256:512], True, True, reads=[bin_[i]], writes=[psb[3]])
                kb.copy("act", gt[i], ps[3][:, 0:256], reads=[psb[3]], writes=[B[5]])
                for g in range(2):
                    kb.tt("dve" if g else "pool", v3(mt[i][:, g * 512:(g + 1) * 512], 4),
                          v3(ee[i][:, g * 512:(g + 1) * 512], 4),
                          gt[i][:, g * 128:(g + 1) * 128].unsqueeze(1).to_broadcast([128, 4, 128]), ALU.mult,
                          reads=[B[4], B[5]], writes=[B[6]])
                if dirn == 0:
                    wl, ol = tri[:, 256:384], tri[:, 0:128]
                else:
                    wl, ol = tri[:, 384:512], None
                kb.mm(ps[4][:, 0:8], wl.bitcast(F32), da[i], True, True, reads=[B[0], cbuf], writes=[psb[4]])
                if dirn == 0:
                    kb.mm(ps[4][:, 8:16], ol.bitcast(F32), da[i], True, True, reads=[B[0], cbuf], writes=[psb[4]])
                else:
                    kb.mm(ps[4][:, 8:16], onesf, da[i], True, False, reads=[B[0], cbuf], writes=[psb[4]])
                    kb.mm(ps[4][:, 8:16], tri[:, 128:256].bitcast(F32).rearrange("p x -> p x"), da[i], False, True,
                          reads=[B[0], cbuf], writes=[psb[4]])
                kb.act(ex[i], ps[4][:, 0:16], AF.Exp, reads=[psb[4]], writes=[B[7]])
                kb.mm(ps[4][:, 16:24], onesf, da[i], True, True, reads=[B[0], cbuf], writes=[psb[4]])
                kb.act(dec[i], ps[4][:, 16:24], AF.Exp, reads=[psb[4]], writes=[B[8]])
                for h in range(8):
                    kb.mm(ps[5][:, h * 64:(h + 1) * 64], mt[i][:, h * 128:(h + 1) * 128], xd[i][:, h * 64:(h + 1) * 64],
                          True, True, reads=[B[6], B[1]], writes=[psb[5]])
                kb.mm(ps[6][:], bct[i][:, 128:256], Sst, True, True, reads=[bin_[i], bS], writes=[psb[6]])
                kb.tt("dve", v3(t2[i], 8), v3(ps[6][:], 8), ex[i][:, 8:16].unsqueeze(2).to_broadcast([128, 8, 64]),
                      ALU.mult, reads=[psb[6], B[7]], writes=[B[9]])
                kb.tt("dve", yt_[i], ps[5][:], t2[i], ALU.add, reads=[psb[5], B[9]], writes=[B[10]])
                kb.tt("pool", v3(xdw[i], 8), v3(xd[i], 8), ex[i][:, 0:8].unsqueeze(2).to_broadcast([128, 8, 64]),
                      ALU.mult, reads=[B[1], B[7]], writes=[B[11]])
                kb.mm(ps[7][:], bt_[i], xdw[i], True, True, reads=[bin_[i], B[11]], writes=[psb[7]])
                for g in range(2):
                    sl = slice(g * 64, (g + 1) * 64)
                    kb.tt("dve", v3(Sst[sl, g * 256:(g + 1) * 256], 4), v3(Sst[sl, g * 256:(g + 1) * 256], 4),
                          dec[i][sl, g * 4:(g + 1) * 4].unsqueeze(2).to_broadcast([64, 4, 64]), ALU.mult,
                          reads=[bS, B[8]], writes=[bS])
                for g in range(2):
                    sl = slice(g * 64, (g + 1) * 64)
                    kb.tt("dve", Sst[sl, g * 256:(g + 1) * 256], Sst[sl, g * 256:(g + 1) * 256],
                          ps[7][sl, g * 256:(g + 1) * 256], ALU.add, reads=[bS, psb[7]], writes=[bS])
                if dirn == 0:
                    kb.tt("pool", v3(t2[i], 8), v3(xs_[i], 8), dsk.unsqueeze(2).to_broadcast([128, 8, 64]), ALU.mult,
                          reads=[bin_[i], bpar, B[10]], writes=[B[9]])
                    kb.tt("dve", yt_[i], yt_[i], t2[i], ALU.add, reads=[B[9], B[10]], writes=[B[10]])
                    kb.dma(YF[r0:r0 + 128, :], yt_[i], reads=[B[10]], writes=[bYF], q="sp")
                else:
                    kb.dma(t2[i], YF[r0:r0 + 128, :], reads=[bYF, B[9], B[10]], writes=[B[9]], q="sp")
                    kb.dma(zz[i], PT[r0:r0 + 128, 0:512], reads=[bPT], writes=[B[12]], q="sp")
                    kb.tt("dve", yt_[i], yt_[i], t2[i], ALU.add, reads=[B[9], B[10]], writes=[B[10]])
                    kb.tt("dve", yt_[i], yt_[i], zz[i], ALU.mult, reads=[B[12], B[10]], writes=[B[10]])
                    kb.P.op("act", lambda e, o=t2[i], a=yt_[i], s=ss[i]: e.activation(o, a, AF.Square, accum_out=s[:, 0:1]),
                            reads=[B[10], B[9]], writes=[B[9], B[13]])
                    kb.ts("dve", ss[i][:, 0:1], ss[i][:, 0:1], 1.0 / 512, EPS, ALU.mult, ALU.add, reads=[B[13]], writes=[B[13]])
                    kb.act(ss[i][:, 0:1], ss[i][:, 0:1], AF.Sqrt, reads=[B[13]], writes=[B[13]])
                    kb.recip(ss[i][:, 0:1], ss[i][:, 0:1], reads=[B[13]], writes=[B[13]])
                    kb.stt("dve", yo[i], yt_[i], ss[i][:, 0:1], gn, ALU.mult, ALU.mult, reads=[B[10], B[13], bpar],
                           writes=[B[14]])
                    for f in range(4):
                        kb.mm(ps[0][:, f * 128:(f + 1) * 128], yo[i][:, f * 128:(f + 1) * 128], ident, True, True,
                              reads=[B[14], cbuf], writes=[psb[0]])
                    kb.copy("act", t2[i], ps[0][:], reads=[psb[0], B[9]], writes=[B[9]])
                    for f in range(4):
                        kb.dma(YTS[f * 128:(f + 1) * 128, r0:r0 + 128], t2[i][:, f * 128:(f + 1) * 128], reads=[B[9]],
                               writes=[bYTS], q="sp")
            if pair and dirn == 0:
                kb.dma(SS[:, 0:512], Sst, reads=[bS], writes=[bSS])

        bQKT, bKTM = Buf("QKT"), Buf("KTM")

        def ret_prep(l):
            kb.new_stage()
            NB = 2
            qk = [kb.Fm(1024) for _ in range(NB)]
            rp = [kb.Fm(64) for _ in range(NB)]
            ro = [kb.R(1024) for _ in range(NB)]
            tmp = [kb.Fm(512) for _ in range(NB)]
            oT = [kb.Fm(512) for _ in range(NB)]
            bb = [[Buf() for _ in range(6)] for _ in range(NB)]
            for c in range(NCH):
                i = c % NB
                B = bb[i]
                r0 = c * 128
                kb.dma(qk[i], PT[r0:r0 + 128, 512:1536], reads=[bPT], writes=[B[0]], q="sp")
                kb.dma(rp[i], I["c_rope"][r0:r0 + 128, :], writes=[B[0]], q="sp")
                cosb = rp[i][:, 0:32].unsqueeze(1).to_broadcast([128, 16, 32])
                sinb = rp[i][:, 32:64].unsqueeze(1).to_broadcast([128, 16, 32])
                x4 = qk[i].rearrange("p (h t d) -> p h t d", h=16, t=2)
                o4 = ro[i].rearrange("p (h t d) -> p h t d", h=16, t=2)
                tm4 = tmp[i].rearrange("p (h d) -> p h d", h=16)
                t1, t2_ = x4[:, :, 0, :], x4[:, :, 1, :]
                kb.tt("dve", tm4, t2_, sinb, ALU.mult, reads=[B[0]], writes=[B[1]])
                kb.tt("pool", o4[:, :, 0, :], t1, cosb, ALU.mult, reads=[B[0]], writes=[B[2]])
                kb.tt("dve", o4[:, :, 0, :], o4[:, :, 0, :], tm4, ALU.subtract, reads=[B[1], B[2]], writes=[B[2]])
                kb.tt("dve", tm4, t1, sinb, ALU.mult, reads=[B[0], B[2]], writes=[B[1]])
                kb.tt("pool", o4[:, :, 1, :], t2_, cosb, ALU.mult, reads=[B[0]], writes=[B[3]])
                kb.tt("dve", o4[:, :, 1, :], o4[:, :, 1, :], tm4, ALU.add, reads=[B[1], B[3]], writes=[B[3]])
                kb.ts("dve", ro[i][:, 512:1024], ro[i][:, 512:1024], 0.125, None, ALU.mult, reads=[B[2], B[3]],
                      writes=[B[2], B[3]])
                kb.dma(KTM[r0:r0 + 128, :], ro[i][:, 512:1024], reads=[B[2], B[3]], writes=[bKTM])
                for f in range(8):
                    pp, bp = ps[f % 2], psb[f % 2]
                    kb.mm(pp[:, 0:128], ro[i][:, f * 128:(f + 1) * 128], ident, True, True, reads=[B[2], B[3], cbuf],
                          writes=[bp])
                    o = oT[i][:, (f % 4) * 128:(f % 4 + 1) * 128]
                    kb.copy("act", o, pp[:, 0:128], reads=[bp], writes=[B[4 + (f % 2)]])
                    for hh in range(2):
                        kb.dma(QK2[c, :, (2 * f + hh) * 128:(2 * f + hh + 1) * 128], o[hh * 64:(hh + 1) * 64, :],
                               reads=[B[4 + (f % 2)]], writes=[bQKT], q="sp")

        bRF, bYTR = Buf("RF"), Buf("YTR")

        def ret_pass(l, dirn):
            kb.new_stage()
            rmask = v3(kb.Fm(1024), 8)
            rdec = kb.Fm(40)
            gn = kb.Fm(512)
            bpar = Buf()
            kb.dma(rmask, v3(I["c_retmask"][:, dirn * 1024:(dirn + 1) * 1024], 8), writes=[bpar], q="sp")
            kb.dma(rdec, I["c_retdec"], writes=[bpar], q="sp")
            kb.dma(gn, I["retnorm_rep"][l], writes=[bpar], q="sp")
            kdec = rdec[:, dirn * 8:dirn * 8 + 8]
            qdec = rdec[:, 16 + dirn * 8:16 + dirn * 8 + 8]
            cdec = rdec[0:64, 32:40]
            Sst = kb.R(512)[0:64, :]
            bS = Buf()
            if pair and dirn == 1:
                recv_combine(Sst, 512, 1024, 64, kb.R(512)[0:64, :], kb.R(512)[0:64, :], bS)
            else:
                kb.ts("dve", Sst, onesf[0:64, 0:1].to_broadcast([64, 512]), 0.0, None, ALU.mult, reads=[cbuf], writes=[bS])
            NB = 2
            qkt = [kb.R(2048) for _ in range(NB)]
            ktm = [kb.R(512) for _ in range(NB)]
            vtm = [kb.R(512) for _ in range(NB)]
            vw = [kb.R(512) for _ in range(NB)]
            mt = [kb.R(1024) for _ in range(NB)]
            yt_ = [kb.Fm(512) for _ in range(NB)]
            t2 = [kb.Fm(512) for _ in range(NB)]
            gg = [kb.Fm(512) for _ in range(NB)]
            yo = [kb.R(512) for _ in range(NB)]
            ss = [kb.Fm(16) for _ in range(NB)]
            bb = [[Buf() for _ in range(12)] for _ in range(NB)]
            order = range(NCH) if dirn == 0 else range(NCH - 1, -1, -1)
            for n, c in enumerate(order):
                i = n % NB
                B = bb[i]
                r0 = c * 128
                q3 = v3(qkt[i][0:64, :], 16)
                kb.dma(qkt[i][0:64, :], QK2[c], reads=[bQKT], writes=[B[0]])
                kb.dma(ktm[i], KTM[r0:r0 + 128, :], reads=[bKTM], writes=[B[0]])
                kb.dma(vtm[i], PT[r0:r0 + 128, 1536:2048], reads=[bPT], writes=[B[0]])
                for h in range(8):
                    kb.mm(ps[h // 4][:, (h % 4) * 128:(h % 4 + 1) * 128], q3[:, 8 + h, :], q3[:, h, :], True, True,
                          reads=[B[0]], writes=[psb[h // 4]])
                for hb in range(2):
                    kb.tt("dve", v3(mt[i][:, hb * 512:(hb + 1) * 512], 4), v3(ps[hb][:], 4), rmask[:, hb * 4:(hb + 1) * 4, :],
                          ALU.mult, reads=[psb[hb], bpar], writes=[B[1]])
                for h in range(8):
                    kb.mm(ps[5][:, h * 64:(h + 1) * 64], mt[i][:, h * 128:(h + 1) * 128], vtm[i][:, h * 64:(h + 1) * 64],
                          True, True, reads=[B[1], B[0]], writes=[psb[5]])
                for h in range(8):
                    kb.mm(ps[6][:, h * 64:(h + 1) * 64], q3[:, h, :], Sst[:, h * 64:(h + 1) * 64], True, True,
                          reads=[B[0], bS], writes=[psb[6]])
                kb.tt("dve", v3(t2[i], 8), v3(ps[6][:], 8), qdec.unsqueeze(2).to_broadcast([128, 8, 64]), ALU.mult,
                      reads=[psb[6], bpar], writes=[B[2]])
                kb.tt("dve", yt_[i], ps[5][:], t2[i], ALU.add, reads=[psb[5], B[2]], writes=[B[3]])
                kb.tt("pool", v3(vw[i], 8), v3(vtm[i], 8), kdec.unsqueeze(2).to_broadcast([128, 8, 64]), ALU.mult,
                      reads=[B[0], bpar], writes=[B[4]])
                for h in range(8):
                    kb.mm(ps[7][0:64, h * 64:(h + 1) * 64], ktm[i][:, h * 64:(h + 1) * 64], vw[i][:, h * 64:(h + 1) * 64],
                          True, True, reads=[B[0], B[4]], writes=[psb[7]])
                kb.tt("dve", v3(Sst, 8), v3(Sst, 8), cdec.unsqueeze(2).to_broadcast([64, 8, 64]), ALU.mult,
                      reads=[bS, bpar], writes=[bS])
                kb.tt("dve", Sst, Sst, ps[7][0:64, :], ALU.add, reads=[bS, psb[7]], writes=[bS])
                if dirn == 0:
                    kb.dma(RF[r0:r0 + 128, :], yt_[i], reads=[B[3]], writes=[bRF], q="sp")
                else:
                    kb.dma(t2[i], RF[r0:r0 + 128, :], reads=[bRF, B[2], B[3]], writes=[B[2]], q="sp")
                    kb.dma(gg[i], PT[r0:r0 + 128, 2048:2560], reads=[bPT], writes=[B[5]], q="sp")
                    kb.tt("dve", yt_[i], yt_[i], t2[i], ALU.add, reads=[B[2], B[3]], writes=[B[3]])
                    kb.tt("pool", t2[i], yt_[i], yt_[i], ALU.mult, reads=[B[3], B[2]], writes=[B[2]])
                    kb.P.op("dve", lambda e, o=ss[i][:, 0:8], a=v3(t2[i], 8): e.tensor_reduce(o, a, AX.X, ALU.add),
                            reads=[B[2]], writes=[B[6]])
                    kb.ts("dve", ss[i][:, 0:8], ss[i][:, 0:8], 1.0 / 64, EPS, ALU.mult, ALU.add, reads=[B[6]], writes=[B[6]])
                    kb.act(ss[i][:, 0:8], ss[i][:, 0:8], AF.Sqrt, reads=[B[6]], writes=[B[6]])
                    kb.recip(ss[i][:, 0:8], ss[i][:, 0:8], reads=[B[6]], writes=[B[6]])
                    kb.tt("dve", v3(yt_[i], 8), v3(yt_[i], 8), ss[i][:, 0:8].unsqueeze(2).to_broadcast([128, 8, 64]),
                          ALU.mult, reads=[B[3], B[6]], writes=[B[3]])
                    kb.tt("pool", yt_[i], yt_[i], gn, ALU.mult, reads=[B[3], bpar], writes=[B[3]])
                    kb.tt("dve", yo[i], yt_[i], gg[i], ALU.mult, reads=[B[3], B[5]], writes=[B[7]])
                    for f in range(4):
                        kb.mm(ps[2][:, f * 128:(f + 1) * 128], yo[i][:, f * 128:(f + 1) * 128], ident, True, True,
                              reads=[B[7], cbuf], writes=[psb[2]])
                    kb.copy("act", t2[i], ps[2][:], reads=[psb[2], B[2]], writes=[B[2]])
                    for f in range(4):
                        kb.dma(YTR[f * 128:(f + 1) * 128, r0:r0 + 128], t2[i][:, f * 128:(f + 1) * 128], reads=[B[2]],
                               writes=[bYTR], q="sp")
            if pair and dirn == 0:
                kb.dma(SS[0:64, 512:1024], Sst, reads=[bS], writes=[bSS])

        bY5 = Buf("Y5")
        bY5A, bY5B = Buf("Y5A"), Buf("Y5B")

        def s5_stage(l, dirs=(0, 1)):
            kb.new_stage()
            bp_ = Buf("s5par")
            def F(n):
                return kb.Fm(n)
            rho = [F(12), F(12)]
            c0 = [F(12), F(12)]
            s0 = [F(12), F(12)]
            C9 = [F(12), F(12)]
            S9 = [F(12), F(12)]
            cfr = [F(12), F(12)]
            cfi = [F(12), F(12)]
            tA, tB, tC, tD = F(12), F(12), F(12), F(12)
            halfpi = F(1)
            kb.memset("dve", halfpi, math.pi / 2, writes=[bp_])
            for d in range(2):
                lre, lim, stp = F(12), F(12), F(12)
                kb.dma(lre, I["s5_lre"][l, d], writes=[bp_], q="sp")
                kb.dma(lim, I["s5_lim"][l, d], writes=[bp_], q="sp")
                kb.dma(stp, I["s5_lstep"][l, d], writes=[bp_], q="sp")
                R_, W_ = [bp_], [bp_]
                kb.ts("dve", lre, lre, -1e-4, None, ALU.min, reads=R_, writes=W_)
                kb.act(stp, stp, AF.Exp, reads=R_, writes=W_)
                kb.tt("dve", tA, lre, stp, ALU.mult, reads=R_, writes=W_)
                kb.act(rho[d], tA, AF.Exp, reads=R_, writes=W_)
                kb.tt("dve", tB, lim, stp, ALU.mult, reads=R_, writes=W_)
                kb.act(s0[d], tB, AF.Sin, reads=R_, writes=W_, scale=1.0 / 32)
                kb.act(c0[d], tB, AF.Sin, reads=R_, writes=W_, scale=1.0 / 32, bias=halfpi[:, 0:1])

                def dbl(cc, sn):
                    kb.tt("dve", tC, cc, cc, ALU.mult, reads=R_, writes=W_)
                    kb.tt("dve", tD, sn, sn, ALU.mult, reads=R_, writes=W_)
                    kb.tt("dve", tD, tC, tD, ALU.subtract, reads=R_, writes=W_)
                    kb.tt("dve", tC, cc, sn, ALU.mult, reads=R_, writes=W_)
                    kb.ts("dve", sn, tC, 2.0, None, ALU.mult, reads=R_, writes=W_)
                    kb.copy("dve", cc, tD, reads=R_, writes=W_)
                for _ in range(5):
                    dbl(c0[d], s0[d])
                kb.copy("dve", C9[d], c0[d], reads=R_, writes=W_)
                kb.copy("dve", S9[d], s0[d], reads=R_, writes=W_)
                for _ in range(9):
                    dbl(C9[d], S9[d])
                lbr, lbi, den = F(12), F(12), F(12)
                kb.tt("dve", lbr, rho[d], c0[d], ALU.mult, reads=R_, writes=W_)
                kb.tt("dve", lbi, rho[d], s0[d], ALU.mult, reads=R_, writes=W_)
                kb.ts("dve", lbr, lbr, -1.0, None, ALU.add, reads=R_, writes=W_)
                kb.tt("dve", den, lre, lre, ALU.mult, reads=R_, writes=W_)
                kb.tt("dve", tC, lim, lim, ALU.mult, reads=R_, writes=W_)
                kb.tt("dve", den, den, tC, ALU.add, reads=R_, writes=W_)
                kb.recip(den, den, reads=R_, writes=W_)
                kb.tt("dve", tC, lbr, lre, ALU.mult, reads=R_, writes=W_)
                kb.tt("dve", tD, lbi, lim, ALU.mult, reads=R_, writes=W_)
                kb.tt("dve", tC, tC, tD, ALU.add, reads=R_, writes=W_)
                kb.tt("dve", cfr[d], tC, den, ALU.mult, reads=R_, writes=W_)
                kb.tt("dve", tC, lbi, lre, ALU.mult, reads=R_, writes=W_)
                kb.tt("dve", tD, lbr, lim, ALU.mult, reads=R_, writes=W_)
                kb.tt("dve", tC, tC, tD, ALU.subtract, reads=R_, writes=W_)
                kb.tt("dve", cfi[d], tC, den, ALU.mult, reads=R_, writes=W_)
            bre = v3(kb.Fm(1536), 12)
            bim = v3(kb.Fm(1536), 12)
            kb.dma(bre, v3(I["s5_bre"][l], 12), writes=[bp_], q="sp")
            kb.dma(bim, v3(I["s5_bim"][l], 12), writes=[bp_], q="sp")
            dsel = v3(kb.R(384), 12)
            kb.dma(dsel, v3(I["s5_dsel"][l], 12), writes=[bp_])
            cre = [v3(kb.R(384), 12) for _ in range(2)]
            cimn = [v3(kb.R(384), 12) for _ in range(2)]
            for d in range(2):
                kb.dma(cre[d], v3(I["s5_cre"][l, d], 12), writes=[bp_])
                kb.dma(cimn[d], v3(I["s5_cim"][l, d], 12), writes=[bp_])
                kb.ts("dve", cimn[d], cimn[d], -1.0, None, ALU.mult, reads=[bp_], writes=[bp_])
            bbr = kb.R(128)
            bbi = kb.R(128)
            tq = kb.Fm(128)
            btr = [kb.R(128), kb.R(128)]
            bti = [kb.R(128), kb.R(128)]
            cosT = [kb.Fm(512), kb.Fm(512)]
            sinT = [kb.Fm(512), kb.Fm(512)]
            rhoT = [kb.Fm(512), kb.Fm(512)]
            tc_, ts_ = kb.Fm(256), kb.Fm(256)
            ub = [kb.R(512) for _ in range(2)]
            vre = [kb.Fm(512) for _ in range(2)]
            vim = [kb.Fm(512) for _ in range(2)]
            wre = [kb.Fm(512) for _ in range(2)]
            wim = [kb.Fm(512) for _ in range(2)]
            hre = [kb.R(512) for _ in range(2)]
            him = [kb.R(512) for _ in range(2)]
            t1 = [kb.Fm(512) for _ in range(2)]
            ini = [[kb.Fm(4), kb.Fm(4)] for _ in range(2)]
            go = [kb.Fm(512) for _ in range(2)]
            bb = [[Buf() for _ in range(10)] for _ in range(2)]
            bt = [Buf(), Buf()]
            bini = [Buf(), Buf()]
            bsc = Buf()
            psi = [(1, 2, 3), (4, 5, 6)]
            for it in range(12):
                for d in dirs:
                    R_, W_ = [bp_, bsc, bt[d]], [bsc, bt[d]]
                    kb.ts("dve", tq, bre[:, it, :], cfr[d][:, it:it + 1], None, ALU.mult, reads=R_, writes=W_)
                    kb.stt("dve", tq, bim[:, it, :], cfi[d][:, it:it + 1], tq, ALU.mult, ALU.subtract, reads=R_, writes=W_)
                    kb.ts("dve", bbr, tq, -1.0, None, ALU.mult, reads=R_, writes=W_)
                    kb.ts("dve", tq, bim[:, it, :], cfr[d][:, it:it + 1], None, ALU.mult, reads=R_, writes=W_)
                    kb.stt("dve", bbi, bre[:, it, :], cfi[d][:, it:it + 1], tq, ALU.mult, ALU.add, reads=R_, writes=W_)
                    kb.mm(ps[0][:, 0:128], bbr, ident, True, True, reads=[bsc, cbuf], writes=[psb[0]])
                    kb.mm(ps[0][:, 128:256], bbi, ident, True, True, reads=[bsc, cbuf], writes=[psb[0]])
                    kb.copy("dve", btr[d], ps[0][:, 0:128], reads=[psb[0]], writes=W_)
                    kb.copy("dve", bti[d], ps[0][:, 128:256], reads=[psb[0]], writes=W_)
                    cT, sT = cosT[d], sinT[d]
                    kb.memset("dve", cT[:, 0:1], 1.0, writes=W_)
                    kb.memset("dve", sT[:, 0:1], 0.0, writes=W_)
                    kb.copy("dve", tc_[:, 0:1], c0[d][:, it:it + 1], reads=R_, writes=W_)
                    kb.copy("dve", ts_[:, 0:1], s0[d][:, it:it + 1], reads=R_, writes=W_)
                    m = 1
                    while m < 512:
                        ck, sk = tc_[:, 0:1], ts_[:, 0:1]
                        tmpv = t1[d][:, 0:m]
                        RX, WX = R_ + [bb[d][2]], W_ + [bb[d][2]]
                        kb.ts("dve", tmpv, sT[:, 0:m], sk, None, ALU.mult, reads=RX, writes=WX)
                        kb.stt("dve", cT[:, m:2 * m], cT[:, 0:m], ck, tmpv, ALU.mult, ALU.subtract, reads=RX, writes=WX)
                        kb.ts("dve", tmpv, cT[:, 0:m], sk, None, ALU.mult, reads=RX, writes=WX)
                        kb.stt("dve", sT[:, m:2 * m], sT[:, 0:m], ck, tmpv, ALU.mult, ALU.add, reads=RX, writes=WX)
                        kb.tt("dve", tc_[:, 1:2], ck, ck, ALU.mult, reads=R_, writes=W_)
                        kb.tt("dve", tc_[:, 2:3], sk, sk, ALU.mult, reads=R_, writes=W_)
                        kb.tt("dve", tc_[:, 3:4], ck, sk, ALU.mult, reads=R_, writes=W_)
                        kb.tt("dve", tc_[:, 0:1], tc_[:, 1:2], tc_[:, 2:3], ALU.subtract, reads=R_, writes=W_)
                        kb.ts("dve", ts_[:, 0:1], tc_[:, 3:4], 2.0, None, ALU.mult, reads=R_, writes=W_)
                        m *= 2
                    kb.copy("dve", rhoT[d], rho[d][:, it:it + 1].to_broadcast([128, 512]), reads=R_, writes=W_)
                    kb.memset("dve", ini[d][0][:, 0:2], 0.0, writes=[bini[d]])
                for n in range(NT):
                    for d in dirs:
                        t = n if d == 0 else NT - 1 - n
                        i = d
                        B = bb[d]
                        p1, p2, p3 = psi[d]
                        cT, sT = cosT[d], sinT[d]
                        kb.dma(ub[i], PF[(6 + it // 4) * 128:(7 + it // 4) * 128, t * 512:(t + 1) * 512], reads=[bPF],
                               writes=[B[0]])
                        kb.mm(ps[p1][:], btr[d], ub[i], True, True, reads=[bt[d], B[0]], writes=[psb[p1]])
                        kb.mm(ps[p2][:], bti[d], ub[i], True, True, reads=[bt[d], B[0]], writes=[psb[p2]])
                        pre = ps[p1][:] if d == 0 else ps[p1][:, ::-1]
                        pim = ps[p2][:] if d == 0 else ps[p2][:, ::-1]
                        kb.tt("dve", vre[i], pre, cT, ALU.mult, reads=[psb[p1], bt[d]], writes=[B[1]])
                        kb.tt("dve", t1[i], pim, sT, ALU.mult, reads=[psb[p2], bt[d]], writes=[B[2]])
                        kb.tt("pool", vre[i], vre[i], t1[i], ALU.add, reads=[B[1], B[2]], writes=[B[1]])
                        kb.tt("dve", vim[i], pim, cT, ALU.mult, reads=[psb[p2], bt[d]], writes=[B[3]])
                        kb.tt("dve", t1[i], pre, sT, ALU.mult, reads=[psb[p1], bt[d], B[1]], writes=[B[2]])
                        kb.tt("pool", vim[i], vim[i], t1[i], ALU.subtract, reads=[B[3], B[2]], writes=[B[3]])
                        kb.scan(wre[i], rhoT[d], vre[i], ini[d][0][:, 0:1], reads=[B[1], bt[d], bini[d]], writes=[B[4]])
                        kb.scan(wim[i], rhoT[d], vim[i], ini[d][0][:, 1:2], reads=[B[3], bt[d], bini[d]], writes=[B[5]])
                        kb.ts("dve", ini[d][1][:, 0:1], wim[i][:, 511:512], S9[d][:, it:it + 1], None, ALU.mult,
                              reads=[B[5], bp_, bini[d]], writes=[bini[d]])
                        kb.ts("dve", ini[d][1][:, 1:2], wre[i][:, 511:512], S9[d][:, it:it + 1], None, ALU.mult,
                              reads=[B[4], bp_, bini[d]], writes=[bini[d]])
                        kb.stt("dve", ini[d][0][:, 0:1], wre[i][:, 511:512], C9[d][:, it:it + 1], ini[d][1][:, 0:1], ALU.mult,
                               ALU.subtract, reads=[B[4], bini[d]], writes=[bini[d]])
                        kb.stt("dve", ini[d][0][:, 1:2], wim[i][:, 511:512], C9[d][:, it:it + 1], ini[d][1][:, 1:2], ALU.mult,
                               ALU.add, reads=[B[5], bini[d]], writes=[bini[d]])
                        kb.tt("pool", hre[i], wre[i], cT, ALU.mult, reads=[B[4], bt[d]], writes=[B[6]])
                        kb.tt("dve", t1[i], wim[i], sT, ALU.mult, reads=[B[5], bt[d], B[2]], writes=[B[2]])
                        kb.tt("pool", hre[i], hre[i], t1[i], ALU.subtract, reads=[B[6], B[2]], writes=[B[6]])
                        kb.tt("pool", him[i], wre[i], sT, ALU.mult, reads=[B[4], bt[d]], writes=[B[7]])
                        kb.tt("dve", t1[i], wim[i], cT, ALU.mult, reads=[B[5], bt[d], B[6]], writes=[B[2]])
                        kb.tt("pool", him[i], him[i], t1[i], ALU.add, reads=[B[7], B[2]], writes=[B[7]])
                        kb.mm(ps[p3][0:32, :], cre[d][:, it, :], hre[i], True, False, reads=[bp_, B[6]], writes=[psb[p3]])
                        kb.mm(ps[p3][0:32, :], cimn[d][:, it, :], him[i], False, d == 1, reads=[bp_, B[7]], writes=[psb[p3]])
                        if d == 0:
                            kb.mm(ps[p3][0:32, :], dsel[:, it, :], ub[i], False, True, reads=[bp_, B[0]], writes=[psb[p3]])
                            kb.copy("act", go[i][0:32, :], ps[p3][0:32, :], reads=[psb[p3]], writes=[B[8]])
                            kb.dma(Y5A[it * 32:(it + 1) * 32, t * 512:(t + 1) * 512], go[i][0:32, :], reads=[B[8]],
                                   writes=[bY5A], q="sp")
                        else:
                            kb.copy("act", go[i][0:32, :], ps[p3][0:32, ::-1], reads=[psb[p3]], writes=[B[8]])
                            kb.dma(Y5B[it * 32:(it + 1) * 32, t * 512:(t + 1) * 512], go[i][0:32, :], reads=[B[8]],
                                   writes=[bY5B], q="sp")
            kb.new_stage()
            ca = [kb.Fm(512), kb.Fm(512)]
            cb_ = [kb.Fm(512), kb.Fm(512)]
            bc = [Buf(), Buf()]
            n = 0
            for r in range(3):
                for t in range(NT):
                    i = n % 2
                    ts0 = slice(t * 512, (t + 1) * 512)
                    kb.dma(ca[i], Y5A[r * 128:(r + 1) * 128, ts0], reads=[bY5A], writes=[bc[i]], q="sp")
                    kb.dma(cb_[i], Y5B[r * 128:(r + 1) * 128, ts0], reads=[bY5B], writes=[bc[i]], q="sp")
                    kb.tt("dve", ca[i], ca[i], cb_[i], ALU.add, reads=[bc[i]], writes=[bc[i]])
                    kb.act(ca[i], ca[i], AF.Gelu, reads=[bc[i]], writes=[bc[i]])
                    kb.dma(Y5[r * 128:(r + 1) * 128, ts0], ca[i], reads=[bc[i]], writes=[bY5], q="sp")
                    n += 1

        def merge_stage(l):
            kb.new_stage()
            wbs = v3(kb.R(4096), 4)
            wbr = v3(kb.R(4096), 4)
            wb5 = v3(kb.R(3072), 3)
            wv = v3(kb.R(1152), 3)
            wg_ = v3(kb.R(1152), 3)
            wo = v3(kb.R(8192), 8)
            bw = Buf()
            kb.dma(wbs, v3(I["wbr_ssd"][l], 4), writes=[bw])
            kb.dma(wbr, v3(I["wbr_ret"][l], 4), writes=[bw])
            kb.dma(wb5, v3(I["wbr_s5"][l], 3), writes=[bw])
            kb.dma(wv, v3(I["glu_wv"][l], 3), writes=[bw])
            kb.dma(wg_, v3(I["glu_wg"][l], 3), writes=[bw])
            kb.dma(wo, v3(I["wout"][l], 8), writes=[bw])
            ys = v3(kb.R(2048), 4)
            yr = v3(kb.R(2048), 4)
            y5 = v3(kb.R(1536), 3)
            y5g = v3(kb.R(1536), 3)
            mixed = v3(kb.R(4096), 8)
            sgt = kb.Fm(512)
            gate = [kb.Fm(512) for _ in range(2)]
            tmp = [kb.Fm(512) for _ in range(2)]
            xr = [kb.Fm(512) for _ in range(2)]
            xo = [kb.Fm(512) for _ in range(2)]
            bi, bg5, bmx = Buf(), Buf(), [Buf() for _ in range(8)]
            bsg = Buf()
            bgate = [Buf(), Buf()]
            btmp = [Buf(), Buf()]
            bxr = [Buf(), Buf()]
            bxo = [Buf(), Buf()]
            for t in range(NT):
                ts0 = slice(t * 512, (t + 1) * 512)
                for f in range(4):
                    kb.dma(ys[:, f, :], YTS[f * 128:(f + 1) * 128, ts0], reads=[bYTS], writes=[bi])
                    kb.dma(yr[:, f, :], YTR[f * 128:(f + 1) * 128, ts0], reads=[bYTR], writes=[bi])
                for f in range(3):
                    kb.dma(y5[:, f, :], Y5[f * 128:(f + 1) * 128, ts0], reads=[bY5], writes=[bi])
                for f in range(3):
                    for k in range(3):
                        kb.mm(ps[0][:], wv[:, k, f * 128:(f + 1) * 128], y5[:, k, :], k == 0, k == 2, reads=[bw, bi],
                              writes=[psb[0]])
                    for k in range(3):
                        kb.mm(ps[1][:], wg_[:, k, f * 128:(f + 1) * 128], y5[:, k, :], k == 0, k == 2, reads=[bw, bi],
                              writes=[psb[1]])
                    kb.act(sgt, ps[1][:], AF.Sigmoid, reads=[psb[1]], writes=[bsg])
                    kb.tt("dve", y5g[:, f, :], ps[0][:], sgt, ALU.mult, reads=[psb[0], bsg], writes=[bg5])
                n = 0
                for i in range(8):
                    for br_, (w_, y_, nk, rb) in enumerate(((wbs, ys, 4, bi), (y5g and wb5, y5g, 3, bg5), (wbr, yr, 4, bi))):
                        pp, bp = ps[2 + n % 2], psb[2 + n % 2]
                        for k in range(nk):
                            kb.mm(pp[:], w_[:, k, i * 128:(i + 1) * 128], y_[:, k, :], k == 0, k == nk - 1, reads=[bw, rb],
                                  writes=[bp])
                        gi = 9 + br_ * 8 + i
                        kb.dma(gate[n % 2], PF[gi * 128:(gi + 1) * 128, ts0], reads=[bPF], writes=[bgate[n % 2]], q="sp")
                        if br_ == 0:
                            kb.tt("dve", tmp[i % 2], pp[:], gate[n % 2], ALU.mult, reads=[bp, bgate[n % 2]],
                                  writes=[btmp[i % 2]])
                        elif br_ == 1:
                            kb.tt("dve", gate[n % 2], pp[:], gate[n % 2], ALU.mult, reads=[bp, bgate[n % 2]],
                                  writes=[bgate[n % 2]])
                            kb.tt("pool", tmp[i % 2], tmp[i % 2], gate[n % 2], ALU.add, reads=[bgate[n % 2], btmp[i % 2]],
                                  writes=[btmp[i % 2]])
                        else:
                            kb.tt("dve", gate[n % 2], pp[:], gate[n % 2], ALU.mult, reads=[bp, bgate[n % 2]],
                                  writes=[bgate[n % 2]])
                            kb.tt("dve", mixed[:, i, :], tmp[i % 2], gate[n % 2], ALU.add,
                                  reads=[bgate[n % 2], btmp[i % 2]], writes=[bmx[i]])
                        n += 1
                for i in range(8):
                    pp, bp = ps[4 + i % 2], psb[4 + i % 2]
                    for k in range(8):
                        kb.mm(pp[:], wo[:, k, i * 128:(i + 1) * 128], mixed[:, k, :], k == 0, k == 7, reads=[bw, bmx[k]],
                              writes=[bp])
                    kb.dma(xr[i % 2], X[i * 128:(i + 1) * 128, ts0], reads=[xb(i, t)], writes=[bxr[i % 2]], q="sp")
                    kb.tt("dve", xo[i % 2], pp[:], xr[i % 2], ALU.add, reads=[bp, bxr[i % 2]], writes=[bxo[i % 2]])
                    kb.dma(X[i * 128:(i + 1) * 128, ts0], xo[i % 2], reads=[bxo[i % 2]], writes=[xb(i, t)], q="sp")

        def final_stage():
            for t in range(NT):
                kb.new_stage()
                xn, bn, xf, bxk = load_norm(X, t, I["g_final"], "z")
                o = v3(kb.Fm(4096), 8)
                bo = Buf()
                for k in range(8):
                    kb.copy("dve" if k % 2 else "act", o[:, k, :], xn[:, k, :].bitcast(F32), reads=[bn], writes=[bo])
                    kb.dma(outT[k * 128:(k + 1) * 128, t * 512:(t + 1) * 512], o[:, k, :], reads=[bo], q="sp")

        kb.new_stage()
        cpb = [kb.Fm(2048), kb.Fm(2048)]
        bcp = [Buf(), Buf()]
        n = 0
        for k in range(8):
            for c0_ in range(0, S, 2048):
                w = min(2048, S - c0_)
                kb.dma(cpb[n % 2][:, 0:w], I["xT"][k * 128:(k + 1) * 128, c0_:c0_ + w], writes=[bcp[n % 2]], q="sp")
                wr = [xb(k, tt) for tt in range(c0_ // 512, (c0_ + w) // 512)]
                kb.dma(X[k * 128:(k + 1) * 128, c0_:c0_ + w], cpb[n % 2][:, 0:w], reads=[bcp[n % 2]], writes=wr, q="sp")
                n += 1
        def on(nm):
            return STAGES is None or nm in STAGES
        for l in range(L):
            if on("ffn1"):
                ffn_stage(l, "g_ffn1", I["wg1"], I["wu1"], I["wd1"])
            if on("inproj"):
                inproj_stage(l)
            if pair:
                halo_exchange()
                ssd_prep(l)
                ret_prep(l)
                ssd_pass(l, 0)
                ret_pass(l, 0)
                s5_stage(l, (0,))
                state_exchange()
                ssd_pass(l, 1)
                ret_pass(l, 1)
                s5_stage(l, (1,))
            else:
                if on("ssd") or on("ssdprep"):
                    ssd_prep(l)
                if on("ssd") or on("ssd0"):
                    ssd_pass(l, 0)
                if on("ssd") or on("ssd1"):
                    ssd_pass(l, 1)
                if on("ret"):
                    ret_prep(l)
                    ret_pass(l, 0)
                    ret_pass(l, 1)
                if on("s5"):
                    s5_stage(l)
            if on("merge"):
                merge_stage(l)
            if on("ffn2"):
                ffn_stage(l, "g_ffn2", I["wg2"], I["wu2"], I["wd2"])
        final_stage()
        P.emit()
    return nc


def _tile_cols(w, nt):
    K, N = w.shape
    return np.ascontiguousarray(w.reshape(K // 128, 128, nt, 128).transpose(2, 1, 0, 3).reshape(nt, 128, (K // 128) * 128))


def _tile_rows(w):
    K, N = w.shape
    return np.ascontiguousarray(w.reshape(K // 128, 128, N).transpose(1, 0, 2).reshape(128, (K // 128) * N))


def _consts(S):
    c = {}
    idx = np.arange(128)
    k, x = idx[:, None], idx[None, :]
    c["c_ident"] = np.eye(128, dtype=np.float32)
    c["c_tri"] = np.concatenate([(k <= x), -1.0 * (k < x), (k > x), (k < x)], axis=1).astype(np.float32)
    NEG = -30000.0
    mf = np.where(x < k, NEG, 0.0)
    mb = np.where(x > k, NEG, 0.0)
    c["c_maskneg"] = np.concatenate([np.tile(mf, (1, 4)), np.tile(mb, (1, 4))], axis=1).astype(np.float32)
    ie = np.zeros((16, 16, 128), np.float32)
    for j in range(16):
        ie[j, j, :] = 1.0
    c["c_iexp"] = ie.reshape(16, 2048)
    c["c_negiexp"] = (-ie).reshape(16, 2048)
    lg = np.log1p(-np.exp2(-5.0 - np.arange(8, dtype=np.float32))).astype(np.float32)
    s_, l_ = idx[:, None].astype(np.float32), idx[None, :].astype(np.float32)
    rm = np.zeros((128, 2, 8, 128), np.float32)
    for h in range(8):
        rm[:, 0, h, :] = np.where(l_ >= s_, np.exp(lg[h] * np.where(l_ >= s_, l_ - s_, 0.0)), 0.0)
        rm[:, 1, h, :] = np.where(s_ > l_, np.exp(lg[h] * np.where(s_ > l_, s_ - l_, 0.0)), 0.0)
    c["c_retmask"] = rm.reshape(128, 2048)
    rd = np.zeros((128, 40), np.float32)
    t = idx.astype(np.float32)[:, None]
    rd[:, 0:8] = np.exp(lg[None, :] * (127.0 - t))
    rd[:, 8:16] = np.exp(lg[None, :] * t)
    rd[:, 16:24] = np.exp(lg[None, :] * (t + 1.0))
    rd[:, 24:32] = np.exp(lg[None, :] * (128.0 - t))
    rd[:, 32:40] = np.exp(lg[None, :] * 128.0)
    c["c_retdec"] = rd
    pos = np.arange(S, dtype=np.float32)
    inv = (10000.0 ** (-np.arange(0, 64, 2, dtype=np.float32) / 64)).astype(np.float32)
    ang = pos[:, None] * inv[None, :]
    c["c_rope"] = np.concatenate([np.cos(ang), np.sin(ang)], axis=1).astype(np.float32)
    b = np.zeros((128, 128), np.float32)
    b[:64, :64] = 1
    b[64:, 64:] = 1
    c["c_blk64"] = b
    return c


def _prep_weights(inp, L):
    f = lambda a: np.ascontiguousarray(np.asarray(a, dtype=np.float32))
    W = {}
    gt = lambda g: np.ascontiguousarray(f(g).reshape(-1, 8, 128).transpose(0, 2, 1))
    W["g_ffn1"], W["g_mix"], W["g_ffn2"] = gt(inp["ffn1_norm"]), gt(inp["mix_norm"]), gt(inp["ffn2_norm"])
    W["g_final"] = gt(inp["final_norm"])[0]
    for n_, a in (("1", "ffn1"), ("2", "ffn2")):
        W["wg" + n_] = np.stack([_tile_cols(f(inp[a + "_w_gate"][l]), NFC) for l in range(L)])
        W["wu" + n_] = np.stack([_tile_cols(f(inp[a + "_w_up"][l]), NFC) for l in range(L)])
        wd = f(inp[a + "_w_down"])
        W["wd" + n_] = np.stack([np.stack([_tile_rows(wd[l][:, i * 128:(i + 1) * 128]) for i in range(8)]) for l in range(L)])
    win = f(inp["w_in"])
    sz = (512, 768, 16, 384, 512, 512, 512, 512, 3072)
    o = np.cumsum((0,) + sz)
    z, xbc, dt, u, q, k, v, g, gates = [win[:, :, o[i]:o[i + 1]] for i in range(9)]
    fm = np.concatenate([xbc, u, gates], axis=2)
    W["win_fm"] = np.stack([_tile_cols(fm[l], NFM) for l in range(L)])
    tm = np.concatenate([z, q, k, v, g, dt], axis=2)
    W["win_tm"] = np.stack([_tile_rows(tm[l]) for l in range(L)])
    W["bgate"] = np.ascontiguousarray(f(inp["b_gate"]).reshape(L, 24, 128).transpose(0, 2, 1))
    cw = f(inp["ssd_conv_w"])
    W["conv_w"] = np.ascontiguousarray(cw.reshape(L, 5, 6, 128).transpose(0, 3, 2, 1).reshape(L, 128, 30))
    W["conv_b"] = np.ascontiguousarray(f(inp["ssd_conv_b"]).reshape(L, 6, 128).transpose(0, 2, 1))
    rep = lambda a: np.ascontiguousarray(np.broadcast_to(a[:, None, :], (L, 128, a.shape[-1])))
    W["dtb_rep"] = rep(f(inp["ssd_dt_bias"]).reshape(L, 16))
    W["alog_rep"] = rep(f(inp["ssd_a_log"]).reshape(L, 16))
    W["dskip_rep"] = rep(f(inp["ssd_d"]))
    W["ssdnorm_rep"] = rep(f(inp["ssd_norm"]))
    W["retnorm_rep"] = rep(f(inp["ret_norm"]))
    W["wbr_ssd"] = np.stack([_tile_rows(f(inp["w_br_ssd"][l])) for l in range(L)])
    W["wbr_ret"] = np.stack([_tile_rows(f(inp["w_br_ret"][l])) for l in range(L)])
    W["wbr_s5"] = np.stack([_tile_rows(f(inp["w_br_s5"][l])) for l in range(L)])
    W["glu_wv"] = np.stack([_tile_rows(f(inp["s5_glu_wv"][l])) for l in range(L)])
    W["glu_wg"] = np.stack([_tile_rows(f(inp["s5_glu_wg"][l])) for l in range(L)])
    W["wout"] = np.stack([_tile_rows(f(inp["w_out"][l])) for l in range(L)])
    st = lambda a: np.ascontiguousarray(a.reshape(L, 2, 12, 128).transpose(0, 1, 3, 2))
    W["s5_lre"], W["s5_lim"] = st(f(inp["s5_lam_re"])), st(f(inp["s5_lam_im"]))
    ls = np.broadcast_to(f(inp["s5_log_step"])[..., None], (L, 2, 24, 64))
    W["s5_lstep"] = st(np.ascontiguousarray(ls))

    def bpad(b):
        out = np.zeros((L, 128, 12, 128), np.float32)
        for g in range(24):
            it, g2, g8 = g // 2, g % 2, g % 8
            out[:, g2 * 64:(g2 + 1) * 64, it, g8 * 16:(g8 + 1) * 16] = b[:, g]
        return out.reshape(L, 128, 12 * 128)
    W["s5_bre"], W["s5_bim"] = bpad(f(inp["s5_b_re"])), bpad(f(inp["s5_b_im"]))

    def cpad(c):
        out = np.zeros((L, 2, 128, 12, 32), np.float32)
        for g in range(24):
            it, g2 = g // 2, g % 2
            out[:, :, g2 * 64:(g2 + 1) * 64, it, g2 * 16:(g2 + 1) * 16] = c[:, :, g].transpose(0, 1, 3, 2)
        return out.reshape(L, 2, 128, 12 * 32)
    W["s5_cre"], W["s5_cim"] = cpad(f(inp["s5_c_re"])), cpad(f(inp["s5_c_im"]))
    dd = f(inp["s5_d"])
    ds = np.zeros((L, 128, 12, 32), np.float32)
    for g in range(24):
        it, g2, g8 = g // 2, g % 2, g % 8
        for h in range(16):
            ds[:, g8 * 16 + h, it, g2 * 16 + h] = dd[:, g, h]
    W["s5_dsel"] = ds.reshape(L, 128, 12 * 32)
    return W


_CACHE = {}
STAGES = None


def _swap_dirs(W):
    V = dict(W)
    sw16 = lambda a: np.ascontiguousarray(np.concatenate([a[..., 8:16], a[..., 0:8]], axis=-1))
    V["dtb_rep"] = sw16(W["dtb_rep"])
    V["alog_rep"] = sw16(W["alog_rep"])
    wt = W["win_tm"].reshape(W["win_tm"].shape[0], 128, 8, NTM).copy()
    wt[..., 2560:2576] = sw16(wt[..., 2560:2576])
    V["win_tm"] = wt.reshape(W["win_tm"].shape)
    cw = W["conv_w"].reshape(-1, 128, 6, 5)
    V["conv_w"] = np.ascontiguousarray(cw[..., ::-1]).reshape(W["conv_w"].shape)
    for k in ("s5_lre", "s5_lim", "s5_lstep", "s5_cre", "s5_cim"):
        V[k] = np.ascontiguousarray(W[k][:, ::-1])
    return V


def run_model(inp, L, pair=True, dbg=()):
    x = np.asarray(inp["x"], dtype=np.float32)
    nseq, Sfull = x.shape[0], x.shape[1]
    W = _prep_weights(inp, L)
    if not pair:
        S = Sfull
        ncore = 8 if nseq == 4 else nseq
        key = (S, L, tuple(dbg), 0)
        if key not in _CACHE:
            _CACHE[key] = build_program(S, L, dbg)
        W.update(_consts(S))
        maps = []
        for c in range(ncore):
            m = dict(W)
            m["xT"] = np.ascontiguousarray(x[c % nseq].T)
            maps.append(m)
        res = run_bass_kernel_spmd(_CACHE[key], maps, core_ids=list(range(ncore)))
        out = np.stack([np.ascontiguousarray(res.results[c]["outT"].T) for c in range(nseq)])
        return out, res
    S = Sfull // 2
    ncore = 2 * nseq
    key = (S, L, tuple(dbg), ncore)
    if key not in _CACHE:
        _CACHE[key] = build_program(S, L, dbg, pair=ncore)
    C0 = _consts(S)
    Wn = dict(W)
    Wn.update(C0)
    Wr = _swap_dirs(W)
    Wr.update(C0)
    idx = np.arange(128)
    s_, l_ = idx[:, None].astype(np.float32), idx[None, :].astype(np.float32)
    lg = np.log1p(-np.exp2(-5.0 - np.arange(8, dtype=np.float32))).astype(np.float32)
    rm = np.zeros((128, 2, 8, 128), np.float32)
    for h in range(8):
        rm[:, 0, h, :] = np.where(l_ > s_, np.exp(lg[h] * np.where(l_ > s_, l_ - s_, 0.0)), 0.0)
        rm[:, 1, h, :] = np.where(s_ >= l_, np.exp(lg[h] * np.where(s_ >= l_, s_ - l_, 0.0)), 0.0)
    Wr["c_retmask"] = rm.reshape(128, 2048)
    inv = (10000.0 ** (-np.arange(0, 64, 2, dtype=np.float32) / 64)).astype(np.float32)

    def rope(pos):
        ang = pos.astype(np.float32)[:, None] * inv[None, :]
        return np.concatenate([np.cos(ang), np.sin(ang)], axis=1).astype(np.float32)
    Wn["c_rope"] = rope(np.arange(S))
    Wr["c_rope"] = rope(Sfull - 1 - np.arange(S))
    Wn["pairsel"] = np.ascontiguousarray(np.broadcast_to(np.array([0.0, 1.0], np.float32), (128, 2)))
    Wr["pairsel"] = np.ascontiguousarray(np.broadcast_to(np.array([1.0, 0.0], np.float32), (128, 2)))
    maps = []
    for c in range(ncore):
        b, hf = c // 2, c % 2
        m = dict(Wn if hf == 0 else Wr)
        xs = x[b, :S] if hf == 0 else x[b, S:][::-1]
        m["xT"] = np.ascontiguousarray(xs.T)
        maps.append(m)
    res = run_bass_kernel_spmd(_CACHE[key], maps, core_ids=list(range(ncore)))
    out = np.empty((nseq, Sfull, D), np.float32)
    for c in range(ncore):
        b, hf = c // 2, c % 2
        o = res.results[c]["outT"].T
        if hf == 0:
            out[b, :S] = o
        else:
            out[b, S:] = o[::-1]
    return out, res


def kernel(**inputs):
    out, _ = run_model(inputs, 2, pair=False)
    return out.astype(np.float32)
```

```python
import math
from contextlib import ExitStack
import numpy as np
import concourse.bass as bass
import concourse.mybir as mybir
from concourse.bass_utils import run_bass_kernel_spmd

F32 = mybir.dt.float32
F32R = mybir.dt.float32r
ALU = mybir.AluOpType
AF = mybir.ActivationFunctionType
AX = mybir.AxisListType

ENGS = ("pe", "dve", "act", "pool", "sp")
N_DMA_SEMS = 12
D = 1024
DFF = 2816
NFC = 22
EPS = 1e-6
NTM = 2576
NFM = 33


class Buf:
    __slots__ = ("name", "w", "r")

    def __init__(self, name=""):
        self.name = name
        self.w = None
        self.r = []


class Prog:
    def __init__(self, nc):
        self.nc = nc
        self.ops = []
        self.by_eng = {e: [] for e in ENGS}
        self.fence_deps = set()
        self.fence_pending = {e: False for e in ENGS}
        self.since_fence = []
        self.trace = None

    def op(self, eng, fn, reads=(), writes=(), dma=False):
        oid = len(self.ops)
        deps = set()
        for b in reads:
            if b.w is not None:
                deps.add(b.w)
        for b in writes:
            if b.w is not None:
                deps.add(b.w)
            deps.update(b.r)
        for b in reads:
            if not dma:
                b.r = [r for r in b.r if self.ops[r]["dma"] or self.ops[r]["eng"] != eng]
            b.r.append(oid)
        for b in writes:
            b.w = oid
            b.r = []
        if self.fence_pending[eng]:
            deps.update(self.fence_deps)
            self.fence_pending[eng] = False
        deps.discard(oid)
        import sys as _s
        fr = _s._getframe(1)
        ln = []
        while fr is not None and len(ln) < 4:
            ln.append(fr.f_lineno)
            fr = fr.f_back
        self.ops.append(dict(eng=eng, fn=fn, deps=deps, dma=dma, id=oid, ln=ln))
        self.by_eng[eng].append(oid)
        self.since_fence.append(oid)
        return oid

    def fence(self):
        last = {}
        deps = set()
        for oid in self.since_fence:
            o = self.ops[oid]
            if o["dma"]:
                deps.add(oid)
            else:
                last[o["eng"]] = oid
        deps.update(last.values())
        for e in ENGS:
            if self.fence_pending[e]:
                deps.update(self.fence_deps)
                break
        self.fence_deps = deps
        self.fence_pending = {e: True for e in ENGS}
        self.since_fence = []

    def emit(self):
        nc = self.nc
        ops = self.ops
        signaled = set()
        for o in ops:
            for d in o["deps"]:
                if o["eng"] == "pe" and ops[d]["eng"] == "pe" and not ops[d]["dma"] and not o["dma"]:
                    continue
                signaled.add(d)
        eng_cnt = {e: 0 for e in ENGS}
        dma_rr = {e: 0 for e in ENGS}
        dma_cnt = {e: [0] * N_DMA_SEMS for e in ENGS}
        tokens = {}
        dma_prev = {}
        for o in ops:
            e = o["eng"]
            if o["dma"]:
                i = dma_rr[e] % N_DMA_SEMS
                dma_rr[e] += 1
                prev = dma_cnt[e][i]
                dma_cnt[e][i] += 16
                tokens[o["id"]] = (("dma", e, i), dma_cnt[e][i])
                if prev:
                    dma_prev[o["id"]] = (("dma", e, i), prev)
            elif o["id"] in signaled:
                eng_cnt[e] += 1
                tokens[o["id"]] = (("eng", e), eng_cnt[e])
        final_waits = {e: {} for e in ENGS}
        for o in ops:
            if o["dma"]:
                k, v = tokens[o["id"]]
                final_waits[o["eng"]][k] = max(final_waits[o["eng"]].get(k, 0), v)
        used = sorted(set(k for k, _ in tokens.values()), key=str)
        self.eng_cnt = eng_cnt
        with ExitStack() as st:
            sems = {k: st.enter_context(nc.semaphore("s_" + "_".join(map(str, k)))) for k in used}
            block = st.enter_context(nc.Block())
            handles = {"pe": block.tensor, "dve": block.vector, "act": block.scalar,
                       "pool": block.gpsimd, "sp": block.sync}

            def make(e):
                def body(eng):
                    seen = {}
                    for oid in self.by_eng[e]:
                        o = ops[oid]
                        waits = {}
                        for d in o["deps"]:
                            if ops[d]["eng"] == e and not ops[d]["dma"] and e == "pe" and not o["dma"]:
                                continue
                            k, v = tokens[d]
                            waits[k] = max(waits.get(k, 0), v)
                        if oid in dma_prev:
                            k, v = dma_prev[oid]
                            waits[k] = max(waits.get(k, 0), v)
                        for k, v in waits.items():
                            if seen.get(k, 0) >= v:
                                continue
                            seen[k] = v
                            eng.wait_ge(sems[k], v)
                        try:
                            ins = o["fn"](eng)
                        except BaseException:
                            print("FAILED OP lines", o["ln"], "eng", e)
                            raise
                        if self.trace is not None:
                            try:
                                self.trace[ins.ins.name] = o["ln"]
                            except Exception:
                                pass
                        if oid in tokens:
                            k, v = tokens[oid]
                            ins.then_inc(sems[k], 16 if o["dma"] else 1)
                    for k, v in final_waits[e].items():
                        if seen.get(k, 0) < v:
                            eng.wait_ge(sems[k], v)
                return body

            for e in ENGS:
                if self.by_eng[e]:
                    handles[e](make(e))


class KB:
    def __init__(self, nc, st, nr, nf):
        self.nc = nc
        self.P = Prog(nc)
        self.arr = st.enter_context(nc.sbuf_tensor("arr", [128, nr], F32R))
        self.arf = st.enter_context(nc.sbuf_tensor("arf", [128, nf], F32))
        self.cr = st.enter_context(nc.sbuf_tensor("cr", [128, 2048], F32R))
        self.cf = st.enter_context(nc.sbuf_tensor("cf", [128, 1344], F32))
        self.nr, self.nf = nr, nf
        self.pr = self.pf = 0
        self.ps = [st.enter_context(nc.psum_tensor("ps%d" % i, [128, 512], F32)) for i in range(8)]
        self.psb = [Buf("ps%d" % i) for i in range(8)]
        self.dq = 0

    def new_stage(self):
        self.P.fence()
        self.pr = self.pf = 0

    def R(self, n, shape=None):
        a = self.arr[:, self.pr:self.pr + n]
        self.pr += n
        assert self.pr <= self.nr, ("arr overflow", self.pr)
        return a

    def Fm(self, n):
        a = self.arf[:, self.pf:self.pf + n]
        self.pf += n
        assert self.pf <= self.nf, ("arf overflow", self.pf)
        return a

    def dma(self, out, in_, reads=(), writes=(), q=None):
        if q is None:
            q = "pool"
        return self.P.op(q, lambda e, o=out, i=in_: e.dma_start(out=o, in_=i), reads, writes, dma=True)

    def mm(self, out, lhsT, rhs, start, stop, reads=(), writes=()):
        return self.P.op("pe", lambda e, o=out, l=lhsT, r=rhs, s=start, t=stop: e.matmul(o, l, r, start=s, stop=t),
                         reads, writes)

    def act(self, out, in_, func, reads=(), writes=(), bias=None, scale=None):
        kw = {}
        if bias is not None:
            kw["bias"] = bias
        if scale is not None:
            kw["scale"] = scale
        return self.P.op("act", lambda e, o=out, i=in_, f=func, kw=kw: e.activation(o, i, f, **kw), reads, writes)

    def tt(self, eng, out, in0, in1, op, reads=(), writes=()):
        return self.P.op(eng, lambda e, o=out, a=in0, b=in1, p=op: e.tensor_tensor(o, a, b, p), reads, writes)

    def ts(self, eng, out, in0, s1, s2, op0, op1=None, reads=(), writes=()):
        if op1 is None:
            return self.P.op(eng, lambda e, o=out, a=in0, x=s1, p=op0: e.tensor_scalar(o, a, x, None, p), reads, writes)
        return self.P.op(eng, lambda e, o=out, a=in0, x=s1, y=s2, p=op0, q=op1: e.tensor_scalar(o, a, x, y, p, q),
                         reads, writes)

    def stt(self, eng, out, in0, scalar, in1, op0, op1, reads=(), writes=()):
        return self.P.op(eng, lambda e, o=out, a=in0, s=scalar, b=in1, p=op0, q=op1:
                         e.scalar_tensor_tensor(o, a, s, b, p, q), reads, writes)

    def copy(self, eng, out, in_, reads=(), writes=()):
        if eng == "act":
            return self.act(out, in_, AF.Copy, reads, writes)
        return self.P.op(eng, lambda e, o=out, i=in_: e.tensor_copy(o, i), reads, writes)

    def memset(self, eng, out, val, writes=()):
        return self.P.op(eng, lambda e, o=out, v=val: e.memset(o, v), (), writes)

    def recip(self, out, in_, reads=(), writes=()):
        return self.P.op("dve", lambda e, o=out, i=in_: e.reciprocal(o, i), reads, writes)

    def scan(self, out, d0, d1, init, reads=(), writes=()):
        return self.P.op("dve", lambda e, o=out, a=d0, b=d1, i=init: e.tensor_tensor_scan(o, a, b, i, ALU.mult, ALU.add),
                         reads, writes)


def v3(ap, a):
    return ap.rearrange("p (a b) -> p a b", a=a)


def build_program(S, L, dbg=(), pair=0):
    nc = bass.Bass("TRN2", target_bir_lowering=False)
    NCH = S // 128
    NT = S // 512
    assert S % 512 == 0

    def din(name, shape):
        return nc.dram_tensor(name, list(shape), F32, kind="ExternalInput").ap()

    def dscr(name, shape):
        kind = "ExternalOutput" if name in dbg else "Internal"
        return nc.dram_tensor(name, list(shape), F32, kind=kind).ap()

    I = {}
    I["xT"] = din("xT", [D, S])
    for nm, shp in [("g_ffn1", [L, 128, 8]), ("g_mix", [L, 128, 8]), ("g_ffn2", [L, 128, 8]), ("g_final", [128, 8]),
                    ("wg1", [L, NFC, 128, 1024]), ("wu1", [L, NFC, 128, 1024]), ("wd1", [L, 8, 128, DFF]),
                    ("wg2", [L, NFC, 128, 1024]), ("wu2", [L, NFC, 128, 1024]), ("wd2", [L, 8, 128, DFF]),
                    ("win_fm", [L, NFM, 128, 1024]), ("win_tm", [L, 128, 8 * NTM]), ("bgate", [L, 128, 24]),
                    ("conv_w", [L, 128, 30]), ("conv_b", [L, 128, 6]),
                    ("dtb_rep", [L, 128, 16]), ("alog_rep", [L, 128, 16]), ("dskip_rep", [L, 128, 8]),
                    ("ssdnorm_rep", [L, 128, 512]), ("retnorm_rep", [L, 128, 512]),
                    ("wbr_ssd", [L, 128, 4096]), ("wbr_ret", [L, 128, 4096]), ("wbr_s5", [L, 128, 3072]),
                    ("glu_wv", [L, 128, 1152]), ("glu_wg", [L, 128, 1152]), ("wout", [L, 128, 8192]),
                    ("s5_lre", [L, 2, 128, 12]), ("s5_lim", [L, 2, 128, 12]), ("s5_lstep", [L, 2, 128, 12]),
                    ("s5_bre", [L, 128, 12 * 128]), ("s5_bim", [L, 128, 12 * 128]),
                    ("s5_cre", [L, 2, 128, 12 * 32]), ("s5_cim", [L, 2, 128, 12 * 32]), ("s5_dsel", [L, 128, 12 * 32]),
                    ("c_ident", [128, 128]), ("c_tri", [128, 4 * 128]), ("c_maskneg", [128, 2 * 512]),
                    ("c_negiexp", [16, 16 * 128]), ("c_iexp", [16, 16 * 128]), ("c_retmask", [128, 16 * 128]),
                    ("c_retdec", [128, 2 * 8 + 2 * 8 + 8]), ("c_rope", [S, 64]), ("c_blk64", [128, 128])]:
        I[nm] = din(nm, shp)
    outT = nc.dram_tensor("outT", [D, S], F32, kind="ExternalOutput").ap()
    if pair:
        I["pairsel"] = din("pairsel", [128, 2])
        HS = nc.dram_tensor("HS", [128, 12], F32).ap()
        HR = nc.dram_tensor("HR", [256, 12], F32).ap()
        SS = nc.dram_tensor("SS", [128, 1048], F32).ap()
        SR = nc.dram_tensor("SR", [256, 1048], F32).ap()
        rgroups = [[2 * i, 2 * i + 1] for i in range(pair // 2)]

    X = dscr("X", [D, S])
    PF = dscr("PF", [NFM * 128, S])
    PT = dscr("PT", [S, NTM])
    XS = dscr("XS", [S, 512])
    BTM = dscr("BTM", [S, 128])
    BCT = dscr("BCT", [256, S])
    YF = dscr("YF", [S, 512])
    YTS = dscr("YTS", [512, S])
    QKT = dscr("QKT", [1024, S])
    KTM = dscr("KTM", [S, 512])
    RF = dscr("RF", [S, 512])
    YTR = dscr("YTR", [512, S])
    Y5 = dscr("Y5", [384, S])
    Y5A = dscr("Y5A", [384, S])
    Y5B = dscr("Y5B", [384, S])
    QK2 = dscr("QK2", [S // 128, 64, 2048])

    with ExitStack() as st:
        kb = KB(nc, st, 33280, 14336)
        P = kb.P
        ps, psb = kb.ps, kb.psb

        cbuf = Buf("consts")
        ident = kb.cr[:, 0:128]
        tri = kb.cr[:, 128:640]
        ones = kb.cr[:, 640:768]
        blk64 = kb.cr[:, 768:896]
        iexp = kb.cr[0:16, 896:896 + 0]
        maskneg = kb.cf[:, 0:1024]
        kb.dma(ident, I["c_ident"], writes=[cbuf])
        kb.dma(tri, I["c_tri"], writes=[cbuf])
        kb.dma(blk64, I["c_blk64"], writes=[cbuf])
        kb.dma(maskneg, I["c_maskneg"], writes=[cbuf], q="sp")
        identf = kb.cf[:, 1024:1152]
        kb.dma(identf, I["c_ident"], writes=[cbuf], q="sp")
        onesf = kb.cf[:, 1152:1280]
        kb.memset("dve", onesf, 1.0, writes=[cbuf])
        kb.copy("dve", ones, onesf, reads=[cbuf], writes=[cbuf])

        halo = kb.cf[:, 1280:1292]
        psel = kb.cf[:, 1292:1294]
        bhalo = Buf("halo")
        bSS, bSR = Buf("SS"), Buf("SR")
        if pair:
            kb.dma(psel, I["pairsel"], writes=[cbuf], q="sp")

        def coll(src, dst, reads, writes):
            return P.op("pool", lambda e, a=src, b=dst: e.collective_compute("AllGather", ALU.bypass, rgroups, [a], [b]),
                        reads, writes, dma=True)

        def recv_combine(out, c0, c1, rows, tmp0, tmp1, wbuf):
            kb.dma(tmp0, SR[0:rows, c0:c1], reads=[bSR], writes=[wbuf])
            kb.dma(tmp1, SR[128:128 + rows, c0:c1], reads=[bSR], writes=[wbuf])
            kb.ts("dve", tmp0, tmp0.bitcast(F32), psel[0:rows, 0:1], None, ALU.mult, reads=[wbuf, cbuf], writes=[wbuf])
            kb.stt("dve", out, tmp1.bitcast(F32), psel[0:rows, 1:2], tmp0.bitcast(F32), ALU.mult, ALU.add,
                   reads=[wbuf, cbuf], writes=[wbuf])

        def halo_exchange():
            kb.new_stage()
            hb = kb.Fm(12)
            hr = kb.Fm(24)
            b = Buf()
            bHS, bHR = Buf(), Buf()
            for f in range(6):
                kb.dma(hb[:, 2 * f:2 * f + 2], PF[f * 128:(f + 1) * 128, S - 2:S], reads=[bPF], writes=[b], q="sp")
            kb.dma(HS[:, :], hb, reads=[b], writes=[bHS], q="sp")
            coll(HS[:, :], HR[:, :], [bHS], [bHR])
            kb.dma(hr[:, 0:12], HR[0:128, :], reads=[bHR], writes=[b], q="sp")
            kb.dma(hr[:, 12:24], HR[128:256, :], reads=[bHR], writes=[b], q="sp")
            kb.ts("dve", hr[:, 0:12], hr[:, 0:12], psel[:, 0:1], None, ALU.mult, reads=[b, cbuf], writes=[b])
            kb.stt("dve", halo, hr[:, 12:24], psel[:, 1:2], hr[:, 0:12], ALU.mult, ALU.add, reads=[b, cbuf], writes=[bhalo])

        def state_exchange():
            kb.new_stage()
            coll(SS[:, :], SR[:, :], [bSS], [bSR])

        xbufs = {}

        def xb(k, t):
            key = (k, t)
            if key not in xbufs:
                xbufs[key] = Buf("x%d_%d" % key)
            return xbufs[key]

        def load_norm(src, t, gain_ap, pfx):
            xf = v3(kb.Fm(4096), 8)
            xn = v3(kb.R(4096), 8)
            sq = [kb.R(512), kb.R(512)]
            sqb = [Buf(), Buf()]
            rstd = kb.Fm(512)
            g = kb.Fm(8)
            bx, bn, br, bg = Buf(), Buf(), Buf(), Buf()
            kb.dma(g, gain_ap, writes=[bg], q="sp")
            bxk = [Buf() for _ in range(8)]
            for k in range(8):
                kb.dma(xf[:, k, :], src[k * 128:(k + 1) * 128, t * 512:(t + 1) * 512], reads=[xb(k, t)],
                       writes=[bxk[k]], q="sp")
                kb.act(sq[k % 2], xf[:, k, :], AF.Square, reads=[bxk[k]], writes=[sqb[k % 2]])
                kb.mm(ps[7][:], ones, sq[k % 2], k == 0, k == 7, reads=[sqb[k % 2], cbuf], writes=[psb[7]])
            kb.ts("dve", rstd, ps[7][:], 1.0 / D, EPS, ALU.mult, ALU.add, reads=[psb[7]], writes=[br])
            kb.act(rstd, rstd, AF.Sqrt, reads=[br], writes=[br])
            kb.recip(rstd, rstd, reads=[br], writes=[br])
            for k in range(8):
                kb.stt("dve", xn[:, k, :], xf[:, k, :], g[:, k:k + 1], rstd, ALU.mult, ALU.mult,
                       reads=[bxk[k], br, bg], writes=[bn])
            return xn, bn, xf, bxk

        def ffn_stage(l, gname, wg, wu, wd, src=None):
            Xs = X if src is None else src
            TT = 1024
            assert S % TT == 0
            HF = NFC // 2
            for t in range(S // TT):
                kb.new_stage()
                c0 = t * TT
                xn = v3(kb.R(8192), 8)
                sq = [kb.R(1024), kb.R(1024)]
                sqb = [Buf(), Buf()]
                rstd = kb.Fm(1024)
                g = kb.Fm(8)
                acc = v3(kb.Fm(8192), 8)
                bacc = [Buf() for _ in range(8)]
                bn, br, bg = Buf(), Buf(), Buf()
                kb.dma(g, I[gname][l], writes=[bg], q="sp")
                for k in range(8):
                    kb.dma(acc[:, k, :], Xs[k * 128:(k + 1) * 128, c0:c0 + TT],
                           reads=([xb(k, 2 * t), xb(k, 2 * t + 1)] if src is None else []), writes=[bacc[k]], q="sp")
                    kb.act(sq[k % 2], acc[:, k, :], AF.Square, reads=[bacc[k]], writes=[sqb[k % 2]])
                    for hh in range(2):
                        kb.mm(ps[6 + hh][:], ones, sq[k % 2][:, hh * 512:(hh + 1) * 512], k == 0, k == 7,
                              reads=[sqb[k % 2], cbuf], writes=[psb[6 + hh]])
                for hh in range(2):
                    kb.ts("dve", rstd[:, hh * 512:(hh + 1) * 512], ps[6 + hh][:], 1.0 / D, EPS, ALU.mult, ALU.add,
                          reads=[psb[6 + hh]], writes=[br])
                kb.act(rstd, rstd, AF.Sqrt, reads=[br], writes=[br])
                kb.recip(rstd, rstd, reads=[br], writes=[br])
                for k in range(8):
                    kb.stt("dve", xn[:, k, :], acc[:, k, :], g[:, k:k + 1], rstd, ALU.mult, ALU.mult,
                           reads=[bacc[k], br, bg], writes=[bn])
                actt = v3(kb.R(HF * 1024), HF)
                bact = [Buf() for _ in range(HF)]
                wgb = [kb.R(1024) for _ in range(2)]
                wub = [kb.R(1024) for _ in range(2)]
                bwg = [Buf(), Buf()]
                bwu = [Buf(), Buf()]
                sg = [kb.Fm(512), kb.Fm(512)]
                bsg = [Buf(), Buf()]
                wdb = [kb.R(HF * 128) for _ in range(2)]
                bwd = [Buf(), Buf()]
                xr = [kb.Fm(1024), kb.Fm(1024)]
                bxr = [Buf(), Buf()]
                xo = [kb.Fm(512), kb.Fm(512)]
                bxo = [Buf(), Buf()]
                n = 0
                m = 0
                for half in range(2):
                    f0 = half * HF
                    for jj in range(HF):
                        j = f0 + jj
                        kb.dma(wgb[j % 2], wg[l, j], writes=[bwg[j % 2]])
                        kb.dma(wub[j % 2], wu[l, j], writes=[bwu[j % 2]])
                        for hh in range(2):
                            pg, pu = ps[n % 2], ps[2 + n % 2]
                            bpg, bpu = psb[n % 2], psb[2 + n % 2]
                            for k in range(8):
                                kb.mm(pg[:], wgb[j % 2][:, k * 128:(k + 1) * 128], xn[:, k, hh * 512:(hh + 1) * 512],
                                      k == 0, k == 7, reads=[bwg[j % 2], bn], writes=[bpg])
                            for k in range(8):
                                kb.mm(pu[:], wub[j % 2][:, k * 128:(k + 1) * 128], xn[:, k, hh * 512:(hh + 1) * 512],
                                      k == 0, k == 7, reads=[bwu[j % 2], bn], writes=[bpu])
                            kb.act(sg[n % 2], pg[:], AF.Silu, reads=[bpg], writes=[bsg[n % 2]])
                            kb.tt("dve", actt[:, jj, hh * 512:(hh + 1) * 512], sg[n % 2], pu[:], ALU.mult,
                                  reads=[bsg[n % 2], bpu], writes=[bact[jj]])
                            n += 1
                    for i in range(8):
                        kb.dma(wdb[i % 2], wd[l, i][:, f0 * 128:(f0 + HF) * 128], writes=[bwd[i % 2]])
                        if half == 1:
                            kb.dma(xr[i % 2], Xs[i * 128:(i + 1) * 128, c0:c0 + TT],
                                   reads=([xb(i, 2 * t), xb(i, 2 * t + 1)] if src is None else []), writes=[bxr[i % 2]], q="sp")
                        for hh in range(2):
                            po, bpo = ps[4 + m % 2], psb[4 + m % 2]
                            for jj in range(HF):
                                kb.mm(po[:], wdb[i % 2][:, jj * 128:(jj + 1) * 128], actt[:, jj, hh * 512:(hh + 1) * 512],
                                      jj == 0, jj == HF - 1, reads=[bwd[i % 2], bact[jj]], writes=[bpo])
                            av = acc[:, i, hh * 512:(hh + 1) * 512]
                            if half == 0:
                                kb.copy("act", av, po[:], reads=[bpo], writes=[bacc[i]])
                            else:
                                kb.tt("dve", av, av, po[:], ALU.add, reads=[bpo, bacc[i]], writes=[bacc[i]])
                                kb.stt("dve", xo[m % 2], av, 0.5, xr[i % 2][:, hh * 512:(hh + 1) * 512], ALU.mult, ALU.add,
                                       reads=[bacc[i], bxr[i % 2]], writes=[bxo[m % 2]])
                                kb.dma(X[i * 128:(i + 1) * 128, c0 + hh * 512:c0 + (hh + 1) * 512], xo[m % 2],
                                       reads=[bxo[m % 2]], writes=[xb(i, 2 * t + hh)], q="sp")
                            m += 1

        bPF = Buf("PF")
        bPT = Buf("PT")

        def inproj_stage(l):
            TT = 1024
            assert S % TT == 0
            for t in range(S // TT):
                kb.new_stage()
                c0t = t * TT
                xn = v3(kb.R(8192), 8)
                sq = [kb.R(1024), kb.R(1024)]
                sqb = [Buf(), Buf()]
                rstd = kb.Fm(1024)
                g = kb.Fm(8)
                xf = v3(kb.Fm(8192), 8)
                bxk = [Buf() for _ in range(8)]
                bn, br, bg = Buf(), Buf(), Buf()
                kb.dma(g, I["g_mix"][l], writes=[bg], q="sp")
                for k in range(8):
                    kb.dma(xf[:, k, :], X[k * 128:(k + 1) * 128, c0t:c0t + TT], reads=[xb(k, 2 * t), xb(k, 2 * t + 1)],
                           writes=[bxk[k]], q="sp")
                    kb.act(sq[k % 2], xf[:, k, :], AF.Square, reads=[bxk[k]], writes=[sqb[k % 2]])
                    for hh in range(2):
                        kb.mm(ps[6 + hh][:], ones, sq[k % 2][:, hh * 512:(hh + 1) * 512], k == 0, k == 7,
                              reads=[sqb[k % 2], cbuf], writes=[psb[6 + hh]])
                for hh in range(2):
                    kb.ts("dve", rstd[:, hh * 512:(hh + 1) * 512], ps[6 + hh][:], 1.0 / D, EPS, ALU.mult, ALU.add,
                          reads=[psb[6 + hh]], writes=[br])
                kb.act(rstd, rstd, AF.Sqrt, reads=[br], writes=[br])
                kb.recip(rstd, rstd, reads=[br], writes=[br])
                for k in range(8):
                    kb.stt("dve", xn[:, k, :], xf[:, k, :], g[:, k:k + 1], rstd, ALU.mult, ALU.mult,
                           reads=[bxk[k], br, bg], writes=[bn])
                bgt = kb.Fm(24)
                bbg = Buf()
                kb.dma(bgt, I["bgate"][l], writes=[bbg], q="sp")
                wb = [kb.R(1024) for _ in range(2)]
                bw = [Buf(), Buf()]
                so = [kb.Fm(512), kb.Fm(512)]
                bso = [Buf(), Buf()]
                n = 0
                for j in range(NFM):
                    kb.dma(wb[j % 2], I["win_fm"][l, j], writes=[bw[j % 2]])
                    for hh in range(2):
                        pp, bp = ps[n % 2], psb[n % 2]
                        for k in range(8):
                            kb.mm(pp[:], wb[j % 2][:, k * 128:(k + 1) * 128], xn[:, k, hh * 512:(hh + 1) * 512], k == 0, k == 7,
                                  reads=[bw[j % 2], bn], writes=[bp])
                        if j >= 9:
                            kb.act(so[n % 2], pp[:], AF.Sigmoid, reads=[bp, bbg], writes=[bso[n % 2]],
                                   bias=bgt[:, j - 9:j - 8])
                        else:
                            kb.copy("dve", so[n % 2], pp[:], reads=[bp], writes=[bso[n % 2]])
                        kb.dma(PF[j * 128:(j + 1) * 128, c0t + hh * 512:c0t + (hh + 1) * 512], so[n % 2], reads=[bso[n % 2]],
                               writes=[bPF], q="sp")
                        n += 1
                dtb = kb.Fm(16)
                bdtb = Buf()
                kb.dma(dtb, I["dtb_rep"][l], writes=[bdtb], q="sp")
                wt = [kb.R(8 * 512) for _ in range(2)]
                bwt = [Buf(), Buf()]
                st_ = [kb.Fm(512) for _ in range(2)]
                bst = [Buf(), Buf()]
                cnt = 0
                for cb in range(6):
                    c0 = cb * 512
                    w = 512 if cb < 5 else 16
                    wv = v3(wt[cb % 2], 8)
                    kb.dma(wv[:, :, 0:w], v3(I["win_tm"][l], 8)[:, :, c0:c0 + w], writes=[bwt[cb % 2]])
                    for c in range(TT // 128):
                        pp = ps[2 + cnt % 2]
                        bp = psb[2 + cnt % 2]
                        for k in range(8):
                            kb.mm(pp[:, 0:w], xn[:, k, c * 128:(c + 1) * 128], wv[:, k, 0:w], k == 0, k == 7,
                                  reads=[bwt[cb % 2], bn], writes=[bp])
                        o = st_[cnt % 2][:, 0:w]
                        bo = bst[cnt % 2]
                        if cb in (0, 4):
                            kb.act(o, pp[:, 0:w], AF.Silu, reads=[bp], writes=[bo])
                        elif cb == 5:
                            kb.tt("dve", o, pp[:, 0:w], dtb, ALU.add, reads=[bp, bdtb], writes=[bo])
                            kb.act(o, o, AF.Exp, reads=[bo], writes=[bo])
                            kb.act(o, o, AF.Ln, reads=[bo], writes=[bo], bias=1.0)
                        else:
                            kb.copy("dve", o, pp[:, 0:w], reads=[bp], writes=[bo])
                        r0 = c0t + c * 128
                        kb.dma(PT[r0:r0 + 128, c0:c0 + w], o, reads=[bo], writes=[bPT], q="sp")
                        cnt += 1

        bXS, bBTM, bBCT = Buf("XS"), Buf("BTM"), Buf("BCT")

        def ssd_prep(l):
            kb.new_stage()
            cw = kb.Fm(30)
            cbias = kb.Fm(6)
            bcw = Buf()
            kb.dma(cw, I["conv_w"][l], writes=[bcw], q="sp")
            kb.dma(cbias, I["conv_b"][l], writes=[bcw], q="sp")
            xin = [kb.Fm(516) for _ in range(2)]
            bxin = [Buf(), Buf()]
            acc = [kb.Fm(512) for _ in range(2)]
            bacc = [Buf(), Buf()]
            cv = [kb.R(512) for _ in range(2)]
            bcv = [Buf(), Buf()]
            tm = [kb.Fm(512) for _ in range(2)]
            btm = [Buf(), Buf()]
            n = 0
            for t in range(NT):
                for f in range(6):
                    xi, bi = xin[n % 2], bxin[n % 2]
                    lo = t * 512 - 2
                    hi = t * 512 + 514
                    a, b = max(lo, 0), min(hi, S)
                    if a > lo:
                        kb.memset("pool", xi[:, 0:2], 0.0, writes=[bi])
                    if b < hi:
                        if pair:
                            kb.copy("pool", xi[:, 514:515], halo[:, 2 * f + 1:2 * f + 2], reads=[bhalo], writes=[bi])
                            kb.copy("pool", xi[:, 515:516], halo[:, 2 * f:2 * f + 1], reads=[bhalo], writes=[bi])
                        else:
                            kb.memset("pool", xi[:, 514:516], 0.0, writes=[bi])
                    kb.dma(xi[:, a - lo:b - lo], PF[f * 128:(f + 1) * 128, a:b], reads=[bPF], writes=[bi], q="sp")
                    ac, ba = acc[n % 2], bacc[n % 2]
                    kb.ts("dve", ac, xi[:, 0:512], cw[:, f * 5:f * 5 + 1], None, ALU.mult, reads=[bi, bcw], writes=[ba])
                    for j in range(1, 5):
                        kb.stt("dve", ac, xi[:, j:j + 512], cw[:, f * 5 + j:f * 5 + j + 1], ac, ALU.mult, ALU.add,
                               reads=[bi, bcw, ba], writes=[ba])
                    c_, bc_ = cv[n % 2], bcv[n % 2]
                    kb.act(c_, ac, AF.Silu, reads=[ba, bcw], writes=[bc_], bias=cbias[:, f:f + 1])
                    if f < 5:
                        for c in range(4):
                            pp, bp = ps[(4 * n + c) % 4], psb[(4 * n + c) % 4]
                            kb.mm(pp[:, 0:128], c_[:, c * 128:(c + 1) * 128], ident, True, True,
                                  reads=[bc_, cbuf], writes=[bp])
                            o, bo = tm[c % 2][:, 0:128], btm[c % 2]
                            kb.copy("act" if c % 2 else "dve", o, pp[:, 0:128], reads=[bp], writes=[bo])
                            r0 = t * 512 + c * 128
                            if f < 4:
                                kb.dma(XS[r0:r0 + 128, f * 128:(f + 1) * 128], o, reads=[bo], writes=[bXS], q="sp")
                            else:
                                kb.dma(BTM[r0:r0 + 128, :], o, reads=[bo], writes=[bBTM], q="sp")
                    if f >= 4:
                        kb.dma(BCT[(f - 4) * 128:(f - 3) * 128, t * 512:(t + 1) * 512], c_, reads=[bc_], writes=[bBCT])
                    n += 1

        bYF, bYTS = Buf("YF"), Buf("YTS")

        def ssd_pass(l, dirn):
            kb.new_stage()
            arep = kb.Fm(16)
            dsk = kb.Fm(8)
            gn = kb.Fm(512)
            bpar = Buf()
            kb.dma(arep, I["alog_rep"][l], writes=[bpar], q="sp")
            kb.dma(dsk, I["dskip_rep"][l], writes=[bpar], q="sp")
            kb.dma(gn, I["ssdnorm_rep"][l], writes=[bpar], q="sp")
            kb.act(arep, arep, AF.Exp, reads=[bpar], writes=[bpar])
            kb.ts("dve", arep, arep, -1.0, None, ALU.mult, reads=[bpar], writes=[bpar])
            niexp = v3(kb.Fm(2048)[0:16, :], 16)
            iex = v3(kb.Fm(2048)[0:16, :], 16)
            kb.dma(niexp, v3(I["c_negiexp"], 16), writes=[bpar], q="sp")
            kb.dma(iex, v3(I["c_iexp"], 16), writes=[bpar], q="sp")
            mk = v3(maskneg[:, dirn * 512:(dirn + 1) * 512], 4)
            Sst = kb.R(512)
            bS = Buf()
            if pair and dirn == 1:
                recv_combine(Sst, 0, 512, 128, kb.R(512), kb.R(512), bS)
            else:
                kb.ts("dve", Sst, onesf[:, 0:1].to_broadcast([128, 512]), 0.0, None, ALU.mult, reads=[cbuf], writes=[bS])
            NB = 2
            xs_ = [kb.Fm(512) for _ in range(NB)]
            dt_ = [kb.Fm(16) for _ in range(NB)]
            bt_ = [kb.R(128) for _ in range(NB)]
            bct = [kb.R(512) for _ in range(NB)]
            bin_ = [Buf() for _ in range(NB)]
            for i_ in range(NB):
                kb.ts("dve", bct[i_][:, 256:512], onesf[:, 0:1].to_broadcast([128, 256]), 0.0, None, ALU.mult,
                      reads=[cbuf], writes=[bin_[i_]])
            da = [kb.Fm(8) for _ in range(NB)]
            xd = [kb.R(512) for _ in range(NB)]
            xdw = [kb.R(512) for _ in range(NB)]
            csf = [kb.Fm(128) for _ in range(NB)]
            zt = [kb.Fm(1024) for _ in range(NB)]
            ee = [kb.Fm(1024) for _ in range(NB)]
            gt = [kb.Fm(256) for _ in range(NB)]
            mt = [kb.R(1024) for _ in range(NB)]
            ex = [kb.Fm(16) for _ in range(NB)]
            dec = [kb.Fm(8) for _ in range(NB)]
            yt_ = [kb.Fm(512) for _ in range(NB)]
            t2 = [kb.Fm(512) for _ in range(NB)]
            zz = [kb.Fm(512) for _ in range(NB)]
            yo = [kb.R(512) for _ in range(NB)]
            ss = [kb.Fm(8) for _ in range(NB)]
            bw_ = [[Buf() for _ in range(16)] for _ in range(NB)]
            order = range(NCH) if dirn == 0 else range(NCH - 1, -1, -1)
            for n, c in enumerate(order):
                i = n % NB
                B = bw_[i]
                r0 = c * 128
                kb.dma(xs_[i], XS[r0:r0 + 128, :], reads=[bXS], writes=[bin_[i]], q="sp")
                kb.dma(dt_[i], PT[r0:r0 + 128, 2560:2576], reads=[bPT], writes=[bin_[i]], q="sp")
                kb.dma(bt_[i], BTM[r0:r0 + 128, :], reads=[bBTM], writes=[bin_[i]])
                kb.dma(bct[i][:, 0:128], BCT[0:128, r0:r0 + 128], reads=[bBCT], writes=[bin_[i]])
                kb.dma(bct[i][:, 128:256], BCT[128:256, r0:r0 + 128], reads=[bBCT], writes=[bin_[i]])
                for g in range(2):
                    kb.dma(bct[i][g * 64:(g + 1) * 64, 256 + g * 128:384 + g * 128],
                           BCT[128 + g * 64:192 + g * 64, r0:r0 + 128], reads=[bBCT], writes=[bin_[i]])
                dtd = dt_[i][:, dirn * 8:(dirn + 1) * 8]
                kb.tt("dve", da[i], dtd, arep[:, dirn * 8:(dirn + 1) * 8], ALU.mult, reads=[bin_[i], bpar], writes=[B[0]])
                kb.tt("pool", v3(xd[i], 8), v3(xs_[i], 8), dtd.unsqueeze(2).to_broadcast([128, 8, 64]), ALU.mult,
                      reads=[bin_[i]], writes=[B[1]])
                trisel = kb.cf
                tsl = tri[:, 0:128] if dirn == 0 else tri[:, 128:256]
                kb.mm(ps[0][0:8, 0:128], da[i].bitcast(F32), tsl.bitcast(F32), True, True, reads=[B[0], cbuf],
                      writes=[psb[0]])
                kb.copy("act", csf[i][0:8, :], ps[0][0:8, 0:128], reads=[psb[0]], writes=[B[2]])
                kb.tt("dve", v3(zt[i][0:8, :], 8), csf[i][0:8, :].unsqueeze(1).to_broadcast([8, 8, 128]),
                      iex[0:8, 0:8, :], ALU.mult, reads=[B[2], bpar], writes=[B[3]])
                for hb in range(2):
                    pp, bp = ps[1 + hb], psb[1 + hb]
                    kb.mm(pp[:], onesf[0:8, :], zt[i][0:8, hb * 512:(hb + 1) * 512], True, False,
                          reads=[B[3], cbuf], writes=[bp])
                    kb.mm(pp[:], csf[i][0:8, :], niexp[0:8, hb * 4:(hb + 1) * 4, :], False, False,
                          reads=[B[2], bpar], writes=[bp])
                    kb.mm(pp[:], identf, mk, False, True, reads=[cbuf], writes=[bp])
                    kb.act(ee[i][:, hb * 512:(hb + 1) * 512], pp[:], AF.Exp, reads=[bp], writes=[B[4]])
                kb.mm(ps[3][:, 0:256], bct[i][:, 0:128], bct[i][:, 256:512], True, True, reads=[bin_[i]], writes=[psb[3]])
                kb.copy("act", gt[i], ps[3][:, 0:256], reads=[psb[3]], writes=[B[5]])
                for g in range(2):
                    kb.tt("dve" if g else "pool", v3(mt[i][:, g * 512:(g + 1) * 512], 4),
                          v3(ee[i][:, g * 512:(g + 1) * 512], 4),
                          gt[i][:, g * 128:(g + 1) * 128].unsqueeze(1).to_broadcast([128, 4, 128]), ALU.mult,
                          reads=[B[4], B[5]], writes=[B[6]])
                if dirn == 0:
                    wl, ol = tri[:, 256:384], tri[:, 0:128]
                else:
                    wl, ol = tri[:, 384:512], None
                kb.mm(ps[4][:, 0:8], wl.bitcast(F32), da[i], True, True, reads=[B[0], cbuf], writes=[psb[4]])
                if dirn == 0:
                    kb.mm(ps[4][:, 8:16], ol.bitcast(F32), da[i], True, True, reads=[B[0], cbuf], writes=[psb[4]])
                else:
                    kb.mm(ps[4][:, 8:16], onesf, da[i], True, False, reads=[B[0], cbuf], writes=[psb[4]])
                    kb.mm(ps[4][:, 8:16], tri[:, 128:256].bitcast(F32).rearrange("p x -> p x"), da[i], False, True,
                          reads=[B[0], cbuf], writes=[psb[4]])
                kb.act(ex[i], ps[4][:, 0:16], AF.Exp, reads=[psb[4]], writes=[B[7]])
                kb.mm(ps[4][:, 16:24], onesf, da[i], True, True, reads=[B[0], cbuf], writes=[psb[4]])
                kb.act(dec[i], ps[4][:, 16:24], AF.Exp, reads=[psb[4]], writes=[B[8]])
                for h in range(8):
                    kb.mm(ps[5][:, h * 64:(h + 1) * 64], mt[i][:, h * 128:(h + 1) * 128], xd[i][:, h * 64:(h + 1) * 64],
                          True, True, reads=[B[6], B[1]], writes=[psb[5]])
                kb.mm(ps[6][:], bct[i][:, 128:256], Sst, True, True, reads=[bin_[i], bS], writes=[psb[6]])
                kb.tt("dve", v3(t2[i], 8), v3(ps[6][:], 8), ex[i][:, 8:16].unsqueeze(2).to_broadcast([128, 8, 64]),
                      ALU.mult, reads=[psb[6], B[7]], writes=[B[9]])
                kb.tt("dve", yt_[i], ps[5][:], t2[i], ALU.add, reads=[psb[5], B[9]], writes=[B[10]])
                kb.tt("pool", v3(xdw[i], 8), v3(xd[i], 8), ex[i][:, 0:8].unsqueeze(2).to_broadcast([128, 8, 64]),
                      ALU.mult, reads=[B[1], B[7]], writes=[B[11]])
                kb.mm(ps[7][:], bt_[i], xdw[i], True, True, reads=[bin_[i], B[11]], writes=[psb[7]])
                for g in range(2):
                    sl = slice(g * 64, (g + 1) * 64)
                    kb.tt("dve", v3(Sst[sl, g * 256:(g + 1) * 256], 4), v3(Sst[sl, g * 256:(g + 1) * 256], 4),
                          dec[i][sl, g * 4:(g + 1) * 4].unsqueeze(2).to_broadcast([64, 4, 64]), ALU.mult,
                          reads=[bS, B[8]], writes=[bS])
                for g in range(2):
                    sl = slice(g * 64, (g + 1) * 64)
                    kb.tt("dve", Sst[sl, g * 256:(g + 1) * 256], Sst[sl, g * 256:(g + 1) * 256],
                          ps[7][sl, g * 256:(g + 1) * 256], ALU.add, reads=[bS, psb[7]], writes=[bS])
                if dirn == 0:
                    kb.tt("pool", v3(t2[i], 8), v3(xs_[i], 8), dsk.unsqueeze(2).to_broadcast([128, 8, 64]), ALU.mult,
                          reads=[bin_[i], bpar, B[10]], writes=[B[9]])
                    kb.tt("dve", yt_[i], yt_[i], t2[i], ALU.add, reads=[B[9], B[10]], writes=[B[10]])
                    kb.dma(YF[r0:r0 + 128, :], yt_[i], reads=[B[10]], writes=[bYF], q="sp")
                else:
                    kb.dma(t2[i], YF[r0:r0 + 128, :], reads=[bYF, B[9], B[10]], writes=[B[9]], q="sp")
                    kb.dma(zz[i], PT[r0:r0 + 128, 0:512], reads=[bPT], writes=[B[12]], q="sp")
                    kb.tt("dve", yt_[i], yt_[i], t2[i], ALU.add, reads=[B[9], B[10]], writes=[B[10]])
                    kb.tt("dve", yt_[i], yt_[i], zz[i], ALU.mult, reads=[B[12], B[10]], writes=[B[10]])
                    kb.P.op("act", lambda e, o=t2[i], a=yt_[i], s=ss[i]: e.activation(o, a, AF.Square, accum_out=s[:, 0:1]),
                            reads=[B[10], B[9]], writes=[B[9], B[13]])
                    kb.ts("dve", ss[i][:, 0:1], ss[i][:, 0:1], 1.0 / 512, EPS, ALU.mult, ALU.add, reads=[B[13]], writes=[B[13]])
                    kb.act(ss[i][:, 0:1], ss[i][:, 0:1], AF.Sqrt, reads=[B[13]], writes=[B[13]])
                    kb.recip(ss[i][:, 0:1], ss[i][:, 0:1], reads=[B[13]], writes=[B[13]])
                    kb.stt("dve", yo[i], yt_[i], ss[i][:, 0:1], gn, ALU.mult, ALU.mult, reads=[B[10], B[13], bpar],
                           writes=[B[14]])
                    for f in range(4):
                        kb.mm(ps[0][:, f * 128:(f + 1) * 128], yo[i][:, f * 128:(f + 1) * 128], ident, True, True,
                              reads=[B[14], cbuf], writes=[psb[0]])
                    kb.copy("act", t2[i], ps[0][:], reads=[psb[0], B[9]], writes=[B[9]])
                    for f in range(4):
                        kb.dma(YTS[f * 128:(f + 1) * 128, r0:r0 + 128], t2[i][:, f * 128:(f + 1) * 128], reads=[B[9]],
                               writes=[bYTS], q="sp")
            if pair and dirn == 0:
                kb.dma(SS[:, 0:512], Sst, reads=[bS], writes=[bSS])

        bQKT, bKTM = Buf("QKT"), Buf("KTM")

        def ret_prep(l):
            kb.new_stage()
            NB = 2
            qk = [kb.Fm(1024) for _ in range(NB)]
            rp = [kb.Fm(64) for _ in range(NB)]
            ro = [kb.R(1024) for _ in range(NB)]
            tmp = [kb.Fm(512) for _ in range(NB)]
            oT = [kb.Fm(512) for _ in range(NB)]
            bb = [[Buf() for _ in range(6)] for _ in range(NB)]
            for c in range(NCH):
                i = c % NB
                B = bb[i]
                r0 = c * 128
                kb.dma(qk[i], PT[r0:r0 + 128, 512:1536], reads=[bPT], writes=[B[0]], q="sp")
                kb.dma(rp[i], I["c_rope"][r0:r0 + 128, :], writes=[B[0]], q="sp")
                cosb = rp[i][:, 0:32].unsqueeze(1).to_broadcast([128, 16, 32])
                sinb = rp[i][:, 32:64].unsqueeze(1).to_broadcast([128, 16, 32])
                x4 = qk[i].rearrange("p (h t d) -> p h t d", h=16, t=2)
                o4 = ro[i].rearrange("p (h t d) -> p h t d", h=16, t=2)
                tm4 = tmp[i].rearrange("p (h d) -> p h d", h=16)
                t1, t2_ = x4[:, :, 0, :], x4[:, :, 1, :]
                kb.tt("dve", tm4, t2_, sinb, ALU.mult, reads=[B[0]], writes=[B[1]])
                kb.tt("pool", o4[:, :, 0, :], t1, cosb, ALU.mult, reads=[B[0]], writes=[B[2]])
                kb.tt("dve", o4[:, :, 0, :], o4[:, :, 0, :], tm4, ALU.subtract, reads=[B[1], B[2]], writes=[B[2]])
                kb.tt("dve", tm4, t1, sinb, ALU.mult, reads=[B[0], B[2]], writes=[B[1]])
                kb.tt("pool", o4[:, :, 1, :], t2_, cosb, ALU.mult, reads=[B[0]], writes=[B[3]])
                kb.tt("dve", o4[:, :, 1, :], o4[:, :, 1, :], tm4, ALU.add, reads=[B[1], B[3]], writes=[B[3]])
                kb.ts("dve", ro[i][:, 512:1024], ro[i][:, 512:1024], 0.125, None, ALU.mult, reads=[B[2], B[3]],
                      writes=[B[2], B[3]])
                kb.dma(KTM[r0:r0 + 128, :], ro[i][:, 512:1024], reads=[B[2], B[3]], writes=[bKTM])
                for f in range(8):
                    pp, bp = ps[f % 2], psb[f % 2]
                    kb.mm(pp[:, 0:128], ro[i][:, f * 128:(f + 1) * 128], ident, True, True, reads=[B[2], B[3], cbuf],
                          writes=[bp])
                    o = oT[i][:, (f % 4) * 128:(f % 4 + 1) * 128]
                    kb.copy("act", o, pp[:, 0:128], reads=[bp], writes=[B[4 + (f % 2)]])
                    for hh in range(2):
                        kb.dma(QK2[c, :, (2 * f + hh) * 128:(2 * f + hh + 1) * 128], o[hh * 64:(hh + 1) * 64, :],
                               reads=[B[4 + (f % 2)]], writes=[bQKT], q="sp")

        bRF, bYTR = Buf("RF"), Buf("YTR")

        def ret_pass(l, dirn):
            kb.new_stage()
            rmask = v3(kb.Fm(1024), 8)
            rdec = kb.Fm(40)
            gn = kb.Fm(512)
            bpar = Buf()
            kb.dma(rmask, v3(I["c_retmask"][:, dirn * 1024:(dirn + 1) * 1024], 8), writes=[bpar], q="sp")
            kb.dma(rdec, I["c_retdec"], writes=[bpar], q="sp")
            kb.dma(gn, I["retnorm_rep"][l], writes=[bpar], q="sp")
            kdec = rdec[:, dirn * 8:dirn * 8 + 8]
            qdec = rdec[:, 16 + dirn * 8:16 + dirn * 8 + 8]
            cdec = rdec[0:64, 32:40]
            Sst = kb.R(512)[0:64, :]
            bS = Buf()
            if pair and dirn == 1:
                recv_combine(Sst, 512, 1024, 64, kb.R(512)[0:64, :], kb.R(512)[0:64, :], bS)
            else:
                kb.ts("dve", Sst, onesf[0:64, 0:1].to_broadcast([64, 512]), 0.0, None, ALU.mult, reads=[cbuf], writes=[bS])
            NB = 2
            qkt = [kb.R(2048) for _ in range(NB)]
            ktm = [kb.R(512) for _ in range(NB)]
            vtm = [kb.R(512) for _ in range(NB)]
            vw = [kb.R(512) for _ in range(NB)]
            mt = [kb.R(1024) for _ in range(NB)]
            yt_ = [kb.Fm(512) for _ in range(NB)]
            t2 = [kb.Fm(512) for _ in range(NB)]
            gg = [kb.Fm(512) for _ in range(NB)]
            yo = [kb.R(512) for _ in range(NB)]
            ss = [kb.Fm(16) for _ in range(NB)]
            bb = [[Buf() for _ in range(12)] for _ in range(NB)]
            order = range(NCH) if dirn == 0 else range(NCH - 1, -1, -1)
            for n, c in enumerate(order):
                i = n % NB
                B = bb[i]
                r0 = c * 128
                q3 = v3(qkt[i][0:64, :], 16)
                kb.dma(qkt[i][0:64, :], QK2[c], reads=[bQKT], writes=[B[0]])
                kb.dma(ktm[i], KTM[r0:r0 + 128, :], reads=[bKTM], writes=[B[0]])
                kb.dma(vtm[i], PT[r0:r0 + 128, 1536:2048], reads=[bPT], writes=[B[0]])
                for h in range(8):
                    kb.mm(ps[h // 4][:, (h % 4) * 128:(h % 4 + 1) * 128], q3[:, 8 + h, :], q3[:, h, :], True, True,
                          reads=[B[0]], writes=[psb[h // 4]])
                for hb in range(2):
                    kb.tt("dve", v3(mt[i][:, hb * 512:(hb + 1) * 512], 4), v3(ps[hb][:], 4), rmask[:, hb * 4:(hb + 1) * 4, :],
                          ALU.mult, reads=[psb[hb], bpar], writes=[B[1]])
                for h in range(8):
                    kb.mm(ps[5][:, h * 64:(h + 1) * 64], mt[i][:, h * 128:(h + 1) * 128], vtm[i][:, h * 64:(h + 1) * 64],
                          True, True, reads=[B[1], B[0]], writes=[psb[5]])
                for h in range(8):
                    kb.mm(ps[6][:, h * 64:(h + 1) * 64], q3[:, h, :], Sst[:, h * 64:(h + 1) * 64], True, True,
                          reads=[B[0], bS], writes=[psb[6]])
                kb.tt("dve", v3(t2[i], 8), v3(ps[6][:], 8), qdec.unsqueeze(2).to_broadcast([128, 8, 64]), ALU.mult,
                      reads=[psb[6], bpar], writes=[B[2]])
                kb.tt("dve", yt_[i], ps[5][:], t2[i], ALU.add, reads=[psb[5], B[2]], writes=[B[3]])
                kb.tt("pool", v3(vw[i], 8), v3(vtm[i], 8), kdec.unsqueeze(2).to_broadcast([128, 8, 64]), ALU.mult,
                      reads=[B[0], bpar], writes=[B[4]])
                for h in range(8):
                    kb.mm(ps[7][0:64, h * 64:(h + 1) * 64], ktm[i][:, h * 64:(h + 1) * 64], vw[i][:, h * 64:(h + 1) * 64],
                          True, True, reads=[B[0], B[4]], writes=[psb[7]])
                kb.tt("dve", v3(Sst, 8), v3(Sst, 8), cdec.unsqueeze(2).to_broadcast([64, 8, 64]), ALU.mult,
                      reads=[bS, bpar], writes=[bS])
                kb.tt("dve", Sst, Sst, ps[7][0:64, :], ALU.add, reads=[bS, psb[7]], writes=[bS])
                if dirn == 0:
                    kb.dma(RF[r0:r0 + 128, :], yt_[i], reads=[B[3]], writes=[bRF], q="sp")
                else:
                    kb.dma(t2[i], RF[r0:r0 + 128, :], reads=[bRF, B[2], B[3]], writes=[B[2]], q="sp")
                    kb.dma(gg[i], PT[r0:r0 + 128, 2048:2560], reads=[bPT], writes=[B[5]], q="sp")
                    kb.tt("dve", yt_[i], yt_[i], t2[i], ALU.add, reads=[B[2], B[3]], writes=[B[3]])
                    kb.tt("pool", t2[i], yt_[i], yt_[i], ALU.mult, reads=[B[3], B[2]], writes=[B[2]])
                    kb.P.op("dve", lambda e, o=ss[i][:, 0:8], a=v3(t2[i], 8): e.tensor_reduce(o, a, AX.X, ALU.add),
                            reads=[B[2]], writes=[B[6]])
                    kb.ts("dve", ss[i][:, 0:8], ss[i][:, 0:8], 1.0 / 64, EPS, ALU.mult, ALU.add, reads=[B[6]], writes=[B[6]])
                    kb.act(ss[i][:, 0:8], ss[i][:, 0:8], AF.Sqrt, reads=[B[6]], writes=[B[6]])
                    kb.recip(ss[i][:, 0:8], ss[i][:, 0:8], reads=[B[6]], writes=[B[6]])
                    kb.tt("dve", v3(yt_[i], 8), v3(yt_[i], 8), ss[i][:, 0:8].unsqueeze(2).to_broadcast([128, 8, 64]),
                          ALU.mult, reads=[B[3], B[6]], writes=[B[3]])
                    kb.tt("pool", yt_[i], yt_[i], gn, ALU.mult, reads=[B[3], bpar], writes=[B[3]])
                    kb.tt("dve", yo[i], yt_[i], gg[i], ALU.mult, reads=[B[3], B[5]], writes=[B[7]])
                    for f in range(4):
                        kb.mm(ps[2][:, f * 128:(f + 1) * 128], yo[i][:, f * 128:(f + 1) * 128], ident, True, True,
                              reads=[B[7], cbuf], writes=[psb[2]])
                    kb.copy("act", t2[i], ps[2][:], reads=[psb[2], B[2]], writes=[B[2]])
                    for f in range(4):
                        kb.dma(YTR[f * 128:(f + 1) * 128, r0:r0 + 128], t2[i][:, f * 128:(f + 1) * 128], reads=[B[2]],
                               writes=[bYTR], q="sp")
            if pair and dirn == 0:
                kb.dma(SS[0:64, 512:1024], Sst, reads=[bS], writes=[bSS])

        bY5 = Buf("Y5")
        bY5A, bY5B = Buf("Y5A"), Buf("Y5B")

        def s5_stage(l, dirs=(0, 1)):
            kb.new_stage()
            bp_ = Buf("s5par")
            def F(n):
                return kb.Fm(n)
            rho = [F(12), F(12)]
            c0 = [F(12), F(12)]
            s0 = [F(12), F(12)]
            C9 = [F(12), F(12)]
            S9 = [F(12), F(12)]
            cfr = [F(12), F(12)]
            cfi = [F(12), F(12)]
            tA, tB, tC, tD = F(12), F(12), F(12), F(12)
            halfpi = F(1)
            kb.memset("dve", halfpi, math.pi / 2, writes=[bp_])
            for d in range(2):
                lre, lim, stp = F(12), F(12), F(12)
                kb.dma(lre, I["s5_lre"][l, d], writes=[bp_], q="sp")
                kb.dma(lim, I["s5_lim"][l, d], writes=[bp_], q="sp")
                kb.dma(stp, I["s5_lstep"][l, d], writes=[bp_], q="sp")
                R_, W_ = [bp_], [bp_]
                kb.ts("dve", lre, lre, -1e-4, None, ALU.min, reads=R_, writes=W_)
                kb.act(stp, stp, AF.Exp, reads=R_, writes=W_)
                kb.tt("dve", tA, lre, stp, ALU.mult, reads=R_, writes=W_)
                kb.act(rho[d], tA, AF.Exp, reads=R_, writes=W_)
                kb.tt("dve", tB, lim, stp, ALU.mult, reads=R_, writes=W_)
                kb.act(s0[d], tB, AF.Sin, reads=R_, writes=W_, scale=1.0 / 32)
                kb.act(c0[d], tB, AF.Sin, reads=R_, writes=W_, scale=1.0 / 32, bias=halfpi[:, 0:1])

                def dbl(cc, sn):
                    kb.tt("dve", tC, cc, cc, ALU.mult, reads=R_, writes=W_)
                    kb.tt("dve", tD, sn, sn, ALU.mult, reads=R_, writes=W_)
                    kb.tt("dve", tD, tC, tD, ALU.subtract, reads=R_, writes=W_)
                    kb.tt("dve", tC, cc, sn, ALU.mult, reads=R_, writes=W_)
                    kb.ts("dve", sn, tC, 2.0, None, ALU.mult, reads=R_, writes=W_)
                    kb.copy("dve", cc, tD, reads=R_, writes=W_)
                for _ in range(5):
                    dbl(c0[d], s0[d])
                kb.copy("dve", C9[d], c0[d], reads=R_, writes=W_)
                kb.copy("dve", S9[d], s0[d], reads=R_, writes=W_)
                for _ in range(9):
                    dbl(C9[d], S9[d])
                lbr, lbi, den = F(12), F(12), F(12)
                kb.tt("dve", lbr, rho[d], c0[d], ALU.mult, reads=R_, writes=W_)
                kb.tt("dve", lbi, rho[d], s0[d], ALU.mult, reads=R_, writes=W_)
                kb.ts("dve", lbr, lbr, -1.0, None, ALU.add, reads=R_, writes=W_)
                kb.tt("dve", den, lre, lre, ALU.mult, reads=R_, writes=W_)
                kb.tt("dve", tC, lim, lim, ALU.mult, reads=R_, writes=W_)
                kb.tt("dve", den, den, tC, ALU.add, reads=R_, writes=W_)
                kb.recip(den, den, reads=R_, writes=W_)
                kb.tt("dve", tC, lbr, lre, ALU.mult, reads=R_, writes=W_)
                kb.tt("dve", tD, lbi, lim, ALU.mult, reads=R_, writes=W_)
                kb.tt("dve", tC, tC, tD, ALU.add, reads=R_, writes=W_)
                kb.tt("dve", cfr[d], tC, den, ALU.mult, reads=R_, writes=W_)
                kb.tt("dve", tC, lbi, lre, ALU.mult, reads=R_, writes=W_)
                kb.tt("dve", tD, lbr, lim, ALU.mult, reads=R_, writes=W_)
                kb.tt("dve", tC, tC, tD, ALU.subtract, reads=R_, writes=W_)
                kb.tt("dve", cfi[d], tC, den, ALU.mult, reads=R_, writes=W_)
            bre = v3(kb.Fm(1536), 12)
            bim = v3(kb.Fm(1536), 12)
            kb.dma(bre, v3(I["s5_bre"][l], 12), writes=[bp_], q="sp")
            kb.dma(bim, v3(I["s5_bim"][l], 12), writes=[bp_], q="sp")
            dsel = v3(kb.R(384), 12)
            kb.dma(dsel, v3(I["s5_dsel"][l], 12), writes=[bp_])
            cre = [v3(kb.R(384), 12) for _ in range(2)]
            cimn = [v3(kb.R(384), 12) for _ in range(2)]
            for d in range(2):
                kb.dma(cre[d], v3(I["s5_cre"][l, d], 12), writes=[bp_])
                kb.dma(cimn[d], v3(I["s5_cim"][l, d], 12), writes=[bp_])
                kb.ts("dve", cimn[d], cimn[d], -1.0, None, ALU.mult, reads=[bp_], writes=[bp_])
            bbr = kb.R(128)
            bbi = kb.R(128)
            tq = kb.Fm(128)
            btr = [kb.R(128), kb.R(128)]
            bti = [kb.R(128), kb.R(128)]
            cosT = [kb.Fm(512), kb.Fm(512)]
            sinT = [kb.Fm(512), kb.Fm(512)]
            rhoT = [kb.Fm(512), kb.Fm(512)]
            tc_, ts_ = kb.Fm(256), kb.Fm(256)
            ub = [kb.R(512) for _ in range(2)]
            vre = [kb.Fm(512) for _ in range(2)]
            vim = [kb.Fm(512) for _ in range(2)]
            wre = [kb.Fm(512) for _ in range(2)]
            wim = [kb.Fm(512) for _ in range(2)]
            hre = [kb.R(512) for _ in range(2)]
            him = [kb.R(512) for _ in range(2)]
            t1 = [kb.Fm(512) for _ in range(2)]
            ini = [[kb.Fm(4), kb.Fm(4)] for _ in range(2)]
            go = [kb.Fm(512) for _ in range(2)]
            bb = [[Buf() for _ in range(10)] for _ in range(2)]
            bt = [Buf(), Buf()]
            bini = [Buf(), Buf()]
            bsc = Buf()
            psi = [(1, 2, 3), (4, 5, 6)]
            for it in range(12):
                for d in dirs:
                    R_, W_ = [bp_, bsc, bt[d]], [bsc, bt[d]]
                    kb.ts("dve", tq, bre[:, it, :], cfr[d][:, it:it + 1], None, ALU.mult, reads=R_, writes=W_)
                    kb.stt("dve", tq, bim[:, it, :], cfi[d][:, it:it + 1], tq, ALU.mult, ALU.subtract, reads=R_, writes=W_)
                    kb.ts("dve", bbr, tq, -1.0, None, ALU.mult, reads=R_, writes=W_)
                    kb.ts("dve", tq, bim[:, it, :], cfr[d][:, it:it + 1], None, ALU.mult, reads=R_, writes=W_)
                    kb.stt("dve", bbi, bre[:, it, :], cfi[d][:, it:it + 1], tq, ALU.mult, ALU.add, reads=R_, writes=W_)
                    kb.mm(ps[0][:, 0:128], bbr, ident, True, True, reads=[bsc, cbuf], writes=[psb[0]])
                    kb.mm(ps[0][:, 128:256], bbi, ident, True, True, reads=[bsc, cbuf], writes=[psb[0]])
                    kb.copy("dve", btr[d], ps[0][:, 0:128], reads=[psb[0]], writes=W_)
                    kb.copy("dve", bti[d], ps[0][:, 128:256], reads=[psb[0]], writes=W_)
                    cT, sT = cosT[d], sinT[d]
                    kb.memset("dve", cT[:, 0:1], 1.0, writes=W_)
                    kb.memset("dve", sT[:, 0:1], 0.0, writes=W_)
                    kb.copy("dve", tc_[:, 0:1], c0[d][:, it:it + 1], reads=R_, writes=W_)
                    kb.copy("dve", ts_[:, 0:1], s0[d][:, it:it + 1], reads=R_, writes=W_)
                    m = 1
                    while m < 512:
                        ck, sk = tc_[:, 0:1], ts_[:, 0:1]
                        tmpv = t1[d][:, 0:m]
                        RX, WX = R_ + [bb[d][2]], W_ + [bb[d][2]]
                        kb.ts("dve", tmpv, sT[:, 0:m], sk, None, ALU.mult, reads=RX, writes=WX)
                        kb.stt("dve", cT[:, m:2 * m], cT[:, 0:m], ck, tmpv, ALU.mult, ALU.subtract, reads=RX, writes=WX)
                        kb.ts("dve", tmpv, cT[:, 0:m], sk, None, ALU.mult, reads=RX, writes=WX)
                        kb.stt("dve", sT[:, m:2 * m], sT[:, 0:m], ck, tmpv, ALU.mult, ALU.add, reads=RX, writes=WX)
                        kb.tt("dve", tc_[:, 1:2], ck, ck, ALU.mult, reads=R_, writes=W_)
                        kb.tt("dve", tc_[:, 2:3], sk, sk, ALU.mult, reads=R_, writes=W_)
                        kb.tt("dve", tc_[:, 3:4], ck, sk, ALU.mult, reads=R_, writes=W_)
                        kb.tt("dve", tc_[:, 0:1], tc_[:, 1:2], tc_[:, 2:3], ALU.subtract, reads=R_, writes=W_)
                        kb.ts("dve", ts_[:, 0:1], tc_[:, 3:4], 2.0, None, ALU.mult, reads=R_, writes=W_)
                        m *= 2
                    kb.copy("dve", rhoT[d], rho[d][:, it:it + 1].to_broadcast([128, 512]), reads=R_, writes=W_)
                    kb.memset("dve", ini[d][0][:, 0:2], 0.0, writes=[bini[d]])
                for n in range(NT):
                    for d in dirs:
                        t = n if d == 0 else NT - 1 - n
                        i = d
                        B = bb[d]
                        p1, p2, p3 = psi[d]
                        cT, sT = cosT[d], sinT[d]
                        kb.dma(ub[i], PF[(6 + it // 4) * 128:(7 + it // 4) * 128, t * 512:(t + 1) * 512], reads=[bPF],
                               writes=[B[0]])
                        kb.mm(ps[p1][:], btr[d], ub[i], True, True, reads=[bt[d], B[0]], writes=[psb[p1]])
                        kb.mm(ps[p2][:], bti[d], ub[i], True, True, reads=[bt[d], B[0]], writes=[psb[p2]])
                        pre = ps[p1][:] if d == 0 else ps[p1][:, ::-1]
                        pim = ps[p2][:] if d == 0 else ps[p2][:, ::-1]
                        kb.tt("dve", vre[i], pre, cT, ALU.mult, reads=[psb[p1], bt[d]], writes=[B[1]])
                        kb.tt("dve", t1[i], pim, sT, ALU.mult, reads=[psb[p2], bt[d]], writes=[B[2]])
                        kb.tt("pool", vre[i], vre[i], t1[i], ALU.add, reads=[B[1], B[2]], writes=[B[1]])
                        kb.tt("dve", vim[i], pim, cT, ALU.mult, reads=[psb[p2], bt[d]], writes=[B[3]])
                        kb.tt("dve", t1[i], pre, sT, ALU.mult, reads=[psb[p1], bt[d], B[1]], writes=[B[2]])
                        kb.tt("pool", vim[i], vim[i], t1[i], ALU.subtract, reads=[B[3], B[2]], writes=[B[3]])
                        kb.scan(wre[i], rhoT[d], vre[i], ini[d][0][:, 0:1], reads=[B[1], bt[d], bini[d]], writes=[B[4]])
                        kb.scan(wim[i], rhoT[d], vim[i], ini[d][0][:, 1:2], reads=[B[3], bt[d], bini[d]], writes=[B[5]])
                        kb.ts("dve", ini[d][1][:, 0:1], wim[i][:, 511:512], S9[d][:, it:it + 1], None, ALU.mult,
                              reads=[B[5], bp_, bini[d]], writes=[bini[d]])
                        kb.ts("dve", ini[d][1][:, 1:2], wre[i][:, 511:512], S9[d][:, it:it + 1], None, ALU.mult,
                              reads=[B[4], bp_, bini[d]], writes=[bini[d]])
                        kb.stt("dve", ini[d][0][:, 0:1], wre[i][:, 511:512], C9[d][:, it:it + 1], ini[d][1][:, 0:1], ALU.mult,
                               ALU.subtract, reads=[B[4], bini[d]], writes=[bini[d]])
                        kb.stt("dve", ini[d][0][:, 1:2], wim[i][:, 511:512], C9[d][:, it:it + 1], ini[d][1][:, 1:2], ALU.mult,
                               ALU.add, reads=[B[5], bini[d]], writes=[bini[d]])
                        kb.tt("pool", hre[i], wre[i], cT, ALU.mult, reads=[B[4], bt[d]], writes=[B[6]])
                        kb.tt("dve", t1[i], wim[i], sT, ALU.mult, reads=[B[5], bt[d], B[2]], writes=[B[2]])
                        kb.tt("pool", hre[i], hre[i], t1[i], ALU.subtract, reads=[B[6], B[2]], writes=[B[6]])
                        kb.tt("pool", him[i], wre[i], sT, ALU.mult, reads=[B[4], bt[d]], writes=[B[7]])
                        kb.tt("dve", t1[i], wim[i], cT, ALU.mult, reads=[B[5], bt[d], B[6]], writes=[B[2]])
                        kb.tt("pool", him[i], him[i], t1[i], ALU.add, reads=[B[7], B[2]], writes=[B[7]])
                        kb.mm(ps[p3][0:32, :], cre[d][:, it, :], hre[i], True, False, reads=[bp_, B[6]], writes=[psb[p3]])
                        kb.mm(ps[p3][0:32, :], cimn[d][:, it, :], him[i], False, d == 1, reads=[bp_, B[7]], writes=[psb[p3]])
                        if d == 0:
                            kb.mm(ps[p3][0:32, :], dsel[:, it, :], ub[i], False, True, reads=[bp_, B[0]], writes=[psb[p3]])
                            kb.copy("act", go[i][0:32, :], ps[p3][0:32, :], reads=[psb[p3]], writes=[B[8]])
                            kb.dma(Y5A[it * 32:(it + 1) * 32, t * 512:(t + 1) * 512], go[i][0:32, :], reads=[B[8]],
                                   writes=[bY5A], q="sp")
                        else:
                            kb.copy("act", go[i][0:32, :], ps[p3][0:32, ::-1], reads=[psb[p3]], writes=[B[8]])
                            kb.dma(Y5B[it * 32:(it + 1) * 32, t * 512:(t + 1) * 512], go[i][0:32, :], reads=[B[8]],
                                   writes=[bY5B], q="sp")
            kb.new_stage()
            ca = [kb.Fm(512), kb.Fm(512)]
            cb_ = [kb.Fm(512), kb.Fm(512)]
            bc = [Buf(), Buf()]
            n = 0
            for r in range(3):
                for t in range(NT):
                    i = n % 2
                    ts0 = slice(t * 512, (t + 1) * 512)
                    kb.dma(ca[i], Y5A[r * 128:(r + 1) * 128, ts0], reads=[bY5A], writes=[bc[i]], q="sp")
                    kb.dma(cb_[i], Y5B[r * 128:(r + 1) * 128, ts0], reads=[bY5B], writes=[bc[i]], q="sp")
                    kb.tt("dve", ca[i], ca[i], cb_[i], ALU.add, reads=[bc[i]], writes=[bc[i]])
                    kb.act(ca[i], ca[i], AF.Gelu, reads=[bc[i]], writes=[bc[i]])
                    kb.dma(Y5[r * 128:(r + 1) * 128, ts0], ca[i], reads=[bc[i]], writes=[bY5], q="sp")
                    n += 1

        def merge_stage(l):
            kb.new_stage()
            wbs = v3(kb.R(4096), 4)
            wbr = v3(kb.R(4096), 4)
            wb5 = v3(kb.R(3072), 3)
            wv = v3(kb.R(1152), 3)
            wg_ = v3(kb.R(1152), 3)
            wo = v3(kb.R(8192), 8)
            bw = Buf()
            kb.dma(wbs, v3(I["wbr_ssd"][l], 4), writes=[bw])
            kb.dma(wbr, v3(I["wbr_ret"][l], 4), writes=[bw])
            kb.dma(wb5, v3(I["wbr_s5"][l], 3), writes=[bw])
            kb.dma(wv, v3(I["glu_wv"][l], 3), writes=[bw])
            kb.dma(wg_, v3(I["glu_wg"][l], 3), writes=[bw])
            kb.dma(wo, v3(I["wout"][l], 8), writes=[bw])
            ys = v3(kb.R(2048), 4)
            yr = v3(kb.R(2048), 4)
            y5 = v3(kb.R(1536), 3)
            y5g = v3(kb.R(1536), 3)
            mixed = v3(kb.R(4096), 8)
            sgt = kb.Fm(512)
            gate = [kb.Fm(512) for _ in range(2)]
            tmp = [kb.Fm(512) for _ in range(2)]
            xr = [kb.Fm(512) for _ in range(2)]
            xo = [kb.Fm(512) for _ in range(2)]
            bi, bg5, bmx = Buf(), Buf(), [Buf() for _ in range(8)]
            bsg = Buf()
            bgate = [Buf(), Buf()]
            btmp = [Buf(), Buf()]
            bxr = [Buf(), Buf()]
            bxo = [Buf(), Buf()]
            for t in range(NT):
                ts0 = slice(t * 512, (t + 1) * 512)
                for f in range(4):
                    kb.dma(ys[:, f, :], YTS[f * 128:(f + 1) * 128, ts0], reads=[bYTS], writes=[bi])
                    kb.dma(yr[:, f, :], YTR[f * 128:(f + 1) * 128, ts0], reads=[bYTR], writes=[bi])
                for f in range(3):
                    kb.dma(y5[:, f, :], Y5[f * 128:(f + 1) * 128, ts0], reads=[bY5], writes=[bi])
                for f in range(3):
                    for k in range(3):
                        kb.mm(ps[0][:], wv[:, k, f * 128:(f + 1) * 128], y5[:, k, :], k == 0, k == 2, reads=[bw, bi],
                              writes=[psb[0]])
                    for k in range(3):
                        kb.mm(ps[1][:], wg_[:, k, f * 128:(f + 1) * 128], y5[:, k, :], k == 0, k == 2, reads=[bw, bi],
                              writes=[psb[1]])
                    kb.act(sgt, ps[1][:], AF.Sigmoid, reads=[psb[1]], writes=[bsg])
                    kb.tt("dve", y5g[:, f, :], ps[0][:], sgt, ALU.mult, reads=[psb[0], bsg], writes=[bg5])
                n = 0
                for i in range(8):
                    for br_, (w_, y_, nk, rb) in enumerate(((wbs, ys, 4, bi), (y5g and wb5, y5g, 3, bg5), (wbr, yr, 4, bi))):
                        pp, bp = ps[2 + n % 2], psb[2 + n % 2]
                        for k in range(nk):
                            kb.mm(pp[:], w_[:, k, i * 128:(i + 1) * 128], y_[:, k, :], k == 0, k == nk - 1, reads=[bw, rb],
                                  writes=[bp])
                        gi = 9 + br_ * 8 + i
                        kb.dma(gate[n % 2], PF[gi * 128:(gi + 1) * 128, ts0], reads=[bPF], writes=[bgate[n % 2]], q="sp")
                        if br_ == 0:
                            kb.tt("dve", tmp[i % 2], pp[:], gate[n % 2], ALU.mult, reads=[bp, bgate[n % 2]],
                                  writes=[btmp[i % 2]])
                        elif br_ == 1:
                            kb.tt("dve", gate[n % 2], pp[:], gate[n % 2], ALU.mult, reads=[bp, bgate[n % 2]],
                                  writes=[bgate[n % 2]])
                            kb.tt("pool", tmp[i % 2], tmp[i % 2], gate[n % 2], ALU.add, reads=[bgate[n % 2], btmp[i % 2]],
                                  writes=[btmp[i % 2]])
                        else:
                            kb.tt("dve", gate[n % 2], pp[:], gate[n % 2], ALU.mult, reads=[bp, bgate[n % 2]],
                                  writes=[bgate[n % 2]])
                            kb.tt("dve", mixed[:, i, :], tmp[i % 2], gate[n % 2], ALU.add,
                                  reads=[bgate[n % 2], btmp[i % 2]], writes=[bmx[i]])
                        n += 1
                for i in range(8):
                    pp, bp = ps[4 + i % 2], psb[4 + i % 2]
                    for k in range(8):
                        kb.mm(pp[:], wo[:, k, i * 128:(i + 1) * 128], mixed[:, k, :], k == 0, k == 7, reads=[bw, bmx[k]],
                              writes=[bp])
                    kb.dma(xr[i % 2], X[i * 128:(i + 1) * 128, ts0], reads=[xb(i, t)], writes=[bxr[i % 2]], q="sp")
                    kb.tt("dve", xo[i % 2], pp[:], xr[i % 2], ALU.add, reads=[bp, bxr[i % 2]], writes=[bxo[i % 2]])
                    kb.dma(X[i * 128:(i + 1) * 128, ts0], xo[i % 2], reads=[bxo[i % 2]], writes=[xb(i, t)], q="sp")

        def final_stage():
            for t in range(NT):
                kb.new_stage()
                xn, bn, xf, bxk = load_norm(X, t, I["g_final"], "z")
                o = v3(kb.Fm(4096), 8)
                bo = Buf()
                for k in range(8):
                    kb.copy("dve" if k % 2 else "act", o[:, k, :], xn[:, k, :].bitcast(F32), reads=[bn], writes=[bo])
                    kb.dma(outT[k * 128:(k + 1) * 128, t * 512:(t + 1) * 512], o[:, k, :], reads=[bo], q="sp")

        kb.new_stage()
        def on(nm):
            return STAGES is None or nm in STAGES
        for l in range(L):
            if on("ffn1"):
                ffn_stage(l, "g_ffn1", I["wg1"], I["wu1"], I["wd1"], src=(I["xT"] if l == 0 else None))
            if on("inproj"):
                inproj_stage(l)
            if pair:
                halo_exchange()
                ssd_prep(l)
                ret_prep(l)
                ssd_pass(l, 0)
                ret_pass(l, 0)
                s5_stage(l, (0,))
                state_exchange()
                ssd_pass(l, 1)
                ret_pass(l, 1)
                s5_stage(l, (1,))
            else:
                if on("ssd") or on("ssdprep"):
                    ssd_prep(l)
                if on("ssd") or on("ssd0"):
                    ssd_pass(l, 0)
                if on("ssd") or on("ssd1"):
                    ssd_pass(l, 1)
                if on("ret"):
                    ret_prep(l)
                    ret_pass(l, 0)
                    ret_pass(l, 1)
                if on("s5"):
                    s5_stage(l)
            if on("merge"):
                merge_stage(l)
            if on("ffn2"):
                ffn_stage(l, "g_ffn2", I["wg2"], I["wu2"], I["wd2"])
        final_stage()
        P.emit()
    return nc


def _tile_cols(w, nt):
    K, N = w.shape
    return np.ascontiguousarray(w.reshape(K // 128, 128, nt, 128).transpose(2, 1, 0, 3).reshape(nt, 128, (K // 128) * 128))


def _tile_rows(w):
    K, N = w.shape
    return np.ascontiguousarray(w.reshape(K // 128, 128, N).transpose(1, 0, 2).reshape(128, (K // 128) * N))


def _consts(S):
    c = {}
    idx = np.arange(128)
    k, x = idx[:, None], idx[None, :]
    c["c_ident"] = np.eye(128, dtype=np.float32)
    c["c_tri"] = np.concatenate([(k <= x), -1.0 * (k < x), (k > x), (k < x)], axis=1).astype(np.float32)
    NEG = -30000.0
    mf = np.where(x < k, NEG, 0.0)
    mb = np.where(x > k, NEG, 0.0)
    c["c_maskneg"] = np.concatenate([np.tile(mf, (1, 4)), np.tile(mb, (1, 4))], axis=1).astype(np.float32)
    ie = np.zeros((16, 16, 128), np.float32)
    for j in range(16):
        ie[j, j, :] = 1.0
    c["c_iexp"] = ie.reshape(16, 2048)
    c["c_negiexp"] = (-ie).reshape(16, 2048)
    lg = np.log1p(-np.exp2(-5.0 - np.arange(8, dtype=np.float32))).astype(np.float32)
    s_, l_ = idx[:, None].astype(np.float32), idx[None, :].astype(np.float32)
    rm = np.zeros((128, 2, 8, 128), np.float32)
    for h in range(8):
        rm[:, 0, h, :] = np.where(l_ >= s_, np.exp(lg[h] * np.where(l_ >= s_, l_ - s_, 0.0)), 0.0)
        rm[:, 1, h, :] = np.where(s_ > l_, np.exp(lg[h] * np.where(s_ > l_, s_ - l_, 0.0)), 0.0)
    c["c_retmask"] = rm.reshape(128, 2048)
    rd = np.zeros((128, 40), np.float32)
    t = idx.astype(np.float32)[:, None]
    rd[:, 0:8] = np.exp(lg[None, :] * (127.0 - t))
    rd[:, 8:16] = np.exp(lg[None, :] * t)
    rd[:, 16:24] = np.exp(lg[None, :] * (t + 1.0))
    rd[:, 24:32] = np.exp(lg[None, :] * (128.0 - t))
    rd[:, 32:40] = np.exp(lg[None, :] * 128.0)
    c["c_retdec"] = rd
    pos = np.arange(S, dtype=np.float32)
    inv = (10000.0 ** (-np.arange(0, 64, 2, dtype=np.float32) / 64)).astype(np.float32)
    ang = pos[:, None] * inv[None, :]
    c["c_rope"] = np.concatenate([np.cos(ang), np.sin(ang)], axis=1).astype(np.float32)
    b = np.zeros((128, 128), np.float32)
    b[:64, :64] = 1
    b[64:, 64:] = 1
    c["c_blk64"] = b
    return c


def _prep_weights(inp, L):
    f = lambda a: np.ascontiguousarray(np.asarray(a, dtype=np.float32))
    W = {}
    gt = lambda g: np.ascontiguousarray(f(g).reshape(-1, 8, 128).transpose(0, 2, 1))
    W["g_ffn1"], W["g_mix"], W["g_ffn2"] = gt(inp["ffn1_norm"]), gt(inp["mix_norm"]), gt(inp["ffn2_norm"])
    W["g_final"] = gt(inp["final_norm"])[0]
    for n_, a in (("1", "ffn1"), ("2", "ffn2")):
        W["wg" + n_] = np.stack([_tile_cols(f(inp[a + "_w_gate"][l]), NFC) for l in range(L)])
        W["wu" + n_] = np.stack([_tile_cols(f(inp[a + "_w_up"][l]), NFC) for l in range(L)])
        wd = f(inp[a + "_w_down"])
        W["wd" + n_] = np.stack([np.stack([_tile_rows(wd[l][:, i * 128:(i + 1) * 128]) for i in range(8)]) for l in range(L)])
    win = f(inp["w_in"])
    sz = (512, 768, 16, 384, 512, 512, 512, 512, 3072)
    o = np.cumsum((0,) + sz)
    z, xbc, dt, u, q, k, v, g, gates = [win[:, :, o[i]:o[i + 1]] for i in range(9)]
    fm = np.concatenate([xbc, u, gates], axis=2)
    W["win_fm"] = np.stack([_tile_cols(fm[l], NFM) for l in range(L)])
    tm = np.concatenate([z, q, k, v, g, dt], axis=2)
    W["win_tm"] = np.stack([_tile_rows(tm[l]) for l in range(L)])
    W["bgate"] = np.ascontiguousarray(f(inp["b_gate"]).reshape(L, 24, 128).transpose(0, 2, 1))
    cw = f(inp["ssd_conv_w"])
    W["conv_w"] = np.ascontiguousarray(cw.reshape(L, 5, 6, 128).transpose(0, 3, 2, 1).reshape(L, 128, 30))
    W["conv_b"] = np.ascontiguousarray(f(inp["ssd_conv_b"]).reshape(L, 6, 128).transpose(0, 2, 1))
    rep = lambda a: np.ascontiguousarray(np.broadcast_to(a[:, None, :], (L, 128, a.shape[-1])))
    W["dtb_rep"] = rep(f(inp["ssd_dt_bias"]).reshape(L, 16))
    W["alog_rep"] = rep(f(inp["ssd_a_log"]).reshape(L, 16))
    W["dskip_rep"] = rep(f(inp["ssd_d"]))
    W["ssdnorm_rep"] = rep(f(inp["ssd_norm"]))
    W["retnorm_rep"] = rep(f(inp["ret_norm"]))
    W["wbr_ssd"] = np.stack([_tile_rows(f(inp["w_br_ssd"][l])) for l in range(L)])
    W["wbr_ret"] = np.stack([_tile_rows(f(inp["w_br_ret"][l])) for l in range(L)])
    W["wbr_s5"] = np.stack([_tile_rows(f(inp["w_br_s5"][l])) for l in range(L)])
    W["glu_wv"] = np.stack([_tile_rows(f(inp["s5_glu_wv"][l])) for l in range(L)])
    W["glu_wg"] = np.stack([_tile_rows(f(inp["s5_glu_wg"][l])) for l in range(L)])
    W["wout"] = np.stack([_tile_rows(f(inp["w_out"][l])) for l in range(L)])
    st = lambda a: np.ascontiguousarray(a.reshape(L, 2, 12, 128).transpose(0, 1, 3, 2))
    W["s5_lre"], W["s5_lim"] = st(f(inp["s5_lam_re"])), st(f(inp["s5_lam_im"]))
    ls = np.broadcast_to(f(inp["s5_log_step"])[..., None], (L, 2, 24, 64))
    W["s5_lstep"] = st(np.ascontiguousarray(ls))

    def bpad(b):
        out = np.zeros((L, 128, 12, 128), np.float32)
        for g in range(24):
            it, g2, g8 = g // 2, g % 2, g % 8
            out[:, g2 * 64:(g2 + 1) * 64, it, g8 * 16:(g8 + 1) * 16] = b[:, g]
        return out.reshape(L, 128, 12 * 128)
    W["s5_bre"], W["s5_bim"] = bpad(f(inp["s5_b_re"])), bpad(f(inp["s5_b_im"]))

    def cpad(c):
        out = np.zeros((L, 2, 128, 12, 32), np.float32)
        for g in range(24):
            it, g2 = g // 2, g % 2
            out[:, :, g2 * 64:(g2 + 1) * 64, it, g2 * 16:(g2 + 1) * 16] = c[:, :, g].transpose(0, 1, 3, 2)
        return out.reshape(L, 2, 128, 12 * 32)
    W["s5_cre"], W["s5_cim"] = cpad(f(inp["s5_c_re"])), cpad(f(inp["s5_c_im"]))
    dd = f(inp["s5_d"])
    ds = np.zeros((L, 128, 12, 32), np.float32)
    for g in range(24):
        it, g2, g8 = g // 2, g % 2, g % 8
        for h in range(16):
            ds[:, g8 * 16 + h, it, g2 * 16 + h] = dd[:, g, h]
    W["s5_dsel"] = ds.reshape(L, 128, 12 * 32)
    return W


_CACHE = {}
STAGES = None


def _swap_dirs(W):
    V = dict(W)
    sw16 = lambda a: np.ascontiguousarray(np.concatenate([a[..., 8:16], a[..., 0:8]], axis=-1))
    V["dtb_rep"] = sw16(W["dtb_rep"])
    V["alog_rep"] = sw16(W["alog_rep"])
    wt = W["win_tm"].reshape(W["win_tm"].shape[0], 128, 8, NTM).copy()
    wt[..., 2560:2576] = sw16(wt[..., 2560:2576])
    V["win_tm"] = wt.reshape(W["win_tm"].shape)
    cw = W["conv_w"].reshape(-1, 128, 6, 5)
    V["conv_w"] = np.ascontiguousarray(cw[..., ::-1]).reshape(W["conv_w"].shape)
    for k in ("s5_lre", "s5_lim", "s5_lstep", "s5_cre", "s5_cim"):
        V[k] = np.ascontiguousarray(W[k][:, ::-1])
    return V


def run_model(inp, L, pair=True, dbg=()):
    x = np.asarray(inp["x"], dtype=np.float32)
    nseq, Sfull = x.shape[0], x.shape[1]
    W = _prep_weights(inp, L)
    if not pair:
        S = Sfull
        ncore = 8 if nseq == 4 else nseq
        key = (S, L, tuple(dbg), 0)
        if key not in _CACHE:
            _CACHE[key] = build_program(S, L, dbg)
        W.update(_consts(S))
        maps = []
        for c in range(ncore):
            m = dict(W)
            m["xT"] = np.ascontiguousarray(x[c % nseq].T)
            maps.append(m)
        res = run_bass_kernel_spmd(_CACHE[key], maps, core_ids=list(range(ncore)))
        out = np.stack([np.ascontiguousarray(res.results[c]["outT"].T) for c in range(nseq)])
        return out, res
    S = Sfull // 2
    ncore = 2 * nseq
    key = (S, L, tuple(dbg), ncore)
    if key not in _CACHE:
        _CACHE[key] = build_program(S, L, dbg, pair=ncore)
    C0 = _consts(S)
    Wn = dict(W)
    Wn.update(C0)
    Wr = _swap_dirs(W)
    Wr.update(C0)
    idx = np.arange(128)
    s_, l_ = idx[:, None].astype(np.float32), idx[None, :].astype(np.float32)
    lg = np.log1p(-np.exp2(-5.0 - np.arange(8, dtype=np.float32))).astype(np.float32)
    rm = np.zeros((128, 2, 8, 128), np.float32)
    for h in range(8):
        rm[:, 0, h, :] = np.where(l_ > s_, np.exp(lg[h] * np.where(l_ > s_, l_ - s_, 0.0)), 0.0)
        rm[:, 1, h, :] = np.where(s_ >= l_, np.exp(lg[h] * np.where(s_ >= l_, s_ - l_, 0.0)), 0.0)
    Wr["c_retmask"] = rm.reshape(128, 2048)
    inv = (10000.0 ** (-np.arange(0, 64, 2, dtype=np.float32) / 64)).astype(np.float32)

    def rope(pos):
        ang = pos.astype(np.float32)[:, None] * inv[None, :]
        return np.concatenate([np.cos(ang), np.sin(ang)], axis=1).astype(np.float32)
    Wn["c_rope"] = rope(np.arange(S))
    Wr["c_rope"] = rope(Sfull - 1 - np.arange(S))
    Wn["pairsel"] = np.ascontiguousarray(np.broadcast_to(np.array([0.0, 1.0], np.float32), (128, 2)))
    Wr["pairsel"] = np.ascontiguousarray(np.broadcast_to(np.array([1.0, 0.0], np.float32), (128, 2)))
    maps = []
    for c in range(ncore):
        b, hf = c // 2, c % 2
        m = dict(Wn if hf == 0 else Wr)
        xs = x[b, :S] if hf == 0 else x[b, S:][::-1]
        m["xT"] = np.ascontiguousarray(xs.T)
        maps.append(m)
    res = run_bass_kernel_spmd(_CACHE[key], maps, core_ids=list(range(ncore)))
    out = np.empty((nseq, Sfull, D), np.float32)
    for c in range(ncore):
        b, hf = c // 2, c % 2
        o = res.results[c]["outT"].T
        if hf == 0:
            out[b, :S] = o
        else:
            out[b, S:] = o[::-1]
    return out, res


def kernel(**inputs):
    out, _ = run_model(inputs, 2, pair=False)
    return out.astype(np.float32)
```
